# Optimizing a Trainium2 kernel written in Bass

```python
import jax, jax.numpy as jnp
from jax import lax
import numpy as np

D_MODEL = 1024
BATCH = 8
SEQ = 2048
DEPTH = 4

CHUNK = 64
N_A = max(1, DEPTH // 2)
N_B = DEPTH - N_A
GDN_HEADS = 8
GDN_HEAD_DIM = 128
GDN_WIDTH = GDN_HEADS * GDN_HEAD_DIM
CONV_WIDTH = 4
SB_HEADS = 16
SB_HEAD_DIM = 64
SB_WIDTH = SB_HEADS * SB_HEAD_DIM
Q_BLOCK = 128
EPS = 1e-6

kernel_name = "yoco_gdn_stickbreaking_trunk"


def _rms(x, gain):
    xf = x.astype(jnp.float32)
    return xf * lax.rsqrt(jnp.mean(xf * xf, axis=-1, keepdims=True) + EPS) * gain.astype(jnp.float32)


def _l2norm(x):
    return x * lax.rsqrt(jnp.sum(x * x, axis=-1, keepdims=True) + EPS)


def _modulated_norm(x, gain, shift, scale):
    n = _rms(x, gain) * (1.0 + scale[:, None, :].astype(jnp.float32)) + shift[:, None, :].astype(jnp.float32)
    return n.astype(x.dtype)


def _causal_conv(x, w):
    C = x.shape[-1]
    return lax.conv_general_dilated(
        x, w[:, None, :].astype(x.dtype), window_strides=(1,),
        padding=[(CONV_WIDTH - 1, 0)], dimension_numbers=("NWC", "WIO", "NWC"),
        feature_group_count=C)


def _chunk_gated_delta(q, k, v, g, beta):
    B, T, H, Dk = q.shape
    Dv = v.shape[-1]
    N = T // CHUNK
    f32 = jnp.float32
    to_c = lambda a: a.astype(f32).reshape(B, N, CHUNK, H, a.shape[-1]).transpose(0, 3, 1, 2, 4)
    q, k, v = to_c(q), to_c(k), to_c(v)
    g = g.astype(f32).reshape(B, N, CHUNK, H).transpose(0, 3, 1, 2)
    beta = beta.astype(f32).reshape(B, N, CHUNK, H).transpose(0, 3, 1, 2)
    g = jnp.cumsum(g, axis=-1)
    k_beta = k * beta[..., None]
    v_beta = v * beta[..., None]
    tril = jnp.tril(jnp.ones((CHUNK, CHUNK), dtype=bool))
    strict = jnp.tril(jnp.ones((CHUNK, CHUNK), dtype=bool), -1)
    diff = g[..., :, None] - g[..., None, :]
    decay = jnp.where(tril, jnp.exp(jnp.where(tril, diff, 0.0)), 0.0)
    L = jnp.where(strict, jnp.einsum("bhncd,bhnsd->bhncs", k_beta, k) * decay, 0.0)
    a_mat = L + jnp.eye(CHUNK, dtype=f32)
    rhs = jnp.concatenate([v_beta, k_beta * jnp.exp(g)[..., None]], axis=-1)
    sol = lax.linalg.triangular_solve(a_mat, rhs, left_side=True, lower=True, unit_diagonal=True)
    u, w = sol[..., :Dv], sol[..., Dv:]
    attn = jnp.where(tril, jnp.einsum("bhncd,bhnsd->bhncs", q, k) * decay, 0.0)
    q_dec = q * jnp.exp(g)[..., None]
    k_dec = k * jnp.exp(g[..., -1:] - g)[..., None]
    g_last = jnp.exp(g[..., -1])

    def step(S, inp):
        q_i, w_i, u_i, a_i, k_i, gl_i = inp
        v_new = u_i - jnp.einsum("bhck,bhkv->bhcv", w_i, S)
        o = jnp.einsum("bhck,bhkv->bhcv", q_i, S) + jnp.einsum("bhcs,bhsv->bhcv", a_i, v_new)
        S = S * gl_i[..., None, None] + jnp.einsum("bhck,bhcv->bhkv", k_i, v_new)
        return S, o

    mv = lambda a: jnp.moveaxis(a, 2, 0)
    S0 = jnp.zeros((B, H, Dk, Dv), f32)
    _, o = lax.scan(step, S0, (mv(q_dec), mv(w), mv(u), mv(attn), mv(k_dec), mv(g_last)))
    return o.transpose(1, 0, 3, 2, 4).reshape(B, T, H, Dv)


def _gdn_mixer(h, w_in, conv_w, a_log, dt_bias, o_gain, w_out):
    B, T, _ = h.shape
    W, H, Dh = GDN_WIDTH, GDN_HEADS, GDN_HEAD_DIM
    proj = h @ w_in
    qkv = jax.nn.silu(_causal_conv(proj[..., :3 * W], conv_w)).astype(jnp.float32)
    z = proj[..., 3 * W:4 * W].astype(jnp.float32).reshape(B, T, H, Dh)
    b_raw = proj[..., 4 * W:4 * W + H].astype(jnp.float32)
    a_raw = proj[..., 4 * W + H:].astype(jnp.float32)
    q = _l2norm(qkv[..., :W].reshape(B, T, H, Dh)) * (Dh ** -0.5)
    k = _l2norm(qkv[..., W:2 * W].reshape(B, T, H, Dh))
    v = qkv[..., 2 * W:].reshape(B, T, H, Dh)
    beta = jax.nn.sigmoid(b_raw)
    g = -jnp.exp(a_log.astype(jnp.float32)) * jax.nn.softplus(a_raw + dt_bias.astype(jnp.float32))
    o = _chunk_gated_delta(q, k, v, g, beta)
    o = _rms(o, o_gain) * jax.nn.silu(z)
    return o.reshape(B, T, W).astype(h.dtype) @ w_out


def _sb_mixer(h, k_sh, v_sh, w_in, q_gain, w_out):
    B, T, _ = h.shape
    W, H, Dh = SB_WIDTH, SB_HEADS, SB_HEAD_DIM
    proj = h @ w_in
    q = _rms(proj[..., :W].reshape(B, T, H, Dh), q_gain).transpose(0, 2, 1, 3)
    z = proj[..., W:].astype(jnp.float32)
    outs = []
    for blk in range(T // Q_BLOCK):
        s0 = blk * Q_BLOCK
        end = s0 + Q_BLOCK
        kb, vb = k_sh[:, :, :end], v_sh[:, :, :end]
        logits = jnp.einsum("bhtd,bhsd->bhts", q[:, :, s0:end], kb) * (Dh ** -0.5)
        t_idx = s0 + jnp.arange(Q_BLOCK)
        s_idx = jnp.arange(end)
        mask = s_idx[None, :] < t_idx[:, None]
        log_keep = jnp.where(mask, jax.nn.log_sigmoid(-logits), 0.0)
        after = jnp.sum(log_keep, axis=-1, keepdims=True) - jnp.cumsum(log_keep, axis=-1)
        wts = jnp.where(mask, jnp.exp(jax.nn.log_sigmoid(logits) + after), 0.0)
        outs.append(jnp.einsum("bhts,bhsd->bhtd", wts, vb))
    o = jnp.concatenate(outs, axis=2).transpose(0, 2, 1, 3).reshape(B, T, W)
    o = o * jax.nn.silu(z)
    return o.astype(h.dtype) @ w_out


def setup_inputs(seed: int = 0) -> dict:
    key = jax.random.key(seed)
    ks = jax.random.split(key, 24)
    D = D_MODEL
    nrm = lambda k, shape, s: jax.random.normal(k, shape, jnp.float32) * s
    dt = jnp.exp(jax.random.uniform(ks[8], (N_A, GDN_HEADS), minval=np.log(1e-3), maxval=np.log(1e-1)))
    return {
        "x": nrm(ks[0], (BATCH, SEQ, D), 1.0),
        "c": nrm(ks[1], (BATCH, D), 1.0),
        "norm_g": 1.0 + nrm(ks[2], (DEPTH, D), 0.02),
        "w_ada": nrm(ks[3], (DEPTH, D, 3 * D), 0.5 * D ** -0.5),
        "b_ada": nrm(ks[4], (DEPTH, 3 * D), 0.02),
        "w_in_a": nrm(ks[5], (N_A, D, 4 * GDN_WIDTH + 2 * GDN_HEADS), D ** -0.5),
        "conv_w_a": nrm(ks[6], (N_A, CONV_WIDTH, 3 * GDN_WIDTH), CONV_WIDTH ** -0.5),
        "a_log_a": jnp.log(jax.random.uniform(ks[7], (N_A, GDN_HEADS), minval=1.0, maxval=16.0)),
        "dt_bias_a": dt + jnp.log(-jnp.expm1(-dt)),
        "o_gain_a": 1.0 + nrm(ks[9], (N_A, GDN_HEAD_DIM), 0.02),
        "w_out_a": nrm(ks[10], (N_A, GDN_WIDTH, D), GDN_WIDTH ** -0.5),
        "kv_norm_g": 1.0 + nrm(ks[11], (D,), 0.02),
        "w_ada_kv": nrm(ks[12], (D, 2 * D), 0.5 * D ** -0.5),
        "b_ada_kv": nrm(ks[13], (2 * D,), 0.02),
        "w_kv": nrm(ks[14], (D, 2 * SB_WIDTH), D ** -0.5),
        "k_gain": 1.0 + nrm(ks[15], (SB_HEAD_DIM,), 0.02),
        "w_in_b": nrm(ks[16], (N_B, D, 2 * SB_WIDTH), D ** -0.5),
        "q_gain_b": 1.0 + nrm(ks[17], (N_B, SB_HEAD_DIM), 0.02),
        "w_out_b": nrm(ks[18], (N_B, SB_WIDTH, D), SB_WIDTH ** -0.5),
    }


def reference(x, c, norm_g, w_ada, b_ada, w_in_a, conv_w_a, a_log_a, dt_bias_a, o_gain_a,
              w_out_a, kv_norm_g, w_ada_kv, b_ada_kv, w_kv, k_gain, w_in_b, q_gain_b, w_out_b):
    B, T, D = x.shape
    c_act = jax.nn.silu(c)
    k_sh = None
    v_sh = None
    for layer in range(DEPTH):
        mod = c_act @ w_ada[layer] + b_ada[layer]
        shift, scale, gate = mod[:, :D], mod[:, D:2 * D], mod[:, 2 * D:]
        h = _modulated_norm(x, norm_g[layer], shift, scale)
        if layer < N_A:
            i = layer
            y = _gdn_mixer(h, w_in_a[i], conv_w_a[i], a_log_a[i], dt_bias_a[i], o_gain_a[i], w_out_a[i])
        else:
            if layer == N_A:
                mkv = c_act @ w_ada_kv + b_ada_kv
                hk = _modulated_norm(x, kv_norm_g, mkv[:, :D], mkv[:, D:])
                kv = hk @ w_kv
                k_sh = _rms(kv[..., :SB_WIDTH].reshape(B, T, SB_HEADS, SB_HEAD_DIM), k_gain).transpose(0, 2, 1, 3)
                v_sh = kv[..., SB_WIDTH:].astype(jnp.float32).reshape(B, T, SB_HEADS, SB_HEAD_DIM).transpose(0, 2, 1, 3)
            i = layer - N_A
            y = _sb_mixer(h, k_sh, v_sh, w_in_b[i], q_gain_b[i], w_out_b[i])
        x = x + (gate[:, None, :] * y).astype(x.dtype)
    return x
```

```python
import numpy as np
import concourse.bass as bass
import concourse.mybir as mybir
from concourse.bass_utils import run_bass_kernel_spmd
from contextlib import ExitStack

F32 = mybir.dt.float32
BF16 = mybir.dt.bfloat16
F32R = mybir.dt.float32r
AF = mybir.ActivationFunctionType
ALU = mybir.AluOpType

T = 2048
D = 1024
NB = 4
EPS = 1e-6


class Tok:
    __slots__ = ("name", "w", "r")

    def __init__(self, name=""):
        self.name = name
        self.w = None
        self.r = []


class Op:
    __slots__ = ("eng", "fn", "deps", "idx", "need_inc", "ev", "is_dma", "dsem", "dval")

    def __init__(self, eng, fn):
        self.eng = eng
        self.fn = fn
        self.deps = []
        self.idx = None
        self.need_inc = False
        self.ev = None
        self.is_dma = False


class Prog:
    ENG = ("pe", "act", "dve", "pool", "sp")
    SEM_LIMIT = 30000
    NDMA = 16
    NEAR = 6

    def __init__(self, nc):
        self.nc = nc
        self.q = {e: [] for e in self.ENG}
        self.toks = {}

    def tk(self, name):
        t = self.toks.get(name)
        if t is None:
            t = Tok(name)
            self.toks[name] = t
        return t

    def op(self, eng, fn, reads=(), writes=(), dma=False):
        o = Op(eng, fn)
        o.is_dma = dma
        o.idx = len(self.q[eng])
        deps = set()
        rd, wr = [], []
        for t in reads:
            if isinstance(t, str) and t.startswith("ps") and t[2].isdigit():
                wr.append(t[:3])
            else:
                rd.append(t)
        for t in writes:
            if isinstance(t, str) and t.startswith("ps") and t[2].isdigit():
                wr.append(t[:3])
            else:
                wr.append(t)
        reads = [self.tk(t) if isinstance(t, str) else t for t in rd]
        writes = [self.tk(t) if isinstance(t, str) else t for t in dict.fromkeys(wr)]
        for t in reads:
            if t.w is not None:
                deps.add(t.w)
        for t in writes:
            if t.w is not None:
                deps.add(t.w)
            for r in t.r:
                deps.add(r)
        deps.discard(o)
        o.deps = list(deps)
        for t in reads:
            t.r.append(o)
        for t in writes:
            t.w = o
            t.r = []
        self.q[eng].append(o)
        return o

    def finalize(self, final_wait_ops=()):
        nc = self.nc
        waits = {}
        for e in self.ENG:
            seen = {}
            for o in self.q[e]:
                best = {}
                res = []
                for d in o.deps:
                    if d.is_dma:
                        res.append(d)
                        continue
                    if d.eng == o.eng:
                        if e == "pe" or o.is_dma:
                            if not o.is_dma:
                                continue
                        if (not o.is_dma) and o.idx - d.idx > self.NEAR:
                            continue
                    if d.eng not in best or best[d.eng].idx < d.idx:
                        best[d.eng] = d
                for d in best.values():
                    if seen.get(d.eng, -1) >= d.idx:
                        continue
                    seen[d.eng] = d.idx
                    res.append(d)
                waits[o] = res
                for d in res:
                    if not d.is_dma:
                        d.need_inc = True
        stack = ExitStack()
        for e in self.ENG:
            n = sum(1 for o in self.q[e] if o.need_inc and not o.is_dma)
            k = max(1, (n + self.SEM_LIMIT - 1) // self.SEM_LIMIT)
            sems = [stack.enter_context(nc.semaphore(f"s_{e}_{i}")) for i in range(k)]
            c = 0
            for o in self.q[e]:
                if o.need_inc and not o.is_dma:
                    o.ev = (sems[c // self.SEM_LIMIT], c % self.SEM_LIMIT + 1)
                    c += 1
        dsems = {e: [stack.enter_context(nc.semaphore(f"s_dma_{e}_{i}")) for i in range(self.NDMA)]
                 for e in ("sp", "pool")}
        for e in ("sp", "pool"):
            di = 0
            for o in self.q[e]:
                if o.is_dma:
                    o.dsem = dsems[e][di % self.NDMA]
                    o.dval = 16 * (di // self.NDMA + 1)
                    o.ev = (o.dsem, o.dval)
                    di += 1
        final = list(final_wait_ops)
        with nc.Block() as block:
            def run(engname, eng):
                for o in self.q[engname]:
                    for d in waits[o]:
                        eng.wait_ge(d.ev[0], d.ev[1])
                    if o.is_dma and o.dval > 16:
                        eng.wait_ge(o.dsem, o.dval - 16)
                    ins = o.fn(eng)
                    if o.is_dma:
                        ins.then_inc(o.dsem, 16)
                    elif o.need_inc:
                        ins.then_inc(o.ev[0], 1)
                if engname == "sp":
                    for o in final:
                        eng.wait_ge(o.ev[0], o.ev[1])

            @block.tensor
            def _(t):
                run("pe", t)

            @block.scalar
            def _(t):
                run("act", t)

            @block.vector
            def _(t):
                run("dve", t)

            @block.gpsimd
            def _(t):
                run("pool", t)

            @block.sync
            def _(t):
                run("sp", t)
        stack.close()


C_IDENT, C_ONES, C_TRIU_I, C_TRIL_S, C_TRIU_S, C_BDTRIL_S, C_OFF, C_BDONES = range(8)
NCST = 8


def _consts():
    p = np.arange(128)[:, None]
    f = np.arange(128)[None, :]
    m = np.zeros((128, NCST, 128), np.float32)
    m[:, C_IDENT] = (p == f)
    m[:, C_ONES] = 1.0
    m[:, C_TRIU_I] = (p <= f)
    m[:, C_TRIL_S] = (f < p)
    m[:, C_TRIU_S] = (p < f)
    m[:, C_BDTRIL_S] = (f < p) & ((p // 64) == (f // 64))
    m[:, C_OFF] = (p >= 64) & (f < 64)
    m[:, C_BDONES] = ((p // 64) == (f // 64))
    return m.reshape(128, NCST * 128)


def _fm(v):
    v = np.asarray(v, np.float32).reshape(-1, 128)
    return np.ascontiguousarray(v.T)


SP_C = 0
SP_LAYER = 8
SP_KV = 136
SP_GDN = 160
SP_KG = 386
SP_QG = 387
NSP = 389


def _small_params(inp, b):
    sp = np.zeros((128, NSP), np.float32)
    sp[:, 0:8] = _fm(inp["c"][b])
    for l in range(4):
        o = SP_LAYER + 32 * l
        sp[:, o:o + 8] = _fm(inp["norm_g"][l])
        sp[:, o + 8:o + 32] = _fm(inp["b_ada"][l])
    sp[:, SP_KV:SP_KV + 8] = _fm(inp["kv_norm_g"])
    sp[:, SP_KV + 8:SP_KV + 24] = _fm(inp["b_ada_kv"])
    for l in range(2):
        o = SP_GDN + 113 * l
        cw = np.asarray(inp["conv_w_a"][l], np.float32)
        for j in range(4):
            sp[:, o + 24 * j:o + 24 * j + 24] = _fm(cw[j])
        sp[:, o + 96] = np.asarray(inp["o_gain_a"][l], np.float32)
        sp[:, o + 97:o + 105] = np.asarray(inp["a_log_a"][l], np.float32)[None, :]
        sp[:, o + 105:o + 113] = np.asarray(inp["dt_bias_a"][l], np.float32)[None, :]
    sp[:, SP_KG] = np.tile(np.asarray(inp["k_gain"], np.float32), 2)
    for l in range(2):
        sp[:, SP_QG + l] = np.tile(np.asarray(inp["q_gain_b"][l], np.float32), 2)
    return sp


def build(nlayers=4):
    nc = bass.Bass("TRN2", target_bir_lowering=False)
    dram = lambda n, s, k="ExternalInput": nc.dram_tensor(n, s, F32, kind=k).ap()
    x_d = dram("x", [T, D])
    spm_d = dram("spm", [128, NSP])
    cst_d = dram("cst", [128, NCST * 128])
    w_ada_d = dram("w_ada", [4, D, 3 * D])
    w_inp_d = dram("w_inp", [2, D, 8 * 512])
    w_ba_d = dram("w_ba", [2, D, 16])
    w_outa_d = dram("w_out_a", [2, D, D])
    w_adakv_d = dram("w_ada_kv", [D, 2 * D])
    w_kv_d = dram("w_kv", [D, 2 * D])
    w_inb_d = dram("w_in_b", [2, D, 2 * D])
    w_outb_d = dram("w_out_b", [2, D, D])
    out_d = dram("out", [T, D], "ExternalOutput")

    def wview(ap2d, c0, n):
        return ap2d[:, c0:c0 + n].rearrange("(c p) n -> p c n", p=128)

    sb = lambda n, s, d=F32: nc.alloc_sbuf_tensor("sb_" + n, s, d).ap()
    P = Prog(nc)

    def v3(ap, a):
        return ap.rearrange("p (a b) -> p a b", a=a)

    xT = sb("xT", [128, 8 * T])
    cst = sb("cst", [128, NCST * 128])
    cstb = sb("cstb", [128, NCST * 128], BF16)
    spm = sb("spm", [128, NSP])
    cact = sb("cact", [128, 8])
    modT = sb("modT", [128, 5 * 24])
    Acol = sb("Acol", [128, 5 * 8])
    hT = sb("hT", [128, 8 * 512], BF16)
    ogT = sb("ogT", [128, 8 * 512], BF16)
    sqb = sb("sqb", [128, 2 * 512], BF16)
    rstd = sb("rstd", [128, 512])
    bscr = sb("bscr", [128, 8])
    tmpn = [sb(f"tmpn{i}", [128, 512]) for i in range(2)]
    wq = [sb(f"wq{i}", [128, 8 * 512], BF16) for i in range(2)]
    ps = [nc.alloc_psum_tensor(f"ps{i}", [128, 512], F32).ap() for i in range(8)]
    gstack = ExitStack()
    sbp = lambda n, s, d=F32: gstack.enter_context(nc.sbuf_tensor("sb_" + n, s, d))[:]
    sb_keep = sb
    lnb = sbp("lnb", [128, 512])
    mrow = sbp("mrow", [1, 256])
    wa = [sbp("wa0", [128, 8 * 256])] * 2
    LU = sbp("LU", [128, 2048])
    xld = [LU[:, i * 1024:(i + 1) * 1024] for i in range(2)]

    def cm(i, bf=False):
        return (cstb if bf else cst)[:, i * 128:(i + 1) * 128]

    def xsl(c, tb):
        return xT[:, c * T + tb * 512:c * T + (tb + 1) * 512]

    P.op("sp", lambda e: e.dma_start(out=cst, in_=cst_d), writes=["cst"], dma=True)
    P.op("sp", lambda e: e.dma_start(out=spm, in_=spm_d), writes=["spm"], dma=True)
    P.op("pool", lambda e: e.tensor_copy(out=cstb, in_=cst), reads=["cst"], writes=["cstb"])
    P.op("act", lambda e: e.activation(out=cact, in_=spm[:, 0:8], func=AF.Silu), reads=["spm"], writes=["cact"])

    wa_i = [0]

    def emit_mod(wd2, ncols, bias0, dst0, slot):
        nf = ncols // 128
        buf = wa[0]
        for piece in range(ncols // 256):
            P.op("sp", lambda e, piece=piece: e.dma_start(out=v3(buf, 8), in_=wview(wd2, piece * 256, 256)),
                 writes=["wa0"], dma=True)
            for c in range(8):
                P.op("pe", lambda e, c=c: e.matmul(ps[7][0:1, 0:256], cact[:, c:c + 1], buf[:, c * 256:(c + 1) * 256],
                                                   start=(c == 0), stop=(c == 7)), reads=["wa0", "cact"], writes=["ps7"])
            P.op("act", lambda e: e.copy(out=mrow, in_=ps[7][0:1, 0:256]), reads=["ps7"], writes=["mrow"])
            for fc in range(2):
                col = piece * 2 + fc
                P.op("pe", lambda e, fc=fc, col=col: e.matmul(ps[7][:, 256 + col:256 + col + 1], mrow[0:1, fc * 128:(fc + 1) * 128],
                                                              cst[0:1, C_ONES * 128:C_ONES * 128 + 1], start=True, stop=True),
                     reads=["mrow", "cst"], writes=["ps7"])
        P.op("dve", lambda e: e.tensor_tensor(out=modT[:, dst0:dst0 + nf], in0=ps[7][:, 256:256 + nf],
                                              in1=spm[:, bias0:bias0 + nf], op=ALU.add),
             reads=["ps7", "spm"], writes=[f"mod{slot}"])

    def emit_layer_mod(l):
        emit_mod(w_ada_d[l], 3 * D, SP_LAYER + 32 * l + 8, 24 * l, l)
        g0 = SP_LAYER + 32 * l
        P.op("dve", lambda e: e.scalar_tensor_tensor(out=Acol[:, 8 * l:8 * l + 8], in0=modT[:, 24 * l + 8:24 * l + 16],
                                                     scalar=1.0, in1=spm[:, g0:g0 + 8], op0=ALU.add, op1=ALU.mult),
             reads=[f"mod{l}", "spm"], writes=[f"A{l}"])

    def emit_kv_mod():
        emit_mod(w_adakv_d, 2 * D, SP_KV + 8, 96, 4)
        P.op("dve", lambda e: e.scalar_tensor_tensor(out=Acol[:, 32:40], in0=modT[:, 104:112],
                                                     scalar=1.0, in1=spm[:, SP_KV:SP_KV + 8], op0=ALU.add, op1=ALU.mult),
             reads=["mod4", "spm"], writes=["A4"])

    for n in range(16):
        bi = n % 2
        tb = n // 4
        P.op("sp", lambda e, n=n, bi=bi: e.dma_start(out=xld[bi], in_=x_d[n * 128:(n + 1) * 128, :]),
             writes=[f"xld{bi}"], dma=True)
        for half in range(2):
            bk = (2 * n + half) % 4
            for c4 in range(4):
                c = half * 4 + c4
                P.op("pe", lambda e, bi=bi, c=c, c4=c4, bk=bk: e.transpose(
                    ps[bk][:, c4 * 128:(c4 + 1) * 128], xld[bi][:, c * 128:(c + 1) * 128], cm(C_IDENT)),
                    reads=[f"xld{bi}", "cst"], writes=[f"ps{bk}"])
            dst = v3(xT[:, half * 4 * T:(half * 4 + 4) * T], 4)[:, :, n * 128:(n + 1) * 128]
            src = v3(ps[bk], 4)
            if half == 0:
                P.op("act", lambda e, dst=dst, src=src: e.copy(out=dst, in_=src), reads=[f"ps{bk}"], writes=[f"xT{tb}"])
            else:
                P.op("dve", lambda e, dst=dst, src=src: e.tensor_copy(out=dst, in_=src), reads=[f"ps{bk}"], writes=[f"xT{tb}"])

    def emit_norm_block(tb, slot, shift0):
        for c in range(8):
            sq_ = sqb[:, (c % 2) * 512:(c % 2 + 1) * 512]
            P.op("act", lambda e, c=c, sq_=sq_: e.activation(out=sq_, in_=xsl(c, tb), func=AF.Square),
                 reads=[f"xT{tb}"], writes=[f"sqb{c % 2}"])
            P.op("pe", lambda e, c=c, sq_=sq_: e.matmul(ps[4], cm(C_ONES, True), sq_,
                                               start=(c == 0), stop=(c == 7)), reads=[f"sqb{c % 2}", "cstb"], writes=["ps4"])
        P.op("act", lambda e: e.activation(out=rstd, in_=ps[4], func=AF.Ln, bias=EPS, scale=1.0 / D), reads=["ps4"], writes=["rstd"])
        P.op("act", lambda e: e.activation(out=rstd, in_=rstd, func=AF.Exp, scale=-0.5), reads=["rstd"], writes=["rstd"])
        for c in range(8):
            tm = tmpn[c % 2]
            P.op("dve", lambda e, c=c, tm=tm: e.scalar_tensor_tensor(
                out=tm, in0=xsl(c, tb), scalar=Acol[:, 8 * slot + c:8 * slot + c + 1], in1=rstd,
                op0=ALU.mult, op1=ALU.mult), reads=[f"xT{tb}", f"A{slot}", "rstd"], writes=[f"tmpn{c % 2}"])
            P.op("act", lambda e, c=c, tm=tm: e.activation(
                out=hT[:, c * 512:(c + 1) * 512], in_=tm, func=AF.Identity,
                bias=modT[:, shift0 + c:shift0 + c + 1], scale=1.0),
                reads=[f"tmpn{c % 2}", f"mod{slot}"], writes=["hT"])

    def emit_outproj(wd2, tb, slot, og=None):
        og = ogT if og is None else og
        for half in range(2):
            P.op("pool", lambda e, half=half: e.dma_start(out=v3(wq[half], 8), in_=wview(wd2, half * 512, 512)),
                 writes=[f"wq{half}"], dma=True)
        for m in range(8):
            half, mm_ = divmod(m, 4)
            bk = m % 4
            for h in range(8):
                P.op("pe", lambda e, half=half, mm_=mm_, h=h, bk=bk: e.matmul(
                    ps[bk], wq[half][:, h * 512 + mm_ * 128:h * 512 + (mm_ + 1) * 128], og[:, h * 512:(h + 1) * 512],
                    start=(h == 0), stop=(h == 7)), reads=[f"wq{half}", "ogT"], writes=[f"ps{bk}"])
            P.op("dve", lambda e, m=m, bk=bk: e.scalar_tensor_tensor(
                out=xsl(m, tb), in0=ps[bk], scalar=modT[:, 24 * slot + 16 + m:24 * slot + 17 + m], in1=xsl(m, tb),
                op0=ALU.mult, op1=ALU.add), reads=[f"ps{bk}", f"mod{slot}", f"xT{tb}"], writes=[f"xT{tb}"])

    sb = sbp
    wba = sb("wba", [128, 8 * 16], BF16)
    carry = sb("carry", [128, 8 * 3 * 4])
    pre = [sb(f"pre{j}", [128, 516]) for j in range(3)]
    acc = [sb(f"acc{j}", [128, 512]) for j in range(3)]
    zs = [sb(f"zs{i}", [128, 512], BF16) for i in range(3)]
    rn = sb("rn", [128, 512])
    rn3 = sb("rn3", [128, 512])
    sqc = sb("sqc", [128, 512], BF16)
    qTb = [sb(f"qTb{i}", [128, 512], BF16) for i in range(3)]
    kTb = [sb(f"kTb{i}", [128, 512], BF16) for i in range(3)]
    qdec = [sb(f"qdec{i}", [128, 512], BF16) for i in range(2)]
    kdec = [sb(f"kdec{i}", [128, 512], BF16) for i in range(3)]
    vb = [sb(f"vb{i}", [128, 512]) for i in range(3)]
    egbc = sb("egbc", [128, 512])
    E1 = sb("E1", [128, 512])
    E2 = sb("E2", [128, 512])
    t1 = sb("t1", [128, 512])
    t2 = sb("t2", [128, 512])
    Lb = [LU[:, i * 512:(i + 1) * 512] for i in range(2)]
    Ub = [LU[:, (2 + i) * 512:(3 + i) * 512] for i in range(2)]
    Yb = [sb(f"Yb{i}", [128, 512]) for i in range(2)]
    Offb = sb("Offb", [128, 512])
    Ybf = [sb(f"Ybf{i}", [128, 512], BF16) for i in range(2)]
    attnT = [sb(f"attnT{i}", [128, 512], BF16) for i in range(2)]
    rv = [sb(f"rv{i}", [128, 128], BF16) for i in range(2)]
    vnw = [sb(f"vnw{i}", [128, 128], BF16) for i in range(2)]
    Sst = sb("Sst", [128, 8 * 128])
    Sbf = sb("Sbf", [128, 8 * 128], BF16)
    ob = sb("ob", [128, 512])
    tkU = sb("tkU", [128, 32])
    tkG = sb("tkG", [128, 32])
    tkBeta = sb("tkBeta", [128, 32])
    tkGc = sb("tkGc", [128, 32])
    tkNbeg = sb("tkNbeg", [128, 32])
    tkGl = sb("tkGl", [128, 32])
    tkDks = sb("tkDks", [128, 32])
    tkEgl = sb("tkEgl", [128, 32])
    negA = sb("negA", [128, 8])
    sb = sb_keep

    def emit_gdn_layer(l):
        slot = l
        g0 = SP_GDN + 113 * l
        P.op("pool", lambda e: e.dma_start(out=v3(wba, 8), in_=w_ba_d[l].rearrange("(c p) n -> p c n", p=128)),
             writes=["wba"], dma=True)
        P.op("pool", lambda e: e.memset(carry, 0.0), writes=["carry"])
        P.op("pool", lambda e: e.memset(Sst, 0.0), writes=[f"S{i}" for i in range(8)])
        P.op("pool", lambda e: e.memset(Sbf, 0.0), writes=[f"Sbf{i}" for i in range(8)])
        P.op("act", lambda e: e.activation(out=negA, in_=spm[:, g0 + 97:g0 + 105], func=AF.Exp), reads=["spm"], writes=["negA"])
        P.op("dve", lambda e: e.tensor_scalar(out=negA, in0=negA, scalar1=-1.0, scalar2=None, op0=ALU.mult), reads=["negA"], writes=["negA"])
        for tb in range(NB):
            emit_norm_block(tb, slot, 24 * l)
            for n in range(4):
                for c in range(8):
                    P.op("pe", lambda e, n=n, c=c: e.matmul(
                        ps[7][:, n * 16:(n + 1) * 16], hT[:, c * 512 + n * 128:c * 512 + (n + 1) * 128],
                        wba[:, c * 16:(c + 1) * 16], start=(c == 0), stop=(c == 7)),
                        reads=["hT", "wba"], writes=["ps7a"])
            ba3 = v3(ps[7][:, 0:64], 4)
            for n in range(4):
                P.op("dve", lambda e, n=n: e.tensor_tensor(out=tkU[:, n * 8:(n + 1) * 8], in0=ps[7][:, n * 16 + 8:n * 16 + 16],
                                                          in1=spm[:, g0 + 105:g0 + 113], op=ALU.add),
                     reads=["ps7a", "spm"], writes=["tkU"])
            P.op("act", lambda e: e.activation(out=v3(tkBeta, 4), in_=ba3[:, :, 0:8], func=AF.Exp, scale=-1.0),
                 reads=["ps7a"], writes=["tkBeta"])
            P.op("act", lambda e: e.activation(out=tkU, in_=tkU, func=AF.Exp), reads=["tkU"], writes=["tkU"])
            P.op("act", lambda e: e.activation(out=tkU, in_=tkU, func=AF.Ln, bias=1.0, scale=1.0), reads=["tkU"], writes=["tkU"])
            for n in range(4):
                P.op("dve", lambda e, n=n: e.tensor_tensor(out=tkG[:, n * 8:(n + 1) * 8], in0=tkU[:, n * 8:(n + 1) * 8],
                                                          in1=negA, op=ALU.mult), reads=["tkU", "negA"], writes=["tkG"])
            P.op("dve", lambda e: e.tensor_scalar(out=tkBeta, in0=tkBeta, scalar1=1.0, scalar2=None, op0=ALU.add),
                 reads=["tkBeta"], writes=["tkBeta"])
            P.op("dve", lambda e: e.reciprocal(out=tkBeta, in_=tkBeta), reads=["tkBeta"], writes=["tkBeta"])
            for n in range(4):
                P.op("pe", lambda e, n=n: e.matmul(ps[7][:, 64 + n * 8:64 + (n + 1) * 8], cm(C_TRIU_I), tkG[:, n * 8:(n + 1) * 8],
                                                   start=True, stop=True), reads=["cst", "tkG"], writes=["ps7b"])
                P.op("pe", lambda e, n=n: e.matmul(ps[7][:, 96 + n * 8:96 + (n + 1) * 8], cm(C_ONES), tkG[:, n * 8:(n + 1) * 8],
                                                   start=True, stop=True), reads=["cst", "tkG"], writes=["ps7b"])
            P.op("dve", lambda e: e.tensor_copy(out=tkGc, in_=ps[7][:, 64:96]), reads=["ps7b"], writes=["tkGc"])
            P.op("dve", lambda e: e.tensor_copy(out=tkGl, in_=ps[7][:, 96:128]), reads=["ps7b"], writes=["tkGl"])
            P.op("act", lambda e: e.activation(out=tkEgl, in_=tkGl, func=AF.Exp), reads=["tkGl"], writes=["tkEgl"])
            P.op("dve", lambda e: e.tensor_tensor(out=tkDks, in0=tkGl, in1=tkGc, op=ALU.subtract), reads=["tkGl", "tkGc"], writes=["tkDks"])
            P.op("act", lambda e: e.activation(out=tkDks, in_=tkDks, func=AF.Exp), reads=["tkDks"], writes=["tkDks"])
            P.op("act", lambda e: e.activation(out=tkNbeg, in_=tkGc, func=AF.Exp), reads=["tkGc"], writes=["tkNbeg"])
            P.op("dve", lambda e: e.scalar_tensor_tensor(out=tkNbeg, in0=tkNbeg, scalar=-1.0, in1=tkBeta, op0=ALU.mult, op1=ALU.mult),
                 reads=["tkNbeg", "tkBeta"], writes=["tkNbeg"])

            emit_gdn_block_heads(l, tb, g0)
            emit_outproj(w_outa_d[l], tb, slot)

    def run_interleaved(gens):
        gens = list(gens)
        while gens:
            for g_ in list(gens):
                try:
                    next(g_)
                except StopIteration:
                    gens.remove(g_)

    def gdn_A(l, tb, h, g0):
        a = h % 3
        wb = wq[h % 2]
        wtk = f"wq{h % 2}"
        P.op("pool", lambda e: e.dma_start(out=v3(wb, 8), in_=wview(w_inp_d[l], h * 512, 512)), writes=[wtk], dma=True)
        yield
        for j in range(3):
            for c in range(8):
                P.op("pe", lambda e, j=j, c=c: e.matmul(ps[j], wb[:, c * 512 + j * 128:c * 512 + (j + 1) * 128],
                                                        hT[:, c * 512:(c + 1) * 512], start=(c == 0), stop=(c == 7)),
                     reads=[wtk, "hT"], writes=[f"ps{j}"])
                yield
        for j in range(3):
            cc = (h * 3 + j) * 4
            P.op("pool", lambda e, j=j, cc=cc: e.tensor_copy(out=pre[j][:, 0:3], in_=carry[:, cc:cc + 3]),
                 reads=["carry"], writes=[f"pre{j}"])
            P.op("act", lambda e, j=j: e.copy(out=pre[j][:, 3:515], in_=ps[j]), reads=[f"ps{j}"], writes=[f"pre{j}"])
            yield
            P.op("pool", lambda e, j=j, cc=cc: e.tensor_copy(out=carry[:, cc:cc + 3], in_=pre[j][:, 512:515]),
                 reads=[f"pre{j}"], writes=["carry"])
            yield
        for c in range(8):
            P.op("pe", lambda e, c=c: e.matmul(ps[0], wb[:, c * 512 + 384:c * 512 + 512],
                                               hT[:, c * 512:(c + 1) * 512], start=(c == 0), stop=(c == 7)),
                 reads=[wtk, "hT"], writes=["ps0"])
            yield
        P.op("act", lambda e: e.activation(out=zs[a], in_=ps[0], func=AF.Silu), reads=["ps0"], writes=[f"zs{a}"])
        yield
        wc = lambda tap, j: spm[:, g0 + 24 * tap + 8 * j + h:g0 + 24 * tap + 8 * j + h + 1]
        for tap in range(4):
            for j in range(3):
                if tap == 0:
                    P.op("dve", lambda e, j=j: e.tensor_scalar(out=acc[j], in0=pre[j][:, 0:512], scalar1=wc(0, j), scalar2=0.0,
                                                               op0=ALU.mult, op1=ALU.add), reads=[f"pre{j}", "spm"], writes=[f"acc{j}"])
                else:
                    P.op("dve", lambda e, j=j, tap=tap: e.scalar_tensor_tensor(
                        out=acc[j], in0=pre[j][:, tap:tap + 512], scalar=wc(tap, j), in1=acc[j], op0=ALU.mult, op1=ALU.add),
                        reads=[f"pre{j}", "spm", f"acc{j}"], writes=[f"acc{j}"])
                yield
        for j in range(3):
            P.op("act", lambda e, j=j: e.activation(out=acc[j], in_=acc[j], func=AF.Silu), reads=[f"acc{j}"], writes=[f"acc{j}"])
            yield
        for j in range(2):
            P.op("act", lambda e, j=j: e.activation(out=sqb[:, j * 512:(j + 1) * 512], in_=acc[j], func=AF.Square),
                 reads=[f"acc{j}"], writes=[f"sqb{j}"])
            yield
            P.op("pe", lambda e, j=j: e.matmul(ps[1 + j], cm(C_ONES, True), sqb[:, j * 512:(j + 1) * 512], start=True, stop=True),
                 reads=[f"sqb{j}", "cstb"], writes=[f"ps{1 + j}"])
            yield
        for j in range(2):
            lb = lnb if j == 0 else rn
            tk_ = "lnb" if j == 0 else "rn"
            P.op("act", lambda e, j=j, lb=lb: e.activation(out=lb, in_=ps[1 + j], func=AF.Ln, bias=EPS, scale=1.0),
                 reads=[f"ps{1 + j}"], writes=[tk_])
            yield
            P.op("act", lambda e, j=j, lb=lb: e.activation(out=lb, in_=lb, func=AF.Exp, scale=-0.5,
                                                          bias=(float(np.log(128.0 ** -0.5)) if j == 0 else 0.0)),
                 reads=[tk_], writes=[tk_])
            yield
        P.op("dve", lambda e: e.tensor_tensor(out=qTb[a], in0=acc[0], in1=lnb, op=ALU.mult), reads=["acc0", "lnb"], writes=[f"qTb{a}"])
        yield
        P.op("dve", lambda e: e.tensor_tensor(out=acc[1], in0=acc[1], in1=rn, op=ALU.mult), reads=["acc1", "rn"], writes=["acc1"])
        yield
        P.op("pool", lambda e: e.tensor_copy(out=kTb[a], in_=acc[1]), reads=["acc1"], writes=[f"kTb{a}"])
        yield
        for n in range(4):
            sl = slice(n * 128, (n + 1) * 128)
            P.op("pe", lambda e, sl=sl: e.transpose(ps[0][:, sl], acc[1][:, sl], cm(C_IDENT)), reads=["acc1", "cst"], writes=["ps0"])
            yield
        for n in range(4):
            sl = slice(n * 128, (n + 1) * 128)
            P.op("pe", lambda e, sl=sl: e.transpose(ps[1][:, sl], acc[2][:, sl], cm(C_IDENT)), reads=["acc2", "cst"], writes=["ps1"])
            yield
        for n in range(4):
            sl = slice(n * 128, (n + 1) * 128)
            col = n * 8 + h
            P.op("act", lambda e, sl=sl, col=col: e.activation(out=kdec[a][:, sl], in_=ps[0][:, sl], func=AF.Identity,
                                                               scale=tkDks[:, col:col + 1]), reads=["ps0", "tkDks"], writes=[f"kdec{a}"])
            yield
        for n in range(4):
            sl = slice(n * 128, (n + 1) * 128)
            col = n * 8 + h
            P.op("act", lambda e, sl=sl, col=col: e.activation(out=vb[a][:, sl], in_=ps[1][:, sl], func=AF.Identity,
                                                               scale=tkBeta[:, col:col + 1]), reads=["ps1", "tkBeta"], writes=[f"vb{a}"])
            yield

    def gdn_B(l, tb, h, g0):
        a = h % 3
        bs = h % 2
        q_, k_, kd_, vb_, zs_ = qTb[a], kTb[a], kdec[a], vb[a], zs[a]
        qt, kt, kdt, vbt, zst = f"qTb{a}", f"kTb{a}", f"kdec{a}", f"vb{a}", f"zs{a}"
        tiles = [(n, slice(n * 128, (n + 1) * 128), n * 8 + h) for n in range(4)]
        for n, sl, col in tiles:
            P.op("pe", lambda e, sl=sl: e.matmul(ps[3][:, sl], k_[:, sl], k_[:, sl], start=True, stop=True), reads=[kt], writes=["ps3"])
            P.op("pe", lambda e, sl=sl: e.matmul(ps[4][:, sl], k_[:, sl], q_[:, sl], start=True, stop=True), reads=[kt, qt], writes=["ps4"])
            P.op("pool", lambda e, sl=sl, col=col: e.tensor_scalar(out=E1[:, sl], in0=cm(C_TRIU_I), scalar1=tkG[:, col:col + 1], scalar2=0.0,
                                                                   op0=ALU.mult, op1=ALU.add), reads=["cst", "tkG"], writes=["E1"])
            yield
            P.op("pe", lambda e, sl=sl: e.matmul(ps[5][:, sl], cm(C_ONES), E1[:, sl], start=True, stop=True), reads=["cst", "E1"], writes=["ps5"])
            yield
        P.op("act", lambda e: e.activation(out=egbc, in_=ps[5], func=AF.Exp), reads=["ps5"], writes=["egbc"])
        yield
        for n, sl, col in tiles:
            P.op("dve", lambda e, sl=sl, col=col: e.tensor_scalar(out=E1[:, sl], in0=ps[5][:, sl], scalar1=tkGc[:, col:col + 1], scalar2=0.0,
                                                                  op0=ALU.subtract, op1=ALU.min), reads=["ps5", "tkGc"], writes=["E1"])
            yield
            P.op("dve", lambda e, sl=sl, col=col: e.tensor_scalar(out=E2[:, sl], in0=ps[5][:, sl], scalar1=tkGc[:, col:col + 1], scalar2=0.0,
                                                                  op0=ALU.subtract, op1=ALU.max), reads=["ps5", "tkGc"], writes=["E2"])
            yield
        P.op("act", lambda e: e.activation(out=E1, in_=E1, func=AF.Exp), reads=["E1"], writes=["E1"])
        yield
        P.op("act", lambda e: e.activation(out=E2, in_=E2, func=AF.Exp, scale=-1.0), reads=["E2"], writes=["E2"])
        yield
        for n, sl, col in tiles:
            P.op("dve", lambda e, sl=sl, col=col: e.scalar_tensor_tensor(out=t1[:, sl], in0=ps[3][:, sl], scalar=tkBeta[:, col:col + 1],
                                                                      in1=E2[:, sl], op0=ALU.mult, op1=ALU.mult),
                 reads=["ps3", "E2", "tkBeta"], writes=["t1"])
            yield
        P.op("dve", lambda e: e.tensor_tensor(out=t2, in0=ps[4], in1=E1, op=ALU.mult), reads=["ps4", "E1"], writes=["t2"])
        yield
        P.op("pool", lambda e: e.tensor_tensor(out=qdec[bs], in0=q_, in1=egbc, op=ALU.mult), reads=[qt, "egbc"], writes=[f"qdec{bs}"])
        yield
        L0, U0, Y0 = Lb[0], Ub[0], Yb[0]
        for n, sl, col in tiles:
            P.op("pool", lambda e, sl=sl: e.tensor_tensor(out=L0[:, sl], in0=t1[:, sl], in1=cm(C_BDTRIL_S), op=ALU.mult),
                 reads=["t1", "cst"], writes=["Lb0"])
            yield
        for n, sl, col in tiles:
            P.op("pe", lambda e, sl=sl: e.transpose(ps[4][:, sl], L0[:, sl], cm(C_IDENT)), reads=["Lb0", "cst"], writes=["ps4"])
            yield
        P.op("act", lambda e: e.copy(out=U0, in_=ps[4]), reads=["ps4"], writes=["Ub0"])
        yield
        for n, sl, col in tiles:
            P.op("pool", lambda e, sl=sl: e.tensor_tensor(out=Y0[:, sl], in0=cm(C_IDENT), in1=U0[:, sl], op=ALU.subtract),
                 reads=["cst", "Ub0"], writes=["Yb0"])
            yield
            P.op("pool", lambda e, sl=sl: e.tensor_tensor(out=Offb[:, sl], in0=t1[:, sl], in1=cm(C_OFF), op=ALU.mult),
                 reads=["t1", "cst"], writes=["Offb"])
            yield
            P.op("pool", lambda e, sl=sl: e.tensor_tensor(out=attnT[bs][:, sl], in0=t2[:, sl], in1=cm(C_TRIU_I), op=ALU.mult),
                 reads=["t2", "cst"], writes=[f"attnT{bs}"])
            yield
        cur = 0
        for k in range(1, 6):
            nx = 1 - cur
            for n, sl, col in tiles:
                P.op("pe", lambda e, sl=sl, cur=cur: e.matmul(ps[3][:, sl], Ub[cur][:, sl], Lb[cur][:, sl], start=True, stop=True),
                     reads=[f"Ub{cur}", f"Lb{cur}"], writes=["ps3"])
                yield
            if k < 5:
                for n, sl, col in tiles:
                    P.op("pe", lambda e, sl=sl, cur=cur: e.matmul(ps[4][:, sl], Lb[cur][:, sl], Ub[cur][:, sl], start=True, stop=True),
                         reads=[f"Ub{cur}", f"Lb{cur}"], writes=["ps4"])
                    yield
            P.op("act", lambda e, nx=nx: e.copy(out=Lb[nx], in_=ps[3]), reads=["ps3"], writes=[f"Lb{nx}"])
            yield
            if k < 5:
                P.op("dve", lambda e, nx=nx: e.tensor_copy(out=Ub[nx], in_=ps[4]), reads=["ps4"], writes=[f"Ub{nx}"])
                yield
            for n, sl, col in tiles:
                P.op("pe", lambda e, sl=sl, nx=nx, cur=cur: e.matmul(ps[5][:, sl], Lb[nx][:, sl], Yb[cur][:, sl], start=True, stop=True),
                     reads=[f"Lb{nx}", f"Yb{cur}"], writes=["ps5"])
                yield
            P.op("dve", lambda e, nx=nx, cur=cur: e.tensor_tensor(out=Yb[nx], in0=ps[5], in1=Yb[cur], op=ALU.add),
                 reads=["ps5", f"Yb{cur}"], writes=[f"Yb{nx}"])
            yield
            cur = nx
        Yd = Yb[cur]
        ytk = f"Yb{cur}"
        for n, sl, col in tiles:
            P.op("pe", lambda e, sl=sl: e.transpose(ps[4][:, sl], Yd[:, sl], cm(C_IDENT)), reads=[ytk, "cst"], writes=["ps4"])
            P.op("pe", lambda e, sl=sl: e.matmul(ps[3][:, sl], Offb[:, sl], Yd[:, sl], start=True, stop=True), reads=["Offb", ytk], writes=["ps3"])
            yield
        P.op("act", lambda e: e.copy(out=t1, in_=ps[4]), reads=["ps4"], writes=["t1"])
        yield
        P.op("dve", lambda e: e.tensor_copy(out=t2, in_=ps[3]), reads=["ps3"], writes=["t2"])
        yield
        for n, sl, col in tiles:
            P.op("pe", lambda e, sl=sl: e.matmul(ps[5][:, sl], t1[:, sl], t2[:, sl], start=True, stop=True), reads=["t1", "t2"], writes=["ps5"])
            yield
        P.op("dve", lambda e: e.tensor_tensor(out=Ybf[bs], in0=Yd, in1=ps[5], op=ALU.subtract), reads=[ytk, "ps5"], writes=[f"Ybf{bs}"])
        yield
    def gdn_C(l, tb, h, g0):
        a = h % 3
        bs = h % 2
        q_, k_, kd_, vb_, zs_ = qTb[a], kTb[a], kdec[a], vb[a], zs[a]
        qt, kt, kdt, vbt, zst = f"qTb{a}", f"kTb{a}", f"kdec{a}", f"vb{a}", f"zs{a}"
        tiles = [(n, slice(n * 128, (n + 1) * 128), n * 8 + h) for n in range(4)]
        Sh = Sst[:, h * 128:(h + 1) * 128]
        Shb = Sbf[:, h * 128:(h + 1) * 128]
        for n, sl, col in tiles:
            r_ = rv[n % 2]
            v_ = vnw[n % 2]
            P.op("pe", lambda e, sl=sl: e.matmul(ps[7][:, 128:256], k_[:, sl], Shb, start=True, stop=True),
                 reads=[kt, f"Sbf{h}"], writes=["ps7"])
            yield
            P.op("dve", lambda e, sl=sl, col=col, r_=r_: e.scalar_tensor_tensor(out=r_, in0=ps[7][:, 128:256], scalar=tkNbeg[:, col:col + 1],
                                                                            in1=vb_[:, sl], op0=ALU.mult, op1=ALU.add),
                 reads=["ps7", "tkNbeg", vbt], writes=[f"rv{n % 2}"])
            yield
            P.op("pe", lambda e, sl=sl, r_=r_: e.matmul(ps[7][:, 384:512], Ybf[bs][:, sl], r_, start=True, stop=True),
                 reads=[f"Ybf{bs}", f"rv{n % 2}"], writes=["ps7"])
            yield
            P.op("act", lambda e, v_=v_: e.copy(out=v_, in_=ps[7][:, 384:512]), reads=["ps7"], writes=[f"vnw{n % 2}"])
            yield
            P.op("pe", lambda e, sl=sl: e.matmul(ps[6][:, sl], Shb, qdec[bs][:, sl], start=True, stop=False),
                 reads=[f"Sbf{h}", f"qdec{bs}"], writes=["ps6"])
            P.op("pe", lambda e, sl=sl, v_=v_: e.matmul(ps[6][:, sl], v_, attnT[bs][:, sl], start=False, stop=True),
                 reads=[f"vnw{n % 2}", f"attnT{bs}"], writes=["ps6"])
            yield
            P.op("pe", lambda e, sl=sl, v_=v_: e.matmul(ps[7][:, 256:384], kd_[:, sl], v_, start=True, stop=True),
                 reads=[kdt, f"vnw{n % 2}"], writes=["ps7"])
            yield
            P.op("dve", lambda e, col=col: e.scalar_tensor_tensor(out=Sh, in0=Sh, scalar=tkEgl[:, col:col + 1], in1=ps[7][:, 256:384],
                                                                  op0=ALU.mult, op1=ALU.add),
                 reads=[f"S{h}", "tkEgl", "ps7"], writes=[f"S{h}"])
            yield
            P.op("pool", lambda e: e.tensor_copy(out=Shb, in_=Sh), reads=[f"S{h}"], writes=[f"Sbf{h}"])
            yield
        P.op("act", lambda e: e.copy(out=ob, in_=ps[6]), reads=["ps6"], writes=["ob"])
        yield
        P.op("act", lambda e: e.activation(out=sqc, in_=ob, func=AF.Square), reads=["ob"], writes=["sqc"])
        yield
        P.op("pe", lambda e: e.matmul(ps[6], cm(C_ONES, True), sqc, start=True, stop=True), reads=["sqc", "cstb"], writes=["ps6"])
        yield
        P.op("act", lambda e: e.activation(out=rn3, in_=ps[6], func=AF.Ln, bias=EPS, scale=1.0 / 128), reads=["ps6"], writes=["rn3"])
        yield
        P.op("act", lambda e: e.activation(out=rn3, in_=rn3, func=AF.Exp, scale=-0.5), reads=["rn3"], writes=["rn3"])
        yield
        P.op("dve", lambda e: e.tensor_tensor(out=ob, in0=ob, in1=rn3, op=ALU.mult), reads=["ob", "rn3"], writes=["ob"])
        yield
        P.op("dve", lambda e: e.scalar_tensor_tensor(out=ogT[:, h * 512:(h + 1) * 512], in0=ob, scalar=spm[:, g0 + 96:g0 + 97], in1=zs_,
                                                     op0=ALU.mult, op1=ALU.mult), reads=["ob", "spm", zst], writes=["ogT"])
        yield

    def emit_gdn_block_heads(l, tb, g0):
        for step in range(-2, 8):
            gens = []
            if 0 <= step < 8:
                gens.append(gdn_C(l, tb, step, g0))
            if 0 <= step + 1 < 8:
                gens.append(gdn_B(l, tb, step + 1, g0))
            if 0 <= step + 2 < 8:
                gens.append(gdn_A(l, tb, step + 2, g0))
            run_interleaved(gens)

    bar_n = [0]

    def emit_barrier():
        k = bar_n[0]
        bar_n[0] += 1
        P.op("act", lambda e: e.activation(out=bscr[:, 0:1], in_=cact[:, 0:1], func=AF.Identity), reads=["cact"], writes=[f"bar{k}_act", "bscr0"])
        P.op("dve", lambda e: e.tensor_copy(out=bscr[:, 1:2], in_=cact[:, 0:1]), reads=["cact"], writes=[f"bar{k}_dve", "bscr1"])
        P.op("pool", lambda e: e.tensor_copy(out=bscr[:, 2:3], in_=cact[:, 0:1]), reads=["cact"], writes=[f"bar{k}_pool", "bscr2"])
        P.op("pe", lambda e: e.matmul(ps[7][:, 0:8], cm(C_ONES), cact[:, 0:8], start=True, stop=True), reads=["cact", "cst"], writes=["ps7", f"bar{k}_pe"])
        allb = [f"bar{k}_{x}" for x in ("act", "dve", "pool", "pe")]
        P.op("act", lambda e: e.activation(out=bscr[:, 3:4], in_=cact[:, 0:1], func=AF.Identity), reads=allb + ["cact"], writes=["bscr3"])
        P.op("dve", lambda e: e.tensor_copy(out=bscr[:, 4:5], in_=ps[7][:, 0:1]), reads=allb, writes=["bscr4", "ps7"])
        P.op("pool", lambda e: e.tensor_copy(out=bscr[:, 5:6], in_=cact[:, 0:1]), reads=allb + ["cact"], writes=["bscr5"])
        P.op("pe", lambda e: e.matmul(ps[7][:, 0:8], cm(C_ONES), cact[:, 0:8], start=True, stop=True), reads=allb + ["cact", "cst"], writes=["ps7"])
        P.op("sp", lambda e: e.dma_start(out=bscr[:, 6:8], in_=spm_d[:, 0:2]), reads=allb, writes=["bscr6"], dma=True)

    emit_barrier()
    emit_layer_mod(0)
    n_a = min(nlayers, 2)
    for l in range(n_a):
        if l == 0 and nlayers > 1:
            emit_layer_mod(1)
        if l == 1 and nlayers > 2:
            emit_kv_mod()
            emit_layer_mod(2)
            if nlayers > 3:
                emit_layer_mod(3)
        emit_gdn_layer(l)
    emit_barrier()
    gstack.close()
    sb = sb_keep

    ost2 = [sb(f"Eo{i}", [128, 512]) for i in range(2)]
    if nlayers > 2:
        KT = sb("KT", [128, 8 * T], BF16)
        Vt = sb("Vt", [128, 16 * D], BF16)
        qn = sb("qn", [128, 8 * 512], BF16)
        Ebuf = ost2
        SPR = [[sb(f"SPR{i}{u}", [128, 512], F32R) for u in range(2)] for i in range(2)]
        dbf = [[sb(f"dbf{i}{u}", [128, 512]) for u in range(2)] for i in range(2)]
        Wb = [sb(f"Wb{i}", [128, 512], BF16) for i in range(2)]
        rn2 = tmpn[1]
        cstr = sb("cstr", [128, 256], F32R)
        P.op("dve", lambda e: e.tensor_copy(out=cstr[:, 0:128], in_=cm(C_TRIL_S)), reads=["cst"], writes=["cstr"])
        P.op("dve", lambda e: e.tensor_copy(out=cstr[:, 128:256], in_=cm(C_TRIU_I)), reads=["cst"], writes=["cstr"])

        def emit_headnorm(psb, gain_col, extra_bias, dst, dst_tok):
            tm = tmpn[0]
            P.op("act", lambda e: e.copy(out=tm, in_=psb), reads=[psb_tok[0]], writes=["tmpn0"])
            P.op("act", lambda e: e.activation(out=sqb[:, 0:512], in_=tm, func=AF.Square), reads=["tmpn0"], writes=["sqb0"])
            P.op("pe", lambda e: e.matmul(ps[4], cm(C_BDONES, True), sqb[:, 0:512], start=True, stop=True), reads=["sqb0", "cstb"], writes=["ps4"])
            P.op("act", lambda e: e.activation(out=rn2, in_=ps[4], func=AF.Ln, bias=EPS, scale=1.0 / 64), reads=["ps4"], writes=["tmpn1"])
            P.op("act", lambda e: e.activation(out=rn2, in_=rn2, func=AF.Exp, scale=-0.5, bias=extra_bias), reads=["tmpn1"], writes=["tmpn1"])
            P.op("dve", lambda e: e.scalar_tensor_tensor(out=dst, in0=tm, scalar=spm[:, gain_col:gain_col + 1], in1=rn2,
                                                         op0=ALU.mult, op1=ALU.mult), reads=["tmpn0", "spm", "tmpn1"], writes=[dst_tok])

        psb_tok = [None]

        def emit_kv():
            for tb in range(NB):
                emit_norm_block(tb, 4, 96)
                for piece in range(2):
                    P.op("pool", lambda e, piece=piece: e.dma_start(out=v3(wq[piece], 8), in_=wview(w_kv_d, piece * 512, 512)),
                         writes=[f"wq{piece}"], dma=True)
                    for fcl in range(4):
                        fc = piece * 4 + fcl
                        bk = fc % 4
                        for c in range(8):
                            P.op("pe", lambda e, piece=piece, fcl=fcl, c=c, bk=bk: e.matmul(
                                ps[bk], wq[piece][:, c * 512 + fcl * 128:c * 512 + (fcl + 1) * 128], hT[:, c * 512:(c + 1) * 512],
                                start=(c == 0), stop=(c == 7)), reads=[f"wq{piece}", "hT"], writes=[f"ps{bk}"])
                        psb_tok[0] = f"ps{bk}"
                        emit_headnorm(ps[bk], SP_KG, 0.0, KT[:, fc * T + tb * 512:fc * T + (tb + 1) * 512], "KT")
                for piece in range(2):
                    P.op("pool", lambda e, piece=piece: e.dma_start(out=v3(wq[piece], 8), in_=wview(w_kv_d, D + piece * 512, 512)),
                         writes=[f"wq{piece}"], dma=True)
                    for n in range(4):
                        bk = n % 4
                        tile = tb * 4 + n
                        for c in range(8):
                            P.op("pe", lambda e, piece=piece, n=n, c=c, bk=bk: e.matmul(
                                ps[bk], hT[:, c * 512 + n * 128:c * 512 + (n + 1) * 128], wq[piece][:, c * 512:(c + 1) * 512],
                                start=(c == 0), stop=(c == 7)), reads=[f"wq{piece}", "hT"], writes=[f"ps{bk}"])
                        dst = Vt[:, tile * D + piece * 512:tile * D + (piece + 1) * 512]
                        if n % 2 == 0:
                            P.op("act", lambda e, dst=dst, bk=bk: e.copy(out=dst, in_=ps[bk]), reads=[f"ps{bk}"], writes=["Vt"])
                        else:
                            P.op("dve", lambda e, dst=dst, bk=bk: e.tensor_copy(out=dst, in_=ps[bk]), reads=[f"ps{bk}"], writes=["Vt"])

        def sb_head(g, ch, hh, s_):
            h = 2 * ch + hh
            base = hh * 64
            bA, bB = (0, 1) if s_ == 0 else (2, 3)
            Eb, wbu = Ebuf[s_], Wb[s_]
            chunks = list(range(4 * g + 3, -1, -1))

            def geom(i):
                r0 = max(i - 4 * g, 0)
                return r0 * 128, (i >= 4 * g)

            def front(k):
                i = chunks[k]
                c0, diag = geom(i)
                u = k % 2
                Sr = SPR[s_][u]
                P.op("pe", lambda e: e.matmul(
                    ps[bA][:, c0:512], KT[base:base + 64, ch * T + i * 128:ch * T + (i + 1) * 128],
                    qn[base:base + 64, ch * 512 + c0:ch * 512 + 512], start=True, stop=True),
                    reads=["KT", "qn"], writes=[f"ps{bA}"])
                yield
                P.op("act", lambda e: e.activation(out=Eb[:, c0:512], in_=ps[bA][:, c0:512], func=AF.Exp),
                     reads=[f"ps{bA}"], writes=[f"E{s_}"])
                yield
                P.op("act", lambda e: e.activation(out=Sr[:, c0:512], in_=Eb[:, c0:512], func=AF.Ln, bias=1.0, scale=1.0),
                     reads=[f"E{s_}"], writes=[f"SPR{s_}{u}"])
                yield
                if diag:
                    P.op("pool", lambda e: e.tensor_tensor(out=Sr[:, c0:c0 + 128], in0=Sr[:, c0:c0 + 128].bitcast(F32),
                                                           in1=cm(C_TRIU_S), op=ALU.mult),
                         reads=[f"SPR{s_}{u}", "cst"], writes=[f"SPR{s_}{u}"])
                    yield

            def back1(k):
                i = chunks[k]
                c0, diag = geom(i)
                u = k % 2
                Sr, dbu = SPR[s_][u], dbf[s_][u]
                P.op("pe", lambda e: e.matmul(
                    ps[bB][:, c0:512], cstr[:, 0:128], Sr[:, c0:512], start=(k == 0), stop=False, skip_group_check=True),
                    reads=[f"SPR{s_}{u}", "cstr"], writes=[f"ps{bB}"])
                yield
                P.op("dve", lambda e: e.tensor_tensor(
                    out=dbu[:, c0:512], in0=ps[bA][:, c0:512], in1=Sr[:, c0:512].bitcast(F32), op=ALU.subtract),
                    reads=[f"ps{bA}", f"SPR{s_}{u}"], writes=[f"dbf{s_}{u}"])
                yield

            def back2(k):
                i = chunks[k]
                c0, diag = geom(i)
                u = k % 2
                Sr, dbu = SPR[s_][u], dbf[s_][u]
                P.op("dve", lambda e: e.tensor_tensor(
                    out=dbu[:, c0:512], in0=dbu[:, c0:512], in1=ps[bB][:, c0:512], op=ALU.subtract),
                    reads=[f"ps{bB}", f"dbf{s_}{u}"], writes=[f"dbf{s_}{u}"])
                yield
                if i > 0:
                    P.op("pe", lambda e: e.matmul(
                        ps[bB][:, c0:512], cstr[:, 128:256], Sr[:, c0:512], start=False, stop=False, skip_group_check=True),
                        reads=[f"SPR{s_}{u}", "cstr"], writes=[f"ps{bB}"])
                    yield
                P.op("act", lambda e: e.activation(out=wbu[:, c0:512], in_=dbu[:, c0:512], func=AF.Exp),
                     reads=[f"dbf{s_}{u}"], writes=[f"Wb{s_}"])
                yield
                if diag:
                    P.op("pool", lambda e: e.tensor_tensor(out=wbu[:, c0:c0 + 128], in0=wbu[:, c0:c0 + 128],
                                                           in1=cm(C_TRIU_S, True), op=ALU.mult),
                         reads=[f"Wb{s_}", "cstb"], writes=[f"Wb{s_}"])
                    yield
                P.op("pe", lambda e: e.matmul(
                    ps[6][base:base + 64, c0:512], Vt[:, i * D + h * 64:i * D + (h + 1) * 64], wbu[:, c0:512],
                    start=False, stop=False, skip_group_check=True, tile_position=(0, base)),
                    reads=["Vt", f"Wb{s_}"], writes=["ps6"])
                yield

            yield from front(0)
            for k in range(len(chunks)):
                yield from back1(k)
                if k + 1 < len(chunks):
                    yield from front(k + 1)
                yield from back2(k)

        def run_interleaved(gens):
            gens = list(gens)
            while gens:
                for g_ in list(gens):
                    try:
                        next(g_)
                    except StopIteration:
                        gens.remove(g_)

        def emit_sb_layer(l2):
            L = 2 + l2
            for g in range(NB):
                emit_norm_block(g, L, 24 * L)
                for piece in range(2):
                    P.op("pool", lambda e, piece=piece: e.dma_start(out=v3(wq[piece], 8), in_=wview(w_inb_d[l2], piece * 512, 512)),
                         writes=[f"wq{piece}"], dma=True)
                    for fcl in range(4):
                        fc = piece * 4 + fcl
                        bk = fc % 4
                        for c in range(8):
                            P.op("pe", lambda e, piece=piece, fcl=fcl, c=c, bk=bk: e.matmul(
                                ps[bk], wq[piece][:, c * 512 + fcl * 128:c * 512 + (fcl + 1) * 128], hT[:, c * 512:(c + 1) * 512],
                                start=(c == 0), stop=(c == 7)), reads=[f"wq{piece}", "hT"], writes=[f"ps{bk}"])
                        psb_tok[0] = f"ps{bk}"
                        emit_headnorm(ps[bk], SP_QG + l2, float(np.log(0.125)), qn[:, fc * 512:(fc + 1) * 512], "qn")
                for piece in range(2):
                    P.op("pool", lambda e, piece=piece: e.dma_start(out=v3(wq[piece], 8), in_=wview(w_inb_d[l2], D + piece * 512, 512)),
                         writes=[f"wq{piece}"], dma=True)
                    for fcl in range(4):
                        fc = piece * 4 + fcl
                        bk = fc % 4
                        for c in range(8):
                            P.op("pe", lambda e, piece=piece, fcl=fcl, c=c, bk=bk: e.matmul(
                                ps[bk], wq[piece][:, c * 512 + fcl * 128:c * 512 + (fcl + 1) * 128], hT[:, c * 512:(c + 1) * 512],
                                start=(c == 0), stop=(c == 7)), reads=[f"wq{piece}", "hT"], writes=[f"ps{bk}"])
                        P.op("act", lambda e, fc=fc, bk=bk: e.activation(out=ogT[:, fc * 512:(fc + 1) * 512], in_=ps[bk], func=AF.Silu),
                             reads=[f"ps{bk}"], writes=["ogT"])
                for ch in range(8):
                    P.op("dve", lambda e: e.memset(ps[6], 0.0), writes=["ps6"])
                    run_interleaved([sb_head(g, ch, 0, 0), sb_head(g, ch, 1, 1)])
                    P.op("dve", lambda e, ch=ch: e.tensor_tensor(out=ogT[:, ch * 512:(ch + 1) * 512], in0=ps[6], in1=ogT[:, ch * 512:(ch + 1) * 512],
                                                                 op=ALU.mult), reads=["ps6", "ogT"], writes=["ogT"])
                emit_outproj(w_outb_d[l2], g, L)

        emit_kv()
        for l2 in range(nlayers - 2):
            emit_sb_layer(l2)

    outs = []
    for n in range(16):
        tb = n // 4
        for half in range(2):
            bk = (2 * n + half) % 4
            oi = (2 * n + half) % 2
            for c4 in range(4):
                c = half * 4 + c4
                P.op("pe", lambda e, n=n, c=c, c4=c4, bk=bk: e.transpose(
                    ps[bk][:, c4 * 128:(c4 + 1) * 128], xT[:, c * T + n * 128:c * T + (n + 1) * 128], cm(C_IDENT)),
                    reads=[f"xT{tb}", "cst"], writes=[f"ps{bk}"])
            if half == 0:
                P.op("act", lambda e, oi=oi, bk=bk: e.copy(out=ost2[oi], in_=ps[bk]), reads=[f"ps{bk}"], writes=[f"E{oi}"])
            else:
                P.op("dve", lambda e, oi=oi, bk=bk: e.tensor_copy(out=ost2[oi], in_=ps[bk]), reads=[f"ps{bk}"], writes=[f"E{oi}"])
            outs.append(P.op("sp", lambda e, n=n, half=half, oi=oi: e.dma_start(
                out=out_d[n * 128:(n + 1) * 128, half * 512:(half + 1) * 512], in_=ost2[oi]),
                reads=[f"E{oi}"], dma=True))
    P.finalize(final_wait_ops=outs)
    return nc


def _prep_inputs(inp):
    inp = {k: np.asarray(v) for k, v in inp.items()}
    w_in_a = inp["w_in_a"].astype(np.float32, copy=False)
    qkvz = w_in_a[:, :, :4096].reshape(2, D, 4, 8, 128).transpose(0, 1, 3, 2, 4).reshape(2, D, 4096)
    shared = {
        "cst": _consts(),
        "w_ada": np.ascontiguousarray(inp["w_ada"], np.float32),
        "w_inp": np.ascontiguousarray(qkvz),
        "w_ba": np.ascontiguousarray(w_in_a[:, :, 4096:4112]),
        "w_out_a": np.ascontiguousarray(inp["w_out_a"], np.float32),
        "w_ada_kv": np.ascontiguousarray(inp["w_ada_kv"], np.float32),
        "w_kv": np.ascontiguousarray(inp["w_kv"], np.float32),
        "w_in_b": np.ascontiguousarray(inp["w_in_b"], np.float32),
        "w_out_b": np.ascontiguousarray(inp["w_out_b"], np.float32),
    }
    in_maps = []
    for b in range(8):
        m = dict(shared)
        m["x"] = np.ascontiguousarray(inp["x"][b], np.float32)
        m["spm"] = _small_params(inp, b)
        in_maps.append(m)
    return in_maps


def kernel(**inputs):
    in_maps = _prep_inputs(inputs)
    nc = build(4)
    res = run_bass_kernel_spmd(nc, in_maps, core_ids=list(range(8)))
    return np.stack([np.asarray(r["out"], np.float32) for r in res.results], axis=0)
```

```python
import numpy as np
import concourse.bass as bass
import concourse.mybir as mybir
from concourse.bass_utils import run_bass_kernel_spmd
from contextlib import ExitStack

F32 = mybir.dt.float32
BF16 = mybir.dt.bfloat16
F32R = mybir.dt.float32r
AF = mybir.ActivationFunctionType
ALU = mybir.AluOpType

T = 2048
D = 1024
NB = 4
EPS = 1e-6
AUX_N = 13
GDN_W = (1, 1, 1)


class Tok:
    __slots__ = ("name", "w", "r")

    def __init__(self, name=""):
        self.name = name
        self.w = None
        self.r = []


class Op:
    __slots__ = ("eng", "fn", "deps", "idx", "need_inc", "ev", "is_dma", "dsem", "dval")

    def __init__(self, eng, fn):
        self.eng = eng
        self.fn = fn
        self.deps = []
        self.idx = None
        self.need_inc = False
        self.ev = None
        self.is_dma = False


class Prog:
    ENG = ("pe", "act", "dve", "pool", "sp")
    SEM_LIMIT = 30000
    NDMA = 16
    NEAR = 6

    def __init__(self, nc):
        self.nc = nc
        self.q = {e: [] for e in self.ENG}
        self.toks = {}

    def tk(self, name):
        t = self.toks.get(name)
        if t is None:
            t = Tok(name)
            self.toks[name] = t
        return t

    def op(self, eng, fn, reads=(), writes=(), dma=False):
        o = Op(eng, fn)
        o.is_dma = dma
        o.idx = len(self.q[eng])
        deps = set()
        rd, wr = [], []
        for t in reads:
            if isinstance(t, str) and t.startswith("ps") and t[2].isdigit():
                wr.append(t[:3])
            else:
                rd.append(t)
        for t in writes:
            if isinstance(t, str) and t.startswith("ps") and t[2].isdigit():
                wr.append(t[:3])
            else:
                wr.append(t)
        reads = [self.tk(t) if isinstance(t, str) else t for t in rd]
        writes = [self.tk(t) if isinstance(t, str) else t for t in dict.fromkeys(wr)]
        for t in reads:
            if t.w is not None:
                deps.add(t.w)
        for t in writes:
            if t.w is not None:
                deps.add(t.w)
            for r in t.r:
                deps.add(r)
        deps.discard(o)
        o.deps = list(deps)
        for t in reads:
            t.r.append(o)
        for t in writes:
            t.w = o
            t.r = []
        self.q[eng].append(o)
        return o

    def finalize(self, final_wait_ops=()):
        nc = self.nc
        waits = {}
        for e in self.ENG:
            seen = {}
            for o in self.q[e]:
                best = {}
                res = []
                for d in o.deps:
                    if d.is_dma:
                        res.append(d)
                        continue
                    if d.eng == o.eng:
                        if e == "pe" or o.is_dma:
                            if not o.is_dma:
                                continue
                        if (not o.is_dma) and o.idx - d.idx > self.NEAR:
                            continue
                    if d.eng not in best or best[d.eng].idx < d.idx:
                        best[d.eng] = d
                for d in best.values():
                    if seen.get(d.eng, -1) >= d.idx:
                        continue
                    seen[d.eng] = d.idx
                    res.append(d)
                waits[o] = res
                for d in res:
                    if not d.is_dma:
                        d.need_inc = True
        stack = ExitStack()
        for e in self.ENG:
            n = sum(1 for o in self.q[e] if o.need_inc and not o.is_dma)
            k = max(1, (n + self.SEM_LIMIT - 1) // self.SEM_LIMIT)
            sems = [stack.enter_context(nc.semaphore(f"s_{e}_{i}")) for i in range(k)]
            c = 0
            for o in self.q[e]:
                if o.need_inc and not o.is_dma:
                    o.ev = (sems[c // self.SEM_LIMIT], c % self.SEM_LIMIT + 1)
                    c += 1
        dsems = {e: [stack.enter_context(nc.semaphore(f"s_dma_{e}_{i}")) for i in range(self.NDMA)]
                 for e in ("sp", "pool")}
        for e in ("sp", "pool"):
            di = 0
            for o in self.q[e]:
                if o.is_dma:
                    o.dsem = dsems[e][di % self.NDMA]
                    o.dval = 16 * (di // self.NDMA + 1)
                    o.ev = (o.dsem, o.dval)
                    di += 1
        final = list(final_wait_ops)
        with nc.Block() as block:
            def run(engname, eng):
                for o in self.q[engname]:
                    for d in waits[o]:
                        eng.wait_ge(d.ev[0], d.ev[1])
                    if o.is_dma and o.dval > 16:
                        eng.wait_ge(o.dsem, o.dval - 16)
                    ins = o.fn(eng)
                    if o.is_dma:
                        ins.then_inc(o.dsem, 16)
                    elif o.need_inc:
                        ins.then_inc(o.ev[0], 1)
                if engname == "sp":
                    for o in final:
                        eng.wait_ge(o.ev[0], o.ev[1])

            @block.tensor
            def _(t):
                run("pe", t)

            @block.scalar
            def _(t):
                run("act", t)

            @block.vector
            def _(t):
                run("dve", t)

            @block.gpsimd
            def _(t):
                run("pool", t)

            @block.sync
            def _(t):
                run("sp", t)
        stack.close()


C_IDENT, C_ONES, C_TRIU_I, C_TRIL_S, C_TRIU_S, C_BDTRIL_S, C_OFF, C_BDONES = range(8)
NCST = 8


def _consts():
    p = np.arange(128)[:, None]
    f = np.arange(128)[None, :]
    m = np.zeros((128, NCST, 128), np.float32)
    m[:, C_IDENT] = (p == f)
    m[:, C_ONES] = 1.0
    m[:, C_TRIU_I] = (p <= f)
    m[:, C_TRIL_S] = (f < p)
    m[:, C_TRIU_S] = (p < f)
    m[:, C_BDTRIL_S] = (f < p) & ((p // 64) == (f // 64))
    m[:, C_OFF] = (p >= 64) & (f < 64)
    m[:, C_BDONES] = ((p // 64) == (f // 64))
    return m.reshape(128, NCST * 128)


def _fm(v):
    v = np.asarray(v, np.float32).reshape(-1, 128)
    return np.ascontiguousarray(v.T)


SP_C = 0
SP_LAYER = 8
SP_KV = 136
SP_GDN = 160
SP_KG = 386
SP_QG = 387
NSP = 389


def _small_params(inp, b):
    sp = np.zeros((128, NSP), np.float32)
    sp[:, 0:8] = _fm(inp["c"][b])
    for l in range(4):
        o = SP_LAYER + 32 * l
        sp[:, o:o + 8] = _fm(inp["norm_g"][l])
        sp[:, o + 8:o + 32] = _fm(inp["b_ada"][l])
    sp[:, SP_KV:SP_KV + 8] = _fm(inp["kv_norm_g"])
    sp[:, SP_KV + 8:SP_KV + 24] = _fm(inp["b_ada_kv"])
    for l in range(2):
        o = SP_GDN + 113 * l
        cw = np.asarray(inp["conv_w_a"][l], np.float32)
        for j in range(4):
            sp[:, o + 24 * j:o + 24 * j + 24] = _fm(cw[j])
        sp[:, o + 96] = np.asarray(inp["o_gain_a"][l], np.float32)
        sp[:, o + 97:o + 105] = np.asarray(inp["a_log_a"][l], np.float32)[None, :]
        sp[:, o + 105:o + 113] = np.asarray(inp["dt_bias_a"][l], np.float32)[None, :]
    sp[:, SP_KG] = np.tile(np.asarray(inp["k_gain"], np.float32), 2)
    for l in range(2):
        sp[:, SP_QG + l] = np.tile(np.asarray(inp["q_gain_b"][l], np.float32), 2)
    return sp


def build(nlayers=4):
    nc = bass.Bass("TRN2", target_bir_lowering=False)
    dram = lambda n, s, k="ExternalInput": nc.dram_tensor(n, s, F32, kind=k).ap()
    x_d = dram("x", [T, D])
    spm_d = dram("spm", [128, NSP])
    cst_d = dram("cst", [128, NCST * 128])
    w_ada_d = dram("w_ada", [4, D, 3 * D])
    w_inp_d = dram("w_inp", [2, D, 8 * 512])
    w_ba_d = dram("w_ba", [2, D, 16])
    w_outa_d = dram("w_out_a", [2, D, D])
    w_adakv_d = dram("w_ada_kv", [D, 2 * D])
    w_kv_d = dram("w_kv", [D, 2 * D])
    w_inb_d = dram("w_in_b", [2, D, 2 * D])
    w_outb_d = dram("w_out_b", [2, D, D])
    out_d = dram("out", [T, D], "ExternalOutput")

    def wview(ap2d, c0, n):
        return ap2d[:, c0:c0 + n].rearrange("(c p) n -> p c n", p=128)

    sb = lambda n, s, d=F32: nc.alloc_sbuf_tensor("sb_" + n, s, d).ap()
    P = Prog(nc)

    def v3(ap, a):
        return ap.rearrange("p (a b) -> p a b", a=a)

    xT = sb("xT", [128, 8 * T])
    cst = sb("cst", [128, NCST * 128])
    cstb = sb("cstb", [128, NCST * 128], BF16)
    spm = sb("spm", [128, NSP])
    cact = sb("cact", [128, 8])
    modT = sb("modT", [128, 5 * 24])
    Acol = sb("Acol", [128, 5 * 8])
    hT = sb("hT", [128, 8 * 512], BF16)
    ogT = sb("ogT", [128, 8 * 512], BF16)
    sqb = sb("sqb", [128, 2 * 512], BF16)
    rstd = sb("rstd", [128, 512])
    bscr = sb("bscr", [128, 8])
    tmpn = [sb(f"tmpn{i}", [128, 512]) for i in range(2)]
    wq = [sb(f"wq{i}", [128, 8 * 512], BF16) for i in range(2)]
    ps = [nc.alloc_psum_tensor(f"ps{i}", [128, 512], F32).ap() for i in range(8)]
    gstack = ExitStack()
    sbp = lambda n, s, d=F32: gstack.enter_context(nc.sbuf_tensor("sb_" + n, s, d))[:]
    sb_keep = sb
    lnb = sbp("lnb", [128, 512])
    mrow = sbp("mrow", [1, 256])
    wa = [sbp("wa0", [128, 8 * 256])] * 2
    LU = sbp("LU", [128, 2048])
    xld = [LU[:, i * 1024:(i + 1) * 1024] for i in range(2)]

    def cm(i, bf=False):
        return (cstb if bf else cst)[:, i * 128:(i + 1) * 128]

    def xsl(c, tb):
        return xT[:, c * T + tb * 512:c * T + (tb + 1) * 512]

    P.op("sp", lambda e: e.dma_start(out=cst, in_=cst_d), writes=["cst"], dma=True)
    P.op("sp", lambda e: e.dma_start(out=spm, in_=spm_d), writes=["spm"], dma=True)
    P.op("pool", lambda e: e.tensor_copy(out=cstb, in_=cst), reads=["cst"], writes=["cstb"])
    P.op("act", lambda e: e.activation(out=cact, in_=spm[:, 0:8], func=AF.Silu), reads=["spm"], writes=["cact"])

    wa_i = [0]

    def gen_mod(wd2, ncols, bias0, dst0, slot, a_slot, g_col0, scale_col0):
        buf = wa[0]
        for piece in range(ncols // 256):
            P.op("sp", lambda e, piece=piece: e.dma_start(out=v3(buf, 8), in_=wview(wd2, piece * 256, 256)),
                 writes=["wa0"], dma=True)
            yield
            for c in range(8):
                P.op("pe", lambda e, c=c: e.matmul(ps[2][0:1, 0:256], cact[:, c:c + 1], buf[:, c * 256:(c + 1) * 256],
                                                   start=(c == 0), stop=(c == 7)), reads=["wa0", "cact"], writes=["ps2"])
                yield
            P.op("act", lambda e: e.copy(out=mrow, in_=ps[2][0:1, 0:256]), reads=["ps2"], writes=["mrow"])
            yield
            for fc in range(2):
                P.op("pe", lambda e, fc=fc: e.matmul(ps[2][:, 256 + fc:257 + fc], mrow[0:1, fc * 128:(fc + 1) * 128],
                                                     cst[0:1, C_ONES * 128:C_ONES * 128 + 1], start=True, stop=True),
                     reads=["mrow", "cst"], writes=["ps2"])
                yield
            d0 = dst0 + piece * 2
            b0 = bias0 + piece * 2
            P.op("dve", lambda e, d0=d0, b0=b0: e.tensor_tensor(out=modT[:, d0:d0 + 2], in0=ps[2][:, 256:258],
                                                                in1=spm[:, b0:b0 + 2], op=ALU.add),
                 reads=["ps2", "spm"], writes=[f"mod{slot}"])
            yield
        P.op("dve", lambda e: e.scalar_tensor_tensor(out=Acol[:, 8 * a_slot:8 * a_slot + 8], in0=modT[:, scale_col0:scale_col0 + 8],
                                                     scalar=1.0, in1=spm[:, g_col0:g_col0 + 8], op0=ALU.add, op1=ALU.mult),
             reads=[f"mod{slot}", "spm"], writes=[f"A{a_slot}"])
        yield

    def gen_layer_mod(l):
        return gen_mod(w_ada_d[l], 3 * D, SP_LAYER + 32 * l + 8, 24 * l, l, l, SP_LAYER + 32 * l, 24 * l + 8)

    def gen_kv_mod():
        return gen_mod(w_adakv_d, 2 * D, SP_KV + 8, 96, 4, 4, SP_KV, 104)

    def drain(gen):
        for _ in gen:
            pass

    def take(gen, n):
        for _ in range(n):
            try:
                next(gen)
            except StopIteration:
                return
            yield

    def chain(*gens):
        for g_ in gens:
            yield from g_

    for n in range(16):
        bi = n % 2
        tb = n // 4
        P.op("sp", lambda e, n=n, bi=bi: e.dma_start(out=xld[bi], in_=x_d[n * 128:(n + 1) * 128, :]),
             writes=[f"xld{bi}"], dma=True)
        for half in range(2):
            bk = (2 * n + half) % 4
            for c4 in range(4):
                c = half * 4 + c4
                P.op("pe", lambda e, bi=bi, c=c, c4=c4, bk=bk: e.transpose(
                    ps[bk][:, c4 * 128:(c4 + 1) * 128], xld[bi][:, c * 128:(c + 1) * 128], cm(C_IDENT)),
                    reads=[f"xld{bi}", "cst"], writes=[f"ps{bk}"])
            dst = v3(xT[:, half * 4 * T:(half * 4 + 4) * T], 4)[:, :, n * 128:(n + 1) * 128]
            src = v3(ps[bk], 4)
            if half == 0:
                P.op("act", lambda e, dst=dst, src=src: e.copy(out=dst, in_=src), reads=[f"ps{bk}"], writes=[f"xT{tb}"])
            else:
                P.op("dve", lambda e, dst=dst, src=src: e.tensor_copy(out=dst, in_=src), reads=[f"ps{bk}"], writes=[f"xT{tb}"])

    def emit_norm_block(tb, slot, shift0):
        for c in range(8):
            sq_ = sqb[:, (c % 2) * 512:(c % 2 + 1) * 512]
            P.op("act", lambda e, c=c, sq_=sq_: e.activation(out=sq_, in_=xsl(c, tb), func=AF.Square),
                 reads=[f"xT{tb}"], writes=[f"sqb{c % 2}"])
            P.op("pe", lambda e, c=c, sq_=sq_: e.matmul(ps[4], cm(C_ONES, True), sq_,
                                               start=(c == 0), stop=(c == 7)), reads=[f"sqb{c % 2}", "cstb"], writes=["ps4"])
        P.op("act", lambda e: e.activation(out=rstd, in_=ps[4], func=AF.Ln, bias=EPS, scale=1.0 / D), reads=["ps4"], writes=["rstd"])
        P.op("act", lambda e: e.activation(out=rstd, in_=rstd, func=AF.Exp, scale=-0.5), reads=["rstd"], writes=["rstd"])
        for c in range(8):
            tm = tmpn[c % 2]
            P.op("dve", lambda e, c=c, tm=tm: e.scalar_tensor_tensor(
                out=tm, in0=xsl(c, tb), scalar=Acol[:, 8 * slot + c:8 * slot + c + 1], in1=rstd,
                op0=ALU.mult, op1=ALU.mult), reads=[f"xT{tb}", f"A{slot}", "rstd"], writes=[f"tmpn{c % 2}"])
            P.op("act", lambda e, c=c, tm=tm: e.activation(
                out=hT[:, c * 512:(c + 1) * 512], in_=tm, func=AF.Identity,
                bias=modT[:, shift0 + c:shift0 + c + 1], scale=1.0),
                reads=[f"tmpn{c % 2}", f"mod{slot}"], writes=["hT"])

    def emit_outproj(wd2, tb, slot, og=None):
        og = ogT if og is None else og
        for half in range(2):
            P.op("pool", lambda e, half=half: e.dma_start(out=v3(wq[half], 8), in_=wview(wd2, half * 512, 512)),
                 writes=[f"wq{half}"], dma=True)
        for m in range(8):
            half, mm_ = divmod(m, 4)
            bk = (0, 1, 3, 4)[m % 4]
            for h in range(8):
                P.op("pe", lambda e, half=half, mm_=mm_, h=h, bk=bk: e.matmul(
                    ps[bk], wq[half][:, h * 512 + mm_ * 128:h * 512 + (mm_ + 1) * 128], og[:, h * 512:(h + 1) * 512],
                    start=(h == 0), stop=(h == 7)), reads=[f"wq{half}", "ogT"], writes=[f"ps{bk}"])
            P.op("dve", lambda e, m=m, bk=bk: e.scalar_tensor_tensor(
                out=xsl(m, tb), in0=ps[bk], scalar=modT[:, 24 * slot + 16 + m:24 * slot + 17 + m], in1=xsl(m, tb),
                op0=ALU.mult, op1=ALU.add), reads=[f"ps{bk}", f"mod{slot}", f"xT{tb}"], writes=[f"xT{tb}"])

    sb = sbp
    wba = sb("wba", [128, 8 * 16], BF16)
    carry = sb("carry", [128, 8 * 3 * 4])
    pre = [sb(f"pre{j}", [128, 516]) for j in range(3)]
    acc = [sb(f"acc{j}", [128, 512]) for j in range(3)]
    zs = [sb(f"zs{i}", [128, 512], BF16) for i in range(3)]
    rn = sb("rn", [128, 512])
    rn3 = sb("rn3", [128, 512])
    sqc = sb("sqc", [128, 512], BF16)
    qTb = [sb(f"qTb{i}", [128, 512], BF16) for i in range(3)]
    kTb = [sb(f"kTb{i}", [128, 512], BF16) for i in range(3)]
    qdec = [sb(f"qdec{i}", [128, 512], BF16) for i in range(2)]
    kdec = [sb(f"kdec{i}", [128, 512], BF16) for i in range(3)]
    vb = [sb(f"vb{i}", [128, 512]) for i in range(3)]
    egbc = sb("egbc", [128, 512])
    E1 = sb("E1", [128, 512])
    E2 = sb("E2", [128, 512])
    t1 = sb("t1", [128, 512])
    t2 = sb("t2", [128, 512])
    Lb = [LU[:, i * 512:(i + 1) * 512] for i in range(2)]
    Ub = [LU[:, (2 + i) * 512:(3 + i) * 512] for i in range(2)]
    Yb = [sb(f"Yb{i}", [128, 512]) for i in range(2)]
    Offb = sb("Offb", [128, 512])
    Ybf = [sb(f"Ybf{i}", [128, 512], BF16) for i in range(2)]
    attnT = [sb(f"attnT{i}", [128, 512], BF16) for i in range(2)]
    rv = [sb(f"rv{i}", [128, 128], BF16) for i in range(2)]
    vnw = [sb(f"vnw{i}", [128, 128], BF16) for i in range(2)]
    Sst = sb("Sst", [128, 8 * 128])
    Sbf = sb("Sbf", [128, 8 * 128], BF16)
    ob = sb("ob", [128, 512])
    tkU = sb("tkU", [128, 32])
    tkG = sb("tkG", [128, 32])
    tkBeta = sb("tkBeta", [128, 32])
    tkGc = sb("tkGc", [128, 32])
    tkNbeg = sb("tkNbeg", [128, 32])
    tkGl = sb("tkGl", [128, 32])
    tkDks = sb("tkDks", [128, 32])
    tkEgl = sb("tkEgl", [128, 32])
    negA = sb("negA", [128, 8])
    sb = sb_keep

    def emit_gdn_layer(l):
        slot = l
        g0 = SP_GDN + 113 * l
        P.op("pool", lambda e: e.dma_start(out=v3(wba, 8), in_=w_ba_d[l].rearrange("(c p) n -> p c n", p=128)),
             writes=["wba"], dma=True)
        P.op("pool", lambda e: e.memset(carry, 0.0), writes=["carry"])
        P.op("pool", lambda e: e.memset(Sst, 0.0), writes=[f"S{i}" for i in range(8)])
        P.op("pool", lambda e: e.memset(Sbf, 0.0), writes=[f"Sbf{i}" for i in range(8)])
        P.op("act", lambda e: e.activation(out=negA, in_=spm[:, g0 + 97:g0 + 105], func=AF.Exp), reads=["spm"], writes=["negA"])
        P.op("dve", lambda e: e.tensor_scalar(out=negA, in0=negA, scalar1=-1.0, scalar2=None, op0=ALU.mult), reads=["negA"], writes=["negA"])
        for tb in range(NB):
            emit_norm_block(tb, slot, 24 * l)
            for n in range(4):
                for c in range(8):
                    P.op("pe", lambda e, n=n, c=c: e.matmul(
                        ps[7][:, n * 16:(n + 1) * 16], hT[:, c * 512 + n * 128:c * 512 + (n + 1) * 128],
                        wba[:, c * 16:(c + 1) * 16], start=(c == 0), stop=(c == 7)),
                        reads=["hT", "wba"], writes=["ps7a"])
            ba3 = v3(ps[7][:, 0:64], 4)
            for n in range(4):
                P.op("dve", lambda e, n=n: e.tensor_tensor(out=tkU[:, n * 8:(n + 1) * 8], in0=ps[7][:, n * 16 + 8:n * 16 + 16],
                                                          in1=spm[:, g0 + 105:g0 + 113], op=ALU.add),
                     reads=["ps7a", "spm"], writes=["tkU"])
            P.op("act", lambda e: e.activation(out=v3(tkBeta, 4), in_=ba3[:, :, 0:8], func=AF.Exp, scale=-1.0),
                 reads=["ps7a"], writes=["tkBeta"])
            P.op("act", lambda e: e.activation(out=tkU, in_=tkU, func=AF.Exp), reads=["tkU"], writes=["tkU"])
            P.op("act", lambda e: e.activation(out=tkU, in_=tkU, func=AF.Ln, bias=1.0, scale=1.0), reads=["tkU"], writes=["tkU"])
            for n in range(4):
                P.op("dve", lambda e, n=n: e.tensor_tensor(out=tkG[:, n * 8:(n + 1) * 8], in0=tkU[:, n * 8:(n + 1) * 8],
                                                          in1=negA, op=ALU.mult), reads=["tkU", "negA"], writes=["tkG"])
            P.op("dve", lambda e: e.tensor_scalar(out=tkBeta, in0=tkBeta, scalar1=1.0, scalar2=None, op0=ALU.add),
                 reads=["tkBeta"], writes=["tkBeta"])
            P.op("dve", lambda e: e.reciprocal(out=tkBeta, in_=tkBeta), reads=["tkBeta"], writes=["tkBeta"])
            for n in range(4):
                P.op("pe", lambda e, n=n: e.matmul(ps[7][:, 64 + n * 8:64 + (n + 1) * 8], cm(C_TRIU_I), tkG[:, n * 8:(n + 1) * 8],
                                                   start=True, stop=True), reads=["cst", "tkG"], writes=["ps7b"])
                P.op("pe", lambda e, n=n: e.matmul(ps[7][:, 96 + n * 8:96 + (n + 1) * 8], cm(C_ONES), tkG[:, n * 8:(n + 1) * 8],
                                                   start=True, stop=True), reads=["cst", "tkG"], writes=["ps7b"])
            P.op("dve", lambda e: e.tensor_copy(out=tkGc, in_=ps[7][:, 64:96]), reads=["ps7b"], writes=["tkGc"])
            P.op("dve", lambda e: e.tensor_copy(out=tkGl, in_=ps[7][:, 96:128]), reads=["ps7b"], writes=["tkGl"])
            P.op("act", lambda e: e.activation(out=tkEgl, in_=tkGl, func=AF.Exp), reads=["tkGl"], writes=["tkEgl"])
            P.op("dve", lambda e: e.tensor_tensor(out=tkDks, in0=tkGl, in1=tkGc, op=ALU.subtract), reads=["tkGl", "tkGc"], writes=["tkDks"])
            P.op("act", lambda e: e.activation(out=tkDks, in_=tkDks, func=AF.Exp), reads=["tkDks"], writes=["tkDks"])
            P.op("act", lambda e: e.activation(out=tkNbeg, in_=tkGc, func=AF.Exp), reads=["tkGc"], writes=["tkNbeg"])
            P.op("dve", lambda e: e.scalar_tensor_tensor(out=tkNbeg, in0=tkNbeg, scalar=-1.0, in1=tkBeta, op0=ALU.mult, op1=ALU.mult),
                 reads=["tkNbeg", "tkBeta"], writes=["tkNbeg"])

            emit_gdn_block_heads(l, tb, g0)
            emit_outproj(w_outa_d[l], tb, slot)

    def run_interleaved(gens, weights=None):
        gens = list(gens)
        weights = list(weights) if weights is not None else [1] * len(gens)
        while gens:
            for g_, w_ in list(zip(gens, weights)):
                for _ in range(w_):
                    try:
                        next(g_)
                    except StopIteration:
                        k_ = gens.index(g_)
                        gens.pop(k_)
                        weights.pop(k_)
                        break

    def gdn_A(l, tb, h, g0):
        a = h % 3
        wb = wq[h % 2]
        wtk = f"wq{h % 2}"
        P.op("pool", lambda e: e.dma_start(out=v3(wb, 8), in_=wview(w_inp_d[l], h * 512, 512)), writes=[wtk], dma=True)
        yield
        def proj(j, bk):
            for c in range(8):
                P.op("pe", lambda e, c=c: e.matmul(ps[bk], wb[:, c * 512 + j * 128:c * 512 + (j + 1) * 128],
                                                   hT[:, c * 512:(c + 1) * 512], start=(c == 0), stop=(c == 7)),
                     reads=[wtk, "hT"], writes=[f"ps{bk}"])
                yield

        def evac(j, bk):
            cc = (h * 3 + j) * 4
            P.op("pool", lambda e: e.tensor_copy(out=pre[j][:, 0:3], in_=carry[:, cc:cc + 3]),
                 reads=["carry"], writes=[f"pre{j}"])
            P.op("act", lambda e: e.copy(out=pre[j][:, 3:515], in_=ps[bk]), reads=[f"ps{bk}"], writes=[f"pre{j}"])
            yield
            P.op("pool", lambda e: e.tensor_copy(out=carry[:, cc:cc + 3], in_=pre[j][:, 512:515]),
                 reads=[f"pre{j}"], writes=["carry"])
            yield

        yield from proj(0, 0)
        yield from proj(1, 1)
        yield from evac(0, 0)
        yield from evac(1, 1)
        yield from proj(2, 0)
        yield from proj(3, 1)
        yield from evac(2, 0)
        P.op("act", lambda e: e.activation(out=zs[a], in_=ps[1], func=AF.Silu), reads=["ps1"], writes=[f"zs{a}"])
        yield
        wc = lambda tap, j: spm[:, g0 + 24 * tap + 8 * j + h:g0 + 24 * tap + 8 * j + h + 1]
        for tap in range(4):
            for j in range(3):
                if tap == 0:
                    P.op("dve", lambda e, j=j: e.tensor_scalar(out=acc[j], in0=pre[j][:, 0:512], scalar1=wc(0, j), scalar2=0.0,
                                                               op0=ALU.mult, op1=ALU.add), reads=[f"pre{j}", "spm"], writes=[f"acc{j}"])
                else:
                    P.op("dve", lambda e, j=j, tap=tap: e.scalar_tensor_tensor(
                        out=acc[j], in0=pre[j][:, tap:tap + 512], scalar=wc(tap, j), in1=acc[j], op0=ALU.mult, op1=ALU.add),
                        reads=[f"pre{j}", "spm", f"acc{j}"], writes=[f"acc{j}"])
                yield
        for j in range(3):
            P.op("act", lambda e, j=j: e.activation(out=acc[j], in_=acc[j], func=AF.Silu), reads=[f"acc{j}"], writes=[f"acc{j}"])
            yield
        for j in range(2):
            P.op("act", lambda e, j=j: e.activation(out=sqb[:, j * 512:(j + 1) * 512], in_=acc[j], func=AF.Square),
                 reads=[f"acc{j}"], writes=[f"sqb{j}"])
            yield
            P.op("pe", lambda e, j=j: e.matmul(ps[j], cm(C_ONES, True), sqb[:, j * 512:(j + 1) * 512], start=True, stop=True),
                 reads=[f"sqb{j}", "cstb"], writes=[f"ps{j}"])
            yield
        for j in range(2):
            lb = lnb if j == 0 else rn
            tk_ = "lnb" if j == 0 else "rn"
            P.op("act", lambda e, j=j, lb=lb: e.activation(out=lb, in_=ps[j], func=AF.Ln, bias=EPS, scale=1.0),
                 reads=[f"ps{j}"], writes=[tk_])
            yield
            P.op("act", lambda e, j=j, lb=lb: e.activation(out=lb, in_=lb, func=AF.Exp, scale=-0.5,
                                                          bias=(float(np.log(128.0 ** -0.5)) if j == 0 else 0.0)),
                 reads=[tk_], writes=[tk_])
            yield
        P.op("dve", lambda e: e.tensor_tensor(out=qTb[a], in0=acc[0], in1=lnb, op=ALU.mult), reads=["acc0", "lnb"], writes=[f"qTb{a}"])
        yield
        P.op("dve", lambda e: e.tensor_tensor(out=acc[1], in0=acc[1], in1=rn, op=ALU.mult), reads=["acc1", "rn"], writes=["acc1"])
        yield
        P.op("pool", lambda e: e.tensor_copy(out=kTb[a], in_=acc[1]), reads=["acc1"], writes=[f"kTb{a}"])
        yield
        for n in range(4):
            sl = slice(n * 128, (n + 1) * 128)
            P.op("pe", lambda e, sl=sl: e.transpose(ps[0][:, sl], acc[1][:, sl], cm(C_IDENT)), reads=["acc1", "cst"], writes=["ps0"])
            yield
        for n in range(4):
            sl = slice(n * 128, (n + 1) * 128)
            P.op("pe", lambda e, sl=sl: e.transpose(ps[1][:, sl], acc[2][:, sl], cm(C_IDENT)), reads=["acc2", "cst"], writes=["ps1"])
            yield
        for n in range(4):
            sl = slice(n * 128, (n + 1) * 128)
            col = n * 8 + h
            P.op("act", lambda e, sl=sl, col=col: e.activation(out=kdec[a][:, sl], in_=ps[0][:, sl], func=AF.Identity,
                                                               scale=tkDks[:, col:col + 1]), reads=["ps0", "tkDks"], writes=[f"kdec{a}"])
            yield
        for n in range(4):
            sl = slice(n * 128, (n + 1) * 128)
            col = n * 8 + h
            P.op("act", lambda e, sl=sl, col=col: e.activation(out=vb[a][:, sl], in_=ps[1][:, sl], func=AF.Identity,
                                                               scale=tkBeta[:, col:col + 1]), reads=["ps1", "tkBeta"], writes=[f"vb{a}"])
            yield

    def gdn_B(l, tb, h, g0):
        a = h % 3
        bs = h % 2
        q_, k_, kd_, vb_, zs_ = qTb[a], kTb[a], kdec[a], vb[a], zs[a]
        qt, kt, kdt, vbt, zst = f"qTb{a}", f"kTb{a}", f"kdec{a}", f"vb{a}", f"zs{a}"
        tiles = [(n, slice(n * 128, (n + 1) * 128), n * 8 + h) for n in range(4)]
        for n, sl, col in tiles:
            P.op("pe", lambda e, sl=sl: e.matmul(ps[3][:, sl], k_[:, sl], k_[:, sl], start=True, stop=True), reads=[kt], writes=["ps3"])
            P.op("pe", lambda e, sl=sl: e.matmul(ps[4][:, sl], k_[:, sl], q_[:, sl], start=True, stop=True), reads=[kt, qt], writes=["ps4"])
            P.op("pool", lambda e, sl=sl, col=col: e.tensor_scalar(out=E1[:, sl], in0=cm(C_TRIU_I), scalar1=tkG[:, col:col + 1], scalar2=0.0,
                                                                   op0=ALU.mult, op1=ALU.add), reads=["cst", "tkG"], writes=["E1"])
            yield
            P.op("pe", lambda e, sl=sl: e.matmul(ps[5][:, sl], cm(C_ONES), E1[:, sl], start=True, stop=True), reads=["cst", "E1"], writes=["ps5"])
            yield
        P.op("act", lambda e: e.activation(out=egbc, in_=ps[5], func=AF.Exp), reads=["ps5"], writes=["egbc"])
        yield
        for n, sl, col in tiles:
            P.op("dve", lambda e, sl=sl, col=col: e.tensor_scalar(out=E1[:, sl], in0=ps[5][:, sl], scalar1=tkGc[:, col:col + 1], scalar2=0.0,
                                                                  op0=ALU.subtract, op1=ALU.min), reads=["ps5", "tkGc"], writes=["E1"])
            yield
            P.op("dve", lambda e, sl=sl, col=col: e.tensor_scalar(out=E2[:, sl], in0=ps[5][:, sl], scalar1=tkGc[:, col:col + 1], scalar2=0.0,
                                                                  op0=ALU.subtract, op1=ALU.max), reads=["ps5", "tkGc"], writes=["E2"])
            yield
        P.op("act", lambda e: e.activation(out=E1, in_=E1, func=AF.Exp), reads=["E1"], writes=["E1"])
        yield
        P.op("act", lambda e: e.activation(out=E2, in_=E2, func=AF.Exp, scale=-1.0), reads=["E2"], writes=["E2"])
        yield
        for n, sl, col in tiles:
            P.op("dve", lambda e, sl=sl, col=col: e.scalar_tensor_tensor(out=t1[:, sl], in0=ps[3][:, sl], scalar=tkBeta[:, col:col + 1],
                                                                      in1=E2[:, sl], op0=ALU.mult, op1=ALU.mult),
                 reads=["ps3", "E2", "tkBeta"], writes=["t1"])
            yield
        P.op("dve", lambda e: e.tensor_tensor(out=t2, in0=ps[4], in1=E1, op=ALU.mult), reads=["ps4", "E1"], writes=["t2"])
        yield
        P.op("pool", lambda e: e.tensor_tensor(out=qdec[bs], in0=q_, in1=egbc, op=ALU.mult), reads=[qt, "egbc"], writes=[f"qdec{bs}"])
        yield
        L0, U0, Y0 = Lb[0], Ub[0], Yb[0]
        for n, sl, col in tiles:
            P.op("pool", lambda e, sl=sl: e.tensor_tensor(out=L0[:, sl], in0=t1[:, sl], in1=cm(C_BDTRIL_S), op=ALU.mult),
                 reads=["t1", "cst"], writes=["Lb0"])
            yield
        for n, sl, col in tiles:
            P.op("pe", lambda e, sl=sl: e.transpose(ps[4][:, sl], L0[:, sl], cm(C_IDENT)), reads=["Lb0", "cst"], writes=["ps4"])
            yield
        P.op("act", lambda e: e.copy(out=U0, in_=ps[4]), reads=["ps4"], writes=["Ub0"])
        yield
        for n, sl, col in tiles:
            P.op("pool", lambda e, sl=sl: e.tensor_tensor(out=Y0[:, sl], in0=cm(C_IDENT), in1=U0[:, sl], op=ALU.subtract),
                 reads=["cst", "Ub0"], writes=["Yb0"])
            yield
            P.op("pool", lambda e, sl=sl: e.tensor_tensor(out=Offb[:, sl], in0=t1[:, sl], in1=cm(C_OFF), op=ALU.mult),
                 reads=["t1", "cst"], writes=["Offb"])
            yield
            P.op("pool", lambda e, sl=sl: e.tensor_tensor(out=attnT[bs][:, sl], in0=t2[:, sl], in1=cm(C_TRIU_I), op=ALU.mult),
                 reads=["t2", "cst"], writes=[f"attnT{bs}"])
            yield
        cur = 0
        for k in range(1, 6):
            nx = 1 - cur
            for n, sl, col in tiles:
                P.op("pe", lambda e, sl=sl, cur=cur: e.matmul(ps[3][:, sl], Ub[cur][:, sl], Lb[cur][:, sl], start=True, stop=True),
                     reads=[f"Ub{cur}", f"Lb{cur}"], writes=["ps3"])
                yield
            if k < 5:
                for n, sl, col in tiles:
                    P.op("pe", lambda e, sl=sl, cur=cur: e.matmul(ps[4][:, sl], Lb[cur][:, sl], Ub[cur][:, sl], start=True, stop=True),
                         reads=[f"Ub{cur}", f"Lb{cur}"], writes=["ps4"])
                    yield
            P.op("act", lambda e, nx=nx: e.copy(out=Lb[nx], in_=ps[3]), reads=["ps3"], writes=[f"Lb{nx}"])
            yield
            if k < 5:
                P.op("dve", lambda e, nx=nx: e.tensor_copy(out=Ub[nx], in_=ps[4]), reads=["ps4"], writes=[f"Ub{nx}"])
                yield
            for n, sl, col in tiles:
                P.op("pe", lambda e, sl=sl, nx=nx, cur=cur: e.matmul(ps[5][:, sl], Lb[nx][:, sl], Yb[cur][:, sl], start=True, stop=True),
                     reads=[f"Lb{nx}", f"Yb{cur}"], writes=["ps5"])
                yield
            P.op("dve", lambda e, nx=nx, cur=cur: e.tensor_tensor(out=Yb[nx], in0=ps[5], in1=Yb[cur], op=ALU.add),
                 reads=["ps5", f"Yb{cur}"], writes=[f"Yb{nx}"])
            yield
            cur = nx
        Yd = Yb[cur]
        ytk = f"Yb{cur}"
        for n, sl, col in tiles:
            P.op("pe", lambda e, sl=sl: e.transpose(ps[4][:, sl], Yd[:, sl], cm(C_IDENT)), reads=[ytk, "cst"], writes=["ps4"])
            P.op("pe", lambda e, sl=sl: e.matmul(ps[3][:, sl], Offb[:, sl], Yd[:, sl], start=True, stop=True), reads=["Offb", ytk], writes=["ps3"])
            yield
        P.op("act", lambda e: e.copy(out=t1, in_=ps[4]), reads=["ps4"], writes=["t1"])
        yield
        P.op("dve", lambda e: e.tensor_copy(out=t2, in_=ps[3]), reads=["ps3"], writes=["t2"])
        yield
        for n, sl, col in tiles:
            P.op("pe", lambda e, sl=sl: e.matmul(ps[5][:, sl], t1[:, sl], t2[:, sl], start=True, stop=True), reads=["t1", "t2"], writes=["ps5"])
            yield
        P.op("dve", lambda e: e.tensor_tensor(out=Ybf[bs], in0=Yd, in1=ps[5], op=ALU.subtract), reads=[ytk, "ps5"], writes=[f"Ybf{bs}"])
        yield
    def gdn_C(l, tb, h, g0):
        a = h % 3
        bs = h % 2
        q_, k_, kd_, vb_, zs_ = qTb[a], kTb[a], kdec[a], vb[a], zs[a]
        qt, kt, kdt, vbt, zst = f"qTb{a}", f"kTb{a}", f"kdec{a}", f"vb{a}", f"zs{a}"
        tiles = [(n, slice(n * 128, (n + 1) * 128), n * 8 + h) for n in range(4)]
        Sh = Sst[:, h * 128:(h + 1) * 128]
        Shb = Sbf[:, h * 128:(h + 1) * 128]
        for n, sl, col in tiles:
            r_ = rv[n % 2]
            v_ = vnw[n % 2]
            P.op("pe", lambda e, sl=sl: e.matmul(ps[7][:, 128:256], k_[:, sl], Shb, start=True, stop=True),
                 reads=[kt, f"Sbf{h}"], writes=["ps7"])
            yield
            P.op("dve", lambda e, sl=sl, col=col, r_=r_: e.scalar_tensor_tensor(out=r_, in0=ps[7][:, 128:256], scalar=tkNbeg[:, col:col + 1],
                                                                            in1=vb_[:, sl], op0=ALU.mult, op1=ALU.add),
                 reads=["ps7", "tkNbeg", vbt], writes=[f"rv{n % 2}"])
            yield
            P.op("pe", lambda e, sl=sl, r_=r_: e.matmul(ps[7][:, 384:512], Ybf[bs][:, sl], r_, start=True, stop=True),
                 reads=[f"Ybf{bs}", f"rv{n % 2}"], writes=["ps7"])
            yield
            P.op("act", lambda e, v_=v_: e.copy(out=v_, in_=ps[7][:, 384:512]), reads=["ps7"], writes=[f"vnw{n % 2}"])
            yield
            P.op("pe", lambda e, sl=sl: e.matmul(ps[6][:, sl], Shb, qdec[bs][:, sl], start=True, stop=False),
                 reads=[f"Sbf{h}", f"qdec{bs}"], writes=["ps6"])
            P.op("pe", lambda e, sl=sl, v_=v_: e.matmul(ps[6][:, sl], v_, attnT[bs][:, sl], start=False, stop=True),
                 reads=[f"vnw{n % 2}", f"attnT{bs}"], writes=["ps6"])
            yield
            P.op("pe", lambda e, sl=sl, v_=v_: e.matmul(ps[7][:, 256:384], kd_[:, sl], v_, start=True, stop=True),
                 reads=[kdt, f"vnw{n % 2}"], writes=["ps7"])
            yield
            P.op("dve", lambda e, col=col: e.scalar_tensor_tensor(out=Sh, in0=Sh, scalar=tkEgl[:, col:col + 1], in1=ps[7][:, 256:384],
                                                                  op0=ALU.mult, op1=ALU.add),
                 reads=[f"S{h}", "tkEgl", "ps7"], writes=[f"S{h}"])
            yield
            P.op("pool", lambda e: e.tensor_copy(out=Shb, in_=Sh), reads=[f"S{h}"], writes=[f"Sbf{h}"])
            yield
        P.op("act", lambda e: e.copy(out=ob, in_=ps[6]), reads=["ps6"], writes=["ob"])
        yield
        P.op("act", lambda e: e.activation(out=sqc, in_=ob, func=AF.Square), reads=["ob"], writes=["sqc"])
        yield
        P.op("pe", lambda e: e.matmul(ps[6], cm(C_ONES, True), sqc, start=True, stop=True), reads=["sqc", "cstb"], writes=["ps6"])
        yield
        P.op("act", lambda e: e.activation(out=rn3, in_=ps[6], func=AF.Ln, bias=EPS, scale=1.0 / 128), reads=["ps6"], writes=["rn3"])
        yield
        P.op("act", lambda e: e.activation(out=rn3, in_=rn3, func=AF.Exp, scale=-0.5), reads=["rn3"], writes=["rn3"])
        yield
        P.op("dve", lambda e: e.tensor_tensor(out=ob, in0=ob, in1=rn3, op=ALU.mult), reads=["ob", "rn3"], writes=["ob"])
        yield
        P.op("dve", lambda e: e.scalar_tensor_tensor(out=ogT[:, h * 512:(h + 1) * 512], in0=ob, scalar=spm[:, g0 + 96:g0 + 97], in1=zs_,
                                                     op0=ALU.mult, op1=ALU.mult), reads=["ob", "spm", zst], writes=["ogT"])
        yield

    aux = [None]

    def emit_gdn_block_heads(l, tb, g0):
        for step in range(-2, 8):
            gens = []
            wts = []
            if aux[0] is not None:
                gens.append(take(aux[0], AUX_N))
                wts.append(1)
            if 0 <= step < 8:
                gens.append(gdn_C(l, tb, step, g0))
                wts.append(GDN_W[0])
            if 0 <= step + 1 < 8:
                gens.append(gdn_B(l, tb, step + 1, g0))
                wts.append(GDN_W[1])
            if 0 <= step + 2 < 8:
                gens.append(gdn_A(l, tb, step + 2, g0))
                wts.append(GDN_W[2])
            run_interleaved(gens, wts)

    bar_n = [0]

    def emit_barrier():
        k = bar_n[0]
        bar_n[0] += 1
        P.op("act", lambda e: e.activation(out=bscr[:, 0:1], in_=cact[:, 0:1], func=AF.Identity), reads=["cact"], writes=[f"bar{k}_act", "bscr0"])
        P.op("dve", lambda e: e.tensor_copy(out=bscr[:, 1:2], in_=cact[:, 0:1]), reads=["cact"], writes=[f"bar{k}_dve", "bscr1"])
        P.op("pool", lambda e: e.tensor_copy(out=bscr[:, 2:3], in_=cact[:, 0:1]), reads=["cact"], writes=[f"bar{k}_pool", "bscr2"])
        P.op("pe", lambda e: e.matmul(ps[7][:, 0:8], cm(C_ONES), cact[:, 0:8], start=True, stop=True), reads=["cact", "cst"], writes=["ps7", f"bar{k}_pe"])
        allb = [f"bar{k}_{x}" for x in ("act", "dve", "pool", "pe")]
        P.op("act", lambda e: e.activation(out=bscr[:, 3:4], in_=cact[:, 0:1], func=AF.Identity), reads=allb + ["cact"], writes=["bscr3"])
        P.op("dve", lambda e: e.tensor_copy(out=bscr[:, 4:5], in_=ps[7][:, 0:1]), reads=allb, writes=["bscr4", "ps7"])
        P.op("pool", lambda e: e.tensor_copy(out=bscr[:, 5:6], in_=cact[:, 0:1]), reads=allb + ["cact"], writes=["bscr5"])
        P.op("pe", lambda e: e.matmul(ps[7][:, 0:8], cm(C_ONES), cact[:, 0:8], start=True, stop=True), reads=allb + ["cact", "cst"], writes=["ps7"])
        P.op("sp", lambda e: e.dma_start(out=bscr[:, 6:8], in_=spm_d[:, 0:2]), reads=allb, writes=["bscr6"], dma=True)

    emit_barrier()
    drain(gen_layer_mod(0))
    n_a = min(nlayers, 2)
    for l in range(n_a):
        if l == 0 and nlayers > 1:
            aux[0] = gen_layer_mod(1)
        if l == 1 and nlayers > 2:
            gl_ = [gen_kv_mod(), gen_layer_mod(2)]
            if nlayers > 3:
                gl_.append(gen_layer_mod(3))
            aux[0] = chain(*gl_)
        emit_gdn_layer(l)
        if aux[0] is not None:
            drain(aux[0])
            aux[0] = None
    emit_barrier()
    gstack.close()
    sb = sb_keep

    ost2 = [sb(f"Eo{i}", [128, 512]) for i in range(2)]
    if nlayers > 2:
        KT = sb("KT", [128, 8 * T], BF16)
        Vt = sb("Vt", [128, 16 * D], BF16)
        qn = sb("qn", [128, 8 * 512], BF16)
        Ebuf = ost2
        SPR = [[sb(f"SPR{i}{u}", [128, 512], F32R) for u in range(2)] for i in range(2)]
        dbf = [[sb(f"dbf{i}{u}", [128, 512]) for u in range(2)] for i in range(2)]
        Wb = [sb(f"Wb{i}", [128, 512], BF16) for i in range(2)]
        rn2 = tmpn[1]
        cstr = sb("cstr", [128, 256], F32R)
        P.op("dve", lambda e: e.tensor_copy(out=cstr[:, 0:128], in_=cm(C_TRIL_S)), reads=["cst"], writes=["cstr"])
        P.op("dve", lambda e: e.tensor_copy(out=cstr[:, 128:256], in_=cm(C_TRIU_I)), reads=["cst"], writes=["cstr"])

        def emit_headnorm(psb, gain_col, extra_bias, dst, dst_tok):
            tm = tmpn[0]
            P.op("act", lambda e: e.copy(out=tm, in_=psb), reads=[psb_tok[0]], writes=["tmpn0"])
            P.op("act", lambda e: e.activation(out=sqb[:, 0:512], in_=tm, func=AF.Square), reads=["tmpn0"], writes=["sqb0"])
            P.op("pe", lambda e: e.matmul(ps[4], cm(C_BDONES, True), sqb[:, 0:512], start=True, stop=True), reads=["sqb0", "cstb"], writes=["ps4"])
            P.op("act", lambda e: e.activation(out=rn2, in_=ps[4], func=AF.Ln, bias=EPS, scale=1.0 / 64), reads=["ps4"], writes=["tmpn1"])
            P.op("act", lambda e: e.activation(out=rn2, in_=rn2, func=AF.Exp, scale=-0.5, bias=extra_bias), reads=["tmpn1"], writes=["tmpn1"])
            P.op("dve", lambda e: e.scalar_tensor_tensor(out=dst, in0=tm, scalar=spm[:, gain_col:gain_col + 1], in1=rn2,
                                                         op0=ALU.mult, op1=ALU.mult), reads=["tmpn0", "spm", "tmpn1"], writes=[dst_tok])

        psb_tok = [None]

        def emit_kv():
            for tb in range(NB):
                emit_norm_block(tb, 4, 96)
                for piece in range(2):
                    P.op("pool", lambda e, piece=piece: e.dma_start(out=v3(wq[piece], 8), in_=wview(w_kv_d, piece * 512, 512)),
                         writes=[f"wq{piece}"], dma=True)
                    for fcl in range(4):
                        fc = piece * 4 + fcl
                        bk = fc % 4
                        for c in range(8):
                            P.op("pe", lambda e, piece=piece, fcl=fcl, c=c, bk=bk: e.matmul(
                                ps[bk], wq[piece][:, c * 512 + fcl * 128:c * 512 + (fcl + 1) * 128], hT[:, c * 512:(c + 1) * 512],
                                start=(c == 0), stop=(c == 7)), reads=[f"wq{piece}", "hT"], writes=[f"ps{bk}"])
                        psb_tok[0] = f"ps{bk}"
                        emit_headnorm(ps[bk], SP_KG, 0.0, KT[:, fc * T + tb * 512:fc * T + (tb + 1) * 512], "KT")
                for piece in range(2):
                    P.op("pool", lambda e, piece=piece: e.dma_start(out=v3(wq[piece], 8), in_=wview(w_kv_d, D + piece * 512, 512)),
                         writes=[f"wq{piece}"], dma=True)
                    for n in range(4):
                        bk = n % 4
                        tile = tb * 4 + n
                        for c in range(8):
                            P.op("pe", lambda e, piece=piece, n=n, c=c, bk=bk: e.matmul(
                                ps[bk], hT[:, c * 512 + n * 128:c * 512 + (n + 1) * 128], wq[piece][:, c * 512:(c + 1) * 512],
                                start=(c == 0), stop=(c == 7)), reads=[f"wq{piece}", "hT"], writes=[f"ps{bk}"])
                        dst = Vt[:, tile * D + piece * 512:tile * D + (piece + 1) * 512]
                        if n % 2 == 0:
                            P.op("act", lambda e, dst=dst, bk=bk: e.copy(out=dst, in_=ps[bk]), reads=[f"ps{bk}"], writes=["Vt"])
                        else:
                            P.op("dve", lambda e, dst=dst, bk=bk: e.tensor_copy(out=dst, in_=ps[bk]), reads=[f"ps{bk}"], writes=["Vt"])

        def sb_head(g, ch, hh, s_):
            h = 2 * ch + hh
            base = hh * 64
            bA, bB = (0, 1) if s_ == 0 else (2, 3)
            Eb, wbu = Ebuf[s_], Wb[s_]
            chunks = list(range(4 * g + 3, -1, -1))

            def geom(i):
                r0 = max(i - 4 * g, 0)
                return r0 * 128, (i >= 4 * g)

            def front(k):
                i = chunks[k]
                c0, diag = geom(i)
                u = k % 2
                Sr = SPR[s_][u]
                P.op("pe", lambda e: e.matmul(
                    ps[bA][:, c0:512], KT[base:base + 64, ch * T + i * 128:ch * T + (i + 1) * 128],
                    qn[base:base + 64, ch * 512 + c0:ch * 512 + 512], start=True, stop=True),
                    reads=["KT", "qn"], writes=[f"ps{bA}"])
                yield
                P.op("act", lambda e: e.activation(out=Eb[:, c0:512], in_=ps[bA][:, c0:512], func=AF.Exp),
                     reads=[f"ps{bA}"], writes=[f"E{s_}"])
                yield
                P.op("act", lambda e: e.activation(out=Sr[:, c0:512], in_=Eb[:, c0:512], func=AF.Ln, bias=1.0, scale=1.0),
                     reads=[f"E{s_}"], writes=[f"SPR{s_}{u}"])
                yield
                if diag:
                    P.op("pool", lambda e: e.tensor_tensor(out=Sr[:, c0:c0 + 128], in0=Sr[:, c0:c0 + 128].bitcast(F32),
                                                           in1=cm(C_TRIU_S), op=ALU.mult),
                         reads=[f"SPR{s_}{u}", "cst"], writes=[f"SPR{s_}{u}"])
                    yield

            def back1(k):
                i = chunks[k]
                c0, diag = geom(i)
                u = k % 2
                Sr, dbu = SPR[s_][u], dbf[s_][u]
                P.op("pe", lambda e: e.matmul(
                    ps[bB][:, c0:512], cstr[:, 0:128], Sr[:, c0:512], start=(k == 0), stop=False, skip_group_check=True),
                    reads=[f"SPR{s_}{u}", "cstr"], writes=[f"ps{bB}"])
                yield
                P.op("dve", lambda e: e.tensor_tensor(
                    out=dbu[:, c0:512], in0=ps[bA][:, c0:512], in1=Sr[:, c0:512].bitcast(F32), op=ALU.subtract),
                    reads=[f"ps{bA}", f"SPR{s_}{u}"], writes=[f"dbf{s_}{u}"])
                yield

            def back2(k):
                i = chunks[k]
                c0, diag = geom(i)
                u = k % 2
                Sr, dbu = SPR[s_][u], dbf[s_][u]
                P.op("dve", lambda e: e.tensor_tensor(
                    out=dbu[:, c0:512], in0=dbu[:, c0:512], in1=ps[bB][:, c0:512], op=ALU.subtract),
                    reads=[f"ps{bB}", f"dbf{s_}{u}"], writes=[f"dbf{s_}{u}"])
                yield
                if i > 0:
                    P.op("pe", lambda e: e.matmul(
                        ps[bB][:, c0:512], cstr[:, 128:256], Sr[:, c0:512], start=False, stop=False, skip_group_check=True),
                        reads=[f"SPR{s_}{u}", "cstr"], writes=[f"ps{bB}"])
                    yield
                P.op("act", lambda e: e.activation(out=wbu[:, c0:512], in_=dbu[:, c0:512], func=AF.Exp),
                     reads=[f"dbf{s_}{u}"], writes=[f"Wb{s_}"])
                yield
                if diag:
                    P.op("pool", lambda e: e.tensor_tensor(out=wbu[:, c0:c0 + 128], in0=wbu[:, c0:c0 + 128],
                                                           in1=cm(C_TRIU_S, True), op=ALU.mult),
                         reads=[f"Wb{s_}", "cstb"], writes=[f"Wb{s_}"])
                    yield
                P.op("pe", lambda e: e.matmul(
                    ps[6][base:base + 64, c0:512], Vt[:, i * D + h * 64:i * D + (h + 1) * 64], wbu[:, c0:512],
                    start=False, stop=False, skip_group_check=True, tile_position=(0, base)),
                    reads=["Vt", f"Wb{s_}"], writes=["ps6"])
                yield

            yield from front(0)
            for k in range(len(chunks)):
                yield from back1(k)
                if k + 1 < len(chunks):
                    yield from front(k + 1)
                yield from back2(k)

        def run_interleaved(gens):
            gens = list(gens)
            while gens:
                for g_ in list(gens):
                    try:
                        next(g_)
                    except StopIteration:
                        gens.remove(g_)

        def emit_sb_layer(l2):
            L = 2 + l2
            for g in range(NB):
                emit_norm_block(g, L, 24 * L)
                for piece in range(2):
                    P.op("pool", lambda e, piece=piece: e.dma_start(out=v3(wq[piece], 8), in_=wview(w_inb_d[l2], piece * 512, 512)),
                         writes=[f"wq{piece}"], dma=True)
                    for fcl in range(4):
                        fc = piece * 4 + fcl
                        bk = fc % 4
                        for c in range(8):
                            P.op("pe", lambda e, piece=piece, fcl=fcl, c=c, bk=bk: e.matmul(
                                ps[bk], wq[piece][:, c * 512 + fcl * 128:c * 512 + (fcl + 1) * 128], hT[:, c * 512:(c + 1) * 512],
                                start=(c == 0), stop=(c == 7)), reads=[f"wq{piece}", "hT"], writes=[f"ps{bk}"])
                        psb_tok[0] = f"ps{bk}"
                        emit_headnorm(ps[bk], SP_QG + l2, float(np.log(0.125)), qn[:, fc * 512:(fc + 1) * 512], "qn")
                for piece in range(2):
                    P.op("pool", lambda e, piece=piece: e.dma_start(out=v3(wq[piece], 8), in_=wview(w_inb_d[l2], D + piece * 512, 512)),
                         writes=[f"wq{piece}"], dma=True)
                    for fcl in range(4):
                        fc = piece * 4 + fcl
                        bk = fc % 4
                        for c in range(8):
                            P.op("pe", lambda e, piece=piece, fcl=fcl, c=c, bk=bk: e.matmul(
                                ps[bk], wq[piece][:, c * 512 + fcl * 128:c * 512 + (fcl + 1) * 128], hT[:, c * 512:(c + 1) * 512],
                                start=(c == 0), stop=(c == 7)), reads=[f"wq{piece}", "hT"], writes=[f"ps{bk}"])
                        P.op("act", lambda e, fc=fc, bk=bk: e.activation(out=ogT[:, fc * 512:(fc + 1) * 512], in_=ps[bk], func=AF.Silu),
                             reads=[f"ps{bk}"], writes=["ogT"])
                for ch in range(8):
                    P.op("dve", lambda e: e.memset(ps[6], 0.0), writes=["ps6"])
                    run_interleaved([sb_head(g, ch, 0, 0), sb_head(g, ch, 1, 1)])
                    P.op("dve", lambda e, ch=ch: e.tensor_tensor(out=ogT[:, ch * 512:(ch + 1) * 512], in0=ps[6], in1=ogT[:, ch * 512:(ch + 1) * 512],
                                                                 op=ALU.mult), reads=["ps6", "ogT"], writes=["ogT"])
                emit_outproj(w_outb_d[l2], g, L)

        emit_kv()
        for l2 in range(nlayers - 2):
            emit_sb_layer(l2)

    outs = []
    for n in range(16):
        tb = n // 4
        for half in range(2):
            bk = (2 * n + half) % 4
            oi = (2 * n + half) % 2
            for c4 in range(4):
                c = half * 4 + c4
                P.op("pe", lambda e, n=n, c=c, c4=c4, bk=bk: e.transpose(
                    ps[bk][:, c4 * 128:(c4 + 1) * 128], xT[:, c * T + n * 128:c * T + (n + 1) * 128], cm(C_IDENT)),
                    reads=[f"xT{tb}", "cst"], writes=[f"ps{bk}"])
            if half == 0:
                P.op("act", lambda e, oi=oi, bk=bk: e.copy(out=ost2[oi], in_=ps[bk]), reads=[f"ps{bk}"], writes=[f"E{oi}"])
            else:
                P.op("dve", lambda e, oi=oi, bk=bk: e.tensor_copy(out=ost2[oi], in_=ps[bk]), reads=[f"ps{bk}"], writes=[f"E{oi}"])
            outs.append(P.op("sp", lambda e, n=n, half=half, oi=oi: e.dma_start(
                out=out_d[n * 128:(n + 1) * 128, half * 512:(half + 1) * 512], in_=ost2[oi]),
                reads=[f"E{oi}"], dma=True))
    P.finalize(final_wait_ops=outs)
    return nc


def _prep_inputs(inp):
    inp = {k: np.asarray(v) for k, v in inp.items()}
    w_in_a = inp["w_in_a"].astype(np.float32, copy=False)
    qkvz = w_in_a[:, :, :4096].reshape(2, D, 4, 8, 128).transpose(0, 1, 3, 2, 4).reshape(2, D, 4096)
    shared = {
        "cst": _consts(),
        "w_ada": np.ascontiguousarray(inp["w_ada"], np.float32),
        "w_inp": np.ascontiguousarray(qkvz),
        "w_ba": np.ascontiguousarray(w_in_a[:, :, 4096:4112]),
        "w_out_a": np.ascontiguousarray(inp["w_out_a"], np.float32),
        "w_ada_kv": np.ascontiguousarray(inp["w_ada_kv"], np.float32),
        "w_kv": np.ascontiguousarray(inp["w_kv"], np.float32),
        "w_in_b": np.ascontiguousarray(inp["w_in_b"], np.float32),
        "w_out_b": np.ascontiguousarray(inp["w_out_b"], np.float32),
    }
    in_maps = []
    for b in range(8):
        m = dict(shared)
        m["x"] = np.ascontiguousarray(inp["x"][b], np.float32)
        m["spm"] = _small_params(inp, b)
        in_maps.append(m)
    return in_maps


def kernel(**inputs):
    in_maps = _prep_inputs(inputs)
    nc = build(4)
    res = run_bass_kernel_spmd(nc, in_maps, core_ids=list(range(8)))
    return np.stack([np.asarray(r["out"], np.float32) for r in res.results], axis=0)
```

```python
import numpy as np
import concourse.bass as bass
import concourse.mybir as mybir
from concourse.bass_utils import run_bass_kernel_spmd
from contextlib import ExitStack

F32 = mybir.dt.float32
BF16 = mybir.dt.bfloat16
F32R = mybir.dt.float32r
AF = mybir.ActivationFunctionType
ALU = mybir.AluOpType

T = 2048
D = 1024
NB = 4
EPS = 1e-6
AUX_N = 13
GDN_W = (1, 1, 1)


class Tok:
    __slots__ = ("name", "w", "r")

    def __init__(self, name=""):
        self.name = name
        self.w = None
        self.r = []


class Op:
    __slots__ = ("eng", "fn", "deps", "idx", "need_inc", "ev", "is_dma", "dsem", "dval")

    def __init__(self, eng, fn):
        self.eng = eng
        self.fn = fn
        self.deps = []
        self.idx = None
        self.need_inc = False
        self.ev = None
        self.is_dma = False


class Prog:
    ENG = ("pe", "act", "dve", "pool", "sp")
    SEM_LIMIT = 30000
    NDMA = 16
    NEAR = 6

    def __init__(self, nc):
        self.nc = nc
        self.q = {e: [] for e in self.ENG}
        self.toks = {}

    def tk(self, name):
        t = self.toks.get(name)
        if t is None:
            t = Tok(name)
            self.toks[name] = t
        return t

    def op(self, eng, fn, reads=(), writes=(), dma=False):
        o = Op(eng, fn)
        o.is_dma = dma
        o.idx = len(self.q[eng])
        deps = set()
        rd, wr = [], []
        for t in reads:
            if isinstance(t, str) and t.startswith("ps") and t[2].isdigit():
                wr.append(t[:3])
            else:
                rd.append(t)
        for t in writes:
            if isinstance(t, str) and t.startswith("ps") and t[2].isdigit():
                wr.append(t[:3])
            else:
                wr.append(t)
        reads = [self.tk(t) if isinstance(t, str) else t for t in rd]
        writes = [self.tk(t) if isinstance(t, str) else t for t in dict.fromkeys(wr)]
        for t in reads:
            if t.w is not None:
                deps.add(t.w)
        for t in writes:
            if t.w is not None:
                deps.add(t.w)
            for r in t.r:
                deps.add(r)
        deps.discard(o)
        o.deps = list(deps)
        for t in reads:
            t.r.append(o)
        for t in writes:
            t.w = o
            t.r = []
        self.q[eng].append(o)
        return o

    def finalize(self, final_wait_ops=()):
        nc = self.nc
        waits = {}
        for e in self.ENG:
            seen = {}
            for o in self.q[e]:
                best = {}
                res = []
                for d in o.deps:
                    if d.is_dma:
                        res.append(d)
                        continue
                    if d.eng == o.eng:
                        if e == "pe" or o.is_dma:
                            if not o.is_dma:
                                continue
                        if (not o.is_dma) and o.idx - d.idx > self.NEAR:
                            continue
                    if d.eng not in best or best[d.eng].idx < d.idx:
                        best[d.eng] = d
                for d in best.values():
                    if seen.get(d.eng, -1) >= d.idx:
                        continue
                    seen[d.eng] = d.idx
                    res.append(d)
                waits[o] = res
                for d in res:
                    if not d.is_dma:
                        d.need_inc = True
        stack = ExitStack()
        for e in self.ENG:
            n = sum(1 for o in self.q[e] if o.need_inc and not o.is_dma)
            k = max(1, (n + self.SEM_LIMIT - 1) // self.SEM_LIMIT)
            sems = [stack.enter_context(nc.semaphore(f"s_{e}_{i}")) for i in range(k)]
            c = 0
            for o in self.q[e]:
                if o.need_inc and not o.is_dma:
                    o.ev = (sems[c // self.SEM_LIMIT], c % self.SEM_LIMIT + 1)
                    c += 1
        dsems = {e: [stack.enter_context(nc.semaphore(f"s_dma_{e}_{i}")) for i in range(self.NDMA)]
                 for e in ("sp", "pool")}
        for e in ("sp", "pool"):
            di = 0
            for o in self.q[e]:
                if o.is_dma:
                    o.dsem = dsems[e][di % self.NDMA]
                    o.dval = 16 * (di // self.NDMA + 1)
                    o.ev = (o.dsem, o.dval)
                    di += 1
        final = list(final_wait_ops)
        with nc.Block() as block:
            def run(engname, eng):
                for o in self.q[engname]:
                    for d in waits[o]:
                        eng.wait_ge(d.ev[0], d.ev[1])
                    if o.is_dma and o.dval > 16:
                        eng.wait_ge(o.dsem, o.dval - 16)
                    ins = o.fn(eng)
                    if o.is_dma:
                        ins.then_inc(o.dsem, 16)
                    elif o.need_inc:
                        ins.then_inc(o.ev[0], 1)
                if engname == "sp":
                    for o in final:
                        eng.wait_ge(o.ev[0], o.ev[1])

            @block.tensor
            def _(t):
                run("pe", t)

            @block.scalar
            def _(t):
                run("act", t)

            @block.vector
            def _(t):
                run("dve", t)

            @block.gpsimd
            def _(t):
                run("pool", t)

            @block.sync
            def _(t):
                run("sp", t)
        stack.close()


C_IDENT, C_ONES, C_TRIU_I, C_TRIL_S, C_TRIU_S, C_BDTRIL_S, C_OFF, C_BDONES = range(8)
NCST = 8


def _consts():
    p = np.arange(128)[:, None]
    f = np.arange(128)[None, :]
    m = np.zeros((128, NCST, 128), np.float32)
    m[:, C_IDENT] = (p == f)
    m[:, C_ONES] = 1.0
    m[:, C_TRIU_I] = (p <= f)
    m[:, C_TRIL_S] = (f < p)
    m[:, C_TRIU_S] = (p < f)
    m[:, C_BDTRIL_S] = (f < p) & ((p // 64) == (f // 64))
    m[:, C_OFF] = (p >= 64) & (f < 64)
    m[:, C_BDONES] = ((p // 64) == (f // 64))
    return m.reshape(128, NCST * 128)


def _fm(v):
    v = np.asarray(v, np.float32).reshape(-1, 128)
    return np.ascontiguousarray(v.T)


SP_C = 0
SP_LAYER = 8
SP_KV = 136
SP_GDN = 160
SP_KG = 386
SP_QG = 387
NSP = 389


def _small_params(inp, b):
    sp = np.zeros((128, NSP), np.float32)
    sp[:, 0:8] = _fm(inp["c"][b])
    for l in range(4):
        o = SP_LAYER + 32 * l
        sp[:, o:o + 8] = _fm(inp["norm_g"][l])
        sp[:, o + 8:o + 32] = _fm(inp["b_ada"][l])
    sp[:, SP_KV:SP_KV + 8] = _fm(inp["kv_norm_g"])
    sp[:, SP_KV + 8:SP_KV + 24] = _fm(inp["b_ada_kv"])
    for l in range(2):
        o = SP_GDN + 113 * l
        cw = np.asarray(inp["conv_w_a"][l], np.float32)
        for j in range(4):
            sp[:, o + 24 * j:o + 24 * j + 24] = _fm(cw[j])
        sp[:, o + 96] = np.asarray(inp["o_gain_a"][l], np.float32)
        sp[:, o + 97:o + 105] = np.asarray(inp["a_log_a"][l], np.float32)[None, :]
        sp[:, o + 105:o + 113] = np.asarray(inp["dt_bias_a"][l], np.float32)[None, :]
    sp[:, SP_KG] = np.tile(np.asarray(inp["k_gain"], np.float32), 2)
    for l in range(2):
        sp[:, SP_QG + l] = np.tile(np.asarray(inp["q_gain_b"][l], np.float32), 2)
    return sp


def build(nlayers=4):
    nc = bass.Bass("TRN2", target_bir_lowering=False)
    dram = lambda n, s, k="ExternalInput": nc.dram_tensor(n, s, F32, kind=k).ap()
    x_d = dram("x", [T, D])
    spm_d = dram("spm", [128, NSP])
    cst_d = dram("cst", [128, NCST * 128])
    w_ada_d = dram("w_ada", [4, D, 3 * D])
    w_inp_d = dram("w_inp", [2, D, 8 * 512])
    w_ba_d = dram("w_ba", [2, D, 16])
    w_outa_d = dram("w_out_a", [2, D, D])
    w_adakv_d = dram("w_ada_kv", [D, 2 * D])
    w_kv_d = dram("w_kv", [D, 2 * D])
    w_inb_d = dram("w_in_b", [2, D, 2 * D])
    w_outb_d = dram("w_out_b", [2, D, D])
    out_d = dram("out", [T, D], "ExternalOutput")

    def wview(ap2d, c0, n):
        return ap2d[:, c0:c0 + n].rearrange("(c p) n -> p c n", p=128)

    sb = lambda n, s, d=F32: nc.alloc_sbuf_tensor("sb_" + n, s, d).ap()
    P = Prog(nc)

    def v3(ap, a):
        return ap.rearrange("p (a b) -> p a b", a=a)

    xT = sb("xT", [128, 8 * T])
    cst = sb("cst", [128, NCST * 128])
    cstb = sb("cstb", [128, NCST * 128], BF16)
    spm = sb("spm", [128, NSP])
    cact = sb("cact", [128, 8])
    modT = sb("modT", [128, 5 * 24])
    Acol = sb("Acol", [128, 5 * 8])
    hT = sb("hT", [128, 8 * 512], BF16)
    ogT = sb("ogT", [128, 8 * 512], BF16)
    sqb = sb("sqb", [128, 2 * 512], BF16)
    rstd = sb("rstd", [128, 512])
    bscr = sb("bscr", [128, 8])
    tmpn = [sb(f"tmpn{i}", [128, 512]) for i in range(2)]
    wq = [sb(f"wq{i}", [128, 8 * 512], BF16) for i in range(2)]
    ps = [nc.alloc_psum_tensor(f"ps{i}", [128, 512], F32).ap() for i in range(8)]
    gstack = ExitStack()
    sbp = lambda n, s, d=F32: gstack.enter_context(nc.sbuf_tensor("sb_" + n, s, d))[:]
    sb_keep = sb
    lnb = sbp("lnb", [128, 512])
    mrow = sbp("mrow", [1, 256])
    wa = [sbp("wa0", [128, 8 * 256])] * 2
    LU = sbp("LU", [128, 2048])
    xld = [LU[:, i * 1024:(i + 1) * 1024] for i in range(2)]

    def cm(i, bf=False):
        return (cstb if bf else cst)[:, i * 128:(i + 1) * 128]

    def xsl(c, tb):
        return xT[:, c * T + tb * 512:c * T + (tb + 1) * 512]

    P.op("sp", lambda e: e.dma_start(out=cst, in_=cst_d), writes=["cst"], dma=True)
    P.op("sp", lambda e: e.dma_start(out=spm, in_=spm_d), writes=["spm"], dma=True)
    P.op("pool", lambda e: e.tensor_copy(out=cstb, in_=cst), reads=["cst"], writes=["cstb"])
    P.op("act", lambda e: e.activation(out=cact, in_=spm[:, 0:8], func=AF.Silu), reads=["spm"], writes=["cact"])

    wa_i = [0]

    def gen_mod(wd2, ncols, bias0, dst0, slot, a_slot, g_col0, scale_col0):
        buf = wa[0]
        for piece in range(ncols // 256):
            P.op("sp", lambda e, piece=piece: e.dma_start(out=v3(buf, 8), in_=wview(wd2, piece * 256, 256)),
                 writes=["wa0"], dma=True)
            yield
            for c in range(8):
                P.op("pe", lambda e, c=c: e.matmul(ps[2][0:1, 0:256], cact[:, c:c + 1], buf[:, c * 256:(c + 1) * 256],
                                                   start=(c == 0), stop=(c == 7)), reads=["wa0", "cact"], writes=["ps2"])
                yield
            P.op("act", lambda e: e.copy(out=mrow, in_=ps[2][0:1, 0:256]), reads=["ps2"], writes=["mrow"])
            yield
            for fc in range(2):
                P.op("pe", lambda e, fc=fc: e.matmul(ps[2][:, 256 + fc:257 + fc], mrow[0:1, fc * 128:(fc + 1) * 128],
                                                     cst[0:1, C_ONES * 128:C_ONES * 128 + 1], start=True, stop=True),
                     reads=["mrow", "cst"], writes=["ps2"])
                yield
            d0 = dst0 + piece * 2
            b0 = bias0 + piece * 2
            P.op("dve", lambda e, d0=d0, b0=b0: e.tensor_tensor(out=modT[:, d0:d0 + 2], in0=ps[2][:, 256:258],
                                                                in1=spm[:, b0:b0 + 2], op=ALU.add),
                 reads=["ps2", "spm"], writes=[f"mod{slot}"])
            yield
        P.op("dve", lambda e: e.scalar_tensor_tensor(out=Acol[:, 8 * a_slot:8 * a_slot + 8], in0=modT[:, scale_col0:scale_col0 + 8],
                                                     scalar=1.0, in1=spm[:, g_col0:g_col0 + 8], op0=ALU.add, op1=ALU.mult),
             reads=[f"mod{slot}", "spm"], writes=[f"A{a_slot}"])
        yield

    def gen_layer_mod(l):
        return gen_mod(w_ada_d[l], 3 * D, SP_LAYER + 32 * l + 8, 24 * l, l, l, SP_LAYER + 32 * l, 24 * l + 8)

    def gen_kv_mod():
        return gen_mod(w_adakv_d, 2 * D, SP_KV + 8, 96, 4, 4, SP_KV, 104)

    def drain(gen):
        for _ in gen:
            pass

    def take(gen, n):
        for _ in range(n):
            try:
                next(gen)
            except StopIteration:
                return
            yield

    def chain(*gens):
        for g_ in gens:
            yield from g_

    for n in range(16):
        bi = n % 2
        tb = n // 4
        P.op("sp", lambda e, n=n, bi=bi: e.dma_start(out=xld[bi], in_=x_d[n * 128:(n + 1) * 128, :]),
             writes=[f"xld{bi}"], dma=True)
        for half in range(2):
            bk = (2 * n + half) % 4
            for c4 in range(4):
                c = half * 4 + c4
                P.op("pe", lambda e, bi=bi, c=c, c4=c4, bk=bk: e.transpose(
                    ps[bk][:, c4 * 128:(c4 + 1) * 128], xld[bi][:, c * 128:(c + 1) * 128], cm(C_IDENT)),
                    reads=[f"xld{bi}", "cst"], writes=[f"ps{bk}"])
            dst = v3(xT[:, half * 4 * T:(half * 4 + 4) * T], 4)[:, :, n * 128:(n + 1) * 128]
            src = v3(ps[bk], 4)
            if half == 0:
                P.op("act", lambda e, dst=dst, src=src: e.copy(out=dst, in_=src), reads=[f"ps{bk}"], writes=[f"xT{tb}"])
            else:
                P.op("dve", lambda e, dst=dst, src=src: e.tensor_copy(out=dst, in_=src), reads=[f"ps{bk}"], writes=[f"xT{tb}"])

    def emit_norm_block(tb, slot, shift0):
        for c in range(8):
            sq_ = sqb[:, (c % 2) * 512:(c % 2 + 1) * 512]
            P.op("act", lambda e, c=c, sq_=sq_: e.activation(out=sq_, in_=xsl(c, tb), func=AF.Square),
                 reads=[f"xT{tb}"], writes=[f"sqb{c % 2}"])
            P.op("pe", lambda e, c=c, sq_=sq_: e.matmul(ps[4], cm(C_ONES, True), sq_,
                                               start=(c == 0), stop=(c == 7)), reads=[f"sqb{c % 2}", "cstb"], writes=["ps4"])
        P.op("act", lambda e: e.activation(out=rstd, in_=ps[4], func=AF.Ln, bias=EPS, scale=1.0 / D), reads=["ps4"], writes=["rstd"])
        P.op("act", lambda e: e.activation(out=rstd, in_=rstd, func=AF.Exp, scale=-0.5), reads=["rstd"], writes=["rstd"])
        for c in range(8):
            tm = tmpn[c % 2]
            P.op("dve", lambda e, c=c, tm=tm: e.scalar_tensor_tensor(
                out=tm, in0=xsl(c, tb), scalar=Acol[:, 8 * slot + c:8 * slot + c + 1], in1=rstd,
                op0=ALU.mult, op1=ALU.mult), reads=[f"xT{tb}", f"A{slot}", "rstd"], writes=[f"tmpn{c % 2}"])
            P.op("act", lambda e, c=c, tm=tm: e.activation(
                out=hT[:, c * 512:(c + 1) * 512], in_=tm, func=AF.Identity,
                bias=modT[:, shift0 + c:shift0 + c + 1], scale=1.0),
                reads=[f"tmpn{c % 2}", f"mod{slot}"], writes=["hT"])

    def emit_outproj(wd2, tb, slot, og=None, og_toks=("ogT",)):
        og = ogT if og is None else og
        for half in range(2):
            P.op("pool", lambda e, half=half: e.dma_start(out=v3(wq[half], 8), in_=wview(wd2, half * 512, 512)),
                 writes=[f"wq{half}"], dma=True)
        for m in range(8):
            half, mm_ = divmod(m, 4)
            bk = (0, 1, 3, 4)[m % 4]
            for h in range(8):
                P.op("pe", lambda e, half=half, mm_=mm_, h=h, bk=bk: e.matmul(
                    ps[bk], wq[half][:, h * 512 + mm_ * 128:h * 512 + (mm_ + 1) * 128], og[:, h * 512:(h + 1) * 512],
                    start=(h == 0), stop=(h == 7)), reads=[f"wq{half}"] + list(og_toks), writes=[f"ps{bk}"])
            P.op("dve", lambda e, m=m, bk=bk: e.scalar_tensor_tensor(
                out=xsl(m, tb), in0=ps[bk], scalar=modT[:, 24 * slot + 16 + m:24 * slot + 17 + m], in1=xsl(m, tb),
                op0=ALU.mult, op1=ALU.add), reads=[f"ps{bk}", f"mod{slot}", f"xT{tb}"], writes=[f"xT{tb}"])

    sb = sbp
    wba = sb("wba", [128, 8 * 16], BF16)
    carry = sb("carry", [128, 8 * 3 * 4])
    pre = [sb(f"pre{j}", [128, 516]) for j in range(3)]
    acc = [sb(f"acc{j}", [128, 512]) for j in range(3)]
    zs = [sb(f"zs{i}", [128, 512], BF16) for i in range(3)]
    rn = sb("rn", [128, 512])
    rn3 = sb("rn3", [128, 512])
    sqc = sb("sqc", [128, 512], BF16)
    qTb = [sb(f"qTb{i}", [128, 512], BF16) for i in range(3)]
    kTb = [sb(f"kTb{i}", [128, 512], BF16) for i in range(3)]
    qdec = [sb(f"qdec{i}", [128, 512], BF16) for i in range(2)]
    kdec = [sb(f"kdec{i}", [128, 512], BF16) for i in range(3)]
    vb = [sb(f"vb{i}", [128, 512]) for i in range(3)]
    egbc = sb("egbc", [128, 512])
    E1 = sb("E1", [128, 512])
    E2 = sb("E2", [128, 512])
    t1 = sb("t1", [128, 512])
    t2 = sb("t2", [128, 512])
    Lb = [LU[:, i * 512:(i + 1) * 512] for i in range(2)]
    Ub = [LU[:, (2 + i) * 512:(3 + i) * 512] for i in range(2)]
    Yb = [sb(f"Yb{i}", [128, 512]) for i in range(2)]
    Offb = sb("Offb", [128, 512])
    Ybf = [sb(f"Ybf{i}", [128, 512], BF16) for i in range(2)]
    attnT = [sb(f"attnT{i}", [128, 512], BF16) for i in range(2)]
    rv = [sb(f"rv{i}", [128, 128], BF16) for i in range(2)]
    vnw = [sb(f"vnw{i}", [128, 128], BF16) for i in range(2)]
    Sst = sb("Sst", [128, 8 * 128])
    Sbf = sb("Sbf", [128, 8 * 128], BF16)
    ob = sb("ob", [128, 512])
    tkU = sb("tkU", [128, 32])
    tkG = sb("tkG", [128, 32])
    tkBeta = sb("tkBeta", [128, 32])
    tkGc = sb("tkGc", [128, 32])
    tkNbeg = sb("tkNbeg", [128, 32])
    tkGl = sb("tkGl", [128, 32])
    tkDks = sb("tkDks", [128, 32])
    tkEgl = sb("tkEgl", [128, 32])
    negA = sb("negA", [128, 8])
    sb = sb_keep

    def emit_gdn_layer(l):
        slot = l
        g0 = SP_GDN + 113 * l
        P.op("pool", lambda e: e.dma_start(out=v3(wba, 8), in_=w_ba_d[l].rearrange("(c p) n -> p c n", p=128)),
             writes=["wba"], dma=True)
        P.op("pool", lambda e: e.memset(carry, 0.0), writes=["carry"])
        P.op("pool", lambda e: e.memset(Sst, 0.0), writes=[f"S{i}" for i in range(8)])
        P.op("pool", lambda e: e.memset(Sbf, 0.0), writes=[f"Sbf{i}" for i in range(8)])
        P.op("act", lambda e: e.activation(out=negA, in_=spm[:, g0 + 97:g0 + 105], func=AF.Exp), reads=["spm"], writes=["negA"])
        P.op("dve", lambda e: e.tensor_scalar(out=negA, in0=negA, scalar1=-1.0, scalar2=None, op0=ALU.mult), reads=["negA"], writes=["negA"])
        for tb in range(NB):
            emit_norm_block(tb, slot, 24 * l)
            for n in range(4):
                for c in range(8):
                    P.op("pe", lambda e, n=n, c=c: e.matmul(
                        ps[7][:, n * 16:(n + 1) * 16], hT[:, c * 512 + n * 128:c * 512 + (n + 1) * 128],
                        wba[:, c * 16:(c + 1) * 16], start=(c == 0), stop=(c == 7)),
                        reads=["hT", "wba"], writes=["ps7a"])
            ba3 = v3(ps[7][:, 0:64], 4)
            for n in range(4):
                P.op("dve", lambda e, n=n: e.tensor_tensor(out=tkU[:, n * 8:(n + 1) * 8], in0=ps[7][:, n * 16 + 8:n * 16 + 16],
                                                          in1=spm[:, g0 + 105:g0 + 113], op=ALU.add),
                     reads=["ps7a", "spm"], writes=["tkU"])
            P.op("act", lambda e: e.activation(out=v3(tkBeta, 4), in_=ba3[:, :, 0:8], func=AF.Exp, scale=-1.0),
                 reads=["ps7a"], writes=["tkBeta"])
            P.op("act", lambda e: e.activation(out=tkU, in_=tkU, func=AF.Exp), reads=["tkU"], writes=["tkU"])
            P.op("act", lambda e: e.activation(out=tkU, in_=tkU, func=AF.Ln, bias=1.0, scale=1.0), reads=["tkU"], writes=["tkU"])
            for n in range(4):
                P.op("dve", lambda e, n=n: e.tensor_tensor(out=tkG[:, n * 8:(n + 1) * 8], in0=tkU[:, n * 8:(n + 1) * 8],
                                                          in1=negA, op=ALU.mult), reads=["tkU", "negA"], writes=["tkG"])
            P.op("dve", lambda e: e.tensor_scalar(out=tkBeta, in0=tkBeta, scalar1=1.0, scalar2=None, op0=ALU.add),
                 reads=["tkBeta"], writes=["tkBeta"])
            P.op("dve", lambda e: e.reciprocal(out=tkBeta, in_=tkBeta), reads=["tkBeta"], writes=["tkBeta"])
            for n in range(4):
                P.op("pe", lambda e, n=n: e.matmul(ps[7][:, 64 + n * 8:64 + (n + 1) * 8], cm(C_TRIU_I), tkG[:, n * 8:(n + 1) * 8],
                                                   start=True, stop=True), reads=["cst", "tkG"], writes=["ps7b"])
                P.op("pe", lambda e, n=n: e.matmul(ps[7][:, 96 + n * 8:96 + (n + 1) * 8], cm(C_ONES), tkG[:, n * 8:(n + 1) * 8],
                                                   start=True, stop=True), reads=["cst", "tkG"], writes=["ps7b"])
            P.op("dve", lambda e: e.tensor_copy(out=tkGc, in_=ps[7][:, 64:96]), reads=["ps7b"], writes=["tkGc"])
            P.op("dve", lambda e: e.tensor_copy(out=tkGl, in_=ps[7][:, 96:128]), reads=["ps7b"], writes=["tkGl"])
            P.op("act", lambda e: e.activation(out=tkEgl, in_=tkGl, func=AF.Exp), reads=["tkGl"], writes=["tkEgl"])
            P.op("dve", lambda e: e.tensor_tensor(out=tkDks, in0=tkGl, in1=tkGc, op=ALU.subtract), reads=["tkGl", "tkGc"], writes=["tkDks"])
            P.op("act", lambda e: e.activation(out=tkDks, in_=tkDks, func=AF.Exp), reads=["tkDks"], writes=["tkDks"])
            P.op("act", lambda e: e.activation(out=tkNbeg, in_=tkGc, func=AF.Exp), reads=["tkGc"], writes=["tkNbeg"])
            P.op("dve", lambda e: e.scalar_tensor_tensor(out=tkNbeg, in0=tkNbeg, scalar=-1.0, in1=tkBeta, op0=ALU.mult, op1=ALU.mult),
                 reads=["tkNbeg", "tkBeta"], writes=["tkNbeg"])

            emit_gdn_block_heads(l, tb, g0)
            emit_outproj(w_outa_d[l], tb, slot)

    def run_interleaved(gens, weights=None):
        gens = list(gens)
        weights = list(weights) if weights is not None else [1] * len(gens)
        while gens:
            for g_, w_ in list(zip(gens, weights)):
                for _ in range(w_):
                    try:
                        next(g_)
                    except StopIteration:
                        k_ = gens.index(g_)
                        gens.pop(k_)
                        weights.pop(k_)
                        break

    def gdn_A(l, tb, h, g0):
        a = h % 3
        wb = wq[h % 2]
        wtk = f"wq{h % 2}"
        P.op("pool", lambda e: e.dma_start(out=v3(wb, 8), in_=wview(w_inp_d[l], h * 512, 512)), writes=[wtk], dma=True)
        yield
        def proj(j, bk):
            for c in range(8):
                P.op("pe", lambda e, c=c: e.matmul(ps[bk], wb[:, c * 512 + j * 128:c * 512 + (j + 1) * 128],
                                                   hT[:, c * 512:(c + 1) * 512], start=(c == 0), stop=(c == 7)),
                     reads=[wtk, "hT"], writes=[f"ps{bk}"])
                yield

        def evac(j, bk):
            cc = (h * 3 + j) * 4
            P.op("pool", lambda e: e.tensor_copy(out=pre[j][:, 0:3], in_=carry[:, cc:cc + 3]),
                 reads=["carry"], writes=[f"pre{j}"])
            P.op("act", lambda e: e.copy(out=pre[j][:, 3:515], in_=ps[bk]), reads=[f"ps{bk}"], writes=[f"pre{j}"])
            yield
            P.op("pool", lambda e: e.tensor_copy(out=carry[:, cc:cc + 3], in_=pre[j][:, 512:515]),
                 reads=[f"pre{j}"], writes=["carry"])
            yield

        yield from proj(0, 0)
        yield from proj(1, 1)
        yield from evac(0, 0)
        yield from evac(1, 1)
        yield from proj(2, 0)
        yield from proj(3, 1)
        yield from evac(2, 0)
        P.op("act", lambda e: e.activation(out=zs[a], in_=ps[1], func=AF.Silu), reads=["ps1"], writes=[f"zs{a}"])
        yield
        wc = lambda tap, j: spm[:, g0 + 24 * tap + 8 * j + h:g0 + 24 * tap + 8 * j + h + 1]
        for tap in range(4):
            for j in range(3):
                if tap == 0:
                    P.op("dve", lambda e, j=j: e.tensor_scalar(out=acc[j], in0=pre[j][:, 0:512], scalar1=wc(0, j), scalar2=0.0,
                                                               op0=ALU.mult, op1=ALU.add), reads=[f"pre{j}", "spm"], writes=[f"acc{j}"])
                else:
                    P.op("dve", lambda e, j=j, tap=tap: e.scalar_tensor_tensor(
                        out=acc[j], in0=pre[j][:, tap:tap + 512], scalar=wc(tap, j), in1=acc[j], op0=ALU.mult, op1=ALU.add),
                        reads=[f"pre{j}", "spm", f"acc{j}"], writes=[f"acc{j}"])
                yield
        for j in range(3):
            P.op("act", lambda e, j=j: e.activation(out=acc[j], in_=acc[j], func=AF.Silu), reads=[f"acc{j}"], writes=[f"acc{j}"])
            yield
        for j in range(2):
            P.op("act", lambda e, j=j: e.activation(out=sqb[:, j * 512:(j + 1) * 512], in_=acc[j], func=AF.Square),
                 reads=[f"acc{j}"], writes=[f"sqb{j}"])
            yield
            P.op("pe", lambda e, j=j: e.matmul(ps[j], cm(C_ONES, True), sqb[:, j * 512:(j + 1) * 512], start=True, stop=True),
                 reads=[f"sqb{j}", "cstb"], writes=[f"ps{j}"])
            yield
        for j in range(2):
            lb = lnb if j == 0 else rn
            tk_ = "lnb" if j == 0 else "rn"
            P.op("act", lambda e, j=j, lb=lb: e.activation(out=lb, in_=ps[j], func=AF.Ln, bias=EPS, scale=1.0),
                 reads=[f"ps{j}"], writes=[tk_])
            yield
            P.op("act", lambda e, j=j, lb=lb: e.activation(out=lb, in_=lb, func=AF.Exp, scale=-0.5,
                                                          bias=(float(np.log(128.0 ** -0.5)) if j == 0 else 0.0)),
                 reads=[tk_], writes=[tk_])
            yield
        P.op("dve", lambda e: e.tensor_tensor(out=qTb[a], in0=acc[0], in1=lnb, op=ALU.mult), reads=["acc0", "lnb"], writes=[f"qTb{a}"])
        yield
        P.op("dve", lambda e: e.tensor_tensor(out=acc[1], in0=acc[1], in1=rn, op=ALU.mult), reads=["acc1", "rn"], writes=["acc1"])
        yield
        P.op("pool", lambda e: e.tensor_copy(out=kTb[a], in_=acc[1]), reads=["acc1"], writes=[f"kTb{a}"])
        yield
        for n in range(4):
            sl = slice(n * 128, (n + 1) * 128)
            P.op("pe", lambda e, sl=sl: e.transpose(ps[0][:, sl], acc[1][:, sl], cm(C_IDENT)), reads=["acc1", "cst"], writes=["ps0"])
            yield
        for n in range(4):
            sl = slice(n * 128, (n + 1) * 128)
            P.op("pe", lambda e, sl=sl: e.transpose(ps[1][:, sl], acc[2][:, sl], cm(C_IDENT)), reads=["acc2", "cst"], writes=["ps1"])
            yield
        for n in range(4):
            sl = slice(n * 128, (n + 1) * 128)
            col = n * 8 + h
            P.op("act", lambda e, sl=sl, col=col: e.activation(out=kdec[a][:, sl], in_=ps[0][:, sl], func=AF.Identity,
                                                               scale=tkDks[:, col:col + 1]), reads=["ps0", "tkDks"], writes=[f"kdec{a}"])
            yield
        for n in range(4):
            sl = slice(n * 128, (n + 1) * 128)
            col = n * 8 + h
            P.op("act", lambda e, sl=sl, col=col: e.activation(out=vb[a][:, sl], in_=ps[1][:, sl], func=AF.Identity,
                                                               scale=tkBeta[:, col:col + 1]), reads=["ps1", "tkBeta"], writes=[f"vb{a}"])
            yield

    def gdn_B(l, tb, h, g0):
        a = h % 3
        bs = h % 2
        q_, k_, kd_, vb_, zs_ = qTb[a], kTb[a], kdec[a], vb[a], zs[a]
        qt, kt, kdt, vbt, zst = f"qTb{a}", f"kTb{a}", f"kdec{a}", f"vb{a}", f"zs{a}"
        tiles = [(n, slice(n * 128, (n + 1) * 128), n * 8 + h) for n in range(4)]
        for n, sl, col in tiles:
            P.op("pe", lambda e, sl=sl: e.matmul(ps[3][:, sl], k_[:, sl], k_[:, sl], start=True, stop=True), reads=[kt], writes=["ps3"])
            P.op("pe", lambda e, sl=sl: e.matmul(ps[4][:, sl], k_[:, sl], q_[:, sl], start=True, stop=True), reads=[kt, qt], writes=["ps4"])
            P.op("pool", lambda e, sl=sl, col=col: e.tensor_scalar(out=E1[:, sl], in0=cm(C_TRIU_I), scalar1=tkG[:, col:col + 1], scalar2=0.0,
                                                                   op0=ALU.mult, op1=ALU.add), reads=["cst", "tkG"], writes=["E1"])
            yield
            P.op("pe", lambda e, sl=sl: e.matmul(ps[5][:, sl], cm(C_ONES), E1[:, sl], start=True, stop=True), reads=["cst", "E1"], writes=["ps5"])
            yield
        P.op("act", lambda e: e.activation(out=egbc, in_=ps[5], func=AF.Exp), reads=["ps5"], writes=["egbc"])
        yield
        for n, sl, col in tiles:
            P.op("dve", lambda e, sl=sl, col=col: e.tensor_scalar(out=E1[:, sl], in0=ps[5][:, sl], scalar1=tkGc[:, col:col + 1], scalar2=0.0,
                                                                  op0=ALU.subtract, op1=ALU.min), reads=["ps5", "tkGc"], writes=["E1"])
            yield
            P.op("dve", lambda e, sl=sl, col=col: e.tensor_scalar(out=E2[:, sl], in0=ps[5][:, sl], scalar1=tkGc[:, col:col + 1], scalar2=0.0,
                                                                  op0=ALU.subtract, op1=ALU.max), reads=["ps5", "tkGc"], writes=["E2"])
            yield
        P.op("act", lambda e: e.activation(out=E1, in_=E1, func=AF.Exp), reads=["E1"], writes=["E1"])
        yield
        P.op("act", lambda e: e.activation(out=E2, in_=E2, func=AF.Exp, scale=-1.0), reads=["E2"], writes=["E2"])
        yield
        for n, sl, col in tiles:
            P.op("dve", lambda e, sl=sl, col=col: e.scalar_tensor_tensor(out=t1[:, sl], in0=ps[3][:, sl], scalar=tkBeta[:, col:col + 1],
                                                                      in1=E2[:, sl], op0=ALU.mult, op1=ALU.mult),
                 reads=["ps3", "E2", "tkBeta"], writes=["t1"])
            yield
        P.op("dve", lambda e: e.tensor_tensor(out=t2, in0=ps[4], in1=E1, op=ALU.mult), reads=["ps4", "E1"], writes=["t2"])
        yield
        P.op("pool", lambda e: e.tensor_tensor(out=qdec[bs], in0=q_, in1=egbc, op=ALU.mult), reads=[qt, "egbc"], writes=[f"qdec{bs}"])
        yield
        L0, U0, Y0 = Lb[0], Ub[0], Yb[0]
        for n, sl, col in tiles:
            P.op("pool", lambda e, sl=sl: e.tensor_tensor(out=L0[:, sl], in0=t1[:, sl], in1=cm(C_BDTRIL_S), op=ALU.mult),
                 reads=["t1", "cst"], writes=["Lb0"])
            yield
        for n, sl, col in tiles:
            P.op("pe", lambda e, sl=sl: e.transpose(ps[4][:, sl], L0[:, sl], cm(C_IDENT)), reads=["Lb0", "cst"], writes=["ps4"])
            yield
        P.op("act", lambda e: e.copy(out=U0, in_=ps[4]), reads=["ps4"], writes=["Ub0"])
        yield
        for n, sl, col in tiles:
            P.op("pool", lambda e, sl=sl: e.tensor_tensor(out=Y0[:, sl], in0=cm(C_IDENT), in1=U0[:, sl], op=ALU.subtract),
                 reads=["cst", "Ub0"], writes=["Yb0"])
            yield
            P.op("pool", lambda e, sl=sl: e.tensor_tensor(out=Offb[:, sl], in0=t1[:, sl], in1=cm(C_OFF), op=ALU.mult),
                 reads=["t1", "cst"], writes=["Offb"])
            yield
            P.op("pool", lambda e, sl=sl: e.tensor_tensor(out=attnT[bs][:, sl], in0=t2[:, sl], in1=cm(C_TRIU_I), op=ALU.mult),
                 reads=["t2", "cst"], writes=[f"attnT{bs}"])
            yield
        cur = 0
        for k in range(1, 6):
            nx = 1 - cur
            for n, sl, col in tiles:
                P.op("pe", lambda e, sl=sl, cur=cur: e.matmul(ps[3][:, sl], Ub[cur][:, sl], Lb[cur][:, sl], start=True, stop=True),
                     reads=[f"Ub{cur}", f"Lb{cur}"], writes=["ps3"])
                yield
            if k < 5:
                for n, sl, col in tiles:
                    P.op("pe", lambda e, sl=sl, cur=cur: e.matmul(ps[4][:, sl], Lb[cur][:, sl], Ub[cur][:, sl], start=True, stop=True),
                         reads=[f"Ub{cur}", f"Lb{cur}"], writes=["ps4"])
                    yield
            P.op("act", lambda e, nx=nx: e.copy(out=Lb[nx], in_=ps[3]), reads=["ps3"], writes=[f"Lb{nx}"])
            yield
            if k < 5:
                P.op("dve", lambda e, nx=nx: e.tensor_copy(out=Ub[nx], in_=ps[4]), reads=["ps4"], writes=[f"Ub{nx}"])
                yield
            for n, sl, col in tiles:
                P.op("pe", lambda e, sl=sl, nx=nx, cur=cur: e.matmul(ps[5][:, sl], Lb[nx][:, sl], Yb[cur][:, sl], start=True, stop=True),
                     reads=[f"Lb{nx}", f"Yb{cur}"], writes=["ps5"])
                yield
            P.op("dve", lambda e, nx=nx, cur=cur: e.tensor_tensor(out=Yb[nx], in0=ps[5], in1=Yb[cur], op=ALU.add),
                 reads=["ps5", f"Yb{cur}"], writes=[f"Yb{nx}"])
            yield
            cur = nx
        Yd = Yb[cur]
        ytk = f"Yb{cur}"
        for n, sl, col in tiles:
            P.op("pe", lambda e, sl=sl: e.transpose(ps[4][:, sl], Yd[:, sl], cm(C_IDENT)), reads=[ytk, "cst"], writes=["ps4"])
            P.op("pe", lambda e, sl=sl: e.matmul(ps[3][:, sl], Offb[:, sl], Yd[:, sl], start=True, stop=True), reads=["Offb", ytk], writes=["ps3"])
            yield
        P.op("act", lambda e: e.copy(out=t1, in_=ps[4]), reads=["ps4"], writes=["t1"])
        yield
        P.op("dve", lambda e: e.tensor_copy(out=t2, in_=ps[3]), reads=["ps3"], writes=["t2"])
        yield
        for n, sl, col in tiles:
            P.op("pe", lambda e, sl=sl: e.matmul(ps[5][:, sl], t1[:, sl], t2[:, sl], start=True, stop=True), reads=["t1", "t2"], writes=["ps5"])
            yield
        P.op("dve", lambda e: e.tensor_tensor(out=Ybf[bs], in0=Yd, in1=ps[5], op=ALU.subtract), reads=[ytk, "ps5"], writes=[f"Ybf{bs}"])
        yield
    def gdn_C(l, tb, h, g0):
        a = h % 3
        bs = h % 2
        q_, k_, kd_, vb_, zs_ = qTb[a], kTb[a], kdec[a], vb[a], zs[a]
        qt, kt, kdt, vbt, zst = f"qTb{a}", f"kTb{a}", f"kdec{a}", f"vb{a}", f"zs{a}"
        tiles = [(n, slice(n * 128, (n + 1) * 128), n * 8 + h) for n in range(4)]
        Sh = Sst[:, h * 128:(h + 1) * 128]
        Shb = Sbf[:, h * 128:(h + 1) * 128]
        for n, sl, col in tiles:
            r_ = rv[n % 2]
            v_ = vnw[n % 2]
            P.op("pe", lambda e, sl=sl: e.matmul(ps[7][:, 128:256], k_[:, sl], Shb, start=True, stop=True),
                 reads=[kt, f"Sbf{h}"], writes=["ps7"])
            yield
            P.op("dve", lambda e, sl=sl, col=col, r_=r_: e.scalar_tensor_tensor(out=r_, in0=ps[7][:, 128:256], scalar=tkNbeg[:, col:col + 1],
                                                                            in1=vb_[:, sl], op0=ALU.mult, op1=ALU.add),
                 reads=["ps7", "tkNbeg", vbt], writes=[f"rv{n % 2}"])
            yield
            P.op("pe", lambda e, sl=sl, r_=r_: e.matmul(ps[7][:, 384:512], Ybf[bs][:, sl], r_, start=True, stop=True),
                 reads=[f"Ybf{bs}", f"rv{n % 2}"], writes=["ps7"])
            yield
            P.op("act", lambda e, v_=v_: e.copy(out=v_, in_=ps[7][:, 384:512]), reads=["ps7"], writes=[f"vnw{n % 2}"])
            yield
            P.op("pe", lambda e, sl=sl: e.matmul(ps[6][:, sl], Shb, qdec[bs][:, sl], start=True, stop=False),
                 reads=[f"Sbf{h}", f"qdec{bs}"], writes=["ps6"])
            P.op("pe", lambda e, sl=sl, v_=v_: e.matmul(ps[6][:, sl], v_, attnT[bs][:, sl], start=False, stop=True),
                 reads=[f"vnw{n % 2}", f"attnT{bs}"], writes=["ps6"])
            yield
            P.op("pe", lambda e, sl=sl, v_=v_: e.matmul(ps[7][:, 256:384], kd_[:, sl], v_, start=True, stop=True),
                 reads=[kdt, f"vnw{n % 2}"], writes=["ps7"])
            yield
            P.op("dve", lambda e, col=col: e.scalar_tensor_tensor(out=Sh, in0=Sh, scalar=tkEgl[:, col:col + 1], in1=ps[7][:, 256:384],
                                                                  op0=ALU.mult, op1=ALU.add),
                 reads=[f"S{h}", "tkEgl", "ps7"], writes=[f"S{h}"])
            yield
            P.op("pool", lambda e: e.tensor_copy(out=Shb, in_=Sh), reads=[f"S{h}"], writes=[f"Sbf{h}"])
            yield
        P.op("act", lambda e: e.copy(out=ob, in_=ps[6]), reads=["ps6"], writes=["ob"])
        yield
        P.op("act", lambda e: e.activation(out=sqc, in_=ob, func=AF.Square), reads=["ob"], writes=["sqc"])
        yield
        P.op("pe", lambda e: e.matmul(ps[6], cm(C_ONES, True), sqc, start=True, stop=True), reads=["sqc", "cstb"], writes=["ps6"])
        yield
        P.op("act", lambda e: e.activation(out=rn3, in_=ps[6], func=AF.Ln, bias=EPS, scale=1.0 / 128), reads=["ps6"], writes=["rn3"])
        yield
        P.op("act", lambda e: e.activation(out=rn3, in_=rn3, func=AF.Exp, scale=-0.5), reads=["rn3"], writes=["rn3"])
        yield
        P.op("dve", lambda e: e.tensor_tensor(out=ob, in0=ob, in1=rn3, op=ALU.mult), reads=["ob", "rn3"], writes=["ob"])
        yield
        P.op("dve", lambda e: e.scalar_tensor_tensor(out=ogT[:, h * 512:(h + 1) * 512], in0=ob, scalar=spm[:, g0 + 96:g0 + 97], in1=zs_,
                                                     op0=ALU.mult, op1=ALU.mult), reads=["ob", "spm", zst], writes=["ogT"])
        yield

    aux = [None]

    def emit_gdn_block_heads(l, tb, g0):
        for step in range(-2, 8):
            gens = []
            wts = []
            if aux[0] is not None:
                gens.append(take(aux[0], AUX_N))
                wts.append(1)
            if 0 <= step < 8:
                gens.append(gdn_C(l, tb, step, g0))
                wts.append(GDN_W[0])
            if 0 <= step + 1 < 8:
                gens.append(gdn_B(l, tb, step + 1, g0))
                wts.append(GDN_W[1])
            if 0 <= step + 2 < 8:
                gens.append(gdn_A(l, tb, step + 2, g0))
                wts.append(GDN_W[2])
            run_interleaved(gens, wts)

    bar_n = [0]

    def emit_barrier():
        k = bar_n[0]
        bar_n[0] += 1
        P.op("act", lambda e: e.activation(out=bscr[:, 0:1], in_=cact[:, 0:1], func=AF.Identity), reads=["cact"], writes=[f"bar{k}_act", "bscr0"])
        P.op("dve", lambda e: e.tensor_copy(out=bscr[:, 1:2], in_=cact[:, 0:1]), reads=["cact"], writes=[f"bar{k}_dve", "bscr1"])
        P.op("pool", lambda e: e.tensor_copy(out=bscr[:, 2:3], in_=cact[:, 0:1]), reads=["cact"], writes=[f"bar{k}_pool", "bscr2"])
        P.op("pe", lambda e: e.matmul(ps[7][:, 0:8], cm(C_ONES), cact[:, 0:8], start=True, stop=True), reads=["cact", "cst"], writes=["ps7", f"bar{k}_pe"])
        allb = [f"bar{k}_{x}" for x in ("act", "dve", "pool", "pe")]
        P.op("act", lambda e: e.activation(out=bscr[:, 3:4], in_=cact[:, 0:1], func=AF.Identity), reads=allb + ["cact"], writes=["bscr3"])
        P.op("dve", lambda e: e.tensor_copy(out=bscr[:, 4:5], in_=ps[7][:, 0:1]), reads=allb, writes=["bscr4", "ps7"])
        P.op("pool", lambda e: e.tensor_copy(out=bscr[:, 5:6], in_=cact[:, 0:1]), reads=allb + ["cact"], writes=["bscr5"])
        P.op("pe", lambda e: e.matmul(ps[7][:, 0:8], cm(C_ONES), cact[:, 0:8], start=True, stop=True), reads=allb + ["cact", "cst"], writes=["ps7"])
        P.op("sp", lambda e: e.dma_start(out=bscr[:, 6:8], in_=spm_d[:, 0:2]), reads=allb, writes=["bscr6"], dma=True)

    emit_barrier()
    drain(gen_layer_mod(0))
    n_a = min(nlayers, 2)
    for l in range(n_a):
        if l == 0 and nlayers > 1:
            aux[0] = gen_layer_mod(1)
        if l == 1 and nlayers > 2:
            gl_ = [gen_kv_mod(), gen_layer_mod(2)]
            if nlayers > 3:
                gl_.append(gen_layer_mod(3))
            aux[0] = chain(*gl_)
        emit_gdn_layer(l)
        if aux[0] is not None:
            drain(aux[0])
            aux[0] = None
    emit_barrier()
    gstack.close()
    sb = sb_keep

    ost2 = [sb(f"Eo{i}", [128, 512]) for i in range(2)]
    if nlayers > 2:
        KT = sb("KT", [128, 8 * T], BF16)
        Vt = sb("Vt", [128, 16 * D], BF16)
        qn = sb("qn", [128, 8 * 512], BF16)
        Ebuf = ost2
        SPR = [[sb(f"SPR{i}{u}", [128, 512], F32R) for u in range(2)] for i in range(2)]
        dbf = [[sb(f"dbf{i}{u}", [128, 512]) for u in range(2)] for i in range(2)]
        Wb = [sb(f"Wb{i}", [128, 512], BF16) for i in range(2)]
        rn2 = tmpn[1]
        cstr = sb("cstr", [128, 256], F32R)
        P.op("dve", lambda e: e.tensor_copy(out=cstr[:, 0:128], in_=cm(C_TRIL_S)), reads=["cst"], writes=["cstr"])
        P.op("dve", lambda e: e.tensor_copy(out=cstr[:, 128:256], in_=cm(C_TRIU_I)), reads=["cst"], writes=["cstr"])

        def emit_headnorm(psb, gain_col, extra_bias, dst, dst_tok):
            tm = tmpn[0]
            P.op("act", lambda e: e.copy(out=tm, in_=psb), reads=[psb_tok[0]], writes=["tmpn0"])
            P.op("act", lambda e: e.activation(out=sqb[:, 0:512], in_=tm, func=AF.Square), reads=["tmpn0"], writes=["sqb0"])
            P.op("pe", lambda e: e.matmul(ps[4], cm(C_BDONES, True), sqb[:, 0:512], start=True, stop=True), reads=["sqb0", "cstb"], writes=["ps4"])
            P.op("act", lambda e: e.activation(out=rn2, in_=ps[4], func=AF.Ln, bias=EPS, scale=1.0 / 64), reads=["ps4"], writes=["tmpn1"])
            P.op("act", lambda e: e.activation(out=rn2, in_=rn2, func=AF.Exp, scale=-0.5, bias=extra_bias), reads=["tmpn1"], writes=["tmpn1"])
            P.op("dve", lambda e: e.scalar_tensor_tensor(out=dst, in0=tm, scalar=spm[:, gain_col:gain_col + 1], in1=rn2,
                                                         op0=ALU.mult, op1=ALU.mult), reads=["tmpn0", "spm", "tmpn1"], writes=[dst_tok])

        psb_tok = [None]

        def emit_kv():
            for tb in range(NB):
                emit_norm_block(tb, 4, 96)
                for piece in range(2):
                    P.op("pool", lambda e, piece=piece: e.dma_start(out=v3(wq[piece], 8), in_=wview(w_kv_d, piece * 512, 512)),
                         writes=[f"wq{piece}"], dma=True)
                    for fcl in range(4):
                        fc = piece * 4 + fcl
                        bk = fc % 4
                        for c in range(8):
                            P.op("pe", lambda e, piece=piece, fcl=fcl, c=c, bk=bk: e.matmul(
                                ps[bk], wq[piece][:, c * 512 + fcl * 128:c * 512 + (fcl + 1) * 128], hT[:, c * 512:(c + 1) * 512],
                                start=(c == 0), stop=(c == 7)), reads=[f"wq{piece}", "hT"], writes=[f"ps{bk}"])
                        psb_tok[0] = f"ps{bk}"
                        emit_headnorm(ps[bk], SP_KG, 0.0, KT[:, fc * T + tb * 512:fc * T + (tb + 1) * 512], "KT")
                for piece in range(2):
                    P.op("pool", lambda e, piece=piece: e.dma_start(out=v3(wq[piece], 8), in_=wview(w_kv_d, D + piece * 512, 512)),
                         writes=[f"wq{piece}"], dma=True)
                    for n in range(4):
                        bk = n % 4
                        tile = tb * 4 + n
                        for c in range(8):
                            P.op("pe", lambda e, piece=piece, n=n, c=c, bk=bk: e.matmul(
                                ps[bk], hT[:, c * 512 + n * 128:c * 512 + (n + 1) * 128], wq[piece][:, c * 512:(c + 1) * 512],
                                start=(c == 0), stop=(c == 7)), reads=[f"wq{piece}", "hT"], writes=[f"ps{bk}"])
                        dst = Vt[:, tile * D + piece * 512:tile * D + (piece + 1) * 512]
                        if n % 2 == 0:
                            P.op("act", lambda e, dst=dst, bk=bk: e.copy(out=dst, in_=ps[bk]), reads=[f"ps{bk}"], writes=["Vt"])
                        else:
                            P.op("dve", lambda e, dst=dst, bk=bk: e.tensor_copy(out=dst, in_=ps[bk]), reads=[f"ps{bk}"], writes=["Vt"])

        def sb_head(g, ch, hh, s_):
            h = 2 * ch + hh
            base = hh * 64
            bA, bB = (0, 1) if s_ == 0 else (2, 3)
            Eb, wbu = Ebuf[s_], Wb[s_]
            chunks = list(range(4 * g + 3, -1, -1))

            def geom(i):
                r0 = max(i - 4 * g, 0)
                return r0 * 128, (i >= 4 * g)

            def front(k):
                i = chunks[k]
                c0, diag = geom(i)
                u = k % 2
                Sr = SPR[s_][u]
                P.op("pe", lambda e: e.matmul(
                    ps[bA][:, c0:512], KT[base:base + 64, ch * T + i * 128:ch * T + (i + 1) * 128],
                    qn[base:base + 64, ch * 512 + c0:ch * 512 + 512], start=True, stop=True),
                    reads=["KT", f"qn{ch}"], writes=[f"ps{bA}"])
                yield
                P.op("act", lambda e: e.activation(out=Eb[:, c0:512], in_=ps[bA][:, c0:512], func=AF.Exp),
                     reads=[f"ps{bA}"], writes=[f"E{s_}"])
                yield
                P.op("act", lambda e: e.activation(out=Sr[:, c0:512], in_=Eb[:, c0:512], func=AF.Ln, bias=1.0, scale=1.0),
                     reads=[f"E{s_}"], writes=[f"SPR{s_}{u}"])
                yield
                if diag:
                    P.op("pool", lambda e: e.tensor_tensor(out=Sr[:, c0:c0 + 128], in0=Sr[:, c0:c0 + 128].bitcast(F32),
                                                           in1=cm(C_TRIU_S), op=ALU.mult),
                         reads=[f"SPR{s_}{u}", "cst"], writes=[f"SPR{s_}{u}"])
                    yield

            def back1(k):
                i = chunks[k]
                c0, diag = geom(i)
                u = k % 2
                Sr, dbu = SPR[s_][u], dbf[s_][u]
                P.op("pe", lambda e: e.matmul(
                    ps[bB][:, c0:512], cstr[:, 0:128], Sr[:, c0:512], start=(k == 0), stop=False, skip_group_check=True),
                    reads=[f"SPR{s_}{u}", "cstr"], writes=[f"ps{bB}"])
                yield
                P.op("dve", lambda e: e.tensor_tensor(
                    out=dbu[:, c0:512], in0=ps[bA][:, c0:512], in1=Sr[:, c0:512].bitcast(F32), op=ALU.subtract),
                    reads=[f"ps{bA}", f"SPR{s_}{u}"], writes=[f"dbf{s_}{u}"])
                yield

            def back2(k):
                i = chunks[k]
                c0, diag = geom(i)
                u = k % 2
                Sr, dbu = SPR[s_][u], dbf[s_][u]
                P.op("dve", lambda e: e.tensor_tensor(
                    out=dbu[:, c0:512], in0=dbu[:, c0:512], in1=ps[bB][:, c0:512], op=ALU.subtract),
                    reads=[f"ps{bB}", f"dbf{s_}{u}"], writes=[f"dbf{s_}{u}"])
                yield
                if i > 0:
                    P.op("pe", lambda e: e.matmul(
                        ps[bB][:, c0:512], cstr[:, 128:256], Sr[:, c0:512], start=False, stop=False, skip_group_check=True),
                        reads=[f"SPR{s_}{u}", "cstr"], writes=[f"ps{bB}"])
                    yield
                P.op("act", lambda e: e.activation(out=wbu[:, c0:512], in_=dbu[:, c0:512], func=AF.Exp),
                     reads=[f"dbf{s_}{u}"], writes=[f"Wb{s_}"])
                yield
                if diag:
                    P.op("pool", lambda e: e.tensor_tensor(out=wbu[:, c0:c0 + 128], in0=wbu[:, c0:c0 + 128],
                                                           in1=cm(C_TRIU_S, True), op=ALU.mult),
                         reads=[f"Wb{s_}", "cstb"], writes=[f"Wb{s_}"])
                    yield
                P.op("pe", lambda e: e.matmul(
                    ps[6][base:base + 64, c0:512], Vt[:, i * D + h * 64:i * D + (h + 1) * 64], wbu[:, c0:512],
                    start=False, stop=False, skip_group_check=True, tile_position=(0, base)),
                    reads=["Vt", f"Wb{s_}"], writes=["ps6"])
                yield

            yield from front(0)
            for k in range(len(chunks)):
                yield from back1(k)
                if k + 1 < len(chunks):
                    yield from front(k + 1)
                yield from back2(k)

        def run_interleaved(gens):
            gens = list(gens)
            while gens:
                for g_ in list(gens):
                    try:
                        next(g_)
                    except StopIteration:
                        gens.remove(g_)

        def sb_projc(g, l2, fc):
            half, fcl = divmod(fc, 4)
            if fcl == 0:
                P.op("pool", lambda e: e.dma_start(out=v3(wq[0], 8), in_=wview(w_inb_d[l2], half * 512, 512)),
                     writes=["wq0"], dma=True)
                yield
                P.op("pool", lambda e: e.dma_start(out=v3(wq[1], 8), in_=wview(w_inb_d[l2], D + half * 512, 512)),
                     writes=["wq1"], dma=True)
                yield
            for c in range(8):
                P.op("pe", lambda e, c=c: e.matmul(
                    ps[4], wq[0][:, c * 512 + fcl * 128:c * 512 + (fcl + 1) * 128], hT[:, c * 512:(c + 1) * 512],
                    start=(c == 0), stop=(c == 7)), reads=["wq0", "hT"], writes=["ps4"])
                yield
            tm = tmpn[0]
            P.op("act", lambda e: e.copy(out=tm, in_=ps[4]), reads=["ps4"], writes=["tmpn0"])
            yield
            for c in range(8):
                P.op("pe", lambda e, c=c: e.matmul(
                    ps[5], wq[1][:, c * 512 + fcl * 128:c * 512 + (fcl + 1) * 128], hT[:, c * 512:(c + 1) * 512],
                    start=(c == 0), stop=(c == 7)), reads=["wq1", "hT"], writes=["ps5"])
                yield
            P.op("act", lambda e: e.activation(out=sqb[:, 0:512], in_=tm, func=AF.Square), reads=["tmpn0"], writes=["sqb0"])
            yield
            P.op("pe", lambda e: e.matmul(ps[7], cm(C_BDONES, True), sqb[:, 0:512], start=True, stop=True), reads=["sqb0", "cstb"], writes=["ps7"])
            yield
            P.op("act", lambda e: e.activation(out=rn2, in_=ps[7], func=AF.Ln, bias=EPS, scale=1.0 / 64), reads=["ps7"], writes=["tmpn1"])
            yield
            P.op("act", lambda e: e.activation(out=rn2, in_=rn2, func=AF.Exp, scale=-0.5, bias=float(np.log(0.125))), reads=["tmpn1"], writes=["tmpn1"])
            yield
            P.op("dve", lambda e: e.scalar_tensor_tensor(out=qn[:, fc * 512:(fc + 1) * 512], in0=tm, scalar=spm[:, SP_QG + l2:SP_QG + l2 + 1], in1=rn2,
                                                         op0=ALU.mult, op1=ALU.mult), reads=["tmpn0", "spm", "tmpn1"], writes=[f"qn{fc}"])
            yield
            P.op("act", lambda e: e.activation(out=ogT[:, fc * 512:(fc + 1) * 512], in_=ps[5], func=AF.Silu),
                 reads=["ps5"], writes=[f"og{fc}"])
            yield

        def emit_sb_layer(l2):
            L = 2 + l2
            for g in range(NB):
                emit_norm_block(g, L, 24 * L)
                drain(sb_projc(g, l2, 0))
                for ch in range(8):
                    P.op("dve", lambda e: e.memset(ps[6], 0.0), writes=["ps6"])
                    gens = [sb_head(g, ch, 0, 0), sb_head(g, ch, 1, 1)]
                    if ch + 1 < 8:
                        gens.append(sb_projc(g, l2, ch + 1))
                    run_interleaved(gens)
                    P.op("dve", lambda e, ch=ch: e.tensor_tensor(out=ogT[:, ch * 512:(ch + 1) * 512], in0=ps[6], in1=ogT[:, ch * 512:(ch + 1) * 512],
                                                                 op=ALU.mult), reads=["ps6", f"og{ch}"], writes=[f"og{ch}"])
                emit_outproj(w_outb_d[l2], g, L, og_toks=[f"og{i}" for i in range(8)])

        emit_kv()
        for l2 in range(nlayers - 2):
            emit_sb_layer(l2)

    outs = []
    for n in range(16):
        tb = n // 4
        for half in range(2):
            bk = (2 * n + half) % 4
            oi = (2 * n + half) % 2
            for c4 in range(4):
                c = half * 4 + c4
                P.op("pe", lambda e, n=n, c=c, c4=c4, bk=bk: e.transpose(
                    ps[bk][:, c4 * 128:(c4 + 1) * 128], xT[:, c * T + n * 128:c * T + (n + 1) * 128], cm(C_IDENT)),
                    reads=[f"xT{tb}", "cst"], writes=[f"ps{bk}"])
            if half == 0:
                P.op("act", lambda e, oi=oi, bk=bk: e.copy(out=ost2[oi], in_=ps[bk]), reads=[f"ps{bk}"], writes=[f"E{oi}"])
            else:
                P.op("dve", lambda e, oi=oi, bk=bk: e.tensor_copy(out=ost2[oi], in_=ps[bk]), reads=[f"ps{bk}"], writes=[f"E{oi}"])
            outs.append(P.op("sp", lambda e, n=n, half=half, oi=oi: e.dma_start(
                out=out_d[n * 128:(n + 1) * 128, half * 512:(half + 1) * 512], in_=ost2[oi]),
                reads=[f"E{oi}"], dma=True))
    P.finalize(final_wait_ops=outs)
    return nc


def _prep_inputs(inp):
    inp = {k: np.asarray(v) for k, v in inp.items()}
    w_in_a = inp["w_in_a"].astype(np.float32, copy=False)
    qkvz = w_in_a[:, :, :4096].reshape(2, D, 4, 8, 128).transpose(0, 1, 3, 2, 4).reshape(2, D, 4096)
    shared = {
        "cst": _consts(),
        "w_ada": np.ascontiguousarray(inp["w_ada"], np.float32),
        "w_inp": np.ascontiguousarray(qkvz),
        "w_ba": np.ascontiguousarray(w_in_a[:, :, 4096:4112]),
        "w_out_a": np.ascontiguousarray(inp["w_out_a"], np.float32),
        "w_ada_kv": np.ascontiguousarray(inp["w_ada_kv"], np.float32),
        "w_kv": np.ascontiguousarray(inp["w_kv"], np.float32),
        "w_in_b": np.ascontiguousarray(inp["w_in_b"], np.float32),
        "w_out_b": np.ascontiguousarray(inp["w_out_b"], np.float32),
    }
    in_maps = []
    for b in range(8):
        m = dict(shared)
        m["x"] = np.ascontiguousarray(inp["x"][b], np.float32)
        m["spm"] = _small_params(inp, b)
        in_maps.append(m)
    return in_maps


def kernel(**inputs):
    in_maps = _prep_inputs(inputs)
    nc = build(4)
    res = run_bass_kernel_spmd(nc, in_maps, core_ids=list(range(8)))
    return np.stack([np.asarray(r["out"], np.float32) for r in res.results], axis=0)
```

```python
import numpy as np
import concourse.bass as bass
import concourse.mybir as mybir
from concourse.bass_utils import run_bass_kernel_spmd
from contextlib import ExitStack

F32 = mybir.dt.float32
BF16 = mybir.dt.bfloat16
F32R = mybir.dt.float32r
AF = mybir.ActivationFunctionType
ALU = mybir.AluOpType

T = 2048
D = 1024
NB = 4
EPS = 1e-6
AUX_N = 13
SCHEDULE = True
SB_WARM = 8
GDN_W = (1, 1, 1)


class Tok:
    __slots__ = ("name", "w", "r")

    def __init__(self, name=""):
        self.name = name
        self.w = None
        self.r = []


class Op:
    __slots__ = ("eng", "fn", "deps", "idx", "need_inc", "ev", "is_dma", "dsem", "dval", "seg", "gidx", "cost")

    def __init__(self, eng, fn):
        self.eng = eng
        self.fn = fn
        self.deps = []
        self.idx = None
        self.need_inc = False
        self.ev = None
        self.is_dma = False


class Prog:
    ENG = ("pe", "act", "dve", "pool", "sp")
    SEM_LIMIT = 30000
    NDMA = 16
    NEAR = 6

    def __init__(self, nc):
        self.nc = nc
        self.q = {e: [] for e in self.ENG}
        self.toks = {}
        self.seg = 0
        self.nops = 0

    def mark_segment(self):
        self.seg += 1

    COST = {"pe": 0.25, "act": 0.55, "dve": 0.6, "pool": 0.45, "sp": 0.15}

    class _Probe:
        def __init__(self):
            self.rec = None

        def __getattr__(self, name):
            def f(*a, **kw):
                self.rec = (name, a, kw)
                return self
            return f

    def estimate(self, o):
        try:
            pr = Prog._Probe()
            o.fn(pr)
            name, a, kw = pr.rec
            out = kw.get("out", a[0] if a else None)
            n = 1
            for d in out.shape[1:]:
                n *= int(d)
            if o.is_dma:
                return 2.0 + out.shape[0] * n * 4 / 150e3
            if o.eng == "pe":
                if name == "transpose":
                    return 0.12
                lhs = kw.get("lhsT", a[1] if len(a) > 1 else None)
                cyc = {F32: 4.0, F32R: 2.5}.get(lhs.dtype, 1.0) * max(n, 64)
                return 0.05 + cyc / 1700.0
            if o.eng == "act":
                return 0.2 + n / 1400.0
            if o.eng == "dve":
                return 0.08 + n / 960.0
            if o.eng == "pool":
                return 0.1 + n / 420.0
        except Exception:
            pass
        return self.COST[o.eng]

    def schedule(self):
        import heapq
        allops = []
        for e in self.ENG:
            allops.extend(self.q[e])
        allops.sort(key=lambda o: o.gidx)
        nseg = self.seg + 1
        bysegs = [[] for _ in range(nseg)]
        for o in allops:
            bysegs[o.seg].append(o)
        newq = {e: [] for e in self.ENG}
        for ops in bysegs:
            if not ops:
                continue
            inseg = set(id(o) for o in ops)
            succ = {}
            indeg = {}
            for o in ops:
                n = 0
                for d in o.deps:
                    if id(d) in inseg:
                        succ.setdefault(id(d), []).append(o)
                        n += 1
                indeg[id(o)] = n
            finish = {}
            free = {e: 0.0 for e in self.ENG}
            heaps = {e: [] for e in self.ENG}
            ready_t = {}
            for o in ops:
                if indeg[id(o)] == 0:
                    ready_t[id(o)] = 0.0
                    heapq.heappush(heaps[o.eng], (0.0, o.gidx, o))
            left = len(ops)
            while left:
                best = None
                for e in self.ENG:
                    h = heaps[e]
                    if not h:
                        continue
                    st = max(h[0][0], free[e])
                    if best is None or (st, h[0][1]) < (best[0], best[1]):
                        best = (st, h[0][1], e)
                st, _, e = best
                h = heaps[e]
                cand = []
                while h and h[0][0] <= st:
                    cand.append(heapq.heappop(h))
                cand.sort(key=lambda c: c[1])
                pick = cand[0]
                for c in cand[1:]:
                    heapq.heappush(h, c)
                o = pick[2]
                c_ = o.cost if o.cost is not None else self.estimate(o)
                if o.is_dma:
                    free[e] = st + (0.15 if e == "sp" else 0.4)
                    fin = st + c_
                else:
                    free[e] = st + c_
                    fin = free[e]
                finish[id(o)] = fin
                newq[e].append(o)
                left -= 1
                for s_ in succ.get(id(o), ()):
                    indeg[id(s_)] -= 1
                    r = max(ready_t.get(id(s_), 0.0), fin + (0.05 if s_.eng == o.eng else 0.3))
                    ready_t[id(s_)] = r
                    if indeg[id(s_)] == 0:
                        heapq.heappush(heaps[s_.eng], (r, s_.gidx, s_))
        for e in self.ENG:
            assert len(newq[e]) == len(self.q[e])
            self.q[e] = newq[e]
            for i, o in enumerate(newq[e]):
                o.idx = i

    def tk(self, name):
        t = self.toks.get(name)
        if t is None:
            t = Tok(name)
            self.toks[name] = t
        return t

    def op(self, eng, fn, reads=(), writes=(), dma=False, cost=None):
        o = Op(eng, fn)
        o.is_dma = dma
        o.seg = self.seg
        o.gidx = self.nops
        o.cost = cost
        self.nops += 1
        o.idx = len(self.q[eng])
        deps = set()
        rd, wr = [], []
        for t in reads:
            if isinstance(t, str) and t.startswith("ps") and t[2].isdigit():
                wr.append(t[:3])
            else:
                rd.append(t)
        for t in writes:
            if isinstance(t, str) and t.startswith("ps") and t[2].isdigit():
                wr.append(t[:3])
            else:
                wr.append(t)
        reads = [self.tk(t) if isinstance(t, str) else t for t in rd]
        writes = [self.tk(t) if isinstance(t, str) else t for t in dict.fromkeys(wr)]
        for t in reads:
            if t.w is not None:
                deps.add(t.w)
        for t in writes:
            if t.w is not None:
                deps.add(t.w)
            for r in t.r:
                deps.add(r)
        deps.discard(o)
        o.deps = list(deps)
        for t in reads:
            t.r.append(o)
        for t in writes:
            t.w = o
            t.r = []
        self.q[eng].append(o)
        return o

    def finalize(self, final_wait_ops=()):
        nc = self.nc
        if SCHEDULE:
            self.schedule()
        waits = {}
        for e in self.ENG:
            seen = {}
            for o in self.q[e]:
                best = {}
                res = []
                for d in o.deps:
                    if d.is_dma:
                        res.append(d)
                        continue
                    if d.eng == o.eng:
                        if e == "pe" or o.is_dma:
                            if not o.is_dma:
                                continue
                        if (not o.is_dma) and o.idx - d.idx > self.NEAR:
                            continue
                    if d.eng not in best or best[d.eng].idx < d.idx:
                        best[d.eng] = d
                for d in best.values():
                    if seen.get(d.eng, -1) >= d.idx:
                        continue
                    seen[d.eng] = d.idx
                    res.append(d)
                waits[o] = res
                for d in res:
                    if not d.is_dma:
                        d.need_inc = True
        stack = ExitStack()
        for e in self.ENG:
            n = sum(1 for o in self.q[e] if o.need_inc and not o.is_dma)
            k = max(1, (n + self.SEM_LIMIT - 1) // self.SEM_LIMIT)
            sems = [stack.enter_context(nc.semaphore(f"s_{e}_{i}")) for i in range(k)]
            c = 0
            for o in self.q[e]:
                if o.need_inc and not o.is_dma:
                    o.ev = (sems[c // self.SEM_LIMIT], c % self.SEM_LIMIT + 1)
                    c += 1
        dsems = {e: [stack.enter_context(nc.semaphore(f"s_dma_{e}_{i}")) for i in range(self.NDMA)]
                 for e in ("sp", "pool")}
        for e in ("sp", "pool"):
            di = 0
            for o in self.q[e]:
                if o.is_dma:
                    o.dsem = dsems[e][di % self.NDMA]
                    o.dval = 16 * (di // self.NDMA + 1)
                    o.ev = (o.dsem, o.dval)
                    di += 1
        final = list(final_wait_ops)
        with nc.Block() as block:
            def run(engname, eng):
                for o in self.q[engname]:
                    for d in waits[o]:
                        eng.wait_ge(d.ev[0], d.ev[1])
                    if o.is_dma and o.dval > 16:
                        eng.wait_ge(o.dsem, o.dval - 16)
                    ins = o.fn(eng)
                    if o.is_dma:
                        ins.then_inc(o.dsem, 16)
                    elif o.need_inc:
                        ins.then_inc(o.ev[0], 1)
                if engname == "sp":
                    for o in final:
                        eng.wait_ge(o.ev[0], o.ev[1])

            @block.tensor
            def _(t):
                run("pe", t)

            @block.scalar
            def _(t):
                run("act", t)

            @block.vector
            def _(t):
                run("dve", t)

            @block.gpsimd
            def _(t):
                run("pool", t)

            @block.sync
            def _(t):
                run("sp", t)
        stack.close()


C_IDENT, C_ONES, C_TRIU_I, C_TRIL_S, C_TRIU_S, C_BDTRIL_S, C_OFF, C_BDONES = range(8)
NCST = 8


def _consts():
    p = np.arange(128)[:, None]
    f = np.arange(128)[None, :]
    m = np.zeros((128, NCST, 128), np.float32)
    m[:, C_IDENT] = (p == f)
    m[:, C_ONES] = 1.0
    m[:, C_TRIU_I] = (p <= f)
    m[:, C_TRIL_S] = (f < p)
    m[:, C_TRIU_S] = (p < f)
    m[:, C_BDTRIL_S] = (f < p) & ((p // 64) == (f // 64))
    m[:, C_OFF] = (p >= 64) & (f < 64)
    m[:, C_BDONES] = ((p // 64) == (f // 64))
    return m.reshape(128, NCST * 128)


def _fm(v):
    v = np.asarray(v, np.float32).reshape(-1, 128)
    return np.ascontiguousarray(v.T)


SP_C = 0
SP_LAYER = 8
SP_KV = 136
SP_GDN = 160
SP_KG = 386
SP_QG = 387
NSP = 389


def _small_params(inp, b):
    sp = np.zeros((128, NSP), np.float32)
    sp[:, 0:8] = _fm(inp["c"][b])
    for l in range(4):
        o = SP_LAYER + 32 * l
        sp[:, o:o + 8] = _fm(inp["norm_g"][l])
        sp[:, o + 8:o + 32] = _fm(inp["b_ada"][l])
    sp[:, SP_KV:SP_KV + 8] = _fm(inp["kv_norm_g"])
    sp[:, SP_KV + 8:SP_KV + 24] = _fm(inp["b_ada_kv"])
    for l in range(2):
        o = SP_GDN + 113 * l
        cw = np.asarray(inp["conv_w_a"][l], np.float32)
        for j in range(4):
            sp[:, o + 24 * j:o + 24 * j + 24] = _fm(cw[j])
        sp[:, o + 96] = np.asarray(inp["o_gain_a"][l], np.float32)
        sp[:, o + 97:o + 105] = np.asarray(inp["a_log_a"][l], np.float32)[None, :]
        sp[:, o + 105:o + 113] = np.asarray(inp["dt_bias_a"][l], np.float32)[None, :]
    sp[:, SP_KG] = np.tile(np.asarray(inp["k_gain"], np.float32), 2)
    for l in range(2):
        sp[:, SP_QG + l] = np.tile(np.asarray(inp["q_gain_b"][l], np.float32), 2)
    return sp


def build(nlayers=4):
    nc = bass.Bass("TRN2", target_bir_lowering=False)
    dram = lambda n, s, k="ExternalInput": nc.dram_tensor(n, s, F32, kind=k).ap()
    x_d = dram("x", [T, D])
    spm_d = dram("spm", [128, NSP])
    cst_d = dram("cst", [128, NCST * 128])
    w_ada_d = dram("w_ada", [4, D, 3 * D])
    w_inp_d = dram("w_inp", [2, D, 8 * 512])
    w_ba_d = dram("w_ba", [2, D, 16])
    w_outa_d = dram("w_out_a", [2, D, D])
    w_adakv_d = dram("w_ada_kv", [D, 2 * D])
    w_kv_d = dram("w_kv", [D, 2 * D])
    w_inb_d = dram("w_in_b", [2, D, 2 * D])
    w_outb_d = dram("w_out_b", [2, D, D])
    out_d = dram("out", [T, D], "ExternalOutput")

    def wview(ap2d, c0, n):
        return ap2d[:, c0:c0 + n].rearrange("(c p) n -> p c n", p=128)

    sb = lambda n, s, d=F32: nc.alloc_sbuf_tensor("sb_" + n, s, d).ap()
    P = Prog(nc)

    def v3(ap, a):
        return ap.rearrange("p (a b) -> p a b", a=a)

    xT = sb("xT", [128, 8 * T])
    cst = sb("cst", [128, NCST * 128])
    cstb = sb("cstb", [128, NCST * 128], BF16)
    spm = sb("spm", [128, NSP])
    cact = sb("cact", [128, 8])
    modT = sb("modT", [128, 5 * 24])
    Acol = sb("Acol", [128, 5 * 8])
    hT = sb("hT", [128, 8 * 512], BF16)
    ogT = sb("ogT", [128, 8 * 512], BF16)
    sqb = sb("sqb", [128, 2 * 512], BF16)
    rstd = sb("rstd", [128, 512])
    bscr = sb("bscr", [128, 8])
    tmpn = [sb(f"tmpn{i}", [128, 512]) for i in range(2)]
    wq = [sb(f"wq{i}", [128, 8 * 512], BF16) for i in range(2)]
    ps = [nc.alloc_psum_tensor(f"ps{i}", [128, 512], F32).ap() for i in range(8)]
    gstack = ExitStack()
    sbp = lambda n, s, d=F32: gstack.enter_context(nc.sbuf_tensor("sb_" + n, s, d))[:]
    sb_keep = sb
    lnb = sbp("lnb", [128, 512])
    mrow = sbp("mrow", [1, 256])
    wa = [sbp("wa0", [128, 8 * 256])] * 2
    LU = sbp("LU", [128, 2048])
    xld = [LU[:, i * 1024:(i + 1) * 1024] for i in range(2)]

    def cm(i, bf=False):
        return (cstb if bf else cst)[:, i * 128:(i + 1) * 128]

    def xsl(c, tb):
        return xT[:, c * T + tb * 512:c * T + (tb + 1) * 512]

    P.op("sp", lambda e: e.dma_start(out=cst, in_=cst_d), writes=["cst"], dma=True)
    P.op("sp", lambda e: e.dma_start(out=spm, in_=spm_d), writes=["spm"], dma=True)
    P.op("pool", lambda e: e.tensor_copy(out=cstb, in_=cst), reads=["cst"], writes=["cstb"])
    P.op("act", lambda e: e.activation(out=cact, in_=spm[:, 0:8], func=AF.Silu), reads=["spm"], writes=["cact"])

    wa_i = [0]

    def gen_mod(wd2, ncols, bias0, dst0, slot, a_slot, g_col0, scale_col0):
        buf = wa[0]
        for piece in range(ncols // 256):
            P.op("sp", lambda e, piece=piece: e.dma_start(out=v3(buf, 8), in_=wview(wd2, piece * 256, 256)),
                 writes=["wa0"], dma=True)
            yield
            for c in range(8):
                P.op("pe", lambda e, c=c: e.matmul(ps[2][0:1, 0:256], cact[:, c:c + 1], buf[:, c * 256:(c + 1) * 256],
                                                   start=(c == 0), stop=(c == 7)), reads=["wa0", "cact"], writes=["ps2"])
                yield
            P.op("act", lambda e: e.copy(out=mrow, in_=ps[2][0:1, 0:256]), reads=["ps2"], writes=["mrow"])
            yield
            for fc in range(2):
                P.op("pe", lambda e, fc=fc: e.matmul(ps[2][:, 256 + fc:257 + fc], mrow[0:1, fc * 128:(fc + 1) * 128],
                                                     cst[0:1, C_ONES * 128:C_ONES * 128 + 1], start=True, stop=True),
                     reads=["mrow", "cst"], writes=["ps2"])
                yield
            d0 = dst0 + piece * 2
            b0 = bias0 + piece * 2
            P.op("dve", lambda e, d0=d0, b0=b0: e.tensor_tensor(out=modT[:, d0:d0 + 2], in0=ps[2][:, 256:258],
                                                                in1=spm[:, b0:b0 + 2], op=ALU.add),
                 reads=["ps2", "spm"], writes=[f"mod{slot}"])
            yield
        P.op("dve", lambda e: e.scalar_tensor_tensor(out=Acol[:, 8 * a_slot:8 * a_slot + 8], in0=modT[:, scale_col0:scale_col0 + 8],
                                                     scalar=1.0, in1=spm[:, g_col0:g_col0 + 8], op0=ALU.add, op1=ALU.mult),
             reads=[f"mod{slot}", "spm"], writes=[f"A{a_slot}"])
        yield

    def gen_layer_mod(l):
        return gen_mod(w_ada_d[l], 3 * D, SP_LAYER + 32 * l + 8, 24 * l, l, l, SP_LAYER + 32 * l, 24 * l + 8)

    def gen_kv_mod():
        return gen_mod(w_adakv_d, 2 * D, SP_KV + 8, 96, 4, 4, SP_KV, 104)

    def drain(gen):
        for _ in gen:
            pass

    def take(gen, n):
        for _ in range(n):
            try:
                next(gen)
            except StopIteration:
                return
            yield

    def chain(*gens):
        for g_ in gens:
            yield from g_

    for n in range(16):
        bi = n % 2
        tb = n // 4
        P.op("sp", lambda e, n=n, bi=bi: e.dma_start(out=xld[bi], in_=x_d[n * 128:(n + 1) * 128, :]),
             writes=[f"xld{bi}"], dma=True)
        for half in range(2):
            bk = (2 * n + half) % 4
            for c4 in range(4):
                c = half * 4 + c4
                P.op("pe", lambda e, bi=bi, c=c, c4=c4, bk=bk: e.transpose(
                    ps[bk][:, c4 * 128:(c4 + 1) * 128], xld[bi][:, c * 128:(c + 1) * 128], cm(C_IDENT)),
                    reads=[f"xld{bi}", "cst"], writes=[f"ps{bk}"])
            dst = v3(xT[:, half * 4 * T:(half * 4 + 4) * T], 4)[:, :, n * 128:(n + 1) * 128]
            src = v3(ps[bk], 4)
            if half == 0:
                P.op("act", lambda e, dst=dst, src=src: e.copy(out=dst, in_=src), reads=[f"ps{bk}"], writes=[f"xT{tb}"])
            else:
                P.op("dve", lambda e, dst=dst, src=src: e.tensor_copy(out=dst, in_=src), reads=[f"ps{bk}"], writes=[f"xT{tb}"])

    def emit_norm_block(tb, slot, shift0):
        for c in range(8):
            sq_ = sqb[:, (c % 2) * 512:(c % 2 + 1) * 512]
            P.op("act", lambda e, c=c, sq_=sq_: e.activation(out=sq_, in_=xsl(c, tb), func=AF.Square),
                 reads=[f"xT{tb}"], writes=[f"sqb{c % 2}"])
            P.op("pe", lambda e, c=c, sq_=sq_: e.matmul(ps[4], cm(C_ONES, True), sq_,
                                               start=(c == 0), stop=(c == 7)), reads=[f"sqb{c % 2}", "cstb"], writes=["ps4"])
        P.op("act", lambda e: e.activation(out=rstd, in_=ps[4], func=AF.Ln, bias=EPS, scale=1.0 / D), reads=["ps4"], writes=["rstd"])
        P.op("act", lambda e: e.activation(out=rstd, in_=rstd, func=AF.Exp, scale=-0.5), reads=["rstd"], writes=["rstd"])
        for c in range(8):
            tm = tmpn[c % 2]
            P.op("dve", lambda e, c=c, tm=tm: e.scalar_tensor_tensor(
                out=tm, in0=xsl(c, tb), scalar=Acol[:, 8 * slot + c:8 * slot + c + 1], in1=rstd,
                op0=ALU.mult, op1=ALU.mult), reads=[f"xT{tb}", f"A{slot}", "rstd"], writes=[f"tmpn{c % 2}"])
            P.op("act", lambda e, c=c, tm=tm: e.activation(
                out=hT[:, c * 512:(c + 1) * 512], in_=tm, func=AF.Identity,
                bias=modT[:, shift0 + c:shift0 + c + 1], scale=1.0),
                reads=[f"tmpn{c % 2}", f"mod{slot}"], writes=["hT"])

    def emit_outproj(wd2, tb, slot, og=None, og_toks=("ogT",)):
        og = ogT if og is None else og
        for half in range(2):
            P.op("pool", lambda e, half=half: e.dma_start(out=v3(wq[half], 8), in_=wview(wd2, half * 512, 512)),
                 writes=[f"wq{half}"], dma=True)
        for m in range(8):
            half, mm_ = divmod(m, 4)
            bk = (0, 1, 3, 4)[m % 4]
            for h in range(8):
                P.op("pe", lambda e, half=half, mm_=mm_, h=h, bk=bk: e.matmul(
                    ps[bk], wq[half][:, h * 512 + mm_ * 128:h * 512 + (mm_ + 1) * 128], og[:, h * 512:(h + 1) * 512],
                    start=(h == 0), stop=(h == 7)), reads=[f"wq{half}"] + list(og_toks), writes=[f"ps{bk}"])
            P.op("dve", lambda e, m=m, bk=bk: e.scalar_tensor_tensor(
                out=xsl(m, tb), in0=ps[bk], scalar=modT[:, 24 * slot + 16 + m:24 * slot + 17 + m], in1=xsl(m, tb),
                op0=ALU.mult, op1=ALU.add), reads=[f"ps{bk}", f"mod{slot}", f"xT{tb}"], writes=[f"xT{tb}"])

    sb = sbp
    wba = sb("wba", [128, 8 * 16], BF16)
    carry = sb("carry", [128, 8 * 3 * 4])
    pre = [sb(f"pre{j}", [128, 516]) for j in range(3)]
    acc = [sb(f"acc{j}", [128, 512]) for j in range(3)]
    zs = [sb(f"zs{i}", [128, 512], BF16) for i in range(3)]
    rn = sb("rn", [128, 512])
    rn3 = sb("rn3", [128, 512])
    sqc = sb("sqc", [128, 512], BF16)
    qTb = [sb(f"qTb{i}", [128, 512], BF16) for i in range(3)]
    kTb = [sb(f"kTb{i}", [128, 512], BF16) for i in range(3)]
    qdec = [sb(f"qdec{i}", [128, 512], BF16) for i in range(2)]
    kdec = [sb(f"kdec{i}", [128, 512], BF16) for i in range(3)]
    vb = [sb(f"vb{i}", [128, 512]) for i in range(3)]
    egbc = sb("egbc", [128, 512])
    E1 = sb("E1", [128, 512])
    E2 = sb("E2", [128, 512])
    t1 = sb("t1", [128, 512])
    t2 = sb("t2", [128, 512])
    Lb = [LU[:, i * 512:(i + 1) * 512] for i in range(2)]
    Ub = [LU[:, (2 + i) * 512:(3 + i) * 512] for i in range(2)]
    Yb = [sb(f"Yb{i}", [128, 512]) for i in range(2)]
    Offb = sb("Offb", [128, 512])
    Ybf = [sb(f"Ybf{i}", [128, 512], BF16) for i in range(2)]
    attnT = [sb(f"attnT{i}", [128, 512], BF16) for i in range(2)]
    rv = [sb(f"rv{i}", [128, 128], BF16) for i in range(2)]
    vnw = [sb(f"vnw{i}", [128, 128], BF16) for i in range(2)]
    Sst = sb("Sst", [128, 8 * 128])
    Sbf = sb("Sbf", [128, 8 * 128], BF16)
    ob = sb("ob", [128, 512])
    tkU = sb("tkU", [128, 32])
    tkG = sb("tkG", [128, 32])
    tkBeta = sb("tkBeta", [128, 32])
    tkGc = sb("tkGc", [128, 32])
    tkNbeg = sb("tkNbeg", [128, 32])
    tkGl = sb("tkGl", [128, 32])
    tkDks = sb("tkDks", [128, 32])
    tkEgl = sb("tkEgl", [128, 32])
    negA = sb("negA", [128, 8])
    sb = sb_keep

    def emit_gdn_layer(l):
        slot = l
        g0 = SP_GDN + 113 * l
        P.op("pool", lambda e: e.dma_start(out=v3(wba, 8), in_=w_ba_d[l].rearrange("(c p) n -> p c n", p=128)),
             writes=["wba"], dma=True)
        P.op("pool", lambda e: e.memset(carry, 0.0), writes=["carry"])
        P.op("pool", lambda e: e.memset(Sst, 0.0), writes=[f"S{i}" for i in range(8)])
        P.op("pool", lambda e: e.memset(Sbf, 0.0), writes=[f"Sbf{i}" for i in range(8)])
        P.op("act", lambda e: e.activation(out=negA, in_=spm[:, g0 + 97:g0 + 105], func=AF.Exp), reads=["spm"], writes=["negA"])
        P.op("dve", lambda e: e.tensor_scalar(out=negA, in0=negA, scalar1=-1.0, scalar2=None, op0=ALU.mult), reads=["negA"], writes=["negA"])
        for tb in range(NB):
            emit_norm_block(tb, slot, 24 * l)
            for n in range(4):
                for c in range(8):
                    P.op("pe", lambda e, n=n, c=c: e.matmul(
                        ps[7][:, n * 16:(n + 1) * 16], hT[:, c * 512 + n * 128:c * 512 + (n + 1) * 128],
                        wba[:, c * 16:(c + 1) * 16], start=(c == 0), stop=(c == 7)),
                        reads=["hT", "wba"], writes=["ps7a"])
            ba3 = v3(ps[7][:, 0:64], 4)
            for n in range(4):
                P.op("dve", lambda e, n=n: e.tensor_tensor(out=tkU[:, n * 8:(n + 1) * 8], in0=ps[7][:, n * 16 + 8:n * 16 + 16],
                                                          in1=spm[:, g0 + 105:g0 + 113], op=ALU.add),
                     reads=["ps7a", "spm"], writes=["tkU"])
            P.op("act", lambda e: e.activation(out=v3(tkBeta, 4), in_=ba3[:, :, 0:8], func=AF.Exp, scale=-1.0),
                 reads=["ps7a"], writes=["tkBeta"])
            P.op("act", lambda e: e.activation(out=tkU, in_=tkU, func=AF.Exp), reads=["tkU"], writes=["tkU"])
            P.op("act", lambda e: e.activation(out=tkU, in_=tkU, func=AF.Ln, bias=1.0, scale=1.0), reads=["tkU"], writes=["tkU"])
            for n in range(4):
                P.op("dve", lambda e, n=n: e.tensor_tensor(out=tkG[:, n * 8:(n + 1) * 8], in0=tkU[:, n * 8:(n + 1) * 8],
                                                          in1=negA, op=ALU.mult), reads=["tkU", "negA"], writes=["tkG"])
            P.op("dve", lambda e: e.tensor_scalar(out=tkBeta, in0=tkBeta, scalar1=1.0, scalar2=None, op0=ALU.add),
                 reads=["tkBeta"], writes=["tkBeta"])
            P.op("dve", lambda e: e.reciprocal(out=tkBeta, in_=tkBeta), reads=["tkBeta"], writes=["tkBeta"])
            for n in range(4):
                P.op("pe", lambda e, n=n: e.matmul(ps[7][:, 64 + n * 8:64 + (n + 1) * 8], cm(C_TRIU_I), tkG[:, n * 8:(n + 1) * 8],
                                                   start=True, stop=True), reads=["cst", "tkG"], writes=["ps7b"])
                P.op("pe", lambda e, n=n: e.matmul(ps[7][:, 96 + n * 8:96 + (n + 1) * 8], cm(C_ONES), tkG[:, n * 8:(n + 1) * 8],
                                                   start=True, stop=True), reads=["cst", "tkG"], writes=["ps7b"])
            P.op("dve", lambda e: e.tensor_copy(out=tkGc, in_=ps[7][:, 64:96]), reads=["ps7b"], writes=["tkGc"])
            P.op("dve", lambda e: e.tensor_copy(out=tkGl, in_=ps[7][:, 96:128]), reads=["ps7b"], writes=["tkGl"])
            P.op("act", lambda e: e.activation(out=tkEgl, in_=tkGl, func=AF.Exp), reads=["tkGl"], writes=["tkEgl"])
            P.op("dve", lambda e: e.tensor_tensor(out=tkDks, in0=tkGl, in1=tkGc, op=ALU.subtract), reads=["tkGl", "tkGc"], writes=["tkDks"])
            P.op("act", lambda e: e.activation(out=tkDks, in_=tkDks, func=AF.Exp), reads=["tkDks"], writes=["tkDks"])
            P.op("act", lambda e: e.activation(out=tkNbeg, in_=tkGc, func=AF.Exp), reads=["tkGc"], writes=["tkNbeg"])
            P.op("dve", lambda e: e.scalar_tensor_tensor(out=tkNbeg, in0=tkNbeg, scalar=-1.0, in1=tkBeta, op0=ALU.mult, op1=ALU.mult),
                 reads=["tkNbeg", "tkBeta"], writes=["tkNbeg"])

            emit_gdn_block_heads(l, tb, g0)
            emit_outproj(w_outa_d[l], tb, slot)

    def run_interleaved(gens, weights=None):
        gens = list(gens)
        weights = list(weights) if weights is not None else [1] * len(gens)
        while gens:
            for g_, w_ in list(zip(gens, weights)):
                for _ in range(w_):
                    try:
                        next(g_)
                    except StopIteration:
                        k_ = gens.index(g_)
                        gens.pop(k_)
                        weights.pop(k_)
                        break

    def gdn_A(l, tb, h, g0):
        a = h % 3
        wb = wq[h % 2]
        wtk = f"wq{h % 2}"
        P.op("pool", lambda e: e.dma_start(out=v3(wb, 8), in_=wview(w_inp_d[l], h * 512, 512)), writes=[wtk], dma=True)
        yield
        def proj(j, bk):
            for c in range(8):
                P.op("pe", lambda e, c=c: e.matmul(ps[bk], wb[:, c * 512 + j * 128:c * 512 + (j + 1) * 128],
                                                   hT[:, c * 512:(c + 1) * 512], start=(c == 0), stop=(c == 7)),
                     reads=[wtk, "hT"], writes=[f"ps{bk}"])
                yield

        def evac(j, bk):
            cc = (h * 3 + j) * 4
            P.op("pool", lambda e: e.tensor_copy(out=pre[j][:, 0:3], in_=carry[:, cc:cc + 3]),
                 reads=["carry"], writes=[f"pre{j}"])
            P.op("act", lambda e: e.copy(out=pre[j][:, 3:515], in_=ps[bk]), reads=[f"ps{bk}"], writes=[f"pre{j}"])
            yield
            P.op("pool", lambda e: e.tensor_copy(out=carry[:, cc:cc + 3], in_=pre[j][:, 512:515]),
                 reads=[f"pre{j}"], writes=["carry"])
            yield

        yield from proj(0, 0)
        yield from proj(1, 1)
        yield from evac(0, 0)
        yield from evac(1, 1)
        yield from proj(2, 0)
        yield from proj(3, 1)
        yield from evac(2, 0)
        P.op("act", lambda e: e.activation(out=zs[a], in_=ps[1], func=AF.Silu), reads=["ps1"], writes=[f"zs{a}"])
        yield
        wc = lambda tap, j: spm[:, g0 + 24 * tap + 8 * j + h:g0 + 24 * tap + 8 * j + h + 1]
        for tap in range(4):
            for j in range(3):
                if tap == 0:
                    P.op("dve", lambda e, j=j: e.tensor_scalar(out=acc[j], in0=pre[j][:, 0:512], scalar1=wc(0, j), scalar2=0.0,
                                                               op0=ALU.mult, op1=ALU.add), reads=[f"pre{j}", "spm"], writes=[f"acc{j}"])
                else:
                    P.op("dve", lambda e, j=j, tap=tap: e.scalar_tensor_tensor(
                        out=acc[j], in0=pre[j][:, tap:tap + 512], scalar=wc(tap, j), in1=acc[j], op0=ALU.mult, op1=ALU.add),
                        reads=[f"pre{j}", "spm", f"acc{j}"], writes=[f"acc{j}"])
                yield
        for j in range(3):
            P.op("act", lambda e, j=j: e.activation(out=acc[j], in_=acc[j], func=AF.Silu), reads=[f"acc{j}"], writes=[f"acc{j}"])
            yield
        for j in range(2):
            P.op("act", lambda e, j=j: e.activation(out=sqb[:, j * 512:(j + 1) * 512], in_=acc[j], func=AF.Square),
                 reads=[f"acc{j}"], writes=[f"sqb{j}"])
            yield
            P.op("pe", lambda e, j=j: e.matmul(ps[j], cm(C_ONES, True), sqb[:, j * 512:(j + 1) * 512], start=True, stop=True),
                 reads=[f"sqb{j}", "cstb"], writes=[f"ps{j}"])
            yield
        for j in range(2):
            lb = lnb if j == 0 else rn
            tk_ = "lnb" if j == 0 else "rn"
            P.op("act", lambda e, j=j, lb=lb: e.activation(out=lb, in_=ps[j], func=AF.Ln, bias=EPS, scale=1.0),
                 reads=[f"ps{j}"], writes=[tk_])
            yield
            P.op("act", lambda e, j=j, lb=lb: e.activation(out=lb, in_=lb, func=AF.Exp, scale=-0.5,
                                                          bias=(float(np.log(128.0 ** -0.5)) if j == 0 else 0.0)),
                 reads=[tk_], writes=[tk_])
            yield
        P.op("dve", lambda e: e.tensor_tensor(out=qTb[a], in0=acc[0], in1=lnb, op=ALU.mult), reads=["acc0", "lnb"], writes=[f"qTb{a}"])
        yield
        P.op("dve", lambda e: e.tensor_tensor(out=acc[1], in0=acc[1], in1=rn, op=ALU.mult), reads=["acc1", "rn"], writes=["acc1"])
        yield
        P.op("pool", lambda e: e.tensor_copy(out=kTb[a], in_=acc[1]), reads=["acc1"], writes=[f"kTb{a}"])
        yield
        for n in range(4):
            sl = slice(n * 128, (n + 1) * 128)
            P.op("pe", lambda e, sl=sl: e.transpose(ps[0][:, sl], acc[1][:, sl], cm(C_IDENT)), reads=["acc1", "cst"], writes=["ps0"])
            yield
        for n in range(4):
            sl = slice(n * 128, (n + 1) * 128)
            P.op("pe", lambda e, sl=sl: e.transpose(ps[1][:, sl], acc[2][:, sl], cm(C_IDENT)), reads=["acc2", "cst"], writes=["ps1"])
            yield
        for n in range(4):
            sl = slice(n * 128, (n + 1) * 128)
            col = n * 8 + h
            P.op("act", lambda e, sl=sl, col=col: e.activation(out=kdec[a][:, sl], in_=ps[0][:, sl], func=AF.Identity,
                                                               scale=tkDks[:, col:col + 1]), reads=["ps0", "tkDks"], writes=[f"kdec{a}"])
            yield
        for n in range(4):
            sl = slice(n * 128, (n + 1) * 128)
            col = n * 8 + h
            P.op("act", lambda e, sl=sl, col=col: e.activation(out=vb[a][:, sl], in_=ps[1][:, sl], func=AF.Identity,
                                                               scale=tkBeta[:, col:col + 1]), reads=["ps1", "tkBeta"], writes=[f"vb{a}"])
            yield

    def gdn_B(l, tb, h, g0):
        a = h % 3
        bs = h % 2
        q_, k_, kd_, vb_, zs_ = qTb[a], kTb[a], kdec[a], vb[a], zs[a]
        qt, kt, kdt, vbt, zst = f"qTb{a}", f"kTb{a}", f"kdec{a}", f"vb{a}", f"zs{a}"
        tiles = [(n, slice(n * 128, (n + 1) * 128), n * 8 + h) for n in range(4)]
        for n, sl, col in tiles:
            P.op("pe", lambda e, sl=sl: e.matmul(ps[3][:, sl], k_[:, sl], k_[:, sl], start=True, stop=True), reads=[kt], writes=["ps3"])
            P.op("pe", lambda e, sl=sl: e.matmul(ps[4][:, sl], k_[:, sl], q_[:, sl], start=True, stop=True), reads=[kt, qt], writes=["ps4"])
            P.op("pool", lambda e, sl=sl, col=col: e.tensor_scalar(out=E1[:, sl], in0=cm(C_TRIU_I), scalar1=tkG[:, col:col + 1], scalar2=0.0,
                                                                   op0=ALU.mult, op1=ALU.add), reads=["cst", "tkG"], writes=["E1"])
            yield
            P.op("pe", lambda e, sl=sl: e.matmul(ps[5][:, sl], cm(C_ONES), E1[:, sl], start=True, stop=True), reads=["cst", "E1"], writes=["ps5"])
            yield
        P.op("act", lambda e: e.activation(out=egbc, in_=ps[5], func=AF.Exp), reads=["ps5"], writes=["egbc"])
        yield
        for n, sl, col in tiles:
            P.op("dve", lambda e, sl=sl, col=col: e.tensor_scalar(out=E1[:, sl], in0=ps[5][:, sl], scalar1=tkGc[:, col:col + 1], scalar2=0.0,
                                                                  op0=ALU.subtract, op1=ALU.min), reads=["ps5", "tkGc"], writes=["E1"])
            yield
            P.op("dve", lambda e, sl=sl, col=col: e.tensor_scalar(out=E2[:, sl], in0=ps[5][:, sl], scalar1=tkGc[:, col:col + 1], scalar2=0.0,
                                                                  op0=ALU.subtract, op1=ALU.max), reads=["ps5", "tkGc"], writes=["E2"])
            yield
        P.op("act", lambda e: e.activation(out=E1, in_=E1, func=AF.Exp), reads=["E1"], writes=["E1"])
        yield
        P.op("act", lambda e: e.activation(out=E2, in_=E2, func=AF.Exp, scale=-1.0), reads=["E2"], writes=["E2"])
        yield
        for n, sl, col in tiles:
            P.op("dve", lambda e, sl=sl, col=col: e.scalar_tensor_tensor(out=t1[:, sl], in0=ps[3][:, sl], scalar=tkBeta[:, col:col + 1],
                                                                      in1=E2[:, sl], op0=ALU.mult, op1=ALU.mult),
                 reads=["ps3", "E2", "tkBeta"], writes=["t1"])
            yield
        P.op("dve", lambda e: e.tensor_tensor(out=t2, in0=ps[4], in1=E1, op=ALU.mult), reads=["ps4", "E1"], writes=["t2"])
        yield
        P.op("pool", lambda e: e.tensor_tensor(out=qdec[bs], in0=q_, in1=egbc, op=ALU.mult), reads=[qt, "egbc"], writes=[f"qdec{bs}"])
        yield
        L0, U0, Y0 = Lb[0], Ub[0], Yb[0]
        for n, sl, col in tiles:
            P.op("pool", lambda e, sl=sl: e.tensor_tensor(out=L0[:, sl], in0=t1[:, sl], in1=cm(C_BDTRIL_S), op=ALU.mult),
                 reads=["t1", "cst"], writes=["Lb0"])
            yield
        for n, sl, col in tiles:
            P.op("pe", lambda e, sl=sl: e.transpose(ps[4][:, sl], L0[:, sl], cm(C_IDENT)), reads=["Lb0", "cst"], writes=["ps4"])
            yield
        P.op("act", lambda e: e.copy(out=U0, in_=ps[4]), reads=["ps4"], writes=["Ub0"])
        yield
        for n, sl, col in tiles:
            P.op("pool", lambda e, sl=sl: e.tensor_tensor(out=Y0[:, sl], in0=cm(C_IDENT), in1=U0[:, sl], op=ALU.subtract),
                 reads=["cst", "Ub0"], writes=["Yb0"])
            yield
            P.op("pool", lambda e, sl=sl: e.tensor_tensor(out=Offb[:, sl], in0=t1[:, sl], in1=cm(C_OFF), op=ALU.mult),
                 reads=["t1", "cst"], writes=["Offb"])
            yield
            P.op("pool", lambda e, sl=sl: e.tensor_tensor(out=attnT[bs][:, sl], in0=t2[:, sl], in1=cm(C_TRIU_I), op=ALU.mult),
                 reads=["t2", "cst"], writes=[f"attnT{bs}"])
            yield
        cur = 0
        for k in range(1, 6):
            nx = 1 - cur
            for n, sl, col in tiles:
                P.op("pe", lambda e, sl=sl, cur=cur: e.matmul(ps[3][:, sl], Ub[cur][:, sl], Lb[cur][:, sl], start=True, stop=True),
                     reads=[f"Ub{cur}", f"Lb{cur}"], writes=["ps3"])
                yield
            if k < 5:
                for n, sl, col in tiles:
                    P.op("pe", lambda e, sl=sl, cur=cur: e.matmul(ps[4][:, sl], Lb[cur][:, sl], Ub[cur][:, sl], start=True, stop=True),
                         reads=[f"Ub{cur}", f"Lb{cur}"], writes=["ps4"])
                    yield
            P.op("act", lambda e, nx=nx: e.copy(out=Lb[nx], in_=ps[3]), reads=["ps3"], writes=[f"Lb{nx}"])
            yield
            if k < 5:
                P.op("dve", lambda e, nx=nx: e.tensor_copy(out=Ub[nx], in_=ps[4]), reads=["ps4"], writes=[f"Ub{nx}"])
                yield
            for n, sl, col in tiles:
                P.op("pe", lambda e, sl=sl, nx=nx, cur=cur: e.matmul(ps[5][:, sl], Lb[nx][:, sl], Yb[cur][:, sl], start=True, stop=True),
                     reads=[f"Lb{nx}", f"Yb{cur}"], writes=["ps5"])
                yield
            P.op("dve", lambda e, nx=nx, cur=cur: e.tensor_tensor(out=Yb[nx], in0=ps[5], in1=Yb[cur], op=ALU.add),
                 reads=["ps5", f"Yb{cur}"], writes=[f"Yb{nx}"])
            yield
            cur = nx
        Yd = Yb[cur]
        ytk = f"Yb{cur}"
        for n, sl, col in tiles:
            P.op("pe", lambda e, sl=sl: e.transpose(ps[4][:, sl], Yd[:, sl], cm(C_IDENT)), reads=[ytk, "cst"], writes=["ps4"])
            P.op("pe", lambda e, sl=sl: e.matmul(ps[3][:, sl], Offb[:, sl], Yd[:, sl], start=True, stop=True), reads=["Offb", ytk], writes=["ps3"])
            yield
        P.op("act", lambda e: e.copy(out=t1, in_=ps[4]), reads=["ps4"], writes=["t1"])
        yield
        P.op("dve", lambda e: e.tensor_copy(out=t2, in_=ps[3]), reads=["ps3"], writes=["t2"])
        yield
        for n, sl, col in tiles:
            P.op("pe", lambda e, sl=sl: e.matmul(ps[5][:, sl], t1[:, sl], t2[:, sl], start=True, stop=True), reads=["t1", "t2"], writes=["ps5"])
            yield
        P.op("dve", lambda e: e.tensor_tensor(out=Ybf[bs], in0=Yd, in1=ps[5], op=ALU.subtract), reads=[ytk, "ps5"], writes=[f"Ybf{bs}"])
        yield
    def gdn_C(l, tb, h, g0):
        a = h % 3
        bs = h % 2
        q_, k_, kd_, vb_, zs_ = qTb[a], kTb[a], kdec[a], vb[a], zs[a]
        qt, kt, kdt, vbt, zst = f"qTb{a}", f"kTb{a}", f"kdec{a}", f"vb{a}", f"zs{a}"
        tiles = [(n, slice(n * 128, (n + 1) * 128), n * 8 + h) for n in range(4)]
        Sh = Sst[:, h * 128:(h + 1) * 128]
        Shb = Sbf[:, h * 128:(h + 1) * 128]
        for n, sl, col in tiles:
            r_ = rv[n % 2]
            v_ = vnw[n % 2]
            P.op("pe", lambda e, sl=sl: e.matmul(ps[7][:, 128:256], k_[:, sl], Shb, start=True, stop=True),
                 reads=[kt, f"Sbf{h}"], writes=["ps7"])
            yield
            P.op("dve", lambda e, sl=sl, col=col, r_=r_: e.scalar_tensor_tensor(out=r_, in0=ps[7][:, 128:256], scalar=tkNbeg[:, col:col + 1],
                                                                            in1=vb_[:, sl], op0=ALU.mult, op1=ALU.add),
                 reads=["ps7", "tkNbeg", vbt], writes=[f"rv{n % 2}"])
            yield
            P.op("pe", lambda e, sl=sl, r_=r_: e.matmul(ps[7][:, 384:512], Ybf[bs][:, sl], r_, start=True, stop=True),
                 reads=[f"Ybf{bs}", f"rv{n % 2}"], writes=["ps7"])
            yield
            P.op("act", lambda e, v_=v_: e.copy(out=v_, in_=ps[7][:, 384:512]), reads=["ps7"], writes=[f"vnw{n % 2}"])
            yield
            P.op("pe", lambda e, sl=sl: e.matmul(ps[6][:, sl], Shb, qdec[bs][:, sl], start=True, stop=False),
                 reads=[f"Sbf{h}", f"qdec{bs}"], writes=["ps6"])
            P.op("pe", lambda e, sl=sl, v_=v_: e.matmul(ps[6][:, sl], v_, attnT[bs][:, sl], start=False, stop=True),
                 reads=[f"vnw{n % 2}", f"attnT{bs}"], writes=["ps6"])
            yield
            P.op("pe", lambda e, sl=sl, v_=v_: e.matmul(ps[7][:, 256:384], kd_[:, sl], v_, start=True, stop=True),
                 reads=[kdt, f"vnw{n % 2}"], writes=["ps7"])
            yield
            P.op("dve", lambda e, col=col: e.scalar_tensor_tensor(out=Sh, in0=Sh, scalar=tkEgl[:, col:col + 1], in1=ps[7][:, 256:384],
                                                                  op0=ALU.mult, op1=ALU.add),
                 reads=[f"S{h}", "tkEgl", "ps7"], writes=[f"S{h}"])
            yield
            P.op("pool", lambda e: e.tensor_copy(out=Shb, in_=Sh), reads=[f"S{h}"], writes=[f"Sbf{h}"])
            yield
        P.op("act", lambda e: e.copy(out=ob, in_=ps[6]), reads=["ps6"], writes=["ob"])
        yield
        P.op("act", lambda e: e.activation(out=sqc, in_=ob, func=AF.Square), reads=["ob"], writes=["sqc"])
        yield
        P.op("pe", lambda e: e.matmul(ps[6], cm(C_ONES, True), sqc, start=True, stop=True), reads=["sqc", "cstb"], writes=["ps6"])
        yield
        P.op("act", lambda e: e.activation(out=rn3, in_=ps[6], func=AF.Ln, bias=EPS, scale=1.0 / 128), reads=["ps6"], writes=["rn3"])
        yield
        P.op("act", lambda e: e.activation(out=rn3, in_=rn3, func=AF.Exp, scale=-0.5), reads=["rn3"], writes=["rn3"])
        yield
        P.op("dve", lambda e: e.tensor_tensor(out=ob, in0=ob, in1=rn3, op=ALU.mult), reads=["ob", "rn3"], writes=["ob"])
        yield
        P.op("dve", lambda e: e.scalar_tensor_tensor(out=ogT[:, h * 512:(h + 1) * 512], in0=ob, scalar=spm[:, g0 + 96:g0 + 97], in1=zs_,
                                                     op0=ALU.mult, op1=ALU.mult), reads=["ob", "spm", zst], writes=["ogT"])
        yield

    aux = [None]

    def emit_gdn_block_heads(l, tb, g0):
        for step in range(-2, 8):
            gens = []
            wts = []
            if aux[0] is not None:
                gens.append(take(aux[0], AUX_N))
                wts.append(1)
            if 0 <= step < 8:
                gens.append(gdn_C(l, tb, step, g0))
                wts.append(GDN_W[0])
            if 0 <= step + 1 < 8:
                gens.append(gdn_B(l, tb, step + 1, g0))
                wts.append(GDN_W[1])
            if 0 <= step + 2 < 8:
                gens.append(gdn_A(l, tb, step + 2, g0))
                wts.append(GDN_W[2])
            run_interleaved(gens, wts)

    bar_n = [0]

    def emit_barrier():
        k = bar_n[0]
        bar_n[0] += 1
        P.mark_segment()
        P.op("act", lambda e: e.activation(out=bscr[:, 0:1], in_=cact[:, 0:1], func=AF.Identity), reads=["cact"], writes=[f"bar{k}_act", "bscr0"])
        P.op("dve", lambda e: e.tensor_copy(out=bscr[:, 1:2], in_=cact[:, 0:1]), reads=["cact"], writes=[f"bar{k}_dve", "bscr1"])
        P.op("pool", lambda e: e.tensor_copy(out=bscr[:, 2:3], in_=cact[:, 0:1]), reads=["cact"], writes=[f"bar{k}_pool", "bscr2"])
        P.op("pe", lambda e: e.matmul(ps[7][:, 0:8], cm(C_ONES), cact[:, 0:8], start=True, stop=True), reads=["cact", "cst"], writes=["ps7", f"bar{k}_pe"])
        allb = [f"bar{k}_{x}" for x in ("act", "dve", "pool", "pe")]
        P.op("act", lambda e: e.activation(out=bscr[:, 3:4], in_=cact[:, 0:1], func=AF.Identity), reads=allb + ["cact"], writes=["bscr3"])
        P.op("dve", lambda e: e.tensor_copy(out=bscr[:, 4:5], in_=ps[7][:, 0:1]), reads=allb, writes=["bscr4", "ps7"])
        P.op("pool", lambda e: e.tensor_copy(out=bscr[:, 5:6], in_=cact[:, 0:1]), reads=allb + ["cact"], writes=["bscr5"])
        P.op("pe", lambda e: e.matmul(ps[7][:, 0:8], cm(C_ONES), cact[:, 0:8], start=True, stop=True), reads=allb + ["cact", "cst"], writes=["ps7"])
        P.op("sp", lambda e: e.dma_start(out=bscr[:, 6:8], in_=spm_d[:, 0:2]), reads=allb, writes=["bscr6"], dma=True)
        P.mark_segment()

    emit_barrier()
    drain(gen_layer_mod(0))
    n_a = min(nlayers, 2)
    for l in range(n_a):
        if l == 0 and nlayers > 1:
            aux[0] = gen_layer_mod(1)
        if l == 1 and nlayers > 2:
            gl_ = [gen_kv_mod(), gen_layer_mod(2)]
            if nlayers > 3:
                gl_.append(gen_layer_mod(3))
            aux[0] = chain(*gl_)
        emit_gdn_layer(l)
        if aux[0] is not None:
            drain(aux[0])
            aux[0] = None
    emit_barrier()
    gstack.close()
    sb = sb_keep

    ost2 = [sb(f"Eo{i}", [128, 512]) for i in range(2)]
    if nlayers > 2:
        KT = sb("KT", [128, 8 * T], BF16)
        Vt = sb("Vt", [128, 16 * D], BF16)
        qn = sb("qn", [128, 8 * 512], BF16)
        Ebuf = ost2
        SPR = [[sb(f"SPR{i}{u}", [128, 512], F32R) for u in range(2)] for i in range(2)]
        dbf = [[sb(f"dbf{i}{u}", [128, 512]) for u in range(2)] for i in range(2)]
        Wb = [sb(f"Wb{i}", [128, 512], BF16) for i in range(2)]
        rn2 = tmpn[1]
        cstr = sb("cstr", [128, 256], F32R)
        P.op("dve", lambda e: e.tensor_copy(out=cstr[:, 0:128], in_=cm(C_TRIL_S)), reads=["cst"], writes=["cstr"])
        P.op("dve", lambda e: e.tensor_copy(out=cstr[:, 128:256], in_=cm(C_TRIU_I)), reads=["cst"], writes=["cstr"])

        def emit_headnorm(psb, gain_col, extra_bias, dst, dst_tok):
            tm = tmpn[0]
            P.op("act", lambda e: e.copy(out=tm, in_=psb), reads=[psb_tok[0]], writes=["tmpn0"])
            P.op("act", lambda e: e.activation(out=sqb[:, 0:512], in_=tm, func=AF.Square), reads=["tmpn0"], writes=["sqb0"])
            P.op("pe", lambda e: e.matmul(ps[4], cm(C_BDONES, True), sqb[:, 0:512], start=True, stop=True), reads=["sqb0", "cstb"], writes=["ps4"])
            P.op("act", lambda e: e.activation(out=rn2, in_=ps[4], func=AF.Ln, bias=EPS, scale=1.0 / 64), reads=["ps4"], writes=["tmpn1"])
            P.op("act", lambda e: e.activation(out=rn2, in_=rn2, func=AF.Exp, scale=-0.5, bias=extra_bias), reads=["tmpn1"], writes=["tmpn1"])
            P.op("dve", lambda e: e.scalar_tensor_tensor(out=dst, in0=tm, scalar=spm[:, gain_col:gain_col + 1], in1=rn2,
                                                         op0=ALU.mult, op1=ALU.mult), reads=["tmpn0", "spm", "tmpn1"], writes=[dst_tok])

        psb_tok = [None]

        def emit_kv():
            for tb in range(NB):
                emit_norm_block(tb, 4, 96)
                for piece in range(2):
                    P.op("pool", lambda e, piece=piece: e.dma_start(out=v3(wq[piece], 8), in_=wview(w_kv_d, piece * 512, 512)),
                         writes=[f"wq{piece}"], dma=True)
                    for fcl in range(4):
                        fc = piece * 4 + fcl
                        bk = fc % 4
                        for c in range(8):
                            P.op("pe", lambda e, piece=piece, fcl=fcl, c=c, bk=bk: e.matmul(
                                ps[bk], wq[piece][:, c * 512 + fcl * 128:c * 512 + (fcl + 1) * 128], hT[:, c * 512:(c + 1) * 512],
                                start=(c == 0), stop=(c == 7)), reads=[f"wq{piece}", "hT"], writes=[f"ps{bk}"])
                        psb_tok[0] = f"ps{bk}"
                        emit_headnorm(ps[bk], SP_KG, 0.0, KT[:, fc * T + tb * 512:fc * T + (tb + 1) * 512], "KT")
                for piece in range(2):
                    P.op("pool", lambda e, piece=piece: e.dma_start(out=v3(wq[piece], 8), in_=wview(w_kv_d, D + piece * 512, 512)),
                         writes=[f"wq{piece}"], dma=True)
                    for n in range(4):
                        bk = n % 4
                        tile = tb * 4 + n
                        for c in range(8):
                            P.op("pe", lambda e, piece=piece, n=n, c=c, bk=bk: e.matmul(
                                ps[bk], hT[:, c * 512 + n * 128:c * 512 + (n + 1) * 128], wq[piece][:, c * 512:(c + 1) * 512],
                                start=(c == 0), stop=(c == 7)), reads=[f"wq{piece}", "hT"], writes=[f"ps{bk}"])
                        dst = Vt[:, tile * D + piece * 512:tile * D + (piece + 1) * 512]
                        if n % 2 == 0:
                            P.op("act", lambda e, dst=dst, bk=bk: e.copy(out=dst, in_=ps[bk]), reads=[f"ps{bk}"], writes=["Vt"])
                        else:
                            P.op("dve", lambda e, dst=dst, bk=bk: e.tensor_copy(out=dst, in_=ps[bk]), reads=[f"ps{bk}"], writes=["Vt"])

        def sb_head(g, ch, hh, s_):
            h = 2 * ch + hh
            base = hh * 64
            bA, bB = (0, 1) if s_ == 0 else (2, 3)
            Eb, wbu = Ebuf[s_], Wb[s_]
            chunks = list(range(4 * g + 3, -1, -1))

            def geom(i):
                r0 = max(i - 4 * g, 0)
                return r0 * 128, (i >= 4 * g)

            def warm():
                for _ in range(SB_WARM):
                    P.op("pe", lambda e: e.matmul(ps[5][:, 0:128], cm(C_ONES, True), cm(C_IDENT, True), start=True, stop=True),
                         reads=["cstb"], writes=["ps5"])

            def front(k):
                i = chunks[k]
                c0, diag = geom(i)
                u = k % 2
                Sr = SPR[s_][u]
                P.op("pe", lambda e: e.matmul(
                    ps[bA][:, c0:512], KT[base:base + 64, ch * T + i * 128:ch * T + (i + 1) * 128],
                    qn[base:base + 64, ch * 512 + c0:ch * 512 + 512], start=True, stop=True),
                    reads=["KT", f"qn{ch}"], writes=[f"ps{bA}"])
                yield
                P.op("act", lambda e: e.activation(out=Eb[:, c0:512], in_=ps[bA][:, c0:512], func=AF.Exp),
                     reads=[f"ps{bA}"], writes=[f"E{s_}"])
                yield
                P.op("act", lambda e: e.activation(out=Sr[:, c0:512], in_=Eb[:, c0:512], func=AF.Ln, bias=1.0, scale=1.0),
                     reads=[f"E{s_}"], writes=[f"SPR{s_}{u}"])
                yield
                if diag:
                    P.op("pool", lambda e: e.tensor_tensor(out=Sr[:, c0:c0 + 128], in0=Sr[:, c0:c0 + 128].bitcast(F32),
                                                           in1=cm(C_TRIU_S), op=ALU.mult),
                         reads=[f"SPR{s_}{u}", "cst"], writes=[f"SPR{s_}{u}"])
                    yield

            def back1(k):
                i = chunks[k]
                c0, diag = geom(i)
                u = k % 2
                Sr, dbu = SPR[s_][u], dbf[s_][u]
                P.op("pe", lambda e: e.matmul(
                    ps[bB][:, c0:512], cstr[:, 0:128], Sr[:, c0:512], start=(k == 0), stop=False, skip_group_check=True),
                    reads=[f"SPR{s_}{u}", "cstr"], writes=[f"ps{bB}"])
                yield
                P.op("dve", lambda e: e.tensor_tensor(
                    out=dbu[:, c0:512], in0=ps[bA][:, c0:512], in1=Sr[:, c0:512].bitcast(F32), op=ALU.subtract),
                    reads=[f"ps{bA}", f"SPR{s_}{u}"], writes=[f"dbf{s_}{u}"])
                yield

            def back2(k):
                i = chunks[k]
                c0, diag = geom(i)
                u = k % 2
                Sr, dbu = SPR[s_][u], dbf[s_][u]
                P.op("dve", lambda e: e.tensor_tensor(
                    out=dbu[:, c0:512], in0=dbu[:, c0:512], in1=ps[bB][:, c0:512], op=ALU.subtract),
                    reads=[f"ps{bB}", f"dbf{s_}{u}"], writes=[f"dbf{s_}{u}"])
                yield
                if i > 0:
                    warm()
                    P.op("pe", lambda e: e.matmul(
                        ps[bB][:, c0:512], cstr[:, 128:256], Sr[:, c0:512], start=False, stop=False, skip_group_check=True),
                        reads=[f"SPR{s_}{u}", "cstr"], writes=[f"ps{bB}"])
                    yield
                P.op("act", lambda e: e.activation(out=wbu[:, c0:512], in_=dbu[:, c0:512], func=AF.Exp),
                     reads=[f"dbf{s_}{u}"], writes=[f"Wb{s_}"])
                yield
                if diag:
                    P.op("pool", lambda e: e.tensor_tensor(out=wbu[:, c0:c0 + 128], in0=wbu[:, c0:c0 + 128],
                                                           in1=cm(C_TRIU_S, True), op=ALU.mult),
                         reads=[f"Wb{s_}", "cstb"], writes=[f"Wb{s_}"])
                    yield
                P.op("pe", lambda e: e.matmul(
                    ps[6][base:base + 64, c0:512], Vt[:, i * D + h * 64:i * D + (h + 1) * 64], wbu[:, c0:512],
                    start=False, stop=False, skip_group_check=True, tile_position=(0, base)),
                    reads=["Vt", f"Wb{s_}"], writes=["ps6"])
                yield

            yield from front(0)
            for k in range(len(chunks)):
                yield from back1(k)
                if k + 1 < len(chunks):
                    yield from front(k + 1)
                yield from back2(k)

        def run_interleaved(gens):
            gens = list(gens)
            while gens:
                for g_ in list(gens):
                    try:
                        next(g_)
                    except StopIteration:
                        gens.remove(g_)

        def sb_projc(g, l2, fc):
            half, fcl = divmod(fc, 4)
            if fcl == 0:
                P.op("pool", lambda e: e.dma_start(out=v3(wq[0], 8), in_=wview(w_inb_d[l2], half * 512, 512)),
                     writes=["wq0"], dma=True)
                yield
                P.op("pool", lambda e: e.dma_start(out=v3(wq[1], 8), in_=wview(w_inb_d[l2], D + half * 512, 512)),
                     writes=["wq1"], dma=True)
                yield
            for c in range(8):
                P.op("pe", lambda e, c=c: e.matmul(
                    ps[4], wq[0][:, c * 512 + fcl * 128:c * 512 + (fcl + 1) * 128], hT[:, c * 512:(c + 1) * 512],
                    start=(c == 0), stop=(c == 7)), reads=["wq0", "hT"], writes=["ps4"])
                yield
            tm = tmpn[0]
            P.op("act", lambda e: e.copy(out=tm, in_=ps[4]), reads=["ps4"], writes=["tmpn0"])
            yield
            for c in range(8):
                P.op("pe", lambda e, c=c: e.matmul(
                    ps[4], wq[1][:, c * 512 + fcl * 128:c * 512 + (fcl + 1) * 128], hT[:, c * 512:(c + 1) * 512],
                    start=(c == 0), stop=(c == 7)), reads=["wq1", "hT"], writes=["ps4"])
                yield
            P.op("act", lambda e: e.activation(out=sqb[:, 0:512], in_=tm, func=AF.Square), reads=["tmpn0"], writes=["sqb0"])
            yield
            P.op("pe", lambda e: e.matmul(ps[7], cm(C_BDONES, True), sqb[:, 0:512], start=True, stop=True), reads=["sqb0", "cstb"], writes=["ps7"])
            yield
            P.op("act", lambda e: e.activation(out=rn2, in_=ps[7], func=AF.Ln, bias=EPS, scale=1.0 / 64), reads=["ps7"], writes=["tmpn1"])
            yield
            P.op("act", lambda e: e.activation(out=rn2, in_=rn2, func=AF.Exp, scale=-0.5, bias=float(np.log(0.125))), reads=["tmpn1"], writes=["tmpn1"])
            yield
            P.op("dve", lambda e: e.scalar_tensor_tensor(out=qn[:, fc * 512:(fc + 1) * 512], in0=tm, scalar=spm[:, SP_QG + l2:SP_QG + l2 + 1], in1=rn2,
                                                         op0=ALU.mult, op1=ALU.mult), reads=["tmpn0", "spm", "tmpn1"], writes=[f"qn{fc}"])
            yield
            P.op("act", lambda e: e.activation(out=ogT[:, fc * 512:(fc + 1) * 512], in_=ps[4], func=AF.Silu),
                 reads=["ps4"], writes=[f"og{fc}"])
            yield

        def emit_sb_layer(l2):
            L = 2 + l2
            for g in range(NB):
                emit_norm_block(g, L, 24 * L)
                drain(sb_projc(g, l2, 0))
                for ch in range(8):
                    P.op("dve", lambda e: e.memset(ps[6], 0.0), writes=["ps6"])
                    gens = [sb_head(g, ch, 0, 0), sb_head(g, ch, 1, 1)]
                    if ch + 1 < 8:
                        gens.append(sb_projc(g, l2, ch + 1))
                    run_interleaved(gens)
                    P.op("dve", lambda e, ch=ch: e.tensor_tensor(out=ogT[:, ch * 512:(ch + 1) * 512], in0=ps[6], in1=ogT[:, ch * 512:(ch + 1) * 512],
                                                                 op=ALU.mult), reads=["ps6", f"og{ch}"], writes=[f"og{ch}"])
                emit_outproj(w_outb_d[l2], g, L, og_toks=[f"og{i}" for i in range(8)])

        emit_kv()
        for l2 in range(nlayers - 2):
            emit_sb_layer(l2)

    outs = []
    for n in range(16):
        tb = n // 4
        for half in range(2):
            bk = (2 * n + half) % 4
            oi = (2 * n + half) % 2
            for c4 in range(4):
                c = half * 4 + c4
                P.op("pe", lambda e, n=n, c=c, c4=c4, bk=bk: e.transpose(
                    ps[bk][:, c4 * 128:(c4 + 1) * 128], xT[:, c * T + n * 128:c * T + (n + 1) * 128], cm(C_IDENT)),
                    reads=[f"xT{tb}", "cst"], writes=[f"ps{bk}"])
            if half == 0:
                P.op("act", lambda e, oi=oi, bk=bk: e.copy(out=ost2[oi], in_=ps[bk]), reads=[f"ps{bk}"], writes=[f"E{oi}"])
            else:
                P.op("dve", lambda e, oi=oi, bk=bk: e.tensor_copy(out=ost2[oi], in_=ps[bk]), reads=[f"ps{bk}"], writes=[f"E{oi}"])
            outs.append(P.op("sp", lambda e, n=n, half=half, oi=oi: e.dma_start(
                out=out_d[n * 128:(n + 1) * 128, half * 512:(half + 1) * 512], in_=ost2[oi]),
                reads=[f"E{oi}"], dma=True))
    P.finalize(final_wait_ops=outs)
    return nc


def _prep_inputs(inp):
    inp = {k: np.asarray(v) for k, v in inp.items()}
    w_in_a = inp["w_in_a"].astype(np.float32, copy=False)
    qkvz = w_in_a[:, :, :4096].reshape(2, D, 4, 8, 128).transpose(0, 1, 3, 2, 4).reshape(2, D, 4096)
    shared = {
        "cst": _consts(),
        "w_ada": np.ascontiguousarray(inp["w_ada"], np.float32),
        "w_inp": np.ascontiguousarray(qkvz),
        "w_ba": np.ascontiguousarray(w_in_a[:, :, 4096:4112]),
        "w_out_a": np.ascontiguousarray(inp["w_out_a"], np.float32),
        "w_ada_kv": np.ascontiguousarray(inp["w_ada_kv"], np.float32),
        "w_kv": np.ascontiguousarray(inp["w_kv"], np.float32),
        "w_in_b": np.ascontiguousarray(inp["w_in_b"], np.float32),
        "w_out_b": np.ascontiguousarray(inp["w_out_b"], np.float32),
    }
    in_maps = []
    for b in range(8):
        m = dict(shared)
        m["x"] = np.ascontiguousarray(inp["x"][b], np.float32)
        m["spm"] = _small_params(inp, b)
        in_maps.append(m)
    return in_maps


def kernel(**inputs):
    in_maps = _prep_inputs(inputs)
    nc = build(4)
    res = run_bass_kernel_spmd(nc, in_maps, core_ids=list(range(8)))
    return np.stack([np.asarray(r["out"], np.float32) for r in res.results], axis=0)
```

```python
import numpy as np
import concourse.bass as bass
import concourse.mybir as mybir
from concourse.bass_utils import run_bass_kernel_spmd
from contextlib import ExitStack

F32 = mybir.dt.float32
BF16 = mybir.dt.bfloat16
F32R = mybir.dt.float32r
AF = mybir.ActivationFunctionType
ALU = mybir.AluOpType

T = 2048
D = 1024
NB = 4
EPS = 1e-6
AUX_N = 13
SCHEDULE = True
PE_F32C = 2.0
XLAT = 0.3
DVE_C0 = 0.22
DVE_R = 1050.0
POOL_C0 = 0.2
POOL_R = 560.0
PE_F32RC = 2.0
DMA_R = 300e3
SB_WARM = 8
GDN_W = (1, 1, 1)


class Tok:
    __slots__ = ("name", "w", "r")

    def __init__(self, name=""):
        self.name = name
        self.w = None
        self.r = []


class Op:
    __slots__ = ("eng", "fn", "deps", "idx", "need_inc", "ev", "is_dma", "dsem", "dval", "seg", "gidx", "cost")

    def __init__(self, eng, fn):
        self.eng = eng
        self.fn = fn
        self.deps = []
        self.idx = None
        self.need_inc = False
        self.ev = None
        self.is_dma = False


class Prog:
    ENG = ("pe", "act", "dve", "pool", "sp")
    SEM_LIMIT = 30000
    NDMA = 16
    NEAR = 6

    def __init__(self, nc):
        self.nc = nc
        self.q = {e: [] for e in self.ENG}
        self.toks = {}
        self.seg = 0
        self.nops = 0

    def mark_segment(self):
        self.seg += 1

    COST = {"pe": 0.25, "act": 0.55, "dve": 0.6, "pool": 0.45, "sp": 0.15}

    class _Probe:
        def __init__(self):
            self.rec = None

        def __getattr__(self, name):
            def f(*a, **kw):
                self.rec = (name, a, kw)
                return self
            return f

    def estimate(self, o):
        try:
            pr = Prog._Probe()
            o.fn(pr)
            name, a, kw = pr.rec
            out = kw.get("out", a[0] if a else None)
            n = 1
            for d in out.shape[1:]:
                n *= int(d)
            if o.is_dma:
                return 2.0 + out.shape[0] * n * 4 / DMA_R
            if o.eng == "pe":
                if name == "transpose":
                    return 0.12
                lhs = kw.get("lhsT", a[1] if len(a) > 1 else None)
                cyc = {F32: PE_F32C, F32R: PE_F32RC}.get(lhs.dtype, 1.0) * max(n, 64)
                return 0.05 + cyc / 1700.0
            if o.eng == "act":
                return 0.2 + n / 1400.0
            if o.eng == "dve":
                return DVE_C0 + n / DVE_R
            if o.eng == "pool":
                return POOL_C0 + n / POOL_R
        except Exception:
            pass
        return self.COST[o.eng]

    def schedule(self):
        import heapq
        allops = []
        for e in self.ENG:
            allops.extend(self.q[e])
        allops.sort(key=lambda o: o.gidx)
        nseg = self.seg + 1
        bysegs = [[] for _ in range(nseg)]
        for o in allops:
            bysegs[o.seg].append(o)
        newq = {e: [] for e in self.ENG}
        for ops in bysegs:
            if not ops:
                continue
            inseg = set(id(o) for o in ops)
            succ = {}
            indeg = {}
            for o in ops:
                n = 0
                for d in o.deps:
                    if id(d) in inseg:
                        succ.setdefault(id(d), []).append(o)
                        n += 1
                indeg[id(o)] = n
            finish = {}
            free = {e: 0.0 for e in self.ENG}
            heaps = {e: [] for e in self.ENG}
            ready_t = {}
            for o in ops:
                if indeg[id(o)] == 0:
                    ready_t[id(o)] = 0.0
                    heapq.heappush(heaps[o.eng], (0.0, o.gidx, o))
            left = len(ops)
            while left:
                best = None
                for e in self.ENG:
                    h = heaps[e]
                    if not h:
                        continue
                    st = max(h[0][0], free[e])
                    if best is None or (st, h[0][1]) < (best[0], best[1]):
                        best = (st, h[0][1], e)
                st, _, e = best
                h = heaps[e]
                cand = []
                while h and h[0][0] <= st:
                    cand.append(heapq.heappop(h))
                cand.sort(key=lambda c: c[1])
                pick = cand[0]
                for c in cand[1:]:
                    heapq.heappush(h, c)
                o = pick[2]
                c_ = o.cost if o.cost is not None else self.estimate(o)
                if o.is_dma:
                    free[e] = st + (0.15 if e == "sp" else 0.4)
                    fin = st + c_
                else:
                    free[e] = st + c_
                    fin = free[e]
                finish[id(o)] = fin
                newq[e].append(o)
                left -= 1
                for s_ in succ.get(id(o), ()):
                    indeg[id(s_)] -= 1
                    r = max(ready_t.get(id(s_), 0.0), fin + (0.05 if s_.eng == o.eng else XLAT))
                    ready_t[id(s_)] = r
                    if indeg[id(s_)] == 0:
                        heapq.heappush(heaps[s_.eng], (r, s_.gidx, s_))
        for e in self.ENG:
            assert len(newq[e]) == len(self.q[e])
            self.q[e] = newq[e]
            for i, o in enumerate(newq[e]):
                o.idx = i

    def tk(self, name):
        t = self.toks.get(name)
        if t is None:
            t = Tok(name)
            self.toks[name] = t
        return t

    def op(self, eng, fn, reads=(), writes=(), dma=False, cost=None):
        o = Op(eng, fn)
        o.is_dma = dma
        o.seg = self.seg
        o.gidx = self.nops
        o.cost = cost
        self.nops += 1
        o.idx = len(self.q[eng])
        deps = set()
        rd, wr = [], []
        for t in reads:
            if isinstance(t, str) and t.startswith("ps") and t[2].isdigit():
                wr.append(t[:3])
            else:
                rd.append(t)
        for t in writes:
            if isinstance(t, str) and t.startswith("ps") and t[2].isdigit():
                wr.append(t[:3])
            else:
                wr.append(t)
        reads = [self.tk(t) if isinstance(t, str) else t for t in rd]
        writes = [self.tk(t) if isinstance(t, str) else t for t in dict.fromkeys(wr)]
        for t in reads:
            if t.w is not None:
                deps.add(t.w)
        for t in writes:
            if t.w is not None:
                deps.add(t.w)
            for r in t.r:
                deps.add(r)
        deps.discard(o)
        o.deps = list(deps)
        for t in reads:
            t.r.append(o)
        for t in writes:
            t.w = o
            t.r = []
        self.q[eng].append(o)
        return o

    def finalize(self, final_wait_ops=()):
        nc = self.nc
        if SCHEDULE:
            self.schedule()
        waits = {}
        for e in self.ENG:
            seen = {}
            for o in self.q[e]:
                best = {}
                res = []
                for d in o.deps:
                    if d.is_dma:
                        res.append(d)
                        continue
                    if d.eng == o.eng:
                        if e == "pe" or o.is_dma:
                            if not o.is_dma:
                                continue
                        if (not o.is_dma) and o.idx - d.idx > self.NEAR:
                            continue
                    if d.eng not in best or best[d.eng].idx < d.idx:
                        best[d.eng] = d
                for d in best.values():
                    if seen.get(d.eng, -1) >= d.idx:
                        continue
                    seen[d.eng] = d.idx
                    res.append(d)
                waits[o] = res
                for d in res:
                    if not d.is_dma:
                        d.need_inc = True
        stack = ExitStack()
        for e in self.ENG:
            n = sum(1 for o in self.q[e] if o.need_inc and not o.is_dma)
            k = max(1, (n + self.SEM_LIMIT - 1) // self.SEM_LIMIT)
            sems = [stack.enter_context(nc.semaphore(f"s_{e}_{i}")) for i in range(k)]
            c = 0
            for o in self.q[e]:
                if o.need_inc and not o.is_dma:
                    o.ev = (sems[c // self.SEM_LIMIT], c % self.SEM_LIMIT + 1)
                    c += 1
        dsems = {e: [stack.enter_context(nc.semaphore(f"s_dma_{e}_{i}")) for i in range(self.NDMA)]
                 for e in ("sp", "pool")}
        for e in ("sp", "pool"):
            di = 0
            for o in self.q[e]:
                if o.is_dma:
                    o.dsem = dsems[e][di % self.NDMA]
                    o.dval = 16 * (di // self.NDMA + 1)
                    o.ev = (o.dsem, o.dval)
                    di += 1
        final = list(final_wait_ops)
        with nc.Block() as block:
            def run(engname, eng):
                for o in self.q[engname]:
                    for d in waits[o]:
                        eng.wait_ge(d.ev[0], d.ev[1])
                    if o.is_dma and o.dval > 16:
                        eng.wait_ge(o.dsem, o.dval - 16)
                    ins = o.fn(eng)
                    if o.is_dma:
                        ins.then_inc(o.dsem, 16)
                    elif o.need_inc:
                        ins.then_inc(o.ev[0], 1)
                if engname == "sp":
                    for o in final:
                        eng.wait_ge(o.ev[0], o.ev[1])

            @block.tensor
            def _(t):
                run("pe", t)

            @block.scalar
            def _(t):
                run("act", t)

            @block.vector
            def _(t):
                run("dve", t)

            @block.gpsimd
            def _(t):
                run("pool", t)

            @block.sync
            def _(t):
                run("sp", t)
        stack.close()


C_IDENT, C_ONES, C_TRIU_I, C_TRIL_S, C_TRIU_S, C_BDTRIL_S, C_OFF, C_BDONES = range(8)
NCST = 8


def _consts():
    p = np.arange(128)[:, None]
    f = np.arange(128)[None, :]
    m = np.zeros((128, NCST, 128), np.float32)
    m[:, C_IDENT] = (p == f)
    m[:, C_ONES] = 1.0
    m[:, C_TRIU_I] = (p <= f)
    m[:, C_TRIL_S] = (f < p)
    m[:, C_TRIU_S] = (p < f)
    m[:, C_BDTRIL_S] = (f < p) & ((p // 64) == (f // 64))
    m[:, C_OFF] = (p >= 64) & (f < 64)
    m[:, C_BDONES] = ((p // 64) == (f // 64))
    return m.reshape(128, NCST * 128)


def _fm(v):
    v = np.asarray(v, np.float32).reshape(-1, 128)
    return np.ascontiguousarray(v.T)


SP_C = 0
SP_LAYER = 8
SP_KV = 136
SP_GDN = 160
SP_KG = 386
SP_QG = 387
NSP = 389


def _small_params(inp, b):
    sp = np.zeros((128, NSP), np.float32)
    sp[:, 0:8] = _fm(inp["c"][b])
    for l in range(4):
        o = SP_LAYER + 32 * l
        sp[:, o:o + 8] = _fm(inp["norm_g"][l])
        sp[:, o + 8:o + 32] = _fm(inp["b_ada"][l])
    sp[:, SP_KV:SP_KV + 8] = _fm(inp["kv_norm_g"])
    sp[:, SP_KV + 8:SP_KV + 24] = _fm(inp["b_ada_kv"])
    for l in range(2):
        o = SP_GDN + 113 * l
        cw = np.asarray(inp["conv_w_a"][l], np.float32)
        for j in range(4):
            sp[:, o + 24 * j:o + 24 * j + 24] = _fm(cw[j])
        sp[:, o + 96] = np.asarray(inp["o_gain_a"][l], np.float32)
        sp[:, o + 97:o + 105] = np.asarray(inp["a_log_a"][l], np.float32)[None, :]
        sp[:, o + 105:o + 113] = np.asarray(inp["dt_bias_a"][l], np.float32)[None, :]
    sp[:, SP_KG] = np.tile(np.asarray(inp["k_gain"], np.float32), 2)
    for l in range(2):
        sp[:, SP_QG + l] = np.tile(np.asarray(inp["q_gain_b"][l], np.float32), 2)
    return sp


def build(nlayers=4):
    nc = bass.Bass("TRN2", target_bir_lowering=False)
    dram = lambda n, s, k="ExternalInput": nc.dram_tensor(n, s, F32, kind=k).ap()
    x_d = dram("x", [T, D])
    spm_d = dram("spm", [128, NSP])
    cst_d = dram("cst", [128, NCST * 128])
    w_ada_d = dram("w_ada", [4, D, 3 * D])
    w_inp_d = dram("w_inp", [2, D, 8 * 512])
    w_ba_d = dram("w_ba", [2, D, 16])
    w_outa_d = dram("w_out_a", [2, D, D])
    w_adakv_d = dram("w_ada_kv", [D, 2 * D])
    w_kv_d = dram("w_kv", [D, 2 * D])
    w_inb_d = dram("w_in_b", [2, D, 2 * D])
    w_outb_d = dram("w_out_b", [2, D, D])
    out_d = dram("out", [T, D], "ExternalOutput")

    def wview(ap2d, c0, n):
        return ap2d[:, c0:c0 + n].rearrange("(c p) n -> p c n", p=128)

    sb = lambda n, s, d=F32: nc.alloc_sbuf_tensor("sb_" + n, s, d).ap()
    P = Prog(nc)

    def v3(ap, a):
        return ap.rearrange("p (a b) -> p a b", a=a)

    xT = sb("xT", [128, 8 * T])
    cst = sb("cst", [128, NCST * 128])
    cstb = sb("cstb", [128, NCST * 128], BF16)
    spm = sb("spm", [128, NSP])
    cact = sb("cact", [128, 8])
    modT = sb("modT", [128, 5 * 24])
    Acol = sb("Acol", [128, 5 * 8])
    hT = sb("hT", [128, 8 * 512], BF16)
    ogT = sb("ogT", [128, 8 * 512], BF16)
    sqb = sb("sqb", [128, 2 * 512], BF16)
    rstd = sb("rstd", [128, 512])
    bscr = sb("bscr", [128, 8])
    tmpn = [sb(f"tmpn{i}", [128, 512]) for i in range(2)]
    wq = [sb(f"wq{i}", [128, 8 * 512], BF16) for i in range(2)]
    ps = [nc.alloc_psum_tensor(f"ps{i}", [128, 512], F32).ap() for i in range(8)]
    gstack = ExitStack()
    sbp = lambda n, s, d=F32: gstack.enter_context(nc.sbuf_tensor("sb_" + n, s, d))[:]
    sb_keep = sb
    lnb = sbp("lnb", [128, 512])
    mrow = sbp("mrow", [1, 256])
    wa = [sbp("wa0", [128, 8 * 256])] * 2
    LU = sbp("LU", [128, 2048])
    xld = [LU[:, i * 1024:(i + 1) * 1024] for i in range(2)]

    def cm(i, bf=False):
        return (cstb if bf else cst)[:, i * 128:(i + 1) * 128]

    def xsl(c, tb):
        return xT[:, c * T + tb * 512:c * T + (tb + 1) * 512]

    P.op("sp", lambda e: e.dma_start(out=cst, in_=cst_d), writes=["cst"], dma=True)
    P.op("sp", lambda e: e.dma_start(out=spm, in_=spm_d), writes=["spm"], dma=True)
    P.op("pool", lambda e: e.tensor_copy(out=cstb, in_=cst), reads=["cst"], writes=["cstb"])
    P.op("act", lambda e: e.activation(out=cact, in_=spm[:, 0:8], func=AF.Silu), reads=["spm"], writes=["cact"])

    wa_i = [0]

    def gen_mod(wd2, ncols, bias0, dst0, slot, a_slot, g_col0, scale_col0):
        buf = wa[0]
        for piece in range(ncols // 256):
            P.op("sp", lambda e, piece=piece: e.dma_start(out=v3(buf, 8), in_=wview(wd2, piece * 256, 256)),
                 writes=["wa0"], dma=True)
            yield
            for c in range(8):
                P.op("pe", lambda e, c=c: e.matmul(ps[2][0:1, 0:256], cact[:, c:c + 1], buf[:, c * 256:(c + 1) * 256],
                                                   start=(c == 0), stop=(c == 7)), reads=["wa0", "cact"], writes=["ps2"])
                yield
            P.op("act", lambda e: e.copy(out=mrow, in_=ps[2][0:1, 0:256]), reads=["ps2"], writes=["mrow"])
            yield
            for fc in range(2):
                P.op("pe", lambda e, fc=fc: e.matmul(ps[2][:, 256 + fc:257 + fc], mrow[0:1, fc * 128:(fc + 1) * 128],
                                                     cst[0:1, C_ONES * 128:C_ONES * 128 + 1], start=True, stop=True),
                     reads=["mrow", "cst"], writes=["ps2"])
                yield
            d0 = dst0 + piece * 2
            b0 = bias0 + piece * 2
            P.op("dve", lambda e, d0=d0, b0=b0: e.tensor_tensor(out=modT[:, d0:d0 + 2], in0=ps[2][:, 256:258],
                                                                in1=spm[:, b0:b0 + 2], op=ALU.add),
                 reads=["ps2", "spm"], writes=[f"mod{slot}"])
            yield
        P.op("dve", lambda e: e.scalar_tensor_tensor(out=Acol[:, 8 * a_slot:8 * a_slot + 8], in0=modT[:, scale_col0:scale_col0 + 8],
                                                     scalar=1.0, in1=spm[:, g_col0:g_col0 + 8], op0=ALU.add, op1=ALU.mult),
             reads=[f"mod{slot}", "spm"], writes=[f"A{a_slot}"])
        yield

    def gen_layer_mod(l):
        return gen_mod(w_ada_d[l], 3 * D, SP_LAYER + 32 * l + 8, 24 * l, l, l, SP_LAYER + 32 * l, 24 * l + 8)

    def gen_kv_mod():
        return gen_mod(w_adakv_d, 2 * D, SP_KV + 8, 96, 4, 4, SP_KV, 104)

    def drain(gen):
        for _ in gen:
            pass

    def take(gen, n):
        for _ in range(n):
            try:
                next(gen)
            except StopIteration:
                return
            yield

    def chain(*gens):
        for g_ in gens:
            yield from g_

    for n in range(16):
        bi = n % 2
        tb = n // 4
        P.op("sp", lambda e, n=n, bi=bi: e.dma_start(out=xld[bi], in_=x_d[n * 128:(n + 1) * 128, :]),
             writes=[f"xld{bi}"], dma=True)
        for half in range(2):
            bk = (2 * n + half) % 4
            for c4 in range(4):
                c = half * 4 + c4
                P.op("pe", lambda e, bi=bi, c=c, c4=c4, bk=bk: e.transpose(
                    ps[bk][:, c4 * 128:(c4 + 1) * 128], xld[bi][:, c * 128:(c + 1) * 128], cm(C_IDENT)),
                    reads=[f"xld{bi}", "cst"], writes=[f"ps{bk}"])
            dst = v3(xT[:, half * 4 * T:(half * 4 + 4) * T], 4)[:, :, n * 128:(n + 1) * 128]
            src = v3(ps[bk], 4)
            if half == 0:
                P.op("act", lambda e, dst=dst, src=src: e.copy(out=dst, in_=src), reads=[f"ps{bk}"], writes=[f"xT{tb}"])
            else:
                P.op("dve", lambda e, dst=dst, src=src: e.tensor_copy(out=dst, in_=src), reads=[f"ps{bk}"], writes=[f"xT{tb}"])

    def emit_norm_block(tb, slot, shift0):
        for c in range(8):
            sq_ = sqb[:, (c % 2) * 512:(c % 2 + 1) * 512]
            P.op("act", lambda e, c=c, sq_=sq_: e.activation(out=sq_, in_=xsl(c, tb), func=AF.Square),
                 reads=[f"xT{tb}"], writes=[f"sqb{c % 2}"])
            P.op("pe", lambda e, c=c, sq_=sq_: e.matmul(ps[4], cm(C_ONES, True), sq_,
                                               start=(c == 0), stop=(c == 7)), reads=[f"sqb{c % 2}", "cstb"], writes=["ps4"])
        P.op("act", lambda e: e.activation(out=rstd, in_=ps[4], func=AF.Ln, bias=EPS, scale=1.0 / D), reads=["ps4"], writes=["rstd"])
        P.op("act", lambda e: e.activation(out=rstd, in_=rstd, func=AF.Exp, scale=-0.5), reads=["rstd"], writes=["rstd"])
        for c in range(8):
            tm = tmpn[c % 2]
            P.op("dve", lambda e, c=c, tm=tm: e.scalar_tensor_tensor(
                out=tm, in0=xsl(c, tb), scalar=Acol[:, 8 * slot + c:8 * slot + c + 1], in1=rstd,
                op0=ALU.mult, op1=ALU.mult), reads=[f"xT{tb}", f"A{slot}", "rstd"], writes=[f"tmpn{c % 2}"])
            P.op("act", lambda e, c=c, tm=tm: e.activation(
                out=hT[:, c * 512:(c + 1) * 512], in_=tm, func=AF.Identity,
                bias=modT[:, shift0 + c:shift0 + c + 1], scale=1.0),
                reads=[f"tmpn{c % 2}", f"mod{slot}"], writes=["hT"])

    def emit_outproj(wd2, tb, slot, og=None, og_toks=("ogT",)):
        og = ogT if og is None else og
        for half in range(2):
            P.op("pool", lambda e, half=half: e.dma_start(out=v3(wq[half], 8), in_=wview(wd2, half * 512, 512)),
                 writes=[f"wq{half}"], dma=True)
        for m in range(8):
            half, mm_ = divmod(m, 4)
            bk = (0, 1, 3, 4)[m % 4]
            for h in range(8):
                P.op("pe", lambda e, half=half, mm_=mm_, h=h, bk=bk: e.matmul(
                    ps[bk], wq[half][:, h * 512 + mm_ * 128:h * 512 + (mm_ + 1) * 128], og[:, h * 512:(h + 1) * 512],
                    start=(h == 0), stop=(h == 7)), reads=[f"wq{half}"] + list(og_toks), writes=[f"ps{bk}"])
            P.op("dve", lambda e, m=m, bk=bk: e.scalar_tensor_tensor(
                out=xsl(m, tb), in0=ps[bk], scalar=modT[:, 24 * slot + 16 + m:24 * slot + 17 + m], in1=xsl(m, tb),
                op0=ALU.mult, op1=ALU.add), reads=[f"ps{bk}", f"mod{slot}", f"xT{tb}"], writes=[f"xT{tb}"])

    sb = sbp
    wba = sb("wba", [128, 8 * 16], BF16)
    carry = sb("carry", [128, 8 * 3 * 4])
    pre = [sb(f"pre{j}", [128, 516]) for j in range(3)]
    acc = [sb(f"acc{j}", [128, 512]) for j in range(3)]
    zs = [sb(f"zs{i}", [128, 512], BF16) for i in range(3)]
    rn = sb("rn", [128, 512])
    rn3 = sb("rn3", [128, 512])
    sqc = sb("sqc", [128, 512], BF16)
    qTb = [sb(f"qTb{i}", [128, 512], BF16) for i in range(3)]
    kTb = [sb(f"kTb{i}", [128, 512], BF16) for i in range(3)]
    qdec = [sb(f"qdec{i}", [128, 512], BF16) for i in range(2)]
    kdec = [sb(f"kdec{i}", [128, 512], BF16) for i in range(3)]
    vb = [sb(f"vb{i}", [128, 512]) for i in range(3)]
    egbc = sb("egbc", [128, 512])
    E1 = sb("E1", [128, 512])
    E2 = sb("E2", [128, 512])
    t1 = sb("t1", [128, 512])
    t2 = sb("t2", [128, 512])
    Lb = [LU[:, i * 512:(i + 1) * 512] for i in range(2)]
    Ub = [LU[:, (2 + i) * 512:(3 + i) * 512] for i in range(2)]
    Yb = [sb(f"Yb{i}", [128, 512]) for i in range(2)]
    Offb = sb("Offb", [128, 512])
    Ybf = [sb(f"Ybf{i}", [128, 512], BF16) for i in range(2)]
    attnT = [sb(f"attnT{i}", [128, 512], BF16) for i in range(2)]
    rv = [sb(f"rv{i}", [128, 128], BF16) for i in range(2)]
    vnw = [sb(f"vnw{i}", [128, 128], BF16) for i in range(2)]
    Sst = sb("Sst", [128, 8 * 128])
    Sbf = sb("Sbf", [128, 8 * 128], BF16)
    ob = sb("ob", [128, 512])
    tkU = sb("tkU", [128, 32])
    tkG = sb("tkG", [128, 32])
    tkBeta = sb("tkBeta", [128, 32])
    tkGc = sb("tkGc", [128, 32])
    tkNbeg = sb("tkNbeg", [128, 32])
    tkGl = sb("tkGl", [128, 32])
    tkDks = sb("tkDks", [128, 32])
    tkEgl = sb("tkEgl", [128, 32])
    negA = sb("negA", [128, 8])
    sb = sb_keep

    def emit_gdn_layer(l):
        slot = l
        g0 = SP_GDN + 113 * l
        P.op("pool", lambda e: e.dma_start(out=v3(wba, 8), in_=w_ba_d[l].rearrange("(c p) n -> p c n", p=128)),
             writes=["wba"], dma=True)
        P.op("pool", lambda e: e.memset(carry, 0.0), writes=["carry"])
        P.op("pool", lambda e: e.memset(Sst, 0.0), writes=[f"S{i}" for i in range(8)])
        P.op("pool", lambda e: e.memset(Sbf, 0.0), writes=[f"Sbf{i}" for i in range(8)])
        P.op("act", lambda e: e.activation(out=negA, in_=spm[:, g0 + 97:g0 + 105], func=AF.Exp), reads=["spm"], writes=["negA"])
        P.op("dve", lambda e: e.tensor_scalar(out=negA, in0=negA, scalar1=-1.0, scalar2=None, op0=ALU.mult), reads=["negA"], writes=["negA"])
        for tb in range(NB):
            emit_norm_block(tb, slot, 24 * l)
            for n in range(4):
                for c in range(8):
                    P.op("pe", lambda e, n=n, c=c: e.matmul(
                        ps[7][:, n * 16:(n + 1) * 16], hT[:, c * 512 + n * 128:c * 512 + (n + 1) * 128],
                        wba[:, c * 16:(c + 1) * 16], start=(c == 0), stop=(c == 7)),
                        reads=["hT", "wba"], writes=["ps7a"])
            ba3 = v3(ps[7][:, 0:64], 4)
            for n in range(4):
                P.op("dve", lambda e, n=n: e.tensor_tensor(out=tkU[:, n * 8:(n + 1) * 8], in0=ps[7][:, n * 16 + 8:n * 16 + 16],
                                                          in1=spm[:, g0 + 105:g0 + 113], op=ALU.add),
                     reads=["ps7a", "spm"], writes=["tkU"])
            P.op("act", lambda e: e.activation(out=v3(tkBeta, 4), in_=ba3[:, :, 0:8], func=AF.Exp, scale=-1.0),
                 reads=["ps7a"], writes=["tkBeta"])
            P.op("act", lambda e: e.activation(out=tkU, in_=tkU, func=AF.Exp), reads=["tkU"], writes=["tkU"])
            P.op("act", lambda e: e.activation(out=tkU, in_=tkU, func=AF.Ln, bias=1.0, scale=1.0), reads=["tkU"], writes=["tkU"])
            for n in range(4):
                P.op("dve", lambda e, n=n: e.tensor_tensor(out=tkG[:, n * 8:(n + 1) * 8], in0=tkU[:, n * 8:(n + 1) * 8],
                                                          in1=negA, op=ALU.mult), reads=["tkU", "negA"], writes=["tkG"])
            P.op("dve", lambda e: e.tensor_scalar(out=tkBeta, in0=tkBeta, scalar1=1.0, scalar2=None, op0=ALU.add),
                 reads=["tkBeta"], writes=["tkBeta"])
            P.op("dve", lambda e: e.reciprocal(out=tkBeta, in_=tkBeta), reads=["tkBeta"], writes=["tkBeta"])
            for n in range(4):
                P.op("pe", lambda e, n=n: e.matmul(ps[7][:, 64 + n * 8:64 + (n + 1) * 8], cm(C_TRIU_I), tkG[:, n * 8:(n + 1) * 8],
                                                   start=True, stop=True), reads=["cst", "tkG"], writes=["ps7b"])
                P.op("pe", lambda e, n=n: e.matmul(ps[7][:, 96 + n * 8:96 + (n + 1) * 8], cm(C_ONES), tkG[:, n * 8:(n + 1) * 8],
                                                   start=True, stop=True), reads=["cst", "tkG"], writes=["ps7b"])
            P.op("dve", lambda e: e.tensor_copy(out=tkGc, in_=ps[7][:, 64:96]), reads=["ps7b"], writes=["tkGc"])
            P.op("dve", lambda e: e.tensor_copy(out=tkGl, in_=ps[7][:, 96:128]), reads=["ps7b"], writes=["tkGl"])
            P.op("act", lambda e: e.activation(out=tkEgl, in_=tkGl, func=AF.Exp), reads=["tkGl"], writes=["tkEgl"])
            P.op("dve", lambda e: e.tensor_tensor(out=tkDks, in0=tkGl, in1=tkGc, op=ALU.subtract), reads=["tkGl", "tkGc"], writes=["tkDks"])
            P.op("act", lambda e: e.activation(out=tkDks, in_=tkDks, func=AF.Exp), reads=["tkDks"], writes=["tkDks"])
            P.op("act", lambda e: e.activation(out=tkNbeg, in_=tkGc, func=AF.Exp), reads=["tkGc"], writes=["tkNbeg"])
            P.op("dve", lambda e: e.scalar_tensor_tensor(out=tkNbeg, in0=tkNbeg, scalar=-1.0, in1=tkBeta, op0=ALU.mult, op1=ALU.mult),
                 reads=["tkNbeg", "tkBeta"], writes=["tkNbeg"])

            emit_gdn_block_heads(l, tb, g0)
            emit_outproj(w_outa_d[l], tb, slot)

    def run_interleaved(gens, weights=None):
        gens = list(gens)
        weights = list(weights) if weights is not None else [1] * len(gens)
        while gens:
            for g_, w_ in list(zip(gens, weights)):
                for _ in range(w_):
                    try:
                        next(g_)
                    except StopIteration:
                        k_ = gens.index(g_)
                        gens.pop(k_)
                        weights.pop(k_)
                        break

    def gdn_A(l, tb, h, g0):
        a = h % 3
        wb = wq[h % 2]
        wtk = f"wq{h % 2}"
        P.op("pool", lambda e: e.dma_start(out=v3(wb, 8), in_=wview(w_inp_d[l], h * 512, 512)), writes=[wtk], dma=True)
        yield
        def proj(j, bk):
            for c in range(8):
                P.op("pe", lambda e, c=c: e.matmul(ps[bk], wb[:, c * 512 + j * 128:c * 512 + (j + 1) * 128],
                                                   hT[:, c * 512:(c + 1) * 512], start=(c == 0), stop=(c == 7)),
                     reads=[wtk, "hT"], writes=[f"ps{bk}"])
                yield

        def evac(j, bk):
            cc = (h * 3 + j) * 4
            P.op("pool", lambda e: e.tensor_copy(out=pre[j][:, 0:3], in_=carry[:, cc:cc + 3]),
                 reads=["carry"], writes=[f"pre{j}"])
            P.op("act", lambda e: e.copy(out=pre[j][:, 3:515], in_=ps[bk]), reads=[f"ps{bk}"], writes=[f"pre{j}"])
            yield
            P.op("pool", lambda e: e.tensor_copy(out=carry[:, cc:cc + 3], in_=pre[j][:, 512:515]),
                 reads=[f"pre{j}"], writes=["carry"])
            yield

        yield from proj(0, 0)
        yield from proj(1, 1)
        yield from evac(0, 0)
        yield from evac(1, 1)
        yield from proj(2, 0)
        yield from proj(3, 1)
        yield from evac(2, 0)
        P.op("act", lambda e: e.activation(out=zs[a], in_=ps[1], func=AF.Silu), reads=["ps1"], writes=[f"zs{a}"])
        yield
        wc = lambda tap, j: spm[:, g0 + 24 * tap + 8 * j + h:g0 + 24 * tap + 8 * j + h + 1]
        for tap in range(4):
            for j in range(3):
                if tap == 0:
                    P.op("dve", lambda e, j=j: e.tensor_scalar(out=acc[j], in0=pre[j][:, 0:512], scalar1=wc(0, j), scalar2=0.0,
                                                               op0=ALU.mult, op1=ALU.add), reads=[f"pre{j}", "spm"], writes=[f"acc{j}"])
                else:
                    P.op("dve", lambda e, j=j, tap=tap: e.scalar_tensor_tensor(
                        out=acc[j], in0=pre[j][:, tap:tap + 512], scalar=wc(tap, j), in1=acc[j], op0=ALU.mult, op1=ALU.add),
                        reads=[f"pre{j}", "spm", f"acc{j}"], writes=[f"acc{j}"])
                yield
        for j in range(3):
            P.op("act", lambda e, j=j: e.activation(out=acc[j], in_=acc[j], func=AF.Silu), reads=[f"acc{j}"], writes=[f"acc{j}"])
            yield
        for j in range(2):
            P.op("act", lambda e, j=j: e.activation(out=sqb[:, j * 512:(j + 1) * 512], in_=acc[j], func=AF.Square),
                 reads=[f"acc{j}"], writes=[f"sqb{j}"])
            yield
            P.op("pe", lambda e, j=j: e.matmul(ps[j], cm(C_ONES, True), sqb[:, j * 512:(j + 1) * 512], start=True, stop=True),
                 reads=[f"sqb{j}", "cstb"], writes=[f"ps{j}"])
            yield
        for j in range(2):
            lb = lnb if j == 0 else rn
            tk_ = "lnb" if j == 0 else "rn"
            P.op("act", lambda e, j=j, lb=lb: e.activation(out=lb, in_=ps[j], func=AF.Ln, bias=EPS, scale=1.0),
                 reads=[f"ps{j}"], writes=[tk_])
            yield
            P.op("act", lambda e, j=j, lb=lb: e.activation(out=lb, in_=lb, func=AF.Exp, scale=-0.5,
                                                          bias=(float(np.log(128.0 ** -0.5)) if j == 0 else 0.0)),
                 reads=[tk_], writes=[tk_])
            yield
        P.op("dve", lambda e: e.tensor_tensor(out=qTb[a], in0=acc[0], in1=lnb, op=ALU.mult), reads=["acc0", "lnb"], writes=[f"qTb{a}"])
        yield
        P.op("dve", lambda e: e.tensor_tensor(out=acc[1], in0=acc[1], in1=rn, op=ALU.mult), reads=["acc1", "rn"], writes=["acc1"])
        yield
        P.op("pool", lambda e: e.tensor_copy(out=kTb[a], in_=acc[1]), reads=["acc1"], writes=[f"kTb{a}"])
        yield
        for n in range(4):
            sl = slice(n * 128, (n + 1) * 128)
            P.op("pe", lambda e, sl=sl: e.transpose(ps[0][:, sl], acc[1][:, sl], cm(C_IDENT)), reads=["acc1", "cst"], writes=["ps0"])
            yield
        for n in range(4):
            sl = slice(n * 128, (n + 1) * 128)
            P.op("pe", lambda e, sl=sl: e.transpose(ps[1][:, sl], acc[2][:, sl], cm(C_IDENT)), reads=["acc2", "cst"], writes=["ps1"])
            yield
        for n in range(4):
            sl = slice(n * 128, (n + 1) * 128)
            col = n * 8 + h
            P.op("act", lambda e, sl=sl, col=col: e.activation(out=kdec[a][:, sl], in_=ps[0][:, sl], func=AF.Identity,
                                                               scale=tkDks[:, col:col + 1]), reads=["ps0", "tkDks"], writes=[f"kdec{a}"])
            yield
        for n in range(4):
            sl = slice(n * 128, (n + 1) * 128)
            col = n * 8 + h
            P.op("act", lambda e, sl=sl, col=col: e.activation(out=vb[a][:, sl], in_=ps[1][:, sl], func=AF.Identity,
                                                               scale=tkBeta[:, col:col + 1]), reads=["ps1", "tkBeta"], writes=[f"vb{a}"])
            yield

    def gdn_B(l, tb, h, g0):
        a = h % 3
        bs = h % 2
        q_, k_, kd_, vb_, zs_ = qTb[a], kTb[a], kdec[a], vb[a], zs[a]
        qt, kt, kdt, vbt, zst = f"qTb{a}", f"kTb{a}", f"kdec{a}", f"vb{a}", f"zs{a}"
        tiles = [(n, slice(n * 128, (n + 1) * 128), n * 8 + h) for n in range(4)]
        for n, sl, col in tiles:
            P.op("pe", lambda e, sl=sl: e.matmul(ps[3][:, sl], k_[:, sl], k_[:, sl], start=True, stop=True), reads=[kt], writes=["ps3"])
            P.op("pe", lambda e, sl=sl: e.matmul(ps[4][:, sl], k_[:, sl], q_[:, sl], start=True, stop=True), reads=[kt, qt], writes=["ps4"])
            P.op("pool", lambda e, sl=sl, col=col: e.tensor_scalar(out=E1[:, sl], in0=cm(C_TRIU_I), scalar1=tkG[:, col:col + 1], scalar2=0.0,
                                                                   op0=ALU.mult, op1=ALU.add), reads=["cst", "tkG"], writes=["E1"])
            yield
            P.op("pe", lambda e, sl=sl: e.matmul(ps[5][:, sl], cm(C_ONES), E1[:, sl], start=True, stop=True), reads=["cst", "E1"], writes=["ps5"])
            yield
        P.op("act", lambda e: e.activation(out=egbc, in_=ps[5], func=AF.Exp), reads=["ps5"], writes=["egbc"])
        yield
        for n, sl, col in tiles:
            P.op("dve", lambda e, sl=sl, col=col: e.tensor_scalar(out=E1[:, sl], in0=ps[5][:, sl], scalar1=tkGc[:, col:col + 1], scalar2=0.0,
                                                                  op0=ALU.subtract, op1=ALU.min), reads=["ps5", "tkGc"], writes=["E1"])
            yield
            P.op("dve", lambda e, sl=sl, col=col: e.tensor_scalar(out=E2[:, sl], in0=ps[5][:, sl], scalar1=tkGc[:, col:col + 1], scalar2=0.0,
                                                                  op0=ALU.subtract, op1=ALU.max), reads=["ps5", "tkGc"], writes=["E2"])
            yield
        P.op("act", lambda e: e.activation(out=E1, in_=E1, func=AF.Exp), reads=["E1"], writes=["E1"])
        yield
        P.op("act", lambda e: e.activation(out=E2, in_=E2, func=AF.Exp, scale=-1.0), reads=["E2"], writes=["E2"])
        yield
        for n, sl, col in tiles:
            P.op("dve", lambda e, sl=sl, col=col: e.scalar_tensor_tensor(out=t1[:, sl], in0=ps[3][:, sl], scalar=tkBeta[:, col:col + 1],
                                                                      in1=E2[:, sl], op0=ALU.mult, op1=ALU.mult),
                 reads=["ps3", "E2", "tkBeta"], writes=["t1"])
            yield
        P.op("dve", lambda e: e.tensor_tensor(out=t2, in0=ps[4], in1=E1, op=ALU.mult), reads=["ps4", "E1"], writes=["t2"])
        yield
        P.op("pool", lambda e: e.tensor_tensor(out=qdec[bs], in0=q_, in1=egbc, op=ALU.mult), reads=[qt, "egbc"], writes=[f"qdec{bs}"])
        yield
        L0, U0, Y0 = Lb[0], Ub[0], Yb[0]
        for n, sl, col in tiles:
            P.op("pool", lambda e, sl=sl: e.tensor_tensor(out=L0[:, sl], in0=t1[:, sl], in1=cm(C_BDTRIL_S), op=ALU.mult),
                 reads=["t1", "cst"], writes=["Lb0"])
            yield
        for n, sl, col in tiles:
            P.op("pe", lambda e, sl=sl: e.transpose(ps[4][:, sl], L0[:, sl], cm(C_IDENT)), reads=["Lb0", "cst"], writes=["ps4"])
            yield
        P.op("act", lambda e: e.copy(out=U0, in_=ps[4]), reads=["ps4"], writes=["Ub0"])
        yield
        for n, sl, col in tiles:
            P.op("pool", lambda e, sl=sl: e.tensor_tensor(out=Y0[:, sl], in0=cm(C_IDENT), in1=U0[:, sl], op=ALU.subtract),
                 reads=["cst", "Ub0"], writes=["Yb0"])
            yield
            P.op("pool", lambda e, sl=sl: e.tensor_tensor(out=Offb[:, sl], in0=t1[:, sl], in1=cm(C_OFF), op=ALU.mult),
                 reads=["t1", "cst"], writes=["Offb"])
            yield
            P.op("pool", lambda e, sl=sl: e.tensor_tensor(out=attnT[bs][:, sl], in0=t2[:, sl], in1=cm(C_TRIU_I), op=ALU.mult),
                 reads=["t2", "cst"], writes=[f"attnT{bs}"])
            yield
        cur = 0
        for k in range(1, 6):
            nx = 1 - cur
            for n, sl, col in tiles:
                P.op("pe", lambda e, sl=sl, cur=cur: e.matmul(ps[3][:, sl], Ub[cur][:, sl], Lb[cur][:, sl], start=True, stop=True),
                     reads=[f"Ub{cur}", f"Lb{cur}"], writes=["ps3"])
                yield
            if k < 5:
                for n, sl, col in tiles:
                    P.op("pe", lambda e, sl=sl, cur=cur: e.matmul(ps[4][:, sl], Lb[cur][:, sl], Ub[cur][:, sl], start=True, stop=True),
                         reads=[f"Ub{cur}", f"Lb{cur}"], writes=["ps4"])
                    yield
            P.op("act", lambda e, nx=nx: e.copy(out=Lb[nx], in_=ps[3]), reads=["ps3"], writes=[f"Lb{nx}"])
            yield
            if k < 5:
                P.op("dve", lambda e, nx=nx: e.tensor_copy(out=Ub[nx], in_=ps[4]), reads=["ps4"], writes=[f"Ub{nx}"])
                yield
            for n, sl, col in tiles:
                P.op("pe", lambda e, sl=sl, nx=nx, cur=cur: e.matmul(ps[5][:, sl], Lb[nx][:, sl], Yb[cur][:, sl], start=True, stop=True),
                     reads=[f"Lb{nx}", f"Yb{cur}"], writes=["ps5"])
                yield
            P.op("dve", lambda e, nx=nx, cur=cur: e.tensor_tensor(out=Yb[nx], in0=ps[5], in1=Yb[cur], op=ALU.add),
                 reads=["ps5", f"Yb{cur}"], writes=[f"Yb{nx}"])
            yield
            cur = nx
        Yd = Yb[cur]
        ytk = f"Yb{cur}"
        for n, sl, col in tiles:
            P.op("pe", lambda e, sl=sl: e.transpose(ps[4][:, sl], Yd[:, sl], cm(C_IDENT)), reads=[ytk, "cst"], writes=["ps4"])
            P.op("pe", lambda e, sl=sl: e.matmul(ps[3][:, sl], Offb[:, sl], Yd[:, sl], start=True, stop=True), reads=["Offb", ytk], writes=["ps3"])
            yield
        P.op("act", lambda e: e.copy(out=t1, in_=ps[4]), reads=["ps4"], writes=["t1"])
        yield
        P.op("dve", lambda e: e.tensor_copy(out=t2, in_=ps[3]), reads=["ps3"], writes=["t2"])
        yield
        for n, sl, col in tiles:
            P.op("pe", lambda e, sl=sl: e.matmul(ps[5][:, sl], t1[:, sl], t2[:, sl], start=True, stop=True), reads=["t1", "t2"], writes=["ps5"])
            yield
        P.op("dve", lambda e: e.tensor_tensor(out=Ybf[bs], in0=Yd, in1=ps[5], op=ALU.subtract), reads=[ytk, "ps5"], writes=[f"Ybf{bs}"])
        yield
    def gdn_C(l, tb, h, g0):
        a = h % 3
        bs = h % 2
        q_, k_, kd_, vb_, zs_ = qTb[a], kTb[a], kdec[a], vb[a], zs[a]
        qt, kt, kdt, vbt, zst = f"qTb{a}", f"kTb{a}", f"kdec{a}", f"vb{a}", f"zs{a}"
        tiles = [(n, slice(n * 128, (n + 1) * 128), n * 8 + h) for n in range(4)]
        Sh = Sst[:, h * 128:(h + 1) * 128]
        Shb = Sbf[:, h * 128:(h + 1) * 128]
        for n, sl, col in tiles:
            r_ = rv[n % 2]
            v_ = vnw[n % 2]
            P.op("pe", lambda e, sl=sl: e.matmul(ps[7][:, 128:256], k_[:, sl], Shb, start=True, stop=True),
                 reads=[kt, f"Sbf{h}"], writes=["ps7"])
            yield
            P.op("dve", lambda e, sl=sl, col=col, r_=r_: e.scalar_tensor_tensor(out=r_, in0=ps[7][:, 128:256], scalar=tkNbeg[:, col:col + 1],
                                                                            in1=vb_[:, sl], op0=ALU.mult, op1=ALU.add),
                 reads=["ps7", "tkNbeg", vbt], writes=[f"rv{n % 2}"])
            yield
            P.op("pe", lambda e, sl=sl, r_=r_: e.matmul(ps[7][:, 384:512], Ybf[bs][:, sl], r_, start=True, stop=True),
                 reads=[f"Ybf{bs}", f"rv{n % 2}"], writes=["ps7"])
            yield
            P.op("act", lambda e, v_=v_: e.copy(out=v_, in_=ps[7][:, 384:512]), reads=["ps7"], writes=[f"vnw{n % 2}"])
            yield
            P.op("pe", lambda e, sl=sl: e.matmul(ps[6][:, sl], Shb, qdec[bs][:, sl], start=True, stop=False),
                 reads=[f"Sbf{h}", f"qdec{bs}"], writes=["ps6"])
            P.op("pe", lambda e, sl=sl, v_=v_: e.matmul(ps[6][:, sl], v_, attnT[bs][:, sl], start=False, stop=True),
                 reads=[f"vnw{n % 2}", f"attnT{bs}"], writes=["ps6"])
            yield
            P.op("pe", lambda e, sl=sl, v_=v_: e.matmul(ps[7][:, 256:384], kd_[:, sl], v_, start=True, stop=True),
                 reads=[kdt, f"vnw{n % 2}"], writes=["ps7"])
            yield
            P.op("dve", lambda e, col=col: e.scalar_tensor_tensor(out=Sh, in0=Sh, scalar=tkEgl[:, col:col + 1], in1=ps[7][:, 256:384],
                                                                  op0=ALU.mult, op1=ALU.add),
                 reads=[f"S{h}", "tkEgl", "ps7"], writes=[f"S{h}"])
            yield
            P.op("pool", lambda e: e.tensor_copy(out=Shb, in_=Sh), reads=[f"S{h}"], writes=[f"Sbf{h}"])
            yield
        P.op("act", lambda e: e.copy(out=ob, in_=ps[6]), reads=["ps6"], writes=["ob"])
        yield
        P.op("act", lambda e: e.activation(out=sqc, in_=ob, func=AF.Square), reads=["ob"], writes=["sqc"])
        yield
        P.op("pe", lambda e: e.matmul(ps[6], cm(C_ONES, True), sqc, start=True, stop=True), reads=["sqc", "cstb"], writes=["ps6"])
        yield
        P.op("act", lambda e: e.activation(out=rn3, in_=ps[6], func=AF.Ln, bias=EPS, scale=1.0 / 128), reads=["ps6"], writes=["rn3"])
        yield
        P.op("act", lambda e: e.activation(out=rn3, in_=rn3, func=AF.Exp, scale=-0.5), reads=["rn3"], writes=["rn3"])
        yield
        P.op("dve", lambda e: e.tensor_tensor(out=ob, in0=ob, in1=rn3, op=ALU.mult), reads=["ob", "rn3"], writes=["ob"])
        yield
        P.op("dve", lambda e: e.scalar_tensor_tensor(out=ogT[:, h * 512:(h + 1) * 512], in0=ob, scalar=spm[:, g0 + 96:g0 + 97], in1=zs_,
                                                     op0=ALU.mult, op1=ALU.mult), reads=["ob", "spm", zst], writes=["ogT"])
        yield

    aux = [None]

    def emit_gdn_block_heads(l, tb, g0):
        for step in range(-2, 8):
            gens = []
            wts = []
            if aux[0] is not None:
                gens.append(take(aux[0], AUX_N))
                wts.append(1)
            if 0 <= step < 8:
                gens.append(gdn_C(l, tb, step, g0))
                wts.append(GDN_W[0])
            if 0 <= step + 1 < 8:
                gens.append(gdn_B(l, tb, step + 1, g0))
                wts.append(GDN_W[1])
            if 0 <= step + 2 < 8:
                gens.append(gdn_A(l, tb, step + 2, g0))
                wts.append(GDN_W[2])
            run_interleaved(gens, wts)

    bar_n = [0]

    def emit_barrier():
        k = bar_n[0]
        bar_n[0] += 1
        P.mark_segment()
        P.op("act", lambda e: e.activation(out=bscr[:, 0:1], in_=cact[:, 0:1], func=AF.Identity), reads=["cact"], writes=[f"bar{k}_act", "bscr0"])
        P.op("dve", lambda e: e.tensor_copy(out=bscr[:, 1:2], in_=cact[:, 0:1]), reads=["cact"], writes=[f"bar{k}_dve", "bscr1"])
        P.op("pool", lambda e: e.tensor_copy(out=bscr[:, 2:3], in_=cact[:, 0:1]), reads=["cact"], writes=[f"bar{k}_pool", "bscr2"])
        P.op("pe", lambda e: e.matmul(ps[7][:, 0:8], cm(C_ONES), cact[:, 0:8], start=True, stop=True), reads=["cact", "cst"], writes=["ps7", f"bar{k}_pe"])
        allb = [f"bar{k}_{x}" for x in ("act", "dve", "pool", "pe")]
        P.op("act", lambda e: e.activation(out=bscr[:, 3:4], in_=cact[:, 0:1], func=AF.Identity), reads=allb + ["cact"], writes=["bscr3"])
        P.op("dve", lambda e: e.tensor_copy(out=bscr[:, 4:5], in_=ps[7][:, 0:1]), reads=allb, writes=["bscr4", "ps7"])
        P.op("pool", lambda e: e.tensor_copy(out=bscr[:, 5:6], in_=cact[:, 0:1]), reads=allb + ["cact"], writes=["bscr5"])
        P.op("pe", lambda e: e.matmul(ps[7][:, 0:8], cm(C_ONES), cact[:, 0:8], start=True, stop=True), reads=allb + ["cact", "cst"], writes=["ps7"])
        P.op("sp", lambda e: e.dma_start(out=bscr[:, 6:8], in_=spm_d[:, 0:2]), reads=allb, writes=["bscr6"], dma=True)
        P.mark_segment()

    emit_barrier()
    drain(gen_layer_mod(0))
    n_a = min(nlayers, 2)
    for l in range(n_a):
        if l == 0 and nlayers > 1:
            aux[0] = gen_layer_mod(1)
        if l == 1 and nlayers > 2:
            gl_ = [gen_kv_mod(), gen_layer_mod(2)]
            if nlayers > 3:
                gl_.append(gen_layer_mod(3))
            aux[0] = chain(*gl_)
        emit_gdn_layer(l)
        if aux[0] is not None:
            drain(aux[0])
            aux[0] = None
    emit_barrier()
    gstack.close()
    sb = sb_keep

    ost2 = [sb(f"Eo{i}", [128, 512]) for i in range(2)]
    if nlayers > 2:
        KT = sb("KT", [128, 8 * T], BF16)
        Vt = sb("Vt", [128, 16 * D], BF16)
        qn = sb("qn", [128, 8 * 512], BF16)
        Ebuf = ost2
        SPR = [[sb(f"SPR{i}{u}", [128, 512], F32R) for u in range(2)] for i in range(2)]
        dbf = [[sb(f"dbf{i}{u}", [128, 512]) for u in range(2)] for i in range(2)]
        Wb = [sb(f"Wb{i}", [128, 512], BF16) for i in range(2)]
        rn2 = tmpn[1]
        cstr = sb("cstr", [128, 256], F32R)
        P.op("dve", lambda e: e.tensor_copy(out=cstr[:, 0:128], in_=cm(C_TRIL_S)), reads=["cst"], writes=["cstr"])
        P.op("dve", lambda e: e.tensor_copy(out=cstr[:, 128:256], in_=cm(C_TRIU_I)), reads=["cst"], writes=["cstr"])

        def emit_headnorm(psb, gain_col, extra_bias, dst, dst_tok):
            tm = tmpn[0]
            P.op("act", lambda e: e.copy(out=tm, in_=psb), reads=[psb_tok[0]], writes=["tmpn0"])
            P.op("act", lambda e: e.activation(out=sqb[:, 0:512], in_=tm, func=AF.Square), reads=["tmpn0"], writes=["sqb0"])
            P.op("pe", lambda e: e.matmul(ps[4], cm(C_BDONES, True), sqb[:, 0:512], start=True, stop=True), reads=["sqb0", "cstb"], writes=["ps4"])
            P.op("act", lambda e: e.activation(out=rn2, in_=ps[4], func=AF.Ln, bias=EPS, scale=1.0 / 64), reads=["ps4"], writes=["tmpn1"])
            P.op("act", lambda e: e.activation(out=rn2, in_=rn2, func=AF.Exp, scale=-0.5, bias=extra_bias), reads=["tmpn1"], writes=["tmpn1"])
            P.op("dve", lambda e: e.scalar_tensor_tensor(out=dst, in0=tm, scalar=spm[:, gain_col:gain_col + 1], in1=rn2,
                                                         op0=ALU.mult, op1=ALU.mult), reads=["tmpn0", "spm", "tmpn1"], writes=[dst_tok])

        psb_tok = [None]

        def emit_kv():
            for tb in range(NB):
                emit_norm_block(tb, 4, 96)
                for piece in range(2):
                    P.op("pool", lambda e, piece=piece: e.dma_start(out=v3(wq[piece], 8), in_=wview(w_kv_d, piece * 512, 512)),
                         writes=[f"wq{piece}"], dma=True)
                    for fcl in range(4):
                        fc = piece * 4 + fcl
                        bk = fc % 4
                        for c in range(8):
                            P.op("pe", lambda e, piece=piece, fcl=fcl, c=c, bk=bk: e.matmul(
                                ps[bk], wq[piece][:, c * 512 + fcl * 128:c * 512 + (fcl + 1) * 128], hT[:, c * 512:(c + 1) * 512],
                                start=(c == 0), stop=(c == 7)), reads=[f"wq{piece}", "hT"], writes=[f"ps{bk}"])
                        psb_tok[0] = f"ps{bk}"
                        emit_headnorm(ps[bk], SP_KG, 0.0, KT[:, fc * T + tb * 512:fc * T + (tb + 1) * 512], "KT")
                for piece in range(2):
                    P.op("pool", lambda e, piece=piece: e.dma_start(out=v3(wq[piece], 8), in_=wview(w_kv_d, D + piece * 512, 512)),
                         writes=[f"wq{piece}"], dma=True)
                    for n in range(4):
                        bk = n % 4
                        tile = tb * 4 + n
                        for c in range(8):
                            P.op("pe", lambda e, piece=piece, n=n, c=c, bk=bk: e.matmul(
                                ps[bk], hT[:, c * 512 + n * 128:c * 512 + (n + 1) * 128], wq[piece][:, c * 512:(c + 1) * 512],
                                start=(c == 0), stop=(c == 7)), reads=[f"wq{piece}", "hT"], writes=[f"ps{bk}"])
                        dst = Vt[:, tile * D + piece * 512:tile * D + (piece + 1) * 512]
                        if n % 2 == 0:
                            P.op("act", lambda e, dst=dst, bk=bk: e.copy(out=dst, in_=ps[bk]), reads=[f"ps{bk}"], writes=["Vt"])
                        else:
                            P.op("dve", lambda e, dst=dst, bk=bk: e.tensor_copy(out=dst, in_=ps[bk]), reads=[f"ps{bk}"], writes=["Vt"])

        def sb_head(g, ch, hh, s_):
            h = 2 * ch + hh
            base = hh * 64
            bA, bB = (0, 1) if s_ == 0 else (2, 3)
            Eb, wbu = Ebuf[s_], Wb[s_]
            chunks = list(range(4 * g + 3, -1, -1))

            def geom(i):
                r0 = max(i - 4 * g, 0)
                return r0 * 128, (i >= 4 * g)

            def warm():
                for _ in range(SB_WARM):
                    P.op("pe", lambda e: e.matmul(ps[5][:, 0:128], cm(C_ONES, True), cm(C_IDENT, True), start=True, stop=True),
                         reads=["cstb"], writes=["ps5"])

            def front(k):
                i = chunks[k]
                c0, diag = geom(i)
                u = k % 2
                Sr = SPR[s_][u]
                P.op("pe", lambda e: e.matmul(
                    ps[bA][:, c0:512], KT[base:base + 64, ch * T + i * 128:ch * T + (i + 1) * 128],
                    qn[base:base + 64, ch * 512 + c0:ch * 512 + 512], start=True, stop=True),
                    reads=["KT", f"qn{ch}"], writes=[f"ps{bA}"])
                yield
                P.op("act", lambda e: e.activation(out=Eb[:, c0:512], in_=ps[bA][:, c0:512], func=AF.Exp),
                     reads=[f"ps{bA}"], writes=[f"E{s_}"])
                yield
                P.op("act", lambda e: e.activation(out=Sr[:, c0:512], in_=Eb[:, c0:512], func=AF.Ln, bias=1.0, scale=1.0),
                     reads=[f"E{s_}"], writes=[f"SPR{s_}{u}"])
                yield
                if diag:
                    P.op("pool", lambda e: e.tensor_tensor(out=Sr[:, c0:c0 + 128], in0=Sr[:, c0:c0 + 128].bitcast(F32),
                                                           in1=cm(C_TRIU_S), op=ALU.mult),
                         reads=[f"SPR{s_}{u}", "cst"], writes=[f"SPR{s_}{u}"])
                    yield

            def back1(k):
                i = chunks[k]
                c0, diag = geom(i)
                u = k % 2
                Sr, dbu = SPR[s_][u], dbf[s_][u]
                P.op("pe", lambda e: e.matmul(
                    ps[bB][:, c0:512], cstr[:, 0:128], Sr[:, c0:512], start=(k == 0), stop=False, skip_group_check=True),
                    reads=[f"SPR{s_}{u}", "cstr"], writes=[f"ps{bB}"])
                yield
                P.op("dve", lambda e: e.tensor_tensor(
                    out=dbu[:, c0:512], in0=ps[bA][:, c0:512], in1=Sr[:, c0:512].bitcast(F32), op=ALU.subtract),
                    reads=[f"ps{bA}", f"SPR{s_}{u}"], writes=[f"dbf{s_}{u}"])
                yield

            def back2(k):
                i = chunks[k]
                c0, diag = geom(i)
                u = k % 2
                Sr, dbu = SPR[s_][u], dbf[s_][u]
                P.op("dve", lambda e: e.tensor_tensor(
                    out=dbu[:, c0:512], in0=dbu[:, c0:512], in1=ps[bB][:, c0:512], op=ALU.subtract),
                    reads=[f"ps{bB}", f"dbf{s_}{u}"], writes=[f"dbf{s_}{u}"])
                yield
                if i > 0:
                    warm()
                    P.op("pe", lambda e: e.matmul(
                        ps[bB][:, c0:512], cstr[:, 128:256], Sr[:, c0:512], start=False, stop=False, skip_group_check=True),
                        reads=[f"SPR{s_}{u}", "cstr"], writes=[f"ps{bB}"])
                    yield
                P.op("act", lambda e: e.activation(out=wbu[:, c0:512], in_=dbu[:, c0:512], func=AF.Exp),
                     reads=[f"dbf{s_}{u}"], writes=[f"Wb{s_}"])
                yield
                if diag:
                    P.op("pool", lambda e: e.tensor_tensor(out=wbu[:, c0:c0 + 128], in0=wbu[:, c0:c0 + 128],
                                                           in1=cm(C_TRIU_S, True), op=ALU.mult),
                         reads=[f"Wb{s_}", "cstb"], writes=[f"Wb{s_}"])
                    yield
                P.op("pe", lambda e: e.matmul(
                    ps[6][base:base + 64, c0:512], Vt[:, i * D + h * 64:i * D + (h + 1) * 64], wbu[:, c0:512],
                    start=False, stop=False, skip_group_check=True, tile_position=(0, base)),
                    reads=["Vt", f"Wb{s_}"], writes=["ps6"])
                yield

            yield from front(0)
            for k in range(len(chunks)):
                yield from back1(k)
                if k + 1 < len(chunks):
                    yield from front(k + 1)
                yield from back2(k)

        def run_interleaved(gens):
            gens = list(gens)
            while gens:
                for g_ in list(gens):
                    try:
                        next(g_)
                    except StopIteration:
                        gens.remove(g_)

        def sb_projc(g, l2, fc):
            half, fcl = divmod(fc, 4)
            if fcl == 0:
                P.op("pool", lambda e: e.dma_start(out=v3(wq[0], 8), in_=wview(w_inb_d[l2], half * 512, 512)),
                     writes=["wq0"], dma=True)
                yield
                P.op("pool", lambda e: e.dma_start(out=v3(wq[1], 8), in_=wview(w_inb_d[l2], D + half * 512, 512)),
                     writes=["wq1"], dma=True)
                yield
            for c in range(8):
                P.op("pe", lambda e, c=c: e.matmul(
                    ps[4], wq[0][:, c * 512 + fcl * 128:c * 512 + (fcl + 1) * 128], hT[:, c * 512:(c + 1) * 512],
                    start=(c == 0), stop=(c == 7)), reads=["wq0", "hT"], writes=["ps4"])
                yield
            tm = tmpn[0]
            P.op("act", lambda e: e.copy(out=tm, in_=ps[4]), reads=["ps4"], writes=["tmpn0"])
            yield
            for c in range(8):
                P.op("pe", lambda e, c=c: e.matmul(
                    ps[4], wq[1][:, c * 512 + fcl * 128:c * 512 + (fcl + 1) * 128], hT[:, c * 512:(c + 1) * 512],
                    start=(c == 0), stop=(c == 7)), reads=["wq1", "hT"], writes=["ps4"])
                yield
            P.op("act", lambda e: e.activation(out=sqb[:, 0:512], in_=tm, func=AF.Square), reads=["tmpn0"], writes=["sqb0"])
            yield
            P.op("pe", lambda e: e.matmul(ps[7], cm(C_BDONES, True), sqb[:, 0:512], start=True, stop=True), reads=["sqb0", "cstb"], writes=["ps7"])
            yield
            P.op("act", lambda e: e.activation(out=rn2, in_=ps[7], func=AF.Ln, bias=EPS, scale=1.0 / 64), reads=["ps7"], writes=["tmpn1"])
            yield
            P.op("act", lambda e: e.activation(out=rn2, in_=rn2, func=AF.Exp, scale=-0.5, bias=float(np.log(0.125))), reads=["tmpn1"], writes=["tmpn1"])
            yield
            P.op("dve", lambda e: e.scalar_tensor_tensor(out=qn[:, fc * 512:(fc + 1) * 512], in0=tm, scalar=spm[:, SP_QG + l2:SP_QG + l2 + 1], in1=rn2,
                                                         op0=ALU.mult, op1=ALU.mult), reads=["tmpn0", "spm", "tmpn1"], writes=[f"qn{fc}"])
            yield
            P.op("act", lambda e: e.activation(out=ogT[:, fc * 512:(fc + 1) * 512], in_=ps[4], func=AF.Silu),
                 reads=["ps4"], writes=[f"og{fc}"])
            yield

        def emit_sb_layer(l2):
            L = 2 + l2
            for g in range(NB):
                emit_norm_block(g, L, 24 * L)
                drain(sb_projc(g, l2, 0))
                for ch in range(8):
                    P.op("dve", lambda e: e.memset(ps[6], 0.0), writes=["ps6"])
                    gens = [sb_head(g, ch, 0, 0), sb_head(g, ch, 1, 1)]
                    if ch + 1 < 8:
                        gens.append(sb_projc(g, l2, ch + 1))
                    run_interleaved(gens)
                    P.op("dve", lambda e, ch=ch: e.tensor_tensor(out=ogT[:, ch * 512:(ch + 1) * 512], in0=ps[6], in1=ogT[:, ch * 512:(ch + 1) * 512],
                                                                 op=ALU.mult), reads=["ps6", f"og{ch}"], writes=[f"og{ch}"])
                emit_outproj(w_outb_d[l2], g, L, og_toks=[f"og{i}" for i in range(8)])

        emit_kv()
        for l2 in range(nlayers - 2):
            emit_sb_layer(l2)

    outs = []
    for n in range(16):
        tb = n // 4
        for half in range(2):
            bk = (2 * n + half) % 4
            oi = (2 * n + half) % 2
            for c4 in range(4):
                c = half * 4 + c4
                P.op("pe", lambda e, n=n, c=c, c4=c4, bk=bk: e.transpose(
                    ps[bk][:, c4 * 128:(c4 + 1) * 128], xT[:, c * T + n * 128:c * T + (n + 1) * 128], cm(C_IDENT)),
                    reads=[f"xT{tb}", "cst"], writes=[f"ps{bk}"])
            if half == 0:
                P.op("act", lambda e, oi=oi, bk=bk: e.copy(out=ost2[oi], in_=ps[bk]), reads=[f"ps{bk}"], writes=[f"E{oi}"])
            else:
                P.op("dve", lambda e, oi=oi, bk=bk: e.tensor_copy(out=ost2[oi], in_=ps[bk]), reads=[f"ps{bk}"], writes=[f"E{oi}"])
            outs.append(P.op("sp", lambda e, n=n, half=half, oi=oi: e.dma_start(
                out=out_d[n * 128:(n + 1) * 128, half * 512:(half + 1) * 512], in_=ost2[oi]),
                reads=[f"E{oi}"], dma=True))
    P.finalize(final_wait_ops=outs)
    return nc


def _prep_inputs(inp):
    inp = {k: np.asarray(v) for k, v in inp.items()}
    w_in_a = inp["w_in_a"].astype(np.float32, copy=False)
    qkvz = w_in_a[:, :, :4096].reshape(2, D, 4, 8, 128).transpose(0, 1, 3, 2, 4).reshape(2, D, 4096)
    shared = {
        "cst": _consts(),
        "w_ada": np.ascontiguousarray(inp["w_ada"], np.float32),
        "w_inp": np.ascontiguousarray(qkvz),
        "w_ba": np.ascontiguousarray(w_in_a[:, :, 4096:4112]),
        "w_out_a": np.ascontiguousarray(inp["w_out_a"], np.float32),
        "w_ada_kv": np.ascontiguousarray(inp["w_ada_kv"], np.float32),
        "w_kv": np.ascontiguousarray(inp["w_kv"], np.float32),
        "w_in_b": np.ascontiguousarray(inp["w_in_b"], np.float32),
        "w_out_b": np.ascontiguousarray(inp["w_out_b"], np.float32),
    }
    in_maps = []
    for b in range(8):
        m = dict(shared)
        m["x"] = np.ascontiguousarray(inp["x"][b], np.float32)
        m["spm"] = _small_params(inp, b)
        in_maps.append(m)
    return in_maps


def kernel(**inputs):
    in_maps = _prep_inputs(inputs)
    nc = build(4)
    res = run_bass_kernel_spmd(nc, in_maps, core_ids=list(range(8)))
    return np.stack([np.asarray(r["out"], np.float32) for r in res.results], axis=0)
```

```python
import numpy as np
import concourse.bass as bass
import concourse.mybir as mybir
from concourse.bass_utils import run_bass_kernel_spmd
from contextlib import ExitStack

F32 = mybir.dt.float32
BF16 = mybir.dt.bfloat16
F32R = mybir.dt.float32r
AF = mybir.ActivationFunctionType
ALU = mybir.AluOpType

T = 2048
D = 1024
NB = 4
EPS = 1e-6
AUX_N = 13
SCHEDULE = True
PRIO_BLEVEL = False
PE_F32C = 2.0
XLAT = 0.3
DVE_C0 = 0.22
DVE_R = 1050.0
POOL_C0 = 0.2
POOL_R = 560.0
PE_F32RC = 2.0
DMA_R = 300e3
SB_WARM = 0
GDN_W = (1, 1, 1)


class Tok:
    __slots__ = ("name", "w", "r")

    def __init__(self, name=""):
        self.name = name
        self.w = None
        self.r = []


class Op:
    __slots__ = ("eng", "fn", "deps", "idx", "need_inc", "ev", "is_dma", "dsem", "dval", "seg", "gidx", "cost")

    def __init__(self, eng, fn):
        self.eng = eng
        self.fn = fn
        self.deps = []
        self.idx = None
        self.need_inc = False
        self.ev = None
        self.is_dma = False


class Prog:
    ENG = ("pe", "act", "dve", "pool", "sp")
    SEM_LIMIT = 30000
    NDMA = 16
    NEAR = 6

    def __init__(self, nc):
        self.nc = nc
        self.q = {e: [] for e in self.ENG}
        self.toks = {}
        self.seg = 0
        self.nops = 0

    def mark_segment(self):
        self.seg += 1

    COST = {"pe": 0.25, "act": 0.55, "dve": 0.6, "pool": 0.45, "sp": 0.15}

    class _Probe:
        def __init__(self):
            self.rec = None

        def __getattr__(self, name):
            def f(*a, **kw):
                self.rec = (name, a, kw)
                return self
            return f

    def estimate(self, o):
        try:
            pr = Prog._Probe()
            o.fn(pr)
            name, a, kw = pr.rec
            out = kw.get("out", a[0] if a else None)
            n = 1
            for d in out.shape[1:]:
                n *= int(d)
            if o.is_dma:
                return 2.0 + out.shape[0] * n * 4 / DMA_R
            if o.eng == "pe":
                if name == "transpose":
                    return 0.12
                lhs = kw.get("lhsT", a[1] if len(a) > 1 else None)
                cyc = {F32: PE_F32C, F32R: PE_F32RC}.get(lhs.dtype, 1.0) * max(n, 64)
                return 0.05 + cyc / 1700.0
            if o.eng == "act":
                return 0.2 + n / 1400.0
            if o.eng == "dve":
                return DVE_C0 + n / DVE_R
            if o.eng == "pool":
                return POOL_C0 + n / POOL_R
        except Exception:
            pass
        return self.COST[o.eng]

    def schedule(self):
        import heapq
        allops = []
        for e in self.ENG:
            allops.extend(self.q[e])
        allops.sort(key=lambda o: o.gidx)
        nseg = self.seg + 1
        bysegs = [[] for _ in range(nseg)]
        for o in allops:
            bysegs[o.seg].append(o)
        newq = {e: [] for e in self.ENG}
        for ops in bysegs:
            if not ops:
                continue
            inseg = set(id(o) for o in ops)
            succ = {}
            indeg = {}
            for o in ops:
                n = 0
                for d in o.deps:
                    if id(d) in inseg:
                        succ.setdefault(id(d), []).append(o)
                        n += 1
                indeg[id(o)] = n
            cost_ = {}
            for o in ops:
                cost_[id(o)] = o.cost if o.cost is not None else self.estimate(o)
            blev = {}
            for o in reversed(ops):
                b_ = 0.0
                for s_ in succ.get(id(o), ()):
                    v_ = blev[id(s_)] + (0.05 if s_.eng == o.eng else XLAT)
                    if v_ > b_:
                        b_ = v_
                blev[id(o)] = b_ + (cost_[id(o)] if not o.is_dma else cost_[id(o)])
            finish = {}
            free = {e: 0.0 for e in self.ENG}
            heaps = {e: [] for e in self.ENG}
            ready_t = {}
            for o in ops:
                if indeg[id(o)] == 0:
                    ready_t[id(o)] = 0.0
                    heapq.heappush(heaps[o.eng], (0.0, o.gidx, o))
            left = len(ops)
            while left:
                best = None
                for e in self.ENG:
                    h = heaps[e]
                    if not h:
                        continue
                    st = max(h[0][0], free[e])
                    if best is None or (st, h[0][1]) < (best[0], best[1]):
                        best = (st, h[0][1], e)
                st, _, e = best
                h = heaps[e]
                cand = []
                while h and h[0][0] <= st:
                    cand.append(heapq.heappop(h))
                if PRIO_BLEVEL:
                    cand.sort(key=lambda c: (-blev[id(c[2])], c[1]))
                else:
                    cand.sort(key=lambda c: c[1])
                pick = cand[0]
                for c in cand[1:]:
                    heapq.heappush(h, c)
                o = pick[2]
                c_ = cost_[id(o)]
                if o.is_dma:
                    free[e] = st + (0.15 if e == "sp" else 0.4)
                    fin = st + c_
                else:
                    free[e] = st + c_
                    fin = free[e]
                finish[id(o)] = fin
                newq[e].append(o)
                left -= 1
                for s_ in succ.get(id(o), ()):
                    indeg[id(s_)] -= 1
                    r = max(ready_t.get(id(s_), 0.0), fin + (0.05 if s_.eng == o.eng else XLAT))
                    ready_t[id(s_)] = r
                    if indeg[id(s_)] == 0:
                        heapq.heappush(heaps[s_.eng], (r, s_.gidx, s_))
        for e in self.ENG:
            assert len(newq[e]) == len(self.q[e])
            self.q[e] = newq[e]
            for i, o in enumerate(newq[e]):
                o.idx = i

    def tk(self, name):
        t = self.toks.get(name)
        if t is None:
            t = Tok(name)
            self.toks[name] = t
        return t

    def op(self, eng, fn, reads=(), writes=(), dma=False, cost=None):
        o = Op(eng, fn)
        o.is_dma = dma
        o.seg = self.seg
        o.gidx = self.nops
        o.cost = cost
        self.nops += 1
        o.idx = len(self.q[eng])
        deps = set()
        rd, wr = [], []
        for t in reads:
            if isinstance(t, str) and t.startswith("ps") and t[2].isdigit():
                wr.append(t[:3])
            else:
                rd.append(t)
        for t in writes:
            if isinstance(t, str) and t.startswith("ps") and t[2].isdigit():
                wr.append(t[:3])
            else:
                wr.append(t)
        reads = [self.tk(t) if isinstance(t, str) else t for t in rd]
        writes = [self.tk(t) if isinstance(t, str) else t for t in dict.fromkeys(wr)]
        for t in reads:
            if t.w is not None:
                deps.add(t.w)
        for t in writes:
            if t.w is not None:
                deps.add(t.w)
            for r in t.r:
                deps.add(r)
        deps.discard(o)
        o.deps = list(deps)
        for t in reads:
            t.r.append(o)
        for t in writes:
            t.w = o
            t.r = []
        self.q[eng].append(o)
        return o

    def finalize(self, final_wait_ops=()):
        nc = self.nc
        if SCHEDULE:
            self.schedule()
        waits = {}
        for e in self.ENG:
            seen = {}
            for o in self.q[e]:
                best = {}
                res = []
                for d in o.deps:
                    if d.is_dma:
                        res.append(d)
                        continue
                    if d.eng == o.eng:
                        if e == "pe" or o.is_dma:
                            if not o.is_dma:
                                continue
                        if (not o.is_dma) and o.idx - d.idx > self.NEAR:
                            continue
                    if d.eng not in best or best[d.eng].idx < d.idx:
                        best[d.eng] = d
                for d in best.values():
                    if seen.get(d.eng, -1) >= d.idx:
                        continue
                    seen[d.eng] = d.idx
                    res.append(d)
                waits[o] = res
                for d in res:
                    if not d.is_dma:
                        d.need_inc = True
        stack = ExitStack()
        for e in self.ENG:
            n = sum(1 for o in self.q[e] if o.need_inc and not o.is_dma)
            k = max(1, (n + self.SEM_LIMIT - 1) // self.SEM_LIMIT)
            sems = [stack.enter_context(nc.semaphore(f"s_{e}_{i}")) for i in range(k)]
            c = 0
            for o in self.q[e]:
                if o.need_inc and not o.is_dma:
                    o.ev = (sems[c // self.SEM_LIMIT], c % self.SEM_LIMIT + 1)
                    c += 1
        dsems = {e: [stack.enter_context(nc.semaphore(f"s_dma_{e}_{i}")) for i in range(self.NDMA)]
                 for e in ("sp", "pool")}
        for e in ("sp", "pool"):
            di = 0
            for o in self.q[e]:
                if o.is_dma:
                    o.dsem = dsems[e][di % self.NDMA]
                    o.dval = 16 * (di // self.NDMA + 1)
                    o.ev = (o.dsem, o.dval)
                    di += 1
        final = list(final_wait_ops)
        with nc.Block() as block:
            def run(engname, eng):
                for o in self.q[engname]:
                    for d in waits[o]:
                        eng.wait_ge(d.ev[0], d.ev[1])
                    if o.is_dma and o.dval > 16:
                        eng.wait_ge(o.dsem, o.dval - 16)
                    ins = o.fn(eng)
                    if o.is_dma:
                        ins.then_inc(o.dsem, 16)
                    elif o.need_inc:
                        ins.then_inc(o.ev[0], 1)
                if engname == "sp":
                    for o in final:
                        eng.wait_ge(o.ev[0], o.ev[1])

            @block.tensor
            def _(t):
                run("pe", t)

            @block.scalar
            def _(t):
                run("act", t)

            @block.vector
            def _(t):
                run("dve", t)

            @block.gpsimd
            def _(t):
                run("pool", t)

            @block.sync
            def _(t):
                run("sp", t)
        stack.close()


C_IDENT, C_ONES, C_TRIU_I, C_TRIL_S, C_TRIU_S, C_BDTRIL_S, C_OFF, C_BDONES = range(8)
NCST = 8


def _consts():
    p = np.arange(128)[:, None]
    f = np.arange(128)[None, :]
    m = np.zeros((128, NCST, 128), np.float32)
    m[:, C_IDENT] = (p == f)
    m[:, C_ONES] = 1.0
    m[:, C_TRIU_I] = (p <= f)
    m[:, C_TRIL_S] = (f < p)
    m[:, C_TRIU_S] = (p < f)
    m[:, C_BDTRIL_S] = (f < p) & ((p // 64) == (f // 64))
    m[:, C_OFF] = (p >= 64) & (f < 64)
    m[:, C_BDONES] = ((p // 64) == (f // 64))
    return m.reshape(128, NCST * 128)


def _fm(v):
    v = np.asarray(v, np.float32).reshape(-1, 128)
    return np.ascontiguousarray(v.T)


SP_C = 0
SP_LAYER = 8
SP_KV = 136
SP_GDN = 160
SP_KG = 386
SP_QG = 387
NSP = 389


def _small_params(inp, b):
    sp = np.zeros((128, NSP), np.float32)
    sp[:, 0:8] = _fm(inp["c"][b])
    for l in range(4):
        o = SP_LAYER + 32 * l
        sp[:, o:o + 8] = _fm(inp["norm_g"][l])
        sp[:, o + 8:o + 32] = _fm(inp["b_ada"][l])
    sp[:, SP_KV:SP_KV + 8] = _fm(inp["kv_norm_g"])
    sp[:, SP_KV + 8:SP_KV + 24] = _fm(inp["b_ada_kv"])
    for l in range(2):
        o = SP_GDN + 113 * l
        cw = np.asarray(inp["conv_w_a"][l], np.float32)
        for j in range(4):
            sp[:, o + 24 * j:o + 24 * j + 24] = _fm(cw[j])
        sp[:, o + 96] = np.asarray(inp["o_gain_a"][l], np.float32)
        sp[:, o + 97:o + 105] = np.asarray(inp["a_log_a"][l], np.float32)[None, :]
        sp[:, o + 105:o + 113] = np.asarray(inp["dt_bias_a"][l], np.float32)[None, :]
    sp[:, SP_KG] = np.tile(np.asarray(inp["k_gain"], np.float32), 2)
    for l in range(2):
        sp[:, SP_QG + l] = np.tile(np.asarray(inp["q_gain_b"][l], np.float32), 2)
    return sp


def build(nlayers=4):
    nc = bass.Bass("TRN2", target_bir_lowering=False)
    dram = lambda n, s, k="ExternalInput": nc.dram_tensor(n, s, F32, kind=k).ap()
    x_d = dram("x", [T, D])
    spm_d = dram("spm", [128, NSP])
    cst_d = dram("cst", [128, NCST * 128])
    w_ada_d = dram("w_ada", [4, D, 3 * D])
    w_inp_d = dram("w_inp", [2, D, 8 * 512])
    w_ba_d = dram("w_ba", [2, D, 16])
    w_outa_d = dram("w_out_a", [2, D, D])
    w_adakv_d = dram("w_ada_kv", [D, 2 * D])
    w_kv_d = dram("w_kv", [D, 2 * D])
    w_inb_d = dram("w_in_b", [2, D, 2 * D])
    w_outb_d = dram("w_out_b", [2, D, D])
    out_d = dram("out", [T, D], "ExternalOutput")

    def wview(ap2d, c0, n):
        return ap2d[:, c0:c0 + n].rearrange("(c p) n -> p c n", p=128)

    sb = lambda n, s, d=F32: nc.alloc_sbuf_tensor("sb_" + n, s, d).ap()
    P = Prog(nc)

    def v3(ap, a):
        return ap.rearrange("p (a b) -> p a b", a=a)

    xT = sb("xT", [128, 8 * T])
    cst = sb("cst", [128, NCST * 128])
    cstb = sb("cstb", [128, NCST * 128], BF16)
    spm = sb("spm", [128, NSP])
    cact = sb("cact", [128, 8])
    modT = sb("modT", [128, 5 * 24])
    Acol = sb("Acol", [128, 5 * 8])
    hT = sb("hT", [128, 8 * 512], BF16)
    ogT = sb("ogT", [128, 8 * 512], BF16)
    sqb = sb("sqb", [128, 2 * 512], BF16)
    rstd = sb("rstd", [128, 512])
    bscr = sb("bscr", [128, 8])
    tmpn = [sb(f"tmpn{i}", [128, 512]) for i in range(2)]
    wq = [sb(f"wq{i}", [128, 8 * 512], BF16) for i in range(2)]
    ps = [nc.alloc_psum_tensor(f"ps{i}", [128, 512], F32).ap() for i in range(8)]
    gstack = ExitStack()
    sbp = lambda n, s, d=F32: gstack.enter_context(nc.sbuf_tensor("sb_" + n, s, d))[:]
    sb_keep = sb
    lnb = sbp("lnb", [128, 512])
    mrow = sbp("mrow", [1, 256])
    wa = [sbp("wa0", [128, 8 * 256])] * 2
    LU = sbp("LU", [128, 2048])
    xld = [LU[:, i * 1024:(i + 1) * 1024] for i in range(2)]

    def cm(i, bf=False):
        return (cstb if bf else cst)[:, i * 128:(i + 1) * 128]

    def xsl(c, tb):
        return xT[:, c * T + tb * 512:c * T + (tb + 1) * 512]

    P.op("sp", lambda e: e.dma_start(out=cst, in_=cst_d), writes=["cst"], dma=True)
    P.op("sp", lambda e: e.dma_start(out=spm, in_=spm_d), writes=["spm"], dma=True)
    P.op("pool", lambda e: e.tensor_copy(out=cstb, in_=cst), reads=["cst"], writes=["cstb"])
    P.op("act", lambda e: e.activation(out=cact, in_=spm[:, 0:8], func=AF.Silu), reads=["spm"], writes=["cact"])

    wa_i = [0]

    def gen_mod(wd2, ncols, bias0, dst0, slot, a_slot, g_col0, scale_col0):
        buf = wa[0]
        for piece in range(ncols // 256):
            P.op("sp", lambda e, piece=piece: e.dma_start(out=v3(buf, 8), in_=wview(wd2, piece * 256, 256)),
                 writes=["wa0"], dma=True)
            yield
            for c in range(8):
                P.op("pe", lambda e, c=c: e.matmul(ps[2][0:1, 0:256], cact[:, c:c + 1], buf[:, c * 256:(c + 1) * 256],
                                                   start=(c == 0), stop=(c == 7)), reads=["wa0", "cact"], writes=["ps2"])
                yield
            P.op("act", lambda e: e.copy(out=mrow, in_=ps[2][0:1, 0:256]), reads=["ps2"], writes=["mrow"])
            yield
            for fc in range(2):
                P.op("pe", lambda e, fc=fc: e.matmul(ps[2][:, 256 + fc:257 + fc], mrow[0:1, fc * 128:(fc + 1) * 128],
                                                     cst[0:1, C_ONES * 128:C_ONES * 128 + 1], start=True, stop=True),
                     reads=["mrow", "cst"], writes=["ps2"])
                yield
            d0 = dst0 + piece * 2
            b0 = bias0 + piece * 2
            P.op("dve", lambda e, d0=d0, b0=b0: e.tensor_tensor(out=modT[:, d0:d0 + 2], in0=ps[2][:, 256:258],
                                                                in1=spm[:, b0:b0 + 2], op=ALU.add),
                 reads=["ps2", "spm"], writes=[f"mod{slot}"])
            yield
        P.op("dve", lambda e: e.scalar_tensor_tensor(out=Acol[:, 8 * a_slot:8 * a_slot + 8], in0=modT[:, scale_col0:scale_col0 + 8],
                                                     scalar=1.0, in1=spm[:, g_col0:g_col0 + 8], op0=ALU.add, op1=ALU.mult),
             reads=[f"mod{slot}", "spm"], writes=[f"A{a_slot}"])
        yield

    def gen_layer_mod(l):
        return gen_mod(w_ada_d[l], 3 * D, SP_LAYER + 32 * l + 8, 24 * l, l, l, SP_LAYER + 32 * l, 24 * l + 8)

    def gen_kv_mod():
        return gen_mod(w_adakv_d, 2 * D, SP_KV + 8, 96, 4, 4, SP_KV, 104)

    def drain(gen):
        for _ in gen:
            pass

    def take(gen, n):
        for _ in range(n):
            try:
                next(gen)
            except StopIteration:
                return
            yield

    def chain(*gens):
        for g_ in gens:
            yield from g_

    for n in range(16):
        bi = n % 2
        tb = n // 4
        P.op("sp", lambda e, n=n, bi=bi: e.dma_start(out=xld[bi], in_=x_d[n * 128:(n + 1) * 128, :]),
             writes=[f"xld{bi}"], dma=True)
        for half in range(2):
            bk = (2 * n + half) % 4
            for c4 in range(4):
                c = half * 4 + c4
                P.op("pe", lambda e, bi=bi, c=c, c4=c4, bk=bk: e.transpose(
                    ps[bk][:, c4 * 128:(c4 + 1) * 128], xld[bi][:, c * 128:(c + 1) * 128], cm(C_IDENT)),
                    reads=[f"xld{bi}", "cst"], writes=[f"ps{bk}"])
            dst = v3(xT[:, half * 4 * T:(half * 4 + 4) * T], 4)[:, :, n * 128:(n + 1) * 128]
            src = v3(ps[bk], 4)
            if half == 0:
                P.op("act", lambda e, dst=dst, src=src: e.copy(out=dst, in_=src), reads=[f"ps{bk}"], writes=[f"xT{tb}"])
            else:
                P.op("dve", lambda e, dst=dst, src=src: e.tensor_copy(out=dst, in_=src), reads=[f"ps{bk}"], writes=[f"xT{tb}"])

    def emit_norm_block(tb, slot, shift0):
        for c in range(8):
            sq_ = sqb[:, (c % 2) * 512:(c % 2 + 1) * 512]
            P.op("act", lambda e, c=c, sq_=sq_: e.activation(out=sq_, in_=xsl(c, tb), func=AF.Square),
                 reads=[f"xT{tb}"], writes=[f"sqb{c % 2}"])
            P.op("pe", lambda e, c=c, sq_=sq_: e.matmul(ps[4], cm(C_ONES, True), sq_,
                                               start=(c == 0), stop=(c == 7)), reads=[f"sqb{c % 2}", "cstb"], writes=["ps4"])
        P.op("act", lambda e: e.activation(out=rstd, in_=ps[4], func=AF.Ln, bias=EPS, scale=1.0 / D), reads=["ps4"], writes=["rstd"])
        P.op("act", lambda e: e.activation(out=rstd, in_=rstd, func=AF.Exp, scale=-0.5), reads=["rstd"], writes=["rstd"])
        for c in range(8):
            tm = tmpn[c % 2]
            P.op("dve", lambda e, c=c, tm=tm: e.scalar_tensor_tensor(
                out=tm, in0=xsl(c, tb), scalar=Acol[:, 8 * slot + c:8 * slot + c + 1], in1=rstd,
                op0=ALU.mult, op1=ALU.mult), reads=[f"xT{tb}", f"A{slot}", "rstd"], writes=[f"tmpn{c % 2}"])
            P.op("act", lambda e, c=c, tm=tm: e.activation(
                out=hT[:, c * 512:(c + 1) * 512], in_=tm, func=AF.Identity,
                bias=modT[:, shift0 + c:shift0 + c + 1], scale=1.0),
                reads=[f"tmpn{c % 2}", f"mod{slot}"], writes=["hT"])

    def emit_outproj(wd2, tb, slot, og=None, og_toks=("ogT",)):
        og = ogT if og is None else og
        for half in range(2):
            P.op("pool", lambda e, half=half: e.dma_start(out=v3(wq[half], 8), in_=wview(wd2, half * 512, 512)),
                 writes=[f"wq{half}"], dma=True)
        for m in range(8):
            half, mm_ = divmod(m, 4)
            bk = (0, 1, 3, 4)[m % 4]
            for h in range(8):
                P.op("pe", lambda e, half=half, mm_=mm_, h=h, bk=bk: e.matmul(
                    ps[bk], wq[half][:, h * 512 + mm_ * 128:h * 512 + (mm_ + 1) * 128], og[:, h * 512:(h + 1) * 512],
                    start=(h == 0), stop=(h == 7)), reads=[f"wq{half}"] + list(og_toks), writes=[f"ps{bk}"])
            P.op("dve", lambda e, m=m, bk=bk: e.scalar_tensor_tensor(
                out=xsl(m, tb), in0=ps[bk], scalar=modT[:, 24 * slot + 16 + m:24 * slot + 17 + m], in1=xsl(m, tb),
                op0=ALU.mult, op1=ALU.add), reads=[f"ps{bk}", f"mod{slot}", f"xT{tb}"], writes=[f"xT{tb}"])

    sb = sbp
    wba = sb("wba", [128, 8 * 16], BF16)
    carry = sb("carry", [128, 8 * 3 * 4])
    pre = [sb(f"pre{j}", [128, 516]) for j in range(3)]
    acc = [sb(f"acc{j}", [128, 512]) for j in range(3)]
    zs = [sb(f"zs{i}", [128, 512], BF16) for i in range(3)]
    rn = sb("rn", [128, 512])
    rn3 = sb("rn3", [128, 512])
    sqc = sb("sqc", [128, 512], BF16)
    qTb = [sb(f"qTb{i}", [128, 512], BF16) for i in range(3)]
    kTb = [sb(f"kTb{i}", [128, 512], BF16) for i in range(3)]
    qdec = [sb(f"qdec{i}", [128, 512], BF16) for i in range(2)]
    kdec = [sb(f"kdec{i}", [128, 512], BF16) for i in range(3)]
    vb = [sb(f"vb{i}", [128, 512]) for i in range(3)]
    egbc = sb("egbc", [128, 512])
    E1 = sb("E1", [128, 512])
    E2 = sb("E2", [128, 512])
    t1 = sb("t1", [128, 512])
    t2 = sb("t2", [128, 512])
    Lb = [LU[:, i * 512:(i + 1) * 512] for i in range(2)]
    Ub = [LU[:, (2 + i) * 512:(3 + i) * 512] for i in range(2)]
    Yb = [sb(f"Yb{i}", [128, 512]) for i in range(2)]
    Offb = sb("Offb", [128, 512])
    Ybf = [sb(f"Ybf{i}", [128, 512], BF16) for i in range(2)]
    attnT = [sb(f"attnT{i}", [128, 512], BF16) for i in range(2)]
    rv = [sb(f"rv{i}", [128, 128], BF16) for i in range(2)]
    vnw = [sb(f"vnw{i}", [128, 128], BF16) for i in range(2)]
    Sst = sb("Sst", [128, 8 * 128])
    Sbf = sb("Sbf", [128, 8 * 128], BF16)
    ob = sb("ob", [128, 512])
    tkU = sb("tkU", [128, 32])
    tkG = sb("tkG", [128, 32])
    tkBeta = sb("tkBeta", [128, 32])
    tkGc = sb("tkGc", [128, 32])
    tkNbeg = sb("tkNbeg", [128, 32])
    tkGl = sb("tkGl", [128, 32])
    tkDks = sb("tkDks", [128, 32])
    tkEgl = sb("tkEgl", [128, 32])
    negA = sb("negA", [128, 8])
    sb = sb_keep

    def emit_gdn_layer(l):
        slot = l
        g0 = SP_GDN + 113 * l
        P.op("pool", lambda e: e.dma_start(out=v3(wba, 8), in_=w_ba_d[l].rearrange("(c p) n -> p c n", p=128)),
             writes=["wba"], dma=True)
        P.op("pool", lambda e: e.memset(carry, 0.0), writes=["carry"])
        P.op("pool", lambda e: e.memset(Sst, 0.0), writes=[f"S{i}" for i in range(8)])
        P.op("pool", lambda e: e.memset(Sbf, 0.0), writes=[f"Sbf{i}" for i in range(8)])
        P.op("act", lambda e: e.activation(out=negA, in_=spm[:, g0 + 97:g0 + 105], func=AF.Exp), reads=["spm"], writes=["negA"])
        P.op("dve", lambda e: e.tensor_scalar(out=negA, in0=negA, scalar1=-1.0, scalar2=None, op0=ALU.mult), reads=["negA"], writes=["negA"])
        for tb in range(NB):
            emit_norm_block(tb, slot, 24 * l)
            for n in range(4):
                for c in range(8):
                    P.op("pe", lambda e, n=n, c=c: e.matmul(
                        ps[7][:, n * 16:(n + 1) * 16], hT[:, c * 512 + n * 128:c * 512 + (n + 1) * 128],
                        wba[:, c * 16:(c + 1) * 16], start=(c == 0), stop=(c == 7)),
                        reads=["hT", "wba"], writes=["ps7a"])
            ba3 = v3(ps[7][:, 0:64], 4)
            for n in range(4):
                P.op("dve", lambda e, n=n: e.tensor_tensor(out=tkU[:, n * 8:(n + 1) * 8], in0=ps[7][:, n * 16 + 8:n * 16 + 16],
                                                          in1=spm[:, g0 + 105:g0 + 113], op=ALU.add),
                     reads=["ps7a", "spm"], writes=["tkU"])
            P.op("act", lambda e: e.activation(out=v3(tkBeta, 4), in_=ba3[:, :, 0:8], func=AF.Exp, scale=-1.0),
                 reads=["ps7a"], writes=["tkBeta"])
            P.op("act", lambda e: e.activation(out=tkU, in_=tkU, func=AF.Exp), reads=["tkU"], writes=["tkU"])
            P.op("act", lambda e: e.activation(out=tkU, in_=tkU, func=AF.Ln, bias=1.0, scale=1.0), reads=["tkU"], writes=["tkU"])
            for n in range(4):
                P.op("dve", lambda e, n=n: e.tensor_tensor(out=tkG[:, n * 8:(n + 1) * 8], in0=tkU[:, n * 8:(n + 1) * 8],
                                                          in1=negA, op=ALU.mult), reads=["tkU", "negA"], writes=["tkG"])
            P.op("dve", lambda e: e.tensor_scalar(out=tkBeta, in0=tkBeta, scalar1=1.0, scalar2=None, op0=ALU.add),
                 reads=["tkBeta"], writes=["tkBeta"])
            P.op("dve", lambda e: e.reciprocal(out=tkBeta, in_=tkBeta), reads=["tkBeta"], writes=["tkBeta"])
            for n in range(4):
                P.op("pe", lambda e, n=n: e.matmul(ps[7][:, 64 + n * 8:64 + (n + 1) * 8], cm(C_TRIU_I), tkG[:, n * 8:(n + 1) * 8],
                                                   start=True, stop=True), reads=["cst", "tkG"], writes=["ps7b"])
                P.op("pe", lambda e, n=n: e.matmul(ps[7][:, 96 + n * 8:96 + (n + 1) * 8], cm(C_ONES), tkG[:, n * 8:(n + 1) * 8],
                                                   start=True, stop=True), reads=["cst", "tkG"], writes=["ps7b"])
            P.op("dve", lambda e: e.tensor_copy(out=tkGc, in_=ps[7][:, 64:96]), reads=["ps7b"], writes=["tkGc"])
            P.op("dve", lambda e: e.tensor_copy(out=tkGl, in_=ps[7][:, 96:128]), reads=["ps7b"], writes=["tkGl"])
            P.op("act", lambda e: e.activation(out=tkEgl, in_=tkGl, func=AF.Exp), reads=["tkGl"], writes=["tkEgl"])
            P.op("dve", lambda e: e.tensor_tensor(out=tkDks, in0=tkGl, in1=tkGc, op=ALU.subtract), reads=["tkGl", "tkGc"], writes=["tkDks"])
            P.op("act", lambda e: e.activation(out=tkDks, in_=tkDks, func=AF.Exp), reads=["tkDks"], writes=["tkDks"])
            P.op("act", lambda e: e.activation(out=tkNbeg, in_=tkGc, func=AF.Exp), reads=["tkGc"], writes=["tkNbeg"])
            P.op("dve", lambda e: e.scalar_tensor_tensor(out=tkNbeg, in0=tkNbeg, scalar=-1.0, in1=tkBeta, op0=ALU.mult, op1=ALU.mult),
                 reads=["tkNbeg", "tkBeta"], writes=["tkNbeg"])

            emit_gdn_block_heads(l, tb, g0)
            emit_outproj(w_outa_d[l], tb, slot)

    def run_interleaved(gens, weights=None):
        gens = list(gens)
        weights = list(weights) if weights is not None else [1] * len(gens)
        while gens:
            for g_, w_ in list(zip(gens, weights)):
                for _ in range(w_):
                    try:
                        next(g_)
                    except StopIteration:
                        k_ = gens.index(g_)
                        gens.pop(k_)
                        weights.pop(k_)
                        break

    def gdn_A(l, tb, h, g0):
        a = h % 3
        wb = wq[h % 2]
        wtk = f"wq{h % 2}"
        P.op("pool", lambda e: e.dma_start(out=v3(wb, 8), in_=wview(w_inp_d[l], h * 512, 512)), writes=[wtk], dma=True)
        yield
        def proj(j, bk):
            for c in range(8):
                P.op("pe", lambda e, c=c: e.matmul(ps[bk], wb[:, c * 512 + j * 128:c * 512 + (j + 1) * 128],
                                                   hT[:, c * 512:(c + 1) * 512], start=(c == 0), stop=(c == 7)),
                     reads=[wtk, "hT"], writes=[f"ps{bk}"])
                yield

        def evac(j, bk):
            cc = (h * 3 + j) * 4
            P.op("pool", lambda e: e.tensor_copy(out=pre[j][:, 0:3], in_=carry[:, cc:cc + 3]),
                 reads=["carry"], writes=[f"pre{j}"])
            P.op("act", lambda e: e.copy(out=pre[j][:, 3:515], in_=ps[bk]), reads=[f"ps{bk}"], writes=[f"pre{j}"])
            yield
            P.op("pool", lambda e: e.tensor_copy(out=carry[:, cc:cc + 3], in_=pre[j][:, 512:515]),
                 reads=[f"pre{j}"], writes=["carry"])
            yield

        yield from proj(0, 0)
        yield from proj(1, 1)
        yield from evac(0, 0)
        yield from evac(1, 1)
        yield from proj(2, 0)
        yield from proj(3, 1)
        yield from evac(2, 0)
        P.op("act", lambda e: e.activation(out=zs[a], in_=ps[1], func=AF.Silu), reads=["ps1"], writes=[f"zs{a}"])
        yield
        wc = lambda tap, j: spm[:, g0 + 24 * tap + 8 * j + h:g0 + 24 * tap + 8 * j + h + 1]
        for tap in range(4):
            for j in range(3):
                if tap == 0:
                    P.op("dve", lambda e, j=j: e.tensor_scalar(out=acc[j], in0=pre[j][:, 0:512], scalar1=wc(0, j), scalar2=0.0,
                                                               op0=ALU.mult, op1=ALU.add), reads=[f"pre{j}", "spm"], writes=[f"acc{j}"])
                else:
                    P.op("dve", lambda e, j=j, tap=tap: e.scalar_tensor_tensor(
                        out=acc[j], in0=pre[j][:, tap:tap + 512], scalar=wc(tap, j), in1=acc[j], op0=ALU.mult, op1=ALU.add),
                        reads=[f"pre{j}", "spm", f"acc{j}"], writes=[f"acc{j}"])
                yield
        for j in range(3):
            P.op("act", lambda e, j=j: e.activation(out=acc[j], in_=acc[j], func=AF.Silu), reads=[f"acc{j}"], writes=[f"acc{j}"])
            yield
        for j in range(2):
            P.op("act", lambda e, j=j: e.activation(out=sqb[:, j * 512:(j + 1) * 512], in_=acc[j], func=AF.Square),
                 reads=[f"acc{j}"], writes=[f"sqb{j}"])
            yield
            P.op("pe", lambda e, j=j: e.matmul(ps[j], cm(C_ONES, True), sqb[:, j * 512:(j + 1) * 512], start=True, stop=True),
                 reads=[f"sqb{j}", "cstb"], writes=[f"ps{j}"])
            yield
        for j in range(2):
            lb = lnb if j == 0 else rn
            tk_ = "lnb" if j == 0 else "rn"
            P.op("act", lambda e, j=j, lb=lb: e.activation(out=lb, in_=ps[j], func=AF.Ln, bias=EPS, scale=1.0),
                 reads=[f"ps{j}"], writes=[tk_])
            yield
            P.op("act", lambda e, j=j, lb=lb: e.activation(out=lb, in_=lb, func=AF.Exp, scale=-0.5,
                                                          bias=(float(np.log(128.0 ** -0.5)) if j == 0 else 0.0)),
                 reads=[tk_], writes=[tk_])
            yield
        P.op("dve", lambda e: e.tensor_tensor(out=qTb[a], in0=acc[0], in1=lnb, op=ALU.mult), reads=["acc0", "lnb"], writes=[f"qTb{a}"])
        yield
        P.op("dve", lambda e: e.tensor_tensor(out=acc[1], in0=acc[1], in1=rn, op=ALU.mult), reads=["acc1", "rn"], writes=["acc1"])
        yield
        P.op("pool", lambda e: e.tensor_copy(out=kTb[a], in_=acc[1]), reads=["acc1"], writes=[f"kTb{a}"])
        yield
        for n in range(4):
            sl = slice(n * 128, (n + 1) * 128)
            P.op("pe", lambda e, sl=sl: e.transpose(ps[0][:, sl], acc[1][:, sl], cm(C_IDENT)), reads=["acc1", "cst"], writes=["ps0"])
            yield
        for n in range(4):
            sl = slice(n * 128, (n + 1) * 128)
            P.op("pe", lambda e, sl=sl: e.transpose(ps[1][:, sl], acc[2][:, sl], cm(C_IDENT)), reads=["acc2", "cst"], writes=["ps1"])
            yield
        for n in range(4):
            sl = slice(n * 128, (n + 1) * 128)
            col = n * 8 + h
            P.op("act", lambda e, sl=sl, col=col: e.activation(out=kdec[a][:, sl], in_=ps[0][:, sl], func=AF.Identity,
                                                               scale=tkDks[:, col:col + 1]), reads=["ps0", "tkDks"], writes=[f"kdec{a}"])
            yield
        for n in range(4):
            sl = slice(n * 128, (n + 1) * 128)
            col = n * 8 + h
            P.op("act", lambda e, sl=sl, col=col: e.activation(out=vb[a][:, sl], in_=ps[1][:, sl], func=AF.Identity,
                                                               scale=tkBeta[:, col:col + 1]), reads=["ps1", "tkBeta"], writes=[f"vb{a}"])
            yield

    def gdn_B(l, tb, h, g0):
        a = h % 3
        bs = h % 2
        q_, k_, kd_, vb_, zs_ = qTb[a], kTb[a], kdec[a], vb[a], zs[a]
        qt, kt, kdt, vbt, zst = f"qTb{a}", f"kTb{a}", f"kdec{a}", f"vb{a}", f"zs{a}"
        tiles = [(n, slice(n * 128, (n + 1) * 128), n * 8 + h) for n in range(4)]
        for n, sl, col in tiles:
            P.op("pe", lambda e, sl=sl: e.matmul(ps[3][:, sl], k_[:, sl], k_[:, sl], start=True, stop=True), reads=[kt], writes=["ps3"])
            P.op("pe", lambda e, sl=sl: e.matmul(ps[4][:, sl], k_[:, sl], q_[:, sl], start=True, stop=True), reads=[kt, qt], writes=["ps4"])
            P.op("pool", lambda e, sl=sl, col=col: e.tensor_scalar(out=E1[:, sl], in0=cm(C_TRIU_I), scalar1=tkG[:, col:col + 1], scalar2=0.0,
                                                                   op0=ALU.mult, op1=ALU.add), reads=["cst", "tkG"], writes=["E1"])
            yield
            P.op("pe", lambda e, sl=sl: e.matmul(ps[5][:, sl], cm(C_ONES), E1[:, sl], start=True, stop=True), reads=["cst", "E1"], writes=["ps5"])
            yield
        P.op("act", lambda e: e.activation(out=egbc, in_=ps[5], func=AF.Exp), reads=["ps5"], writes=["egbc"])
        yield
        for n, sl, col in tiles:
            P.op("dve", lambda e, sl=sl, col=col: e.tensor_scalar(out=E1[:, sl], in0=ps[5][:, sl], scalar1=tkGc[:, col:col + 1], scalar2=0.0,
                                                                  op0=ALU.subtract, op1=ALU.min), reads=["ps5", "tkGc"], writes=["E1"])
            yield
            P.op("dve", lambda e, sl=sl, col=col: e.tensor_scalar(out=E2[:, sl], in0=ps[5][:, sl], scalar1=tkGc[:, col:col + 1], scalar2=0.0,
                                                                  op0=ALU.subtract, op1=ALU.max), reads=["ps5", "tkGc"], writes=["E2"])
            yield
        P.op("act", lambda e: e.activation(out=E1, in_=E1, func=AF.Exp), reads=["E1"], writes=["E1"])
        yield
        P.op("act", lambda e: e.activation(out=E2, in_=E2, func=AF.Exp, scale=-1.0), reads=["E2"], writes=["E2"])
        yield
        for n, sl, col in tiles:
            P.op("dve", lambda e, sl=sl, col=col: e.scalar_tensor_tensor(out=t1[:, sl], in0=ps[3][:, sl], scalar=tkBeta[:, col:col + 1],
                                                                      in1=E2[:, sl], op0=ALU.mult, op1=ALU.mult),
                 reads=["ps3", "E2", "tkBeta"], writes=["t1"])
            yield
        P.op("dve", lambda e: e.tensor_tensor(out=t2, in0=ps[4], in1=E1, op=ALU.mult), reads=["ps4", "E1"], writes=["t2"])
        yield
        P.op("pool", lambda e: e.tensor_tensor(out=qdec[bs], in0=q_, in1=egbc, op=ALU.mult), reads=[qt, "egbc"], writes=[f"qdec{bs}"])
        yield
        L0, U0, Y0 = Lb[0], Ub[0], Yb[0]
        for n, sl, col in tiles:
            P.op("pool", lambda e, sl=sl: e.tensor_tensor(out=L0[:, sl], in0=t1[:, sl], in1=cm(C_BDTRIL_S), op=ALU.mult),
                 reads=["t1", "cst"], writes=["Lb0"])
            yield
        for n, sl, col in tiles:
            P.op("pe", lambda e, sl=sl: e.transpose(ps[4][:, sl], L0[:, sl], cm(C_IDENT)), reads=["Lb0", "cst"], writes=["ps4"])
            yield
        P.op("act", lambda e: e.copy(out=U0, in_=ps[4]), reads=["ps4"], writes=["Ub0"])
        yield
        for n, sl, col in tiles:
            P.op("pool", lambda e, sl=sl: e.tensor_tensor(out=Y0[:, sl], in0=cm(C_IDENT), in1=U0[:, sl], op=ALU.subtract),
                 reads=["cst", "Ub0"], writes=["Yb0"])
            yield
            P.op("pool", lambda e, sl=sl: e.tensor_tensor(out=Offb[:, sl], in0=t1[:, sl], in1=cm(C_OFF), op=ALU.mult),
                 reads=["t1", "cst"], writes=["Offb"])
            yield
            P.op("pool", lambda e, sl=sl: e.tensor_tensor(out=attnT[bs][:, sl], in0=t2[:, sl], in1=cm(C_TRIU_I), op=ALU.mult),
                 reads=["t2", "cst"], writes=[f"attnT{bs}"])
            yield
        cur = 0
        for k in range(1, 6):
            nx = 1 - cur
            for n, sl, col in tiles:
                P.op("pe", lambda e, sl=sl, cur=cur: e.matmul(ps[3][:, sl], Ub[cur][:, sl], Lb[cur][:, sl], start=True, stop=True),
                     reads=[f"Ub{cur}", f"Lb{cur}"], writes=["ps3"])
                yield
            if k < 5:
                for n, sl, col in tiles:
                    P.op("pe", lambda e, sl=sl, cur=cur: e.matmul(ps[4][:, sl], Lb[cur][:, sl], Ub[cur][:, sl], start=True, stop=True),
                         reads=[f"Ub{cur}", f"Lb{cur}"], writes=["ps4"])
                    yield
            P.op("act", lambda e, nx=nx: e.copy(out=Lb[nx], in_=ps[3]), reads=["ps3"], writes=[f"Lb{nx}"])
            yield
            if k < 5:
                P.op("dve", lambda e, nx=nx: e.tensor_copy(out=Ub[nx], in_=ps[4]), reads=["ps4"], writes=[f"Ub{nx}"])
                yield
            for n, sl, col in tiles:
                P.op("pe", lambda e, sl=sl, nx=nx, cur=cur: e.matmul(ps[5][:, sl], Lb[nx][:, sl], Yb[cur][:, sl], start=True, stop=True),
                     reads=[f"Lb{nx}", f"Yb{cur}"], writes=["ps5"])
                yield
            P.op("dve", lambda e, nx=nx, cur=cur: e.tensor_tensor(out=Yb[nx], in0=ps[5], in1=Yb[cur], op=ALU.add),
                 reads=["ps5", f"Yb{cur}"], writes=[f"Yb{nx}"])
            yield
            cur = nx
        Yd = Yb[cur]
        ytk = f"Yb{cur}"
        for n, sl, col in tiles:
            P.op("pe", lambda e, sl=sl: e.transpose(ps[4][:, sl], Yd[:, sl], cm(C_IDENT)), reads=[ytk, "cst"], writes=["ps4"])
            P.op("pe", lambda e, sl=sl: e.matmul(ps[3][:, sl], Offb[:, sl], Yd[:, sl], start=True, stop=True), reads=["Offb", ytk], writes=["ps3"])
            yield
        P.op("act", lambda e: e.copy(out=t1, in_=ps[4]), reads=["ps4"], writes=["t1"])
        yield
        P.op("dve", lambda e: e.tensor_copy(out=t2, in_=ps[3]), reads=["ps3"], writes=["t2"])
        yield
        for n, sl, col in tiles:
            P.op("pe", lambda e, sl=sl: e.matmul(ps[5][:, sl], t1[:, sl], t2[:, sl], start=True, stop=True), reads=["t1", "t2"], writes=["ps5"])
            yield
        P.op("dve", lambda e: e.tensor_tensor(out=Ybf[bs], in0=Yd, in1=ps[5], op=ALU.subtract), reads=[ytk, "ps5"], writes=[f"Ybf{bs}"])
        yield
    def gdn_C(l, tb, h, g0):
        a = h % 3
        bs = h % 2
        q_, k_, kd_, vb_, zs_ = qTb[a], kTb[a], kdec[a], vb[a], zs[a]
        qt, kt, kdt, vbt, zst = f"qTb{a}", f"kTb{a}", f"kdec{a}", f"vb{a}", f"zs{a}"
        tiles = [(n, slice(n * 128, (n + 1) * 128), n * 8 + h) for n in range(4)]
        Sh = Sst[:, h * 128:(h + 1) * 128]
        Shb = Sbf[:, h * 128:(h + 1) * 128]
        for n, sl, col in tiles:
            r_ = rv[n % 2]
            v_ = vnw[n % 2]
            P.op("pe", lambda e, sl=sl: e.matmul(ps[7][:, 128:256], k_[:, sl], Shb, start=True, stop=True),
                 reads=[kt, f"Sbf{h}"], writes=["ps7"])
            yield
            P.op("dve", lambda e, sl=sl, col=col, r_=r_: e.scalar_tensor_tensor(out=r_, in0=ps[7][:, 128:256], scalar=tkNbeg[:, col:col + 1],
                                                                            in1=vb_[:, sl], op0=ALU.mult, op1=ALU.add),
                 reads=["ps7", "tkNbeg", vbt], writes=[f"rv{n % 2}"])
            yield
            P.op("pe", lambda e, sl=sl, r_=r_: e.matmul(ps[7][:, 384:512], Ybf[bs][:, sl], r_, start=True, stop=True),
                 reads=[f"Ybf{bs}", f"rv{n % 2}"], writes=["ps7"])
            yield
            P.op("act", lambda e, v_=v_: e.copy(out=v_, in_=ps[7][:, 384:512]), reads=["ps7"], writes=[f"vnw{n % 2}"])
            yield
            P.op("pe", lambda e, sl=sl: e.matmul(ps[6][:, sl], Shb, qdec[bs][:, sl], start=True, stop=False),
                 reads=[f"Sbf{h}", f"qdec{bs}"], writes=["ps6"])
            P.op("pe", lambda e, sl=sl, v_=v_: e.matmul(ps[6][:, sl], v_, attnT[bs][:, sl], start=False, stop=True),
                 reads=[f"vnw{n % 2}", f"attnT{bs}"], writes=["ps6"])
            yield
            P.op("pe", lambda e, sl=sl, v_=v_: e.matmul(ps[7][:, 256:384], kd_[:, sl], v_, start=True, stop=True),
                 reads=[kdt, f"vnw{n % 2}"], writes=["ps7"])
            yield
            P.op("dve", lambda e, col=col: e.scalar_tensor_tensor(out=Sh, in0=Sh, scalar=tkEgl[:, col:col + 1], in1=ps[7][:, 256:384],
                                                                  op0=ALU.mult, op1=ALU.add),
                 reads=[f"S{h}", "tkEgl", "ps7"], writes=[f"S{h}"])
            yield
            P.op("pool", lambda e: e.tensor_copy(out=Shb, in_=Sh), reads=[f"S{h}"], writes=[f"Sbf{h}"])
            yield
        P.op("act", lambda e: e.copy(out=ob, in_=ps[6]), reads=["ps6"], writes=["ob"])
        yield
        P.op("act", lambda e: e.activation(out=sqc, in_=ob, func=AF.Square), reads=["ob"], writes=["sqc"])
        yield
        P.op("pe", lambda e: e.matmul(ps[6], cm(C_ONES, True), sqc, start=True, stop=True), reads=["sqc", "cstb"], writes=["ps6"])
        yield
        P.op("act", lambda e: e.activation(out=rn3, in_=ps[6], func=AF.Ln, bias=EPS, scale=1.0 / 128), reads=["ps6"], writes=["rn3"])
        yield
        P.op("act", lambda e: e.activation(out=rn3, in_=rn3, func=AF.Exp, scale=-0.5), reads=["rn3"], writes=["rn3"])
        yield
        P.op("dve", lambda e: e.tensor_tensor(out=ob, in0=ob, in1=rn3, op=ALU.mult), reads=["ob", "rn3"], writes=["ob"])
        yield
        P.op("dve", lambda e: e.scalar_tensor_tensor(out=ogT[:, h * 512:(h + 1) * 512], in0=ob, scalar=spm[:, g0 + 96:g0 + 97], in1=zs_,
                                                     op0=ALU.mult, op1=ALU.mult), reads=["ob", "spm", zst], writes=["ogT"])
        yield

    aux = [None]

    def emit_gdn_block_heads(l, tb, g0):
        for step in range(-2, 8):
            gens = []
            wts = []
            if aux[0] is not None:
                gens.append(take(aux[0], AUX_N))
                wts.append(1)
            if 0 <= step < 8:
                gens.append(gdn_C(l, tb, step, g0))
                wts.append(GDN_W[0])
            if 0 <= step + 1 < 8:
                gens.append(gdn_B(l, tb, step + 1, g0))
                wts.append(GDN_W[1])
            if 0 <= step + 2 < 8:
                gens.append(gdn_A(l, tb, step + 2, g0))
                wts.append(GDN_W[2])
            run_interleaved(gens, wts)

    bar_n = [0]

    def emit_barrier():
        k = bar_n[0]
        bar_n[0] += 1
        P.mark_segment()
        P.op("act", lambda e: e.activation(out=bscr[:, 0:1], in_=cact[:, 0:1], func=AF.Identity), reads=["cact"], writes=[f"bar{k}_act", "bscr0"])
        P.op("dve", lambda e: e.tensor_copy(out=bscr[:, 1:2], in_=cact[:, 0:1]), reads=["cact"], writes=[f"bar{k}_dve", "bscr1"])
        P.op("pool", lambda e: e.tensor_copy(out=bscr[:, 2:3], in_=cact[:, 0:1]), reads=["cact"], writes=[f"bar{k}_pool", "bscr2"])
        P.op("pe", lambda e: e.matmul(ps[7][:, 0:8], cm(C_ONES), cact[:, 0:8], start=True, stop=True), reads=["cact", "cst"], writes=["ps7", f"bar{k}_pe"])
        allb = [f"bar{k}_{x}" for x in ("act", "dve", "pool", "pe")]
        P.op("act", lambda e: e.activation(out=bscr[:, 3:4], in_=cact[:, 0:1], func=AF.Identity), reads=allb + ["cact"], writes=["bscr3"])
        P.op("dve", lambda e: e.tensor_copy(out=bscr[:, 4:5], in_=ps[7][:, 0:1]), reads=allb, writes=["bscr4", "ps7"])
        P.op("pool", lambda e: e.tensor_copy(out=bscr[:, 5:6], in_=cact[:, 0:1]), reads=allb + ["cact"], writes=["bscr5"])
        P.op("pe", lambda e: e.matmul(ps[7][:, 0:8], cm(C_ONES), cact[:, 0:8], start=True, stop=True), reads=allb + ["cact", "cst"], writes=["ps7"])
        P.op("sp", lambda e: e.dma_start(out=bscr[:, 6:8], in_=spm_d[:, 0:2]), reads=allb, writes=["bscr6"], dma=True)
        P.mark_segment()

    emit_barrier()
    drain(gen_layer_mod(0))
    n_a = min(nlayers, 2)
    for l in range(n_a):
        if l == 0 and nlayers > 1:
            aux[0] = gen_layer_mod(1)
        if l == 1 and nlayers > 2:
            gl_ = [gen_kv_mod(), gen_layer_mod(2)]
            if nlayers > 3:
                gl_.append(gen_layer_mod(3))
            aux[0] = chain(*gl_)
        emit_gdn_layer(l)
        if aux[0] is not None:
            drain(aux[0])
            aux[0] = None
    emit_barrier()
    gstack.close()
    sb = sb_keep

    ost2 = [sb(f"Eo{i}", [128, 512]) for i in range(2)]
    if nlayers > 2:
        KT = sb("KT", [128, 8 * T], BF16)
        Vt = sb("Vt", [128, 16 * D], BF16)
        qn = sb("qn", [128, 8 * 512], BF16)
        Ebuf = ost2
        SPR = [[sb(f"SPR{i}{u}", [128, 512], F32R) for u in range(2)] for i in range(2)]
        dbf = [[sb(f"dbf{i}{u}", [128, 512]) for u in range(2)] for i in range(2)]
        Wb = [sb(f"Wb{i}", [128, 512], BF16) for i in range(2)]
        rn2 = tmpn[1]
        cstr = sb("cstr", [128, 256], F32R)
        P.op("dve", lambda e: e.tensor_copy(out=cstr[:, 0:128], in_=cm(C_TRIL_S)), reads=["cst"], writes=["cstr"])
        P.op("dve", lambda e: e.tensor_copy(out=cstr[:, 128:256], in_=cm(C_TRIU_I)), reads=["cst"], writes=["cstr"])

        def emit_headnorm(psb, gain_col, extra_bias, dst, dst_tok):
            tm = tmpn[0]
            P.op("act", lambda e: e.copy(out=tm, in_=psb), reads=[psb_tok[0]], writes=["tmpn0"])
            P.op("act", lambda e: e.activation(out=sqb[:, 0:512], in_=tm, func=AF.Square), reads=["tmpn0"], writes=["sqb0"])
            P.op("pe", lambda e: e.matmul(ps[4], cm(C_BDONES, True), sqb[:, 0:512], start=True, stop=True), reads=["sqb0", "cstb"], writes=["ps4"])
            P.op("act", lambda e: e.activation(out=rn2, in_=ps[4], func=AF.Ln, bias=EPS, scale=1.0 / 64), reads=["ps4"], writes=["tmpn1"])
            P.op("act", lambda e: e.activation(out=rn2, in_=rn2, func=AF.Exp, scale=-0.5, bias=extra_bias), reads=["tmpn1"], writes=["tmpn1"])
            P.op("dve", lambda e: e.scalar_tensor_tensor(out=dst, in0=tm, scalar=spm[:, gain_col:gain_col + 1], in1=rn2,
                                                         op0=ALU.mult, op1=ALU.mult), reads=["tmpn0", "spm", "tmpn1"], writes=[dst_tok])

        psb_tok = [None]

        def emit_kv():
            for tb in range(NB):
                emit_norm_block(tb, 4, 96)
                for piece in range(2):
                    P.op("pool", lambda e, piece=piece: e.dma_start(out=v3(wq[piece], 8), in_=wview(w_kv_d, piece * 512, 512)),
                         writes=[f"wq{piece}"], dma=True)
                    for fcl in range(4):
                        fc = piece * 4 + fcl
                        bk = fc % 4
                        for c in range(8):
                            P.op("pe", lambda e, piece=piece, fcl=fcl, c=c, bk=bk: e.matmul(
                                ps[bk], wq[piece][:, c * 512 + fcl * 128:c * 512 + (fcl + 1) * 128], hT[:, c * 512:(c + 1) * 512],
                                start=(c == 0), stop=(c == 7)), reads=[f"wq{piece}", "hT"], writes=[f"ps{bk}"])
                        psb_tok[0] = f"ps{bk}"
                        emit_headnorm(ps[bk], SP_KG, 0.0, KT[:, fc * T + tb * 512:fc * T + (tb + 1) * 512], "KT")
                for piece in range(2):
                    P.op("pool", lambda e, piece=piece: e.dma_start(out=v3(wq[piece], 8), in_=wview(w_kv_d, D + piece * 512, 512)),
                         writes=[f"wq{piece}"], dma=True)
                    for n in range(4):
                        bk = n % 4
                        tile = tb * 4 + n
                        for c in range(8):
                            P.op("pe", lambda e, piece=piece, n=n, c=c, bk=bk: e.matmul(
                                ps[bk], hT[:, c * 512 + n * 128:c * 512 + (n + 1) * 128], wq[piece][:, c * 512:(c + 1) * 512],
                                start=(c == 0), stop=(c == 7)), reads=[f"wq{piece}", "hT"], writes=[f"ps{bk}"])
                        dst = Vt[:, tile * D + piece * 512:tile * D + (piece + 1) * 512]
                        if n % 2 == 0:
                            P.op("act", lambda e, dst=dst, bk=bk: e.copy(out=dst, in_=ps[bk]), reads=[f"ps{bk}"], writes=["Vt"])
                        else:
                            P.op("dve", lambda e, dst=dst, bk=bk: e.tensor_copy(out=dst, in_=ps[bk]), reads=[f"ps{bk}"], writes=["Vt"])

        def sb_head(g, ch, hh, s_):
            h = 2 * ch + hh
            base = hh * 64
            bA, bB = (0, 1) if s_ == 0 else (2, 3)
            Eb, wbu = Ebuf[s_], Wb[s_]
            chunks = list(range(4 * g + 3, -1, -1))

            def geom(i):
                r0 = max(i - 4 * g, 0)
                return r0 * 128, (i >= 4 * g)

            def warm():
                for _ in range(SB_WARM):
                    P.op("pe", lambda e: e.matmul(ps[5][:, 0:128], cm(C_ONES, True), cm(C_IDENT, True), start=True, stop=True),
                         reads=["cstb"], writes=["ps5"])

            def front(k):
                i = chunks[k]
                c0, diag = geom(i)
                u = k % 2
                Sr = SPR[s_][u]
                P.op("pe", lambda e: e.matmul(
                    ps[bA][:, c0:512], KT[base:base + 64, ch * T + i * 128:ch * T + (i + 1) * 128],
                    qn[base:base + 64, ch * 512 + c0:ch * 512 + 512], start=True, stop=True),
                    reads=["KT", f"qn{ch}"], writes=[f"ps{bA}"])
                yield
                P.op("act", lambda e: e.activation(out=Eb[:, c0:512], in_=ps[bA][:, c0:512], func=AF.Exp),
                     reads=[f"ps{bA}"], writes=[f"E{s_}"])
                yield
                P.op("act", lambda e: e.activation(out=Sr[:, c0:512], in_=Eb[:, c0:512], func=AF.Ln, bias=1.0, scale=1.0),
                     reads=[f"E{s_}"], writes=[f"SPR{s_}{u}"])
                yield
                if diag:
                    P.op("pool", lambda e: e.tensor_tensor(out=Sr[:, c0:c0 + 128], in0=Sr[:, c0:c0 + 128].bitcast(F32),
                                                           in1=cm(C_TRIU_S), op=ALU.mult),
                         reads=[f"SPR{s_}{u}", "cst"], writes=[f"SPR{s_}{u}"])
                    yield

            def back1(k):
                i = chunks[k]
                c0, diag = geom(i)
                u = k % 2
                Sr, dbu = SPR[s_][u], dbf[s_][u]
                P.op("pe", lambda e: e.matmul(
                    ps[bB][:, c0:512], cstr[:, 0:128], Sr[:, c0:512], start=(k == 0), stop=False, skip_group_check=True),
                    reads=[f"SPR{s_}{u}", "cstr"], writes=[f"ps{bB}"])
                yield
                P.op("dve", lambda e: e.tensor_tensor(
                    out=dbu[:, c0:512], in0=ps[bA][:, c0:512], in1=Sr[:, c0:512].bitcast(F32), op=ALU.subtract),
                    reads=[f"ps{bA}", f"SPR{s_}{u}"], writes=[f"dbf{s_}{u}"])
                yield

            def back2(k):
                i = chunks[k]
                c0, diag = geom(i)
                u = k % 2
                Sr, dbu = SPR[s_][u], dbf[s_][u]
                P.op("dve", lambda e: e.tensor_tensor(
                    out=dbu[:, c0:512], in0=dbu[:, c0:512], in1=ps[bB][:, c0:512], op=ALU.subtract),
                    reads=[f"ps{bB}", f"dbf{s_}{u}"], writes=[f"dbf{s_}{u}"])
                yield
                if i > 0:
                    warm()
                    P.op("pe", lambda e: e.matmul(
                        ps[bB][:, c0:512], cstr[:, 128:256], Sr[:, c0:512], start=False, stop=False, skip_group_check=True),
                        reads=[f"SPR{s_}{u}", "cstr"], writes=[f"ps{bB}"])
                    yield
                P.op("act", lambda e: e.activation(out=wbu[:, c0:512], in_=dbu[:, c0:512], func=AF.Exp),
                     reads=[f"dbf{s_}{u}"], writes=[f"Wb{s_}"])
                yield
                if diag:
                    P.op("pool", lambda e: e.tensor_tensor(out=wbu[:, c0:c0 + 128], in0=wbu[:, c0:c0 + 128],
                                                           in1=cm(C_TRIU_S, True), op=ALU.mult),
                         reads=[f"Wb{s_}", "cstb"], writes=[f"Wb{s_}"])
                    yield
                P.op("pe", lambda e: e.matmul(
                    ps[6][base:base + 64, c0:512], Vt[:, i * D + h * 64:i * D + (h + 1) * 64], wbu[:, c0:512],
                    start=False, stop=False, skip_group_check=True, tile_position=(0, base)),
                    reads=["Vt", f"Wb{s_}"], writes=["ps6"])
                yield

            yield from front(0)
            for k in range(len(chunks)):
                yield from back1(k)
                if k + 1 < len(chunks):
                    yield from front(k + 1)
                yield from back2(k)

        def run_interleaved(gens):
            gens = list(gens)
            while gens:
                for g_ in list(gens):
                    try:
                        next(g_)
                    except StopIteration:
                        gens.remove(g_)

        def sb_projc(g, l2, fc):
            half, fcl = divmod(fc, 4)
            if fcl == 0:
                P.op("pool", lambda e: e.dma_start(out=v3(wq[0], 8), in_=wview(w_inb_d[l2], half * 512, 512)),
                     writes=["wq0"], dma=True)
                yield
                P.op("pool", lambda e: e.dma_start(out=v3(wq[1], 8), in_=wview(w_inb_d[l2], D + half * 512, 512)),
                     writes=["wq1"], dma=True)
                yield
            for c in range(8):
                P.op("pe", lambda e, c=c: e.matmul(
                    ps[4], wq[0][:, c * 512 + fcl * 128:c * 512 + (fcl + 1) * 128], hT[:, c * 512:(c + 1) * 512],
                    start=(c == 0), stop=(c == 7)), reads=["wq0", "hT"], writes=["ps4"])
                yield
            tm = tmpn[0]
            P.op("act", lambda e: e.copy(out=tm, in_=ps[4]), reads=["ps4"], writes=["tmpn0"])
            yield
            for c in range(8):
                P.op("pe", lambda e, c=c: e.matmul(
                    ps[4], wq[1][:, c * 512 + fcl * 128:c * 512 + (fcl + 1) * 128], hT[:, c * 512:(c + 1) * 512],
                    start=(c == 0), stop=(c == 7)), reads=["wq1", "hT"], writes=["ps4"])
                yield
            P.op("act", lambda e: e.activation(out=sqb[:, 0:512], in_=tm, func=AF.Square), reads=["tmpn0"], writes=["sqb0"])
            yield
            P.op("pe", lambda e: e.matmul(ps[7], cm(C_BDONES, True), sqb[:, 0:512], start=True, stop=True), reads=["sqb0", "cstb"], writes=["ps7"])
            yield
            P.op("act", lambda e: e.activation(out=rn2, in_=ps[7], func=AF.Ln, bias=EPS, scale=1.0 / 64), reads=["ps7"], writes=["tmpn1"])
            yield
            P.op("act", lambda e: e.activation(out=rn2, in_=rn2, func=AF.Exp, scale=-0.5, bias=float(np.log(0.125))), reads=["tmpn1"], writes=["tmpn1"])
            yield
            P.op("dve", lambda e: e.scalar_tensor_tensor(out=qn[:, fc * 512:(fc + 1) * 512], in0=tm, scalar=spm[:, SP_QG + l2:SP_QG + l2 + 1], in1=rn2,
                                                         op0=ALU.mult, op1=ALU.mult), reads=["tmpn0", "spm", "tmpn1"], writes=[f"qn{fc}"])
            yield
            P.op("act", lambda e: e.activation(out=ogT[:, fc * 512:(fc + 1) * 512], in_=ps[4], func=AF.Silu),
                 reads=["ps4"], writes=[f"og{fc}"])
            yield

        def emit_sb_layer(l2):
            L = 2 + l2
            for g in range(NB):
                emit_norm_block(g, L, 24 * L)
                drain(sb_projc(g, l2, 0))
                for ch in range(8):
                    P.op("dve", lambda e: e.memset(ps[6], 0.0), writes=["ps6"])
                    gens = [sb_head(g, ch, 0, 0), sb_head(g, ch, 1, 1)]
                    if ch + 1 < 8:
                        gens.append(sb_projc(g, l2, ch + 1))
                    run_interleaved(gens)
                    P.op("dve", lambda e, ch=ch: e.tensor_tensor(out=ogT[:, ch * 512:(ch + 1) * 512], in0=ps[6], in1=ogT[:, ch * 512:(ch + 1) * 512],
                                                                 op=ALU.mult), reads=["ps6", f"og{ch}"], writes=[f"og{ch}"])
                emit_outproj(w_outb_d[l2], g, L, og_toks=[f"og{i}" for i in range(8)])

        emit_kv()
        for l2 in range(nlayers - 2):
            emit_sb_layer(l2)

    outs = []
    for n in range(16):
        tb = n // 4
        for half in range(2):
            bk = (2 * n + half) % 4
            oi = (2 * n + half) % 2
            for c4 in range(4):
                c = half * 4 + c4
                P.op("pe", lambda e, n=n, c=c, c4=c4, bk=bk: e.transpose(
                    ps[bk][:, c4 * 128:(c4 + 1) * 128], xT[:, c * T + n * 128:c * T + (n + 1) * 128], cm(C_IDENT)),
                    reads=[f"xT{tb}", "cst"], writes=[f"ps{bk}"])
            if half == 0:
                P.op("act", lambda e, oi=oi, bk=bk: e.copy(out=ost2[oi], in_=ps[bk]), reads=[f"ps{bk}"], writes=[f"E{oi}"])
            else:
                P.op("dve", lambda e, oi=oi, bk=bk: e.tensor_copy(out=ost2[oi], in_=ps[bk]), reads=[f"ps{bk}"], writes=[f"E{oi}"])
            outs.append(P.op("sp", lambda e, n=n, half=half, oi=oi: e.dma_start(
                out=out_d[n * 128:(n + 1) * 128, half * 512:(half + 1) * 512], in_=ost2[oi]),
                reads=[f"E{oi}"], dma=True))
    P.finalize(final_wait_ops=outs)
    return nc


def _prep_inputs(inp):
    inp = {k: np.asarray(v) for k, v in inp.items()}
    w_in_a = inp["w_in_a"].astype(np.float32, copy=False)
    qkvz = w_in_a[:, :, :4096].reshape(2, D, 4, 8, 128).transpose(0, 1, 3, 2, 4).reshape(2, D, 4096)
    shared = {
        "cst": _consts(),
        "w_ada": np.ascontiguousarray(inp["w_ada"], np.float32),
        "w_inp": np.ascontiguousarray(qkvz),
        "w_ba": np.ascontiguousarray(w_in_a[:, :, 4096:4112]),
        "w_out_a": np.ascontiguousarray(inp["w_out_a"], np.float32),
        "w_ada_kv": np.ascontiguousarray(inp["w_ada_kv"], np.float32),
        "w_kv": np.ascontiguousarray(inp["w_kv"], np.float32),
        "w_in_b": np.ascontiguousarray(inp["w_in_b"], np.float32),
        "w_out_b": np.ascontiguousarray(inp["w_out_b"], np.float32),
    }
    in_maps = []
    for b in range(8):
        m = dict(shared)
        m["x"] = np.ascontiguousarray(inp["x"][b], np.float32)
        m["spm"] = _small_params(inp, b)
        in_maps.append(m)
    return in_maps


def kernel(**inputs):
    in_maps = _prep_inputs(inputs)
    nc = build(4)
    res = run_bass_kernel_spmd(nc, in_maps, core_ids=list(range(8)))
    return np.stack([np.asarray(r["out"], np.float32) for r in res.results], axis=0)
```

```python
import numpy as np
import concourse.bass as bass
import concourse.mybir as mybir
from concourse.bass_utils import run_bass_kernel_spmd
from contextlib import ExitStack

F32 = mybir.dt.float32
BF16 = mybir.dt.bfloat16
F32R = mybir.dt.float32r
AF = mybir.ActivationFunctionType
ALU = mybir.AluOpType

T = 2048
D = 1024
NB = 4
EPS = 1e-6
AUX_N = 13
SCHEDULE = True
PRIO_BLEVEL = False
PE_F32C = 2.0
XLAT = 0.3
DVE_C0 = 0.22
DVE_R = 1050.0
POOL_C0 = 0.2
POOL_R = 560.0
PE_F32RC = 2.0
DMA_R = 300e3
SB_WARM = 0
GDN_W = (1, 1, 1)


class Tok:
    __slots__ = ("name", "w", "r")

    def __init__(self, name=""):
        self.name = name
        self.w = None
        self.r = []


class Op:
    __slots__ = ("eng", "fn", "deps", "idx", "need_inc", "ev", "is_dma", "dsem", "dval", "seg", "gidx", "cost")

    def __init__(self, eng, fn):
        self.eng = eng
        self.fn = fn
        self.deps = []
        self.idx = None
        self.need_inc = False
        self.ev = None
        self.is_dma = False


class Prog:
    ENG = ("pe", "act", "dve", "pool", "sp")
    SEM_LIMIT = 30000
    NDMA = 16
    NEAR = 6

    def __init__(self, nc):
        self.nc = nc
        self.q = {e: [] for e in self.ENG}
        self.toks = {}
        self.seg = 0
        self.nops = 0

    def mark_segment(self):
        self.seg += 1

    COST = {"pe": 0.25, "act": 0.55, "dve": 0.6, "pool": 0.45, "sp": 0.15}

    class _Probe:
        def __init__(self):
            self.rec = None

        def __getattr__(self, name):
            def f(*a, **kw):
                self.rec = (name, a, kw)
                return self
            return f

    def estimate(self, o):
        try:
            pr = Prog._Probe()
            o.fn(pr)
            name, a, kw = pr.rec
            out = kw.get("out", a[0] if a else None)
            n = 1
            for d in out.shape[1:]:
                n *= int(d)
            if o.is_dma:
                return 2.0 + out.shape[0] * n * 4 / DMA_R
            if o.eng == "pe":
                if name == "transpose":
                    return 0.12
                lhs = kw.get("lhsT", a[1] if len(a) > 1 else None)
                cyc = {F32: PE_F32C, F32R: PE_F32RC}.get(lhs.dtype, 1.0) * max(n, 64)
                return 0.05 + cyc / 1700.0
            if o.eng == "act":
                return 0.2 + n / 1400.0
            if o.eng == "dve":
                return DVE_C0 + n / DVE_R
            if o.eng == "pool":
                return POOL_C0 + n / POOL_R
        except Exception:
            pass
        return self.COST[o.eng]

    def schedule(self):
        import heapq
        allops = []
        for e in self.ENG:
            allops.extend(self.q[e])
        allops.sort(key=lambda o: o.gidx)
        nseg = self.seg + 1
        bysegs = [[] for _ in range(nseg)]
        for o in allops:
            bysegs[o.seg].append(o)
        newq = {e: [] for e in self.ENG}
        for ops in bysegs:
            if not ops:
                continue
            inseg = set(id(o) for o in ops)
            succ = {}
            indeg = {}
            for o in ops:
                n = 0
                for d in o.deps:
                    if id(d) in inseg:
                        succ.setdefault(id(d), []).append(o)
                        n += 1
                indeg[id(o)] = n
            cost_ = {}
            for o in ops:
                cost_[id(o)] = o.cost if o.cost is not None else self.estimate(o)
            blev = {}
            for o in reversed(ops):
                b_ = 0.0
                for s_ in succ.get(id(o), ()):
                    v_ = blev[id(s_)] + (0.05 if s_.eng == o.eng else XLAT)
                    if v_ > b_:
                        b_ = v_
                blev[id(o)] = b_ + (cost_[id(o)] if not o.is_dma else cost_[id(o)])
            finish = {}
            free = {e: 0.0 for e in self.ENG}
            heaps = {e: [] for e in self.ENG}
            ready_t = {}
            for o in ops:
                if indeg[id(o)] == 0:
                    ready_t[id(o)] = 0.0
                    heapq.heappush(heaps[o.eng], (0.0, o.gidx, o))
            left = len(ops)
            while left:
                best = None
                for e in self.ENG:
                    h = heaps[e]
                    if not h:
                        continue
                    st = max(h[0][0], free[e])
                    if best is None or (st, h[0][1]) < (best[0], best[1]):
                        best = (st, h[0][1], e)
                st, _, e = best
                h = heaps[e]
                cand = []
                while h and h[0][0] <= st:
                    cand.append(heapq.heappop(h))
                if PRIO_BLEVEL:
                    cand.sort(key=lambda c: (-blev[id(c[2])], c[1]))
                else:
                    cand.sort(key=lambda c: c[1])
                pick = cand[0]
                for c in cand[1:]:
                    heapq.heappush(h, c)
                o = pick[2]
                c_ = cost_[id(o)]
                if o.is_dma:
                    free[e] = st + (0.15 if e == "sp" else 0.4)
                    fin = st + c_
                else:
                    free[e] = st + c_
                    fin = free[e]
                finish[id(o)] = fin
                newq[e].append(o)
                left -= 1
                for s_ in succ.get(id(o), ()):
                    indeg[id(s_)] -= 1
                    r = max(ready_t.get(id(s_), 0.0), fin + (0.05 if s_.eng == o.eng else XLAT))
                    ready_t[id(s_)] = r
                    if indeg[id(s_)] == 0:
                        heapq.heappush(heaps[s_.eng], (r, s_.gidx, s_))
        for e in self.ENG:
            assert len(newq[e]) == len(self.q[e])
            self.q[e] = newq[e]
            for i, o in enumerate(newq[e]):
                o.idx = i

    def tk(self, name):
        t = self.toks.get(name)
        if t is None:
            t = Tok(name)
            self.toks[name] = t
        return t

    def op(self, eng, fn, reads=(), writes=(), dma=False, cost=None):
        o = Op(eng, fn)
        o.is_dma = dma
        o.seg = self.seg
        o.gidx = self.nops
        o.cost = cost
        self.nops += 1
        o.idx = len(self.q[eng])
        deps = set()
        rd, wr = [], []
        for t in reads:
            if isinstance(t, str) and t.startswith("ps") and t[2].isdigit():
                wr.append(t[:3])
            else:
                rd.append(t)
        for t in writes:
            if isinstance(t, str) and t.startswith("ps") and t[2].isdigit():
                wr.append(t[:3])
            else:
                wr.append(t)
        reads = [self.tk(t) if isinstance(t, str) else t for t in rd]
        writes = [self.tk(t) if isinstance(t, str) else t for t in dict.fromkeys(wr)]
        for t in reads:
            if t.w is not None:
                deps.add(t.w)
        for t in writes:
            if t.w is not None:
                deps.add(t.w)
            for r in t.r:
                deps.add(r)
        deps.discard(o)
        o.deps = list(deps)
        for t in reads:
            t.r.append(o)
        for t in writes:
            t.w = o
            t.r = []
        self.q[eng].append(o)
        return o

    def finalize(self, final_wait_ops=()):
        nc = self.nc
        if SCHEDULE:
            self.schedule()
        waits = {}
        for e in self.ENG:
            seen = {}
            for o in self.q[e]:
                best = {}
                res = []
                for d in o.deps:
                    if d.is_dma:
                        res.append(d)
                        continue
                    if d.eng == o.eng:
                        if e == "pe" or o.is_dma:
                            if not o.is_dma:
                                continue
                        if (not o.is_dma) and o.idx - d.idx > self.NEAR:
                            continue
                    if d.eng not in best or best[d.eng].idx < d.idx:
                        best[d.eng] = d
                for d in best.values():
                    if seen.get(d.eng, -1) >= d.idx:
                        continue
                    seen[d.eng] = d.idx
                    res.append(d)
                waits[o] = res
                for d in res:
                    if not d.is_dma:
                        d.need_inc = True
        stack = ExitStack()
        for e in self.ENG:
            n = sum(1 for o in self.q[e] if o.need_inc and not o.is_dma)
            k = max(1, (n + self.SEM_LIMIT - 1) // self.SEM_LIMIT)
            sems = [stack.enter_context(nc.semaphore(f"s_{e}_{i}")) for i in range(k)]
            c = 0
            for o in self.q[e]:
                if o.need_inc and not o.is_dma:
                    o.ev = (sems[c // self.SEM_LIMIT], c % self.SEM_LIMIT + 1)
                    c += 1
        dsems = {e: [stack.enter_context(nc.semaphore(f"s_dma_{e}_{i}")) for i in range(self.NDMA)]
                 for e in ("sp", "pool")}
        for e in ("sp", "pool"):
            di = 0
            for o in self.q[e]:
                if o.is_dma:
                    o.dsem = dsems[e][di % self.NDMA]
                    o.dval = 16 * (di // self.NDMA + 1)
                    o.ev = (o.dsem, o.dval)
                    di += 1
        final = list(final_wait_ops)
        with nc.Block() as block:
            def run(engname, eng):
                for o in self.q[engname]:
                    for d in waits[o]:
                        eng.wait_ge(d.ev[0], d.ev[1])
                    if o.is_dma and o.dval > 16:
                        eng.wait_ge(o.dsem, o.dval - 16)
                    ins = o.fn(eng)
                    if o.is_dma:
                        ins.then_inc(o.dsem, 16)
                    elif o.need_inc:
                        ins.then_inc(o.ev[0], 1)
                if engname == "sp":
                    for o in final:
                        eng.wait_ge(o.ev[0], o.ev[1])

            @block.tensor
            def _(t):
                run("pe", t)

            @block.scalar
            def _(t):
                run("act", t)

            @block.vector
            def _(t):
                run("dve", t)

            @block.gpsimd
            def _(t):
                run("pool", t)

            @block.sync
            def _(t):
                run("sp", t)
        stack.close()


C_IDENT, C_ONES, C_TRIU_I, C_TRIL_S, C_TRIU_S, C_BDTRIL_S, C_OFF, C_BDONES = range(8)
NCST = 8


def _consts():
    p = np.arange(128)[:, None]
    f = np.arange(128)[None, :]
    m = np.zeros((128, NCST, 128), np.float32)
    m[:, C_IDENT] = (p == f)
    m[:, C_ONES] = 1.0
    m[:, C_TRIU_I] = (p <= f)
    m[:, C_TRIL_S] = (f < p)
    m[:, C_TRIU_S] = (p < f)
    m[:, C_BDTRIL_S] = (f < p) & ((p // 64) == (f // 64))
    m[:, C_OFF] = (p >= 64) & (f < 64)
    m[:, C_BDONES] = ((p // 64) == (f // 64))
    return m.reshape(128, NCST * 128)


def _fm(v):
    v = np.asarray(v, np.float32).reshape(-1, 128)
    return np.ascontiguousarray(v.T)


SP_C = 0
SP_LAYER = 8
SP_KV = 136
SP_GDN = 160
SP_KG = 386
SP_QG = 387
NSP = 389


def _small_params(inp, b):
    sp = np.zeros((128, NSP), np.float32)
    sp[:, 0:8] = _fm(inp["c"][b])
    for l in range(4):
        o = SP_LAYER + 32 * l
        sp[:, o:o + 8] = _fm(inp["norm_g"][l])
        sp[:, o + 8:o + 32] = _fm(inp["b_ada"][l])
    sp[:, SP_KV:SP_KV + 8] = _fm(inp["kv_norm_g"])
    sp[:, SP_KV + 8:SP_KV + 24] = _fm(inp["b_ada_kv"])
    for l in range(2):
        o = SP_GDN + 113 * l
        cw = np.asarray(inp["conv_w_a"][l], np.float32)
        for j in range(4):
            sp[:, o + 24 * j:o + 24 * j + 24] = _fm(cw[j])
        sp[:, o + 96] = np.asarray(inp["o_gain_a"][l], np.float32)
        sp[:, o + 97:o + 105] = np.asarray(inp["a_log_a"][l], np.float32)[None, :]
        sp[:, o + 105:o + 113] = np.asarray(inp["dt_bias_a"][l], np.float32)[None, :]
    sp[:, SP_KG] = np.tile(np.asarray(inp["k_gain"], np.float32), 2)
    for l in range(2):
        sp[:, SP_QG + l] = np.tile(np.asarray(inp["q_gain_b"][l], np.float32), 2)
    return sp


def build(nlayers=4):
    nc = bass.Bass("TRN2", target_bir_lowering=False)
    dram = lambda n, s, k="ExternalInput": nc.dram_tensor(n, s, F32, kind=k).ap()
    x_d = dram("x", [T, D])
    spm_d = dram("spm", [128, NSP])
    cst_d = dram("cst", [128, NCST * 128])
    w_ada_d = dram("w_ada", [4, D, 3 * D])
    w_inp_d = dram("w_inp", [2, D, 8 * 512])
    w_ba_d = dram("w_ba", [2, D, 16])
    w_outa_d = dram("w_out_a", [2, D, D])
    w_adakv_d = dram("w_ada_kv", [D, 2 * D])
    w_kv_d = dram("w_kv", [D, 2 * D])
    w_inb_d = dram("w_in_b", [2, D, 2 * D])
    w_outb_d = dram("w_out_b", [2, D, D])
    out_d = dram("out", [T, D], "ExternalOutput")

    def wview(ap2d, c0, n):
        return ap2d[:, c0:c0 + n].rearrange("(c p) n -> p c n", p=128)

    sb = lambda n, s, d=F32: nc.alloc_sbuf_tensor("sb_" + n, s, d).ap()
    P = Prog(nc)

    def v3(ap, a):
        return ap.rearrange("p (a b) -> p a b", a=a)

    xT = sb("xT", [128, 8 * T])
    cst = sb("cst", [128, NCST * 128])
    cstb = sb("cstb", [128, NCST * 128], BF16)
    spm = sb("spm", [128, NSP])
    cact = sb("cact", [128, 8])
    modT = sb("modT", [128, 5 * 24])
    Acol = sb("Acol", [128, 5 * 8])
    hT = sb("hT", [128, 8 * 512], BF16)
    ogT = sb("ogT", [128, 8 * 512], BF16)
    sqb = sb("sqb", [128, 2 * 512], BF16)
    rstd = sb("rstd", [128, 512])
    bscr = sb("bscr", [128, 8])
    tmpn = [sb(f"tmpn{i}", [128, 512]) for i in range(2)]
    wq = [sb(f"wq{i}", [128, 8 * 512], BF16) for i in range(2)]
    ps = [nc.alloc_psum_tensor(f"ps{i}", [128, 512], F32).ap() for i in range(8)]
    gstack = ExitStack()
    sbp = lambda n, s, d=F32: gstack.enter_context(nc.sbuf_tensor("sb_" + n, s, d))[:]
    sb_keep = sb
    lnb = sbp("lnb", [128, 512])
    mrow = sbp("mrow", [1, 256])
    wa = [sbp("wa0", [128, 8 * 256])] * 2
    LU = sbp("LU", [128, 2048])
    xld = [LU[:, i * 1024:(i + 1) * 1024] for i in range(2)]

    def cm(i, bf=False):
        return (cstb if bf else cst)[:, i * 128:(i + 1) * 128]

    def xsl(c, tb):
        return xT[:, c * T + tb * 512:c * T + (tb + 1) * 512]

    P.op("sp", lambda e: e.dma_start(out=cst, in_=cst_d), writes=["cst"], dma=True)
    P.op("sp", lambda e: e.dma_start(out=spm, in_=spm_d), writes=["spm"], dma=True)
    P.op("pool", lambda e: e.tensor_copy(out=cstb, in_=cst), reads=["cst"], writes=["cstb"])
    P.op("act", lambda e: e.activation(out=cact, in_=spm[:, 0:8], func=AF.Silu), reads=["spm"], writes=["cact"])

    wa_i = [0]

    def gen_mod(wd2, ncols, bias0, dst0, slot, a_slot, g_col0, scale_col0):
        buf = wa[0]
        for piece in range(ncols // 256):
            P.op("sp", lambda e, piece=piece: e.dma_start(out=v3(buf, 8), in_=wview(wd2, piece * 256, 256)),
                 writes=["wa0"], dma=True)
            yield
            for c in range(8):
                P.op("pe", lambda e, c=c: e.matmul(ps[2][0:1, 0:256], cact[:, c:c + 1], buf[:, c * 256:(c + 1) * 256],
                                                   start=(c == 0), stop=(c == 7)), reads=["wa0", "cact"], writes=["ps2"])
                yield
            P.op("act", lambda e: e.copy(out=mrow, in_=ps[2][0:1, 0:256]), reads=["ps2"], writes=["mrow"])
            yield
            for fc in range(2):
                P.op("pe", lambda e, fc=fc: e.matmul(ps[2][:, 256 + fc:257 + fc], mrow[0:1, fc * 128:(fc + 1) * 128],
                                                     cst[0:1, C_ONES * 128:C_ONES * 128 + 1], start=True, stop=True),
                     reads=["mrow", "cst"], writes=["ps2"])
                yield
            d0 = dst0 + piece * 2
            b0 = bias0 + piece * 2
            P.op("dve", lambda e, d0=d0, b0=b0: e.tensor_tensor(out=modT[:, d0:d0 + 2], in0=ps[2][:, 256:258],
                                                                in1=spm[:, b0:b0 + 2], op=ALU.add),
                 reads=["ps2", "spm"], writes=[f"mod{slot}"])
            yield
        P.op("dve", lambda e: e.scalar_tensor_tensor(out=Acol[:, 8 * a_slot:8 * a_slot + 8], in0=modT[:, scale_col0:scale_col0 + 8],
                                                     scalar=1.0, in1=spm[:, g_col0:g_col0 + 8], op0=ALU.add, op1=ALU.mult),
             reads=[f"mod{slot}", "spm"], writes=[f"A{a_slot}"])
        yield

    def gen_layer_mod(l):
        return gen_mod(w_ada_d[l], 3 * D, SP_LAYER + 32 * l + 8, 24 * l, l, l, SP_LAYER + 32 * l, 24 * l + 8)

    def gen_kv_mod():
        return gen_mod(w_adakv_d, 2 * D, SP_KV + 8, 96, 4, 4, SP_KV, 104)

    def drain(gen):
        for _ in gen:
            pass

    def take(gen, n):
        for _ in range(n):
            try:
                next(gen)
            except StopIteration:
                return
            yield

    def chain(*gens):
        for g_ in gens:
            yield from g_

    drain(gen_layer_mod(0))

    for n in range(16):
        bi = n % 2
        tb = n // 4
        P.op("sp", lambda e, n=n, bi=bi: e.dma_start(out=xld[bi], in_=x_d[n * 128:(n + 1) * 128, :]),
             writes=[f"xld{bi}"], dma=True)
        for half in range(2):
            bk = (0, 1, 3, 4)[(2 * n + half) % 4]
            for c4 in range(4):
                c = half * 4 + c4
                P.op("pe", lambda e, bi=bi, c=c, c4=c4, bk=bk: e.transpose(
                    ps[bk][:, c4 * 128:(c4 + 1) * 128], xld[bi][:, c * 128:(c + 1) * 128], cm(C_IDENT)),
                    reads=[f"xld{bi}", "cst"], writes=[f"ps{bk}"])
            dst = v3(xT[:, half * 4 * T:(half * 4 + 4) * T], 4)[:, :, n * 128:(n + 1) * 128]
            src = v3(ps[bk], 4)
            if half == 0:
                P.op("act", lambda e, dst=dst, src=src: e.copy(out=dst, in_=src), reads=[f"ps{bk}"], writes=[f"xT{tb}"])
            else:
                P.op("dve", lambda e, dst=dst, src=src: e.tensor_copy(out=dst, in_=src), reads=[f"ps{bk}"], writes=[f"xT{tb}"])

    def emit_norm_block(tb, slot, shift0):
        for c in range(8):
            sq_ = sqb[:, (c % 2) * 512:(c % 2 + 1) * 512]
            P.op("act", lambda e, c=c, sq_=sq_: e.activation(out=sq_, in_=xsl(c, tb), func=AF.Square),
                 reads=[f"xT{tb}"], writes=[f"sqb{c % 2}"])
            P.op("pe", lambda e, c=c, sq_=sq_: e.matmul(ps[4], cm(C_ONES, True), sq_,
                                               start=(c == 0), stop=(c == 7)), reads=[f"sqb{c % 2}", "cstb"], writes=["ps4"])
        P.op("act", lambda e: e.activation(out=rstd, in_=ps[4], func=AF.Ln, bias=EPS, scale=1.0 / D), reads=["ps4"], writes=["rstd"])
        P.op("act", lambda e: e.activation(out=rstd, in_=rstd, func=AF.Exp, scale=-0.5), reads=["rstd"], writes=["rstd"])
        for c in range(8):
            tm = tmpn[c % 2]
            P.op("dve", lambda e, c=c, tm=tm: e.scalar_tensor_tensor(
                out=tm, in0=xsl(c, tb), scalar=Acol[:, 8 * slot + c:8 * slot + c + 1], in1=rstd,
                op0=ALU.mult, op1=ALU.mult), reads=[f"xT{tb}", f"A{slot}", "rstd"], writes=[f"tmpn{c % 2}"])
            P.op("act", lambda e, c=c, tm=tm: e.activation(
                out=hT[:, c * 512:(c + 1) * 512], in_=tm, func=AF.Identity,
                bias=modT[:, shift0 + c:shift0 + c + 1], scale=1.0),
                reads=[f"tmpn{c % 2}", f"mod{slot}"], writes=["hT"])

    def emit_outproj(wd2, tb, slot, og=None, og_toks=("ogT",)):
        og = ogT if og is None else og
        for half in range(2):
            P.op("pool", lambda e, half=half: e.dma_start(out=v3(wq[half], 8), in_=wview(wd2, half * 512, 512)),
                 writes=[f"wq{half}"], dma=True)
        for m in range(8):
            half, mm_ = divmod(m, 4)
            bk = (0, 1, 3, 4)[m % 4]
            for h in range(8):
                P.op("pe", lambda e, half=half, mm_=mm_, h=h, bk=bk: e.matmul(
                    ps[bk], wq[half][:, h * 512 + mm_ * 128:h * 512 + (mm_ + 1) * 128], og[:, h * 512:(h + 1) * 512],
                    start=(h == 0), stop=(h == 7)), reads=[f"wq{half}"] + list(og_toks), writes=[f"ps{bk}"])
            P.op("dve", lambda e, m=m, bk=bk: e.scalar_tensor_tensor(
                out=xsl(m, tb), in0=ps[bk], scalar=modT[:, 24 * slot + 16 + m:24 * slot + 17 + m], in1=xsl(m, tb),
                op0=ALU.mult, op1=ALU.add), reads=[f"ps{bk}", f"mod{slot}", f"xT{tb}"], writes=[f"xT{tb}"])

    sb = sbp
    wba = sb("wba", [128, 8 * 16], BF16)
    carry = sb("carry", [128, 8 * 3 * 4])
    pre = [sb(f"pre{j}", [128, 516]) for j in range(3)]
    acc = [sb(f"acc{j}", [128, 512]) for j in range(3)]
    zs = [sb(f"zs{i}", [128, 512], BF16) for i in range(3)]
    rn = sb("rn", [128, 512])
    rn3 = sb("rn3", [128, 512])
    sqc = sb("sqc", [128, 512], BF16)
    qTb = [sb(f"qTb{i}", [128, 512], BF16) for i in range(3)]
    kTb = [sb(f"kTb{i}", [128, 512], BF16) for i in range(3)]
    qdec = [sb(f"qdec{i}", [128, 512], BF16) for i in range(2)]
    kdec = [sb(f"kdec{i}", [128, 512], BF16) for i in range(3)]
    vb = [sb(f"vb{i}", [128, 512]) for i in range(3)]
    egbc = sb("egbc", [128, 512])
    E1 = sb("E1", [128, 512])
    E2 = sb("E2", [128, 512])
    t1 = sb("t1", [128, 512])
    t2 = sb("t2", [128, 512])
    Lb = [LU[:, i * 512:(i + 1) * 512] for i in range(2)]
    Ub = [LU[:, (2 + i) * 512:(3 + i) * 512] for i in range(2)]
    Yb = [sb(f"Yb{i}", [128, 512]) for i in range(2)]
    Offb = sb("Offb", [128, 512])
    Ybf = [sb(f"Ybf{i}", [128, 512], BF16) for i in range(2)]
    attnT = [sb(f"attnT{i}", [128, 512], BF16) for i in range(2)]
    rv = [sb(f"rv{i}", [128, 128], BF16) for i in range(2)]
    vnw = [sb(f"vnw{i}", [128, 128], BF16) for i in range(2)]
    Sst = sb("Sst", [128, 8 * 128])
    Sbf = sb("Sbf", [128, 8 * 128], BF16)
    ob = sb("ob", [128, 512])
    tkU = sb("tkU", [128, 32])
    tkG = sb("tkG", [128, 32])
    tkBeta = sb("tkBeta", [128, 32])
    tkGc = sb("tkGc", [128, 32])
    tkNbeg = sb("tkNbeg", [128, 32])
    tkGl = sb("tkGl", [128, 32])
    tkDks = sb("tkDks", [128, 32])
    tkEgl = sb("tkEgl", [128, 32])
    negA = sb("negA", [128, 8])
    sb = sb_keep

    def emit_gdn_layer(l):
        slot = l
        g0 = SP_GDN + 113 * l
        P.op("pool", lambda e: e.dma_start(out=v3(wba, 8), in_=w_ba_d[l].rearrange("(c p) n -> p c n", p=128)),
             writes=["wba"], dma=True)
        P.op("pool", lambda e: e.memset(carry, 0.0), writes=["carry"])
        P.op("pool", lambda e: e.memset(Sst, 0.0), writes=[f"S{i}" for i in range(8)])
        P.op("pool", lambda e: e.memset(Sbf, 0.0), writes=[f"Sbf{i}" for i in range(8)])
        P.op("act", lambda e: e.activation(out=negA, in_=spm[:, g0 + 97:g0 + 105], func=AF.Exp), reads=["spm"], writes=["negA"])
        P.op("dve", lambda e: e.tensor_scalar(out=negA, in0=negA, scalar1=-1.0, scalar2=None, op0=ALU.mult), reads=["negA"], writes=["negA"])
        for tb in range(NB):
            emit_norm_block(tb, slot, 24 * l)
            for n in range(4):
                for c in range(8):
                    P.op("pe", lambda e, n=n, c=c: e.matmul(
                        ps[7][:, n * 16:(n + 1) * 16], hT[:, c * 512 + n * 128:c * 512 + (n + 1) * 128],
                        wba[:, c * 16:(c + 1) * 16], start=(c == 0), stop=(c == 7)),
                        reads=["hT", "wba"], writes=["ps7a"])
            ba3 = v3(ps[7][:, 0:64], 4)
            for n in range(4):
                P.op("dve", lambda e, n=n: e.tensor_tensor(out=tkU[:, n * 8:(n + 1) * 8], in0=ps[7][:, n * 16 + 8:n * 16 + 16],
                                                          in1=spm[:, g0 + 105:g0 + 113], op=ALU.add),
                     reads=["ps7a", "spm"], writes=["tkU"])
            P.op("act", lambda e: e.activation(out=v3(tkBeta, 4), in_=ba3[:, :, 0:8], func=AF.Exp, scale=-1.0),
                 reads=["ps7a"], writes=["tkBeta"])
            P.op("act", lambda e: e.activation(out=tkU, in_=tkU, func=AF.Exp), reads=["tkU"], writes=["tkU"])
            P.op("act", lambda e: e.activation(out=tkU, in_=tkU, func=AF.Ln, bias=1.0, scale=1.0), reads=["tkU"], writes=["tkU"])
            for n in range(4):
                P.op("dve", lambda e, n=n: e.tensor_tensor(out=tkG[:, n * 8:(n + 1) * 8], in0=tkU[:, n * 8:(n + 1) * 8],
                                                          in1=negA, op=ALU.mult), reads=["tkU", "negA"], writes=["tkG"])
            P.op("dve", lambda e: e.tensor_scalar(out=tkBeta, in0=tkBeta, scalar1=1.0, scalar2=None, op0=ALU.add),
                 reads=["tkBeta"], writes=["tkBeta"])
            P.op("dve", lambda e: e.reciprocal(out=tkBeta, in_=tkBeta), reads=["tkBeta"], writes=["tkBeta"])
            for n in range(4):
                P.op("pe", lambda e, n=n: e.matmul(ps[7][:, 64 + n * 8:64 + (n + 1) * 8], cm(C_TRIU_I), tkG[:, n * 8:(n + 1) * 8],
                                                   start=True, stop=True), reads=["cst", "tkG"], writes=["ps7b"])
                P.op("pe", lambda e, n=n: e.matmul(ps[7][:, 96 + n * 8:96 + (n + 1) * 8], cm(C_ONES), tkG[:, n * 8:(n + 1) * 8],
                                                   start=True, stop=True), reads=["cst", "tkG"], writes=["ps7b"])
            P.op("dve", lambda e: e.tensor_copy(out=tkGc, in_=ps[7][:, 64:96]), reads=["ps7b"], writes=["tkGc"])
            P.op("dve", lambda e: e.tensor_copy(out=tkGl, in_=ps[7][:, 96:128]), reads=["ps7b"], writes=["tkGl"])
            P.op("act", lambda e: e.activation(out=tkEgl, in_=tkGl, func=AF.Exp), reads=["tkGl"], writes=["tkEgl"])
            P.op("dve", lambda e: e.tensor_tensor(out=tkDks, in0=tkGl, in1=tkGc, op=ALU.subtract), reads=["tkGl", "tkGc"], writes=["tkDks"])
            P.op("act", lambda e: e.activation(out=tkDks, in_=tkDks, func=AF.Exp), reads=["tkDks"], writes=["tkDks"])
            P.op("act", lambda e: e.activation(out=tkNbeg, in_=tkGc, func=AF.Exp), reads=["tkGc"], writes=["tkNbeg"])
            P.op("dve", lambda e: e.scalar_tensor_tensor(out=tkNbeg, in0=tkNbeg, scalar=-1.0, in1=tkBeta, op0=ALU.mult, op1=ALU.mult),
                 reads=["tkNbeg", "tkBeta"], writes=["tkNbeg"])

            emit_gdn_block_heads(l, tb, g0)
            emit_outproj(w_outa_d[l], tb, slot)

    def run_interleaved(gens, weights=None):
        gens = list(gens)
        weights = list(weights) if weights is not None else [1] * len(gens)
        while gens:
            for g_, w_ in list(zip(gens, weights)):
                for _ in range(w_):
                    try:
                        next(g_)
                    except StopIteration:
                        k_ = gens.index(g_)
                        gens.pop(k_)
                        weights.pop(k_)
                        break

    def gdn_A(l, tb, h, g0):
        a = h % 3
        wb = wq[h % 2]
        wtk = f"wq{h % 2}"
        P.op("pool", lambda e: e.dma_start(out=v3(wb, 8), in_=wview(w_inp_d[l], h * 512, 512)), writes=[wtk], dma=True)
        yield
        def proj(j, bk):
            for c in range(8):
                P.op("pe", lambda e, c=c: e.matmul(ps[bk], wb[:, c * 512 + j * 128:c * 512 + (j + 1) * 128],
                                                   hT[:, c * 512:(c + 1) * 512], start=(c == 0), stop=(c == 7)),
                     reads=[wtk, "hT"], writes=[f"ps{bk}"])
                yield

        def evac(j, bk):
            cc = (h * 3 + j) * 4
            P.op("pool", lambda e: e.tensor_copy(out=pre[j][:, 0:3], in_=carry[:, cc:cc + 3]),
                 reads=["carry"], writes=[f"pre{j}"])
            P.op("act", lambda e: e.copy(out=pre[j][:, 3:515], in_=ps[bk]), reads=[f"ps{bk}"], writes=[f"pre{j}"])
            yield
            P.op("pool", lambda e: e.tensor_copy(out=carry[:, cc:cc + 3], in_=pre[j][:, 512:515]),
                 reads=[f"pre{j}"], writes=["carry"])
            yield

        yield from proj(0, 0)
        yield from proj(1, 1)
        yield from evac(0, 0)
        yield from evac(1, 1)
        yield from proj(2, 0)
        yield from proj(3, 1)
        yield from evac(2, 0)
        P.op("act", lambda e: e.activation(out=zs[a], in_=ps[1], func=AF.Silu), reads=["ps1"], writes=[f"zs{a}"])
        yield
        wc = lambda tap, j: spm[:, g0 + 24 * tap + 8 * j + h:g0 + 24 * tap + 8 * j + h + 1]
        for tap in range(4):
            for j in range(3):
                if tap == 0:
                    P.op("dve", lambda e, j=j: e.tensor_scalar(out=acc[j], in0=pre[j][:, 0:512], scalar1=wc(0, j), scalar2=0.0,
                                                               op0=ALU.mult, op1=ALU.add), reads=[f"pre{j}", "spm"], writes=[f"acc{j}"])
                else:
                    P.op("dve", lambda e, j=j, tap=tap: e.scalar_tensor_tensor(
                        out=acc[j], in0=pre[j][:, tap:tap + 512], scalar=wc(tap, j), in1=acc[j], op0=ALU.mult, op1=ALU.add),
                        reads=[f"pre{j}", "spm", f"acc{j}"], writes=[f"acc{j}"])
                yield
        for j in range(3):
            P.op("act", lambda e, j=j: e.activation(out=acc[j], in_=acc[j], func=AF.Silu), reads=[f"acc{j}"], writes=[f"acc{j}"])
            yield
        for j in range(2):
            P.op("act", lambda e, j=j: e.activation(out=sqb[:, j * 512:(j + 1) * 512], in_=acc[j], func=AF.Square),
                 reads=[f"acc{j}"], writes=[f"sqb{j}"])
            yield
            P.op("pe", lambda e, j=j: e.matmul(ps[j], cm(C_ONES, True), sqb[:, j * 512:(j + 1) * 512], start=True, stop=True),
                 reads=[f"sqb{j}", "cstb"], writes=[f"ps{j}"])
            yield
        for j in range(2):
            lb = lnb if j == 0 else rn
            tk_ = "lnb" if j == 0 else "rn"
            P.op("act", lambda e, j=j, lb=lb: e.activation(out=lb, in_=ps[j], func=AF.Ln, bias=EPS, scale=1.0),
                 reads=[f"ps{j}"], writes=[tk_])
            yield
            P.op("act", lambda e, j=j, lb=lb: e.activation(out=lb, in_=lb, func=AF.Exp, scale=-0.5,
                                                          bias=(float(np.log(128.0 ** -0.5)) if j == 0 else 0.0)),
                 reads=[tk_], writes=[tk_])
            yield
        P.op("dve", lambda e: e.tensor_tensor(out=qTb[a], in0=acc[0], in1=lnb, op=ALU.mult), reads=["acc0", "lnb"], writes=[f"qTb{a}"])
        yield
        P.op("dve", lambda e: e.tensor_tensor(out=acc[1], in0=acc[1], in1=rn, op=ALU.mult), reads=["acc1", "rn"], writes=["acc1"])
        yield
        P.op("pool", lambda e: e.tensor_copy(out=kTb[a], in_=acc[1]), reads=["acc1"], writes=[f"kTb{a}"])
        yield
        for n in range(4):
            sl = slice(n * 128, (n + 1) * 128)
            P.op("pe", lambda e, sl=sl: e.transpose(ps[0][:, sl], acc[1][:, sl], cm(C_IDENT)), reads=["acc1", "cst"], writes=["ps0"])
            yield
        for n in range(4):
            sl = slice(n * 128, (n + 1) * 128)
            P.op("pe", lambda e, sl=sl: e.transpose(ps[1][:, sl], acc[2][:, sl], cm(C_IDENT)), reads=["acc2", "cst"], writes=["ps1"])
            yield
        for n in range(4):
            sl = slice(n * 128, (n + 1) * 128)
            col = n * 8 + h
            P.op("act", lambda e, sl=sl, col=col: e.activation(out=kdec[a][:, sl], in_=ps[0][:, sl], func=AF.Identity,
                                                               scale=tkDks[:, col:col + 1]), reads=["ps0", "tkDks"], writes=[f"kdec{a}"])
            yield
        for n in range(4):
            sl = slice(n * 128, (n + 1) * 128)
            col = n * 8 + h
            P.op("act", lambda e, sl=sl, col=col: e.activation(out=vb[a][:, sl], in_=ps[1][:, sl], func=AF.Identity,
                                                               scale=tkBeta[:, col:col + 1]), reads=["ps1", "tkBeta"], writes=[f"vb{a}"])
            yield

    def gdn_B(l, tb, h, g0):
        a = h % 3
        bs = h % 2
        q_, k_, kd_, vb_, zs_ = qTb[a], kTb[a], kdec[a], vb[a], zs[a]
        qt, kt, kdt, vbt, zst = f"qTb{a}", f"kTb{a}", f"kdec{a}", f"vb{a}", f"zs{a}"
        tiles = [(n, slice(n * 128, (n + 1) * 128), n * 8 + h) for n in range(4)]
        for n, sl, col in tiles:
            P.op("pe", lambda e, sl=sl: e.matmul(ps[3][:, sl], k_[:, sl], k_[:, sl], start=True, stop=True), reads=[kt], writes=["ps3"])
            P.op("pe", lambda e, sl=sl: e.matmul(ps[4][:, sl], k_[:, sl], q_[:, sl], start=True, stop=True), reads=[kt, qt], writes=["ps4"])
            P.op("pool", lambda e, sl=sl, col=col: e.tensor_scalar(out=E1[:, sl], in0=cm(C_TRIU_I), scalar1=tkG[:, col:col + 1], scalar2=0.0,
                                                                   op0=ALU.mult, op1=ALU.add), reads=["cst", "tkG"], writes=["E1"])
            yield
            P.op("pe", lambda e, sl=sl: e.matmul(ps[5][:, sl], cm(C_ONES), E1[:, sl], start=True, stop=True), reads=["cst", "E1"], writes=["ps5"])
            yield
        P.op("act", lambda e: e.activation(out=egbc, in_=ps[5], func=AF.Exp), reads=["ps5"], writes=["egbc"])
        yield
        for n, sl, col in tiles:
            P.op("dve", lambda e, sl=sl, col=col: e.tensor_scalar(out=E1[:, sl], in0=ps[5][:, sl], scalar1=tkGc[:, col:col + 1], scalar2=0.0,
                                                                  op0=ALU.subtract, op1=ALU.min), reads=["ps5", "tkGc"], writes=["E1"])
            yield
            P.op("dve", lambda e, sl=sl, col=col: e.tensor_scalar(out=E2[:, sl], in0=ps[5][:, sl], scalar1=tkGc[:, col:col + 1], scalar2=0.0,
                                                                  op0=ALU.subtract, op1=ALU.max), reads=["ps5", "tkGc"], writes=["E2"])
            yield
        P.op("act", lambda e: e.activation(out=E1, in_=E1, func=AF.Exp), reads=["E1"], writes=["E1"])
        yield
        P.op("act", lambda e: e.activation(out=E2, in_=E2, func=AF.Exp, scale=-1.0), reads=["E2"], writes=["E2"])
        yield
        for n, sl, col in tiles:
            P.op("dve", lambda e, sl=sl, col=col: e.scalar_tensor_tensor(out=t1[:, sl], in0=ps[3][:, sl], scalar=tkBeta[:, col:col + 1],
                                                                      in1=E2[:, sl], op0=ALU.mult, op1=ALU.mult),
                 reads=["ps3", "E2", "tkBeta"], writes=["t1"])
            yield
        P.op("dve", lambda e: e.tensor_tensor(out=t2, in0=ps[4], in1=E1, op=ALU.mult), reads=["ps4", "E1"], writes=["t2"])
        yield
        P.op("pool", lambda e: e.tensor_tensor(out=qdec[bs], in0=q_, in1=egbc, op=ALU.mult), reads=[qt, "egbc"], writes=[f"qdec{bs}"])
        yield
        L0, U0, Y0 = Lb[0], Ub[0], Yb[0]
        for n, sl, col in tiles:
            P.op("pool", lambda e, sl=sl: e.tensor_tensor(out=L0[:, sl], in0=t1[:, sl], in1=cm(C_BDTRIL_S), op=ALU.mult),
                 reads=["t1", "cst"], writes=["Lb0"])
            yield
        for n, sl, col in tiles:
            P.op("pe", lambda e, sl=sl: e.transpose(ps[4][:, sl], L0[:, sl], cm(C_IDENT)), reads=["Lb0", "cst"], writes=["ps4"])
            yield
        P.op("act", lambda e: e.copy(out=U0, in_=ps[4]), reads=["ps4"], writes=["Ub0"])
        yield
        for n, sl, col in tiles:
            P.op("pool", lambda e, sl=sl: e.tensor_tensor(out=Y0[:, sl], in0=cm(C_IDENT), in1=U0[:, sl], op=ALU.subtract),
                 reads=["cst", "Ub0"], writes=["Yb0"])
            yield
            P.op("pool", lambda e, sl=sl: e.tensor_tensor(out=Offb[:, sl], in0=t1[:, sl], in1=cm(C_OFF), op=ALU.mult),
                 reads=["t1", "cst"], writes=["Offb"])
            yield
            P.op("pool", lambda e, sl=sl: e.tensor_tensor(out=attnT[bs][:, sl], in0=t2[:, sl], in1=cm(C_TRIU_I), op=ALU.mult),
                 reads=["t2", "cst"], writes=[f"attnT{bs}"])
            yield
        cur = 0
        for k in range(1, 6):
            nx = 1 - cur
            for n, sl, col in tiles:
                P.op("pe", lambda e, sl=sl, cur=cur: e.matmul(ps[3][:, sl], Ub[cur][:, sl], Lb[cur][:, sl], start=True, stop=True),
                     reads=[f"Ub{cur}", f"Lb{cur}"], writes=["ps3"])
                yield
            if k < 5:
                for n, sl, col in tiles:
                    P.op("pe", lambda e, sl=sl, cur=cur: e.matmul(ps[4][:, sl], Lb[cur][:, sl], Ub[cur][:, sl], start=True, stop=True),
                         reads=[f"Ub{cur}", f"Lb{cur}"], writes=["ps4"])
                    yield
            P.op("act", lambda e, nx=nx: e.copy(out=Lb[nx], in_=ps[3]), reads=["ps3"], writes=[f"Lb{nx}"])
            yield
            if k < 5:
                P.op("dve", lambda e, nx=nx: e.tensor_copy(out=Ub[nx], in_=ps[4]), reads=["ps4"], writes=[f"Ub{nx}"])
                yield
            for n, sl, col in tiles:
                P.op("pe", lambda e, sl=sl, nx=nx, cur=cur: e.matmul(ps[5][:, sl], Lb[nx][:, sl], Yb[cur][:, sl], start=True, stop=True),
                     reads=[f"Lb{nx}", f"Yb{cur}"], writes=["ps5"])
                yield
            P.op("dve", lambda e, nx=nx, cur=cur: e.tensor_tensor(out=Yb[nx], in0=ps[5], in1=Yb[cur], op=ALU.add),
                 reads=["ps5", f"Yb{cur}"], writes=[f"Yb{nx}"])
            yield
            cur = nx
        Yd = Yb[cur]
        ytk = f"Yb{cur}"
        for n, sl, col in tiles:
            P.op("pe", lambda e, sl=sl: e.transpose(ps[4][:, sl], Yd[:, sl], cm(C_IDENT)), reads=[ytk, "cst"], writes=["ps4"])
            P.op("pe", lambda e, sl=sl: e.matmul(ps[3][:, sl], Offb[:, sl], Yd[:, sl], start=True, stop=True), reads=["Offb", ytk], writes=["ps3"])
            yield
        P.op("act", lambda e: e.copy(out=t1, in_=ps[4]), reads=["ps4"], writes=["t1"])
        yield
        P.op("dve", lambda e: e.tensor_copy(out=t2, in_=ps[3]), reads=["ps3"], writes=["t2"])
        yield
        for n, sl, col in tiles:
            P.op("pe", lambda e, sl=sl: e.matmul(ps[5][:, sl], t1[:, sl], t2[:, sl], start=True, stop=True), reads=["t1", "t2"], writes=["ps5"])
            yield
        P.op("dve", lambda e: e.tensor_tensor(out=Ybf[bs], in0=Yd, in1=ps[5], op=ALU.subtract), reads=[ytk, "ps5"], writes=[f"Ybf{bs}"])
        yield
    def gdn_C(l, tb, h, g0):
        a = h % 3
        bs = h % 2
        q_, k_, kd_, vb_, zs_ = qTb[a], kTb[a], kdec[a], vb[a], zs[a]
        qt, kt, kdt, vbt, zst = f"qTb{a}", f"kTb{a}", f"kdec{a}", f"vb{a}", f"zs{a}"
        tiles = [(n, slice(n * 128, (n + 1) * 128), n * 8 + h) for n in range(4)]
        Sh = Sst[:, h * 128:(h + 1) * 128]
        Shb = Sbf[:, h * 128:(h + 1) * 128]
        for n, sl, col in tiles:
            r_ = rv[n % 2]
            v_ = vnw[n % 2]
            P.op("pe", lambda e, sl=sl: e.matmul(ps[7][:, 128:256], k_[:, sl], Shb, start=True, stop=True),
                 reads=[kt, f"Sbf{h}"], writes=["ps7"])
            yield
            P.op("dve", lambda e, sl=sl, col=col, r_=r_: e.scalar_tensor_tensor(out=r_, in0=ps[7][:, 128:256], scalar=tkNbeg[:, col:col + 1],
                                                                            in1=vb_[:, sl], op0=ALU.mult, op1=ALU.add),
                 reads=["ps7", "tkNbeg", vbt], writes=[f"rv{n % 2}"])
            yield
            P.op("pe", lambda e, sl=sl, r_=r_: e.matmul(ps[7][:, 384:512], Ybf[bs][:, sl], r_, start=True, stop=True),
                 reads=[f"Ybf{bs}", f"rv{n % 2}"], writes=["ps7"])
            yield
            P.op("act", lambda e, v_=v_: e.copy(out=v_, in_=ps[7][:, 384:512]), reads=["ps7"], writes=[f"vnw{n % 2}"])
            yield
            P.op("pe", lambda e, sl=sl: e.matmul(ps[6][:, sl], Shb, qdec[bs][:, sl], start=True, stop=False),
                 reads=[f"Sbf{h}", f"qdec{bs}"], writes=["ps6"])
            P.op("pe", lambda e, sl=sl, v_=v_: e.matmul(ps[6][:, sl], v_, attnT[bs][:, sl], start=False, stop=True),
                 reads=[f"vnw{n % 2}", f"attnT{bs}"], writes=["ps6"])
            yield
            P.op("pe", lambda e, sl=sl, v_=v_: e.matmul(ps[7][:, 256:384], kd_[:, sl], v_, start=True, stop=True),
                 reads=[kdt, f"vnw{n % 2}"], writes=["ps7"])
            yield
            P.op("dve", lambda e, col=col: e.scalar_tensor_tensor(out=Sh, in0=Sh, scalar=tkEgl[:, col:col + 1], in1=ps[7][:, 256:384],
                                                                  op0=ALU.mult, op1=ALU.add),
                 reads=[f"S{h}", "tkEgl", "ps7"], writes=[f"S{h}"])
            yield
            P.op("pool", lambda e: e.tensor_copy(out=Shb, in_=Sh), reads=[f"S{h}"], writes=[f"Sbf{h}"])
            yield
        P.op("act", lambda e: e.copy(out=ob, in_=ps[6]), reads=["ps6"], writes=["ob"])
        yield
        P.op("act", lambda e: e.activation(out=sqc, in_=ob, func=AF.Square), reads=["ob"], writes=["sqc"])
        yield
        P.op("pe", lambda e: e.matmul(ps[6], cm(C_ONES, True), sqc, start=True, stop=True), reads=["sqc", "cstb"], writes=["ps6"])
        yield
        P.op("act", lambda e: e.activation(out=rn3, in_=ps[6], func=AF.Ln, bias=EPS, scale=1.0 / 128), reads=["ps6"], writes=["rn3"])
        yield
        P.op("act", lambda e: e.activation(out=rn3, in_=rn3, func=AF.Exp, scale=-0.5), reads=["rn3"], writes=["rn3"])
        yield
        P.op("dve", lambda e: e.tensor_tensor(out=ob, in0=ob, in1=rn3, op=ALU.mult), reads=["ob", "rn3"], writes=["ob"])
        yield
        P.op("dve", lambda e: e.scalar_tensor_tensor(out=ogT[:, h * 512:(h + 1) * 512], in0=ob, scalar=spm[:, g0 + 96:g0 + 97], in1=zs_,
                                                     op0=ALU.mult, op1=ALU.mult), reads=["ob", "spm", zst], writes=["ogT"])
        yield

    aux = [None]

    def emit_gdn_block_heads(l, tb, g0):
        for step in range(-2, 8):
            gens = []
            wts = []
            if aux[0] is not None:
                gens.append(take(aux[0], AUX_N))
                wts.append(1)
            if 0 <= step < 8:
                gens.append(gdn_C(l, tb, step, g0))
                wts.append(GDN_W[0])
            if 0 <= step + 1 < 8:
                gens.append(gdn_B(l, tb, step + 1, g0))
                wts.append(GDN_W[1])
            if 0 <= step + 2 < 8:
                gens.append(gdn_A(l, tb, step + 2, g0))
                wts.append(GDN_W[2])
            run_interleaved(gens, wts)

    bar_n = [0]

    def emit_barrier():
        k = bar_n[0]
        bar_n[0] += 1
        P.mark_segment()
        P.op("act", lambda e: e.activation(out=bscr[:, 0:1], in_=cact[:, 0:1], func=AF.Identity), reads=["cact"], writes=[f"bar{k}_act", "bscr0"])
        P.op("dve", lambda e: e.tensor_copy(out=bscr[:, 1:2], in_=cact[:, 0:1]), reads=["cact"], writes=[f"bar{k}_dve", "bscr1"])
        P.op("pool", lambda e: e.tensor_copy(out=bscr[:, 2:3], in_=cact[:, 0:1]), reads=["cact"], writes=[f"bar{k}_pool", "bscr2"])
        P.op("pe", lambda e: e.matmul(ps[7][:, 0:8], cm(C_ONES), cact[:, 0:8], start=True, stop=True), reads=["cact", "cst"], writes=["ps7", f"bar{k}_pe"])
        allb = [f"bar{k}_{x}" for x in ("act", "dve", "pool", "pe")]
        P.op("act", lambda e: e.activation(out=bscr[:, 3:4], in_=cact[:, 0:1], func=AF.Identity), reads=allb + ["cact"], writes=["bscr3"])
        P.op("dve", lambda e: e.tensor_copy(out=bscr[:, 4:5], in_=ps[7][:, 0:1]), reads=allb, writes=["bscr4", "ps7"])
        P.op("pool", lambda e: e.tensor_copy(out=bscr[:, 5:6], in_=cact[:, 0:1]), reads=allb + ["cact"], writes=["bscr5"])
        P.op("pe", lambda e: e.matmul(ps[7][:, 0:8], cm(C_ONES), cact[:, 0:8], start=True, stop=True), reads=allb + ["cact", "cst"], writes=["ps7"])
        P.op("sp", lambda e: e.dma_start(out=bscr[:, 6:8], in_=spm_d[:, 0:2]), reads=allb, writes=["bscr6"], dma=True)
        P.mark_segment()

    emit_barrier()
    n_a = min(nlayers, 2)
    for l in range(n_a):
        if l == 0 and nlayers > 1:
            aux[0] = gen_layer_mod(1)
        if l == 1 and nlayers > 2:
            gl_ = [gen_kv_mod(), gen_layer_mod(2)]
            if nlayers > 3:
                gl_.append(gen_layer_mod(3))
            aux[0] = chain(*gl_)
        emit_gdn_layer(l)
        if aux[0] is not None:
            drain(aux[0])
            aux[0] = None
    emit_barrier()
    gstack.close()
    sb = sb_keep

    ost2 = [sb(f"Eo{i}", [128, 512]) for i in range(2)]
    if nlayers > 2:
        KT = sb("KT", [128, 8 * T], BF16)
        Vt = sb("Vt", [128, 16 * D], BF16)
        qn = sb("qn", [128, 8 * 512], BF16)
        Ebuf = ost2
        SPR = [[sb(f"SPR{i}{u}", [128, 512], F32R) for u in range(2)] for i in range(2)]
        dbf = [[sb(f"dbf{i}{u}", [128, 512]) for u in range(2)] for i in range(2)]
        Wb = [sb(f"Wb{i}", [128, 512], BF16) for i in range(2)]
        rn2 = tmpn[1]
        cstr = sb("cstr", [128, 256], F32R)
        P.op("dve", lambda e: e.tensor_copy(out=cstr[:, 0:128], in_=cm(C_TRIL_S)), reads=["cst"], writes=["cstr"])
        P.op("dve", lambda e: e.tensor_copy(out=cstr[:, 128:256], in_=cm(C_TRIU_I)), reads=["cst"], writes=["cstr"])

        def emit_headnorm(psb, gain_col, extra_bias, dst, dst_tok):
            tm = tmpn[0]
            P.op("act", lambda e: e.copy(out=tm, in_=psb), reads=[psb_tok[0]], writes=["tmpn0"])
            P.op("act", lambda e: e.activation(out=sqb[:, 0:512], in_=tm, func=AF.Square), reads=["tmpn0"], writes=["sqb0"])
            P.op("pe", lambda e: e.matmul(ps[4], cm(C_BDONES, True), sqb[:, 0:512], start=True, stop=True), reads=["sqb0", "cstb"], writes=["ps4"])
            P.op("act", lambda e: e.activation(out=rn2, in_=ps[4], func=AF.Ln, bias=EPS, scale=1.0 / 64), reads=["ps4"], writes=["tmpn1"])
            P.op("act", lambda e: e.activation(out=rn2, in_=rn2, func=AF.Exp, scale=-0.5, bias=extra_bias), reads=["tmpn1"], writes=["tmpn1"])
            P.op("dve", lambda e: e.scalar_tensor_tensor(out=dst, in0=tm, scalar=spm[:, gain_col:gain_col + 1], in1=rn2,
                                                         op0=ALU.mult, op1=ALU.mult), reads=["tmpn0", "spm", "tmpn1"], writes=[dst_tok])

        psb_tok = [None]

        def emit_kv():
            for tb in range(NB):
                emit_norm_block(tb, 4, 96)
                for piece in range(2):
                    P.op("pool", lambda e, piece=piece: e.dma_start(out=v3(wq[piece], 8), in_=wview(w_kv_d, piece * 512, 512)),
                         writes=[f"wq{piece}"], dma=True)
                    for fcl in range(4):
                        fc = piece * 4 + fcl
                        bk = fc % 4
                        for c in range(8):
                            P.op("pe", lambda e, piece=piece, fcl=fcl, c=c, bk=bk: e.matmul(
                                ps[bk], wq[piece][:, c * 512 + fcl * 128:c * 512 + (fcl + 1) * 128], hT[:, c * 512:(c + 1) * 512],
                                start=(c == 0), stop=(c == 7)), reads=[f"wq{piece}", "hT"], writes=[f"ps{bk}"])
                        psb_tok[0] = f"ps{bk}"
                        emit_headnorm(ps[bk], SP_KG, 0.0, KT[:, fc * T + tb * 512:fc * T + (tb + 1) * 512], "KT")
                for piece in range(2):
                    P.op("pool", lambda e, piece=piece: e.dma_start(out=v3(wq[piece], 8), in_=wview(w_kv_d, D + piece * 512, 512)),
                         writes=[f"wq{piece}"], dma=True)
                    for n in range(4):
                        bk = n % 4
                        tile = tb * 4 + n
                        for c in range(8):
                            P.op("pe", lambda e, piece=piece, n=n, c=c, bk=bk: e.matmul(
                                ps[bk], hT[:, c * 512 + n * 128:c * 512 + (n + 1) * 128], wq[piece][:, c * 512:(c + 1) * 512],
                                start=(c == 0), stop=(c == 7)), reads=[f"wq{piece}", "hT"], writes=[f"ps{bk}"])
                        dst = Vt[:, tile * D + piece * 512:tile * D + (piece + 1) * 512]
                        if n % 2 == 0:
                            P.op("act", lambda e, dst=dst, bk=bk: e.copy(out=dst, in_=ps[bk]), reads=[f"ps{bk}"], writes=["Vt"])
                        else:
                            P.op("dve", lambda e, dst=dst, bk=bk: e.tensor_copy(out=dst, in_=ps[bk]), reads=[f"ps{bk}"], writes=["Vt"])

        def sb_head(g, ch, hh, s_):
            h = 2 * ch + hh
            base = hh * 64
            bA, bB = (0, 1) if s_ == 0 else (2, 3)
            Eb, wbu = Ebuf[s_], Wb[s_]
            chunks = list(range(4 * g + 3, -1, -1))

            def geom(i):
                r0 = max(i - 4 * g, 0)
                return r0 * 128, (i >= 4 * g)

            def warm():
                for _ in range(SB_WARM):
                    P.op("pe", lambda e: e.matmul(ps[5][:, 0:128], cm(C_ONES, True), cm(C_IDENT, True), start=True, stop=True),
                         reads=["cstb"], writes=["ps5"])

            def front(k):
                i = chunks[k]
                c0, diag = geom(i)
                u = k % 2
                Sr = SPR[s_][u]
                P.op("pe", lambda e: e.matmul(
                    ps[bA][:, c0:512], KT[base:base + 64, ch * T + i * 128:ch * T + (i + 1) * 128],
                    qn[base:base + 64, ch * 512 + c0:ch * 512 + 512], start=True, stop=True),
                    reads=["KT", f"qn{ch}"], writes=[f"ps{bA}"])
                yield
                P.op("act", lambda e: e.activation(out=Eb[:, c0:512], in_=ps[bA][:, c0:512], func=AF.Exp),
                     reads=[f"ps{bA}"], writes=[f"E{s_}"])
                yield
                P.op("act", lambda e: e.activation(out=Sr[:, c0:512], in_=Eb[:, c0:512], func=AF.Ln, bias=1.0, scale=1.0),
                     reads=[f"E{s_}"], writes=[f"SPR{s_}{u}"])
                yield
                if diag:
                    P.op("pool", lambda e: e.tensor_tensor(out=Sr[:, c0:c0 + 128], in0=Sr[:, c0:c0 + 128].bitcast(F32),
                                                           in1=cm(C_TRIU_S), op=ALU.mult),
                         reads=[f"SPR{s_}{u}", "cst"], writes=[f"SPR{s_}{u}"])
                    yield

            def back1(k):
                i = chunks[k]
                c0, diag = geom(i)
                u = k % 2
                Sr, dbu = SPR[s_][u], dbf[s_][u]
                P.op("pe", lambda e: e.matmul(
                    ps[bB][:, c0:512], cstr[:, 0:128], Sr[:, c0:512], start=(k == 0), stop=False, skip_group_check=True),
                    reads=[f"SPR{s_}{u}", "cstr"], writes=[f"ps{bB}"])
                yield
                P.op("dve", lambda e: e.tensor_tensor(
                    out=dbu[:, c0:512], in0=ps[bA][:, c0:512], in1=Sr[:, c0:512].bitcast(F32), op=ALU.subtract),
                    reads=[f"ps{bA}", f"SPR{s_}{u}"], writes=[f"dbf{s_}{u}"])
                yield

            def back2(k):
                i = chunks[k]
                c0, diag = geom(i)
                u = k % 2
                Sr, dbu = SPR[s_][u], dbf[s_][u]
                P.op("dve", lambda e: e.tensor_tensor(
                    out=dbu[:, c0:512], in0=dbu[:, c0:512], in1=ps[bB][:, c0:512], op=ALU.subtract),
                    reads=[f"ps{bB}", f"dbf{s_}{u}"], writes=[f"dbf{s_}{u}"])
                yield
                if i > 0:
                    warm()
                    P.op("pe", lambda e: e.matmul(
                        ps[bB][:, c0:512], cstr[:, 128:256], Sr[:, c0:512], start=False, stop=False, skip_group_check=True),
                        reads=[f"SPR{s_}{u}", "cstr"], writes=[f"ps{bB}"])
                    yield
                P.op("act", lambda e: e.activation(out=wbu[:, c0:512], in_=dbu[:, c0:512], func=AF.Exp),
                     reads=[f"dbf{s_}{u}"], writes=[f"Wb{s_}"])
                yield
                if diag:
                    P.op("pool", lambda e: e.tensor_tensor(out=wbu[:, c0:c0 + 128], in0=wbu[:, c0:c0 + 128],
                                                           in1=cm(C_TRIU_S, True), op=ALU.mult),
                         reads=[f"Wb{s_}", "cstb"], writes=[f"Wb{s_}"])
                    yield
                P.op("pe", lambda e: e.matmul(
                    ps[6][base:base + 64, c0:512], Vt[:, i * D + h * 64:i * D + (h + 1) * 64], wbu[:, c0:512],
                    start=False, stop=False, skip_group_check=True, tile_position=(0, base)),
                    reads=["Vt", f"Wb{s_}"], writes=["ps6"])
                yield

            yield from front(0)
            for k in range(len(chunks)):
                yield from back1(k)
                if k + 1 < len(chunks):
                    yield from front(k + 1)
                yield from back2(k)

        def run_interleaved(gens):
            gens = list(gens)
            while gens:
                for g_ in list(gens):
                    try:
                        next(g_)
                    except StopIteration:
                        gens.remove(g_)

        def sb_projc(g, l2, fc):
            half, fcl = divmod(fc, 4)
            if fcl == 0:
                P.op("pool", lambda e: e.dma_start(out=v3(wq[0], 8), in_=wview(w_inb_d[l2], half * 512, 512)),
                     writes=["wq0"], dma=True)
                yield
                P.op("pool", lambda e: e.dma_start(out=v3(wq[1], 8), in_=wview(w_inb_d[l2], D + half * 512, 512)),
                     writes=["wq1"], dma=True)
                yield
            for c in range(8):
                P.op("pe", lambda e, c=c: e.matmul(
                    ps[4], wq[0][:, c * 512 + fcl * 128:c * 512 + (fcl + 1) * 128], hT[:, c * 512:(c + 1) * 512],
                    start=(c == 0), stop=(c == 7)), reads=["wq0", "hT"], writes=["ps4"])
                yield
            tm = tmpn[0]
            P.op("act", lambda e: e.copy(out=tm, in_=ps[4]), reads=["ps4"], writes=["tmpn0"])
            yield
            for c in range(8):
                P.op("pe", lambda e, c=c: e.matmul(
                    ps[4], wq[1][:, c * 512 + fcl * 128:c * 512 + (fcl + 1) * 128], hT[:, c * 512:(c + 1) * 512],
                    start=(c == 0), stop=(c == 7)), reads=["wq1", "hT"], writes=["ps4"])
                yield
            P.op("act", lambda e: e.activation(out=sqb[:, 0:512], in_=tm, func=AF.Square), reads=["tmpn0"], writes=["sqb0"])
            yield
            P.op("pe", lambda e: e.matmul(ps[7], cm(C_BDONES, True), sqb[:, 0:512], start=True, stop=True), reads=["sqb0", "cstb"], writes=["ps7"])
            yield
            P.op("act", lambda e: e.activation(out=rn2, in_=ps[7], func=AF.Ln, bias=EPS, scale=1.0 / 64), reads=["ps7"], writes=["tmpn1"])
            yield
            P.op("act", lambda e: e.activation(out=rn2, in_=rn2, func=AF.Exp, scale=-0.5, bias=float(np.log(0.125))), reads=["tmpn1"], writes=["tmpn1"])
            yield
            P.op("dve", lambda e: e.scalar_tensor_tensor(out=qn[:, fc * 512:(fc + 1) * 512], in0=tm, scalar=spm[:, SP_QG + l2:SP_QG + l2 + 1], in1=rn2,
                                                         op0=ALU.mult, op1=ALU.mult), reads=["tmpn0", "spm", "tmpn1"], writes=[f"qn{fc}"])
            yield
            P.op("act", lambda e: e.activation(out=ogT[:, fc * 512:(fc + 1) * 512], in_=ps[4], func=AF.Silu),
                 reads=["ps4"], writes=[f"og{fc}"])
            yield

        def emit_sb_layer(l2):
            L = 2 + l2
            for g in range(NB):
                emit_norm_block(g, L, 24 * L)
                drain(sb_projc(g, l2, 0))
                for ch in range(8):
                    P.op("dve", lambda e: e.memset(ps[6], 0.0), writes=["ps6"])
                    gens = [sb_head(g, ch, 0, 0), sb_head(g, ch, 1, 1)]
                    if ch + 1 < 8:
                        gens.append(sb_projc(g, l2, ch + 1))
                    run_interleaved(gens)
                    P.op("dve", lambda e, ch=ch: e.tensor_tensor(out=ogT[:, ch * 512:(ch + 1) * 512], in0=ps[6], in1=ogT[:, ch * 512:(ch + 1) * 512],
                                                                 op=ALU.mult), reads=["ps6", f"og{ch}"], writes=[f"og{ch}"])
                emit_outproj(w_outb_d[l2], g, L, og_toks=[f"og{i}" for i in range(8)])

        emit_kv()
        for l2 in range(nlayers - 2):
            emit_sb_layer(l2)

    outs = []
    for n in range(16):
        tb = n // 4
        for half in range(2):
            bk = (2 * n + half) % 4
            oi = (2 * n + half) % 2
            for c4 in range(4):
                c = half * 4 + c4
                P.op("pe", lambda e, n=n, c=c, c4=c4, bk=bk: e.transpose(
                    ps[bk][:, c4 * 128:(c4 + 1) * 128], xT[:, c * T + n * 128:c * T + (n + 1) * 128], cm(C_IDENT)),
                    reads=[f"xT{tb}", "cst"], writes=[f"ps{bk}"])
            if half == 0:
                P.op("act", lambda e, oi=oi, bk=bk: e.copy(out=ost2[oi], in_=ps[bk]), reads=[f"ps{bk}"], writes=[f"E{oi}"])
            else:
                P.op("dve", lambda e, oi=oi, bk=bk: e.tensor_copy(out=ost2[oi], in_=ps[bk]), reads=[f"ps{bk}"], writes=[f"E{oi}"])
            outs.append(P.op("sp", lambda e, n=n, half=half, oi=oi: e.dma_start(
                out=out_d[n * 128:(n + 1) * 128, half * 512:(half + 1) * 512], in_=ost2[oi]),
                reads=[f"E{oi}"], dma=True))
    P.finalize(final_wait_ops=outs)
    return nc


def _prep_inputs(inp):
    inp = {k: np.asarray(v) for k, v in inp.items()}
    w_in_a = inp["w_in_a"].astype(np.float32, copy=False)
    qkvz = w_in_a[:, :, :4096].reshape(2, D, 4, 8, 128).transpose(0, 1, 3, 2, 4).reshape(2, D, 4096)
    shared = {
        "cst": _consts(),
        "w_ada": np.ascontiguousarray(inp["w_ada"], np.float32),
        "w_inp": np.ascontiguousarray(qkvz),
        "w_ba": np.ascontiguousarray(w_in_a[:, :, 4096:4112]),
        "w_out_a": np.ascontiguousarray(inp["w_out_a"], np.float32),
        "w_ada_kv": np.ascontiguousarray(inp["w_ada_kv"], np.float32),
        "w_kv": np.ascontiguousarray(inp["w_kv"], np.float32),
        "w_in_b": np.ascontiguousarray(inp["w_in_b"], np.float32),
        "w_out_b": np.ascontiguousarray(inp["w_out_b"], np.float32),
    }
    in_maps = []
    for b in range(8):
        m = dict(shared)
        m["x"] = np.ascontiguousarray(inp["x"][b], np.float32)
        m["spm"] = _small_params(inp, b)
        in_maps.append(m)
    return in_maps


def kernel(**inputs):
    in_maps = _prep_inputs(inputs)
    nc = build(4)
    res = run_bass_kernel_spmd(nc, in_maps, core_ids=list(range(8)))
    return np.stack([np.asarray(r["out"], np.float32) for r in res.results], axis=0)
```

```python
import numpy as np
import concourse.bass as bass
import concourse.mybir as mybir
from concourse.bass_utils import run_bass_kernel_spmd
from contextlib import ExitStack

F32 = mybir.dt.float32
BF16 = mybir.dt.bfloat16
F32R = mybir.dt.float32r
AF = mybir.ActivationFunctionType
ALU = mybir.AluOpType

T = 2048
D = 1024
NB = 4
EPS = 1e-6
AUX_N = 13
SCHEDULE = True
PRIO_BLEVEL = False
PE_F32C = 2.0
XLAT = 0.3
DVE_C0 = 0.22
DVE_R = 1050.0
POOL_C0 = 0.2
POOL_R = 560.0
PE_F32RC = 2.0
DMA_R = 300e3
SB_WARM = 0
GDN_W = (1, 1, 1)


class Tok:
    __slots__ = ("name", "w", "r")

    def __init__(self, name=""):
        self.name = name
        self.w = None
        self.r = []


class Op:
    __slots__ = ("eng", "fn", "deps", "idx", "need_inc", "ev", "is_dma", "dsem", "dval", "seg", "gidx", "cost")

    def __init__(self, eng, fn):
        self.eng = eng
        self.fn = fn
        self.deps = []
        self.idx = None
        self.need_inc = False
        self.ev = None
        self.is_dma = False


class Prog:
    ENG = ("pe", "act", "dve", "pool", "sp")
    SEM_LIMIT = 30000
    NDMA = 16
    NEAR = 6

    def __init__(self, nc):
        self.nc = nc
        self.q = {e: [] for e in self.ENG}
        self.toks = {}
        self.seg = 0
        self.nops = 0

    def mark_segment(self):
        self.seg += 1

    COST = {"pe": 0.25, "act": 0.55, "dve": 0.6, "pool": 0.45, "sp": 0.15}

    class _Probe:
        def __init__(self):
            self.rec = None

        def __getattr__(self, name):
            def f(*a, **kw):
                self.rec = (name, a, kw)
                return self
            return f

    def estimate(self, o):
        try:
            pr = Prog._Probe()
            o.fn(pr)
            name, a, kw = pr.rec
            out = kw.get("out", a[0] if a else None)
            n = 1
            for d in out.shape[1:]:
                n *= int(d)
            if o.is_dma:
                return 2.0 + out.shape[0] * n * 4 / DMA_R
            if o.eng == "pe":
                if name == "transpose":
                    return 0.12
                lhs = kw.get("lhsT", a[1] if len(a) > 1 else None)
                cyc = {F32: PE_F32C, F32R: PE_F32RC}.get(lhs.dtype, 1.0) * max(n, 64)
                return 0.05 + cyc / 1700.0
            if o.eng == "act":
                return 0.2 + n / 1400.0
            if o.eng == "dve":
                return DVE_C0 + n / DVE_R
            if o.eng == "pool":
                return POOL_C0 + n / POOL_R
        except Exception:
            pass
        return self.COST[o.eng]

    def schedule(self):
        import heapq
        allops = []
        for e in self.ENG:
            allops.extend(self.q[e])
        allops.sort(key=lambda o: o.gidx)
        nseg = self.seg + 1
        bysegs = [[] for _ in range(nseg)]
        for o in allops:
            bysegs[o.seg].append(o)
        newq = {e: [] for e in self.ENG}
        for ops in bysegs:
            if not ops:
                continue
            inseg = set(id(o) for o in ops)
            succ = {}
            indeg = {}
            for o in ops:
                n = 0
                for d in o.deps:
                    if id(d) in inseg:
                        succ.setdefault(id(d), []).append(o)
                        n += 1
                indeg[id(o)] = n
            cost_ = {}
            for o in ops:
                cost_[id(o)] = o.cost if o.cost is not None else self.estimate(o)
            blev = {}
            for o in reversed(ops):
                b_ = 0.0
                for s_ in succ.get(id(o), ()):
                    v_ = blev[id(s_)] + (0.05 if s_.eng == o.eng else XLAT)
                    if v_ > b_:
                        b_ = v_
                blev[id(o)] = b_ + (cost_[id(o)] if not o.is_dma else cost_[id(o)])
            finish = {}
            free = {e: 0.0 for e in self.ENG}
            heaps = {e: [] for e in self.ENG}
            ready_t = {}
            for o in ops:
                if indeg[id(o)] == 0:
                    ready_t[id(o)] = 0.0
                    heapq.heappush(heaps[o.eng], (0.0, o.gidx, o))
            left = len(ops)
            while left:
                best = None
                for e in self.ENG:
                    h = heaps[e]
                    if not h:
                        continue
                    st = max(h[0][0], free[e])
                    if best is None or (st, h[0][1]) < (best[0], best[1]):
                        best = (st, h[0][1], e)
                st, _, e = best
                h = heaps[e]
                cand = []
                while h and h[0][0] <= st:
                    cand.append(heapq.heappop(h))
                if PRIO_BLEVEL:
                    cand.sort(key=lambda c: (-blev[id(c[2])], c[1]))
                else:
                    cand.sort(key=lambda c: c[1])
                pick = cand[0]
                for c in cand[1:]:
                    heapq.heappush(h, c)
                o = pick[2]
                c_ = cost_[id(o)]
                if o.is_dma:
                    free[e] = st + (0.15 if e == "sp" else 0.4)
                    fin = st + c_
                else:
                    free[e] = st + c_
                    fin = free[e]
                finish[id(o)] = fin
                newq[e].append(o)
                left -= 1
                for s_ in succ.get(id(o), ()):
                    indeg[id(s_)] -= 1
                    r = max(ready_t.get(id(s_), 0.0), fin + (0.05 if s_.eng == o.eng else XLAT))
                    ready_t[id(s_)] = r
                    if indeg[id(s_)] == 0:
                        heapq.heappush(heaps[s_.eng], (r, s_.gidx, s_))
        for e in self.ENG:
            assert len(newq[e]) == len(self.q[e])
            self.q[e] = newq[e]
            for i, o in enumerate(newq[e]):
                o.idx = i

    def tk(self, name):
        t = self.toks.get(name)
        if t is None:
            t = Tok(name)
            self.toks[name] = t
        return t

    def op(self, eng, fn, reads=(), writes=(), dma=False, cost=None):
        o = Op(eng, fn)
        o.is_dma = dma
        o.seg = self.seg
        o.gidx = self.nops
        o.cost = cost
        self.nops += 1
        o.idx = len(self.q[eng])
        deps = set()
        rd, wr = [], []
        for t in reads:
            if isinstance(t, str) and t.startswith("ps") and t[2].isdigit():
                wr.append(t[:3])
            else:
                rd.append(t)
        for t in writes:
            if isinstance(t, str) and t.startswith("ps") and t[2].isdigit():
                wr.append(t[:3])
            else:
                wr.append(t)
        reads = [self.tk(t) if isinstance(t, str) else t for t in rd]
        writes = [self.tk(t) if isinstance(t, str) else t for t in dict.fromkeys(wr)]
        for t in reads:
            if t.w is not None:
                deps.add(t.w)
        for t in writes:
            if t.w is not None:
                deps.add(t.w)
            for r in t.r:
                deps.add(r)
        deps.discard(o)
        o.deps = list(deps)
        for t in reads:
            t.r.append(o)
        for t in writes:
            t.w = o
            t.r = []
        self.q[eng].append(o)
        return o

    def finalize(self, final_wait_ops=()):
        nc = self.nc
        if SCHEDULE:
            self.schedule()
        waits = {}
        for e in self.ENG:
            seen = {}
            for o in self.q[e]:
                best = {}
                res = []
                for d in o.deps:
                    if d.is_dma:
                        res.append(d)
                        continue
                    if d.eng == o.eng:
                        if e == "pe" or o.is_dma:
                            if not o.is_dma:
                                continue
                        if (not o.is_dma) and o.idx - d.idx > self.NEAR:
                            continue
                    if d.eng not in best or best[d.eng].idx < d.idx:
                        best[d.eng] = d
                for d in best.values():
                    if seen.get(d.eng, -1) >= d.idx:
                        continue
                    seen[d.eng] = d.idx
                    res.append(d)
                waits[o] = res
                for d in res:
                    if not d.is_dma:
                        d.need_inc = True
        stack = ExitStack()
        for e in self.ENG:
            n = sum(1 for o in self.q[e] if o.need_inc and not o.is_dma)
            k = max(1, (n + self.SEM_LIMIT - 1) // self.SEM_LIMIT)
            sems = [stack.enter_context(nc.semaphore(f"s_{e}_{i}")) for i in range(k)]
            c = 0
            for o in self.q[e]:
                if o.need_inc and not o.is_dma:
                    o.ev = (sems[c // self.SEM_LIMIT], c % self.SEM_LIMIT + 1)
                    c += 1
        dsems = {e: [stack.enter_context(nc.semaphore(f"s_dma_{e}_{i}")) for i in range(self.NDMA)]
                 for e in ("sp", "pool")}
        for e in ("sp", "pool"):
            di = 0
            for o in self.q[e]:
                if o.is_dma:
                    o.dsem = dsems[e][di % self.NDMA]
                    o.dval = 16 * (di // self.NDMA + 1)
                    o.ev = (o.dsem, o.dval)
                    di += 1
        final = list(final_wait_ops)
        with nc.Block() as block:
            def run(engname, eng):
                for o in self.q[engname]:
                    for d in waits[o]:
                        eng.wait_ge(d.ev[0], d.ev[1])
                    if o.is_dma and o.dval > 16:
                        eng.wait_ge(o.dsem, o.dval - 16)
                    ins = o.fn(eng)
                    if o.is_dma:
                        ins.then_inc(o.dsem, 16)
                    elif o.need_inc:
                        ins.then_inc(o.ev[0], 1)
                if engname == "sp":
                    for o in final:
                        eng.wait_ge(o.ev[0], o.ev[1])

            @block.tensor
            def _(t):
                run("pe", t)

            @block.scalar
            def _(t):
                run("act", t)

            @block.vector
            def _(t):
                run("dve", t)

            @block.gpsimd
            def _(t):
                run("pool", t)

            @block.sync
            def _(t):
                run("sp", t)
        stack.close()


C_IDENT, C_ONES, C_TRIU_I, C_TRIL_S, C_TRIU_S, C_BDTRIL_S, C_OFF, C_BDONES = range(8)
NCST = 8


def _consts():
    p = np.arange(128)[:, None]
    f = np.arange(128)[None, :]
    m = np.zeros((128, NCST, 128), np.float32)
    m[:, C_IDENT] = (p == f)
    m[:, C_ONES] = 1.0
    m[:, C_TRIU_I] = (p <= f)
    m[:, C_TRIL_S] = (f < p)
    m[:, C_TRIU_S] = (p < f)
    m[:, C_BDTRIL_S] = (f < p) & ((p // 64) == (f // 64))
    m[:, C_OFF] = (p >= 64) & (f < 64)
    m[:, C_BDONES] = ((p // 64) == (f // 64))
    return m.reshape(128, NCST * 128)


def _fm(v):
    v = np.asarray(v, np.float32).reshape(-1, 128)
    return np.ascontiguousarray(v.T)


SP_C = 0
SP_LAYER = 8
SP_KV = 136
SP_GDN = 160
SP_KG = 386
SP_QG = 387
NSP = 389


def _small_params(inp, b):
    sp = np.zeros((128, NSP), np.float32)
    sp[:, 0:8] = _fm(inp["c"][b])
    for l in range(4):
        o = SP_LAYER + 32 * l
        sp[:, o:o + 8] = _fm(inp["norm_g"][l])
        sp[:, o + 8:o + 32] = _fm(inp["b_ada"][l])
    sp[:, SP_KV:SP_KV + 8] = _fm(inp["kv_norm_g"])
    sp[:, SP_KV + 8:SP_KV + 24] = _fm(inp["b_ada_kv"])
    for l in range(2):
        o = SP_GDN + 113 * l
        cw = np.asarray(inp["conv_w_a"][l], np.float32)
        for j in range(4):
            sp[:, o + 24 * j:o + 24 * j + 24] = _fm(cw[j])
        sp[:, o + 96] = np.asarray(inp["o_gain_a"][l], np.float32)
        sp[:, o + 97:o + 105] = np.asarray(inp["a_log_a"][l], np.float32)[None, :]
        sp[:, o + 105:o + 113] = np.asarray(inp["dt_bias_a"][l], np.float32)[None, :]
    sp[:, SP_KG] = np.tile(np.asarray(inp["k_gain"], np.float32), 2)
    for l in range(2):
        sp[:, SP_QG + l] = np.tile(np.asarray(inp["q_gain_b"][l], np.float32), 2)
    return sp


def build(nlayers=4):
    nc = bass.Bass("TRN2", target_bir_lowering=False)
    dram = lambda n, s, k="ExternalInput": nc.dram_tensor(n, s, F32, kind=k).ap()
    x_d = dram("x", [T, D])
    spm_d = dram("spm", [128, NSP])
    cst_d = dram("cst", [128, NCST * 128])
    w_ada_d = dram("w_ada", [4, D, 3 * D])
    w_inp_d = dram("w_inp", [2, D, 8 * 512])
    w_ba_d = dram("w_ba", [2, D, 16])
    w_outa_d = dram("w_out_a", [2, D, D])
    w_adakv_d = dram("w_ada_kv", [D, 2 * D])
    w_kv_d = dram("w_kv", [D, 2 * D])
    w_inb_d = dram("w_in_b", [2, D, 2 * D])
    w_outb_d = dram("w_out_b", [2, D, D])
    out_d = dram("out", [T, D], "ExternalOutput")

    def wview(ap2d, c0, n):
        return ap2d[:, c0:c0 + n].rearrange("(c p) n -> p c n", p=128)

    sb = lambda n, s, d=F32: nc.alloc_sbuf_tensor("sb_" + n, s, d).ap()
    P = Prog(nc)

    def v3(ap, a):
        return ap.rearrange("p (a b) -> p a b", a=a)

    xT = sb("xT", [128, 8 * T])
    cst = sb("cst", [128, NCST * 128])
    cstb = sb("cstb", [128, NCST * 128], BF16)
    spm = sb("spm", [128, NSP])
    cact = sb("cact", [128, 8])
    modT = sb("modT", [128, 5 * 24])
    Acol = sb("Acol", [128, 5 * 8])
    hT = sb("hT", [128, 8 * 512], BF16)
    ogT = sb("ogT", [128, 8 * 512], BF16)
    sqb = sb("sqb", [128, 2 * 512], BF16)
    rstd = sb("rstd", [128, 512])
    bscr = sb("bscr", [128, 8])
    tmpn = [sb(f"tmpn{i}", [128, 512]) for i in range(2)]
    wq = [sb(f"wq{i}", [128, 8 * 512], BF16) for i in range(2)]
    ps = [nc.alloc_psum_tensor(f"ps{i}", [128, 512], F32).ap() for i in range(8)]
    gstack = ExitStack()
    sbp = lambda n, s, d=F32: gstack.enter_context(nc.sbuf_tensor("sb_" + n, s, d))[:]
    sb_keep = sb
    lnb = sbp("lnb", [128, 512])
    mrow = sbp("mrow", [1, 256])
    wa = [sbp("wa0", [128, 8 * 256])] * 2
    LU = sbp("LU", [128, 2048])
    xld = [LU[:, i * 1024:(i + 1) * 1024] for i in range(2)]

    def cm(i, bf=False):
        return (cstb if bf else cst)[:, i * 128:(i + 1) * 128]

    def xsl(c, tb):
        return xT[:, c * T + tb * 512:c * T + (tb + 1) * 512]

    P.op("sp", lambda e: e.dma_start(out=cst, in_=cst_d), writes=["cst"], dma=True)
    P.op("sp", lambda e: e.dma_start(out=spm, in_=spm_d), writes=["spm"], dma=True)
    P.op("pool", lambda e: e.tensor_copy(out=cstb, in_=cst), reads=["cst"], writes=["cstb"])
    P.op("act", lambda e: e.activation(out=cact, in_=spm[:, 0:8], func=AF.Silu), reads=["spm"], writes=["cact"])

    wa_i = [0]

    def gen_mod(wd2, ncols, bias0, dst0, slot, a_slot, g_col0, scale_col0):
        buf = wa[0]
        for piece in range(ncols // 256):
            P.op("sp", lambda e, piece=piece: e.dma_start(out=v3(buf, 8), in_=wview(wd2, piece * 256, 256)),
                 writes=["wa0"], dma=True)
            yield
            for c in range(8):
                P.op("pe", lambda e, c=c: e.matmul(ps[2][0:1, 0:256], cact[:, c:c + 1], buf[:, c * 256:(c + 1) * 256],
                                                   start=(c == 0), stop=(c == 7)), reads=["wa0", "cact"], writes=["ps2"])
                yield
            P.op("act", lambda e: e.copy(out=mrow, in_=ps[2][0:1, 0:256]), reads=["ps2"], writes=["mrow"])
            yield
            for fc in range(2):
                P.op("pe", lambda e, fc=fc: e.matmul(ps[2][:, 256 + fc:257 + fc], mrow[0:1, fc * 128:(fc + 1) * 128],
                                                     cst[0:1, C_ONES * 128:C_ONES * 128 + 1], start=True, stop=True),
                     reads=["mrow", "cst"], writes=["ps2"])
                yield
            d0 = dst0 + piece * 2
            b0 = bias0 + piece * 2
            P.op("dve", lambda e, d0=d0, b0=b0: e.tensor_tensor(out=modT[:, d0:d0 + 2], in0=ps[2][:, 256:258],
                                                                in1=spm[:, b0:b0 + 2], op=ALU.add),
                 reads=["ps2", "spm"], writes=[f"mod{slot}"])
            yield
        P.op("dve", lambda e: e.scalar_tensor_tensor(out=Acol[:, 8 * a_slot:8 * a_slot + 8], in0=modT[:, scale_col0:scale_col0 + 8],
                                                     scalar=1.0, in1=spm[:, g_col0:g_col0 + 8], op0=ALU.add, op1=ALU.mult),
             reads=[f"mod{slot}", "spm"], writes=[f"A{a_slot}"])
        yield

    def gen_layer_mod(l):
        return gen_mod(w_ada_d[l], 3 * D, SP_LAYER + 32 * l + 8, 24 * l, l, l, SP_LAYER + 32 * l, 24 * l + 8)

    def gen_kv_mod():
        return gen_mod(w_adakv_d, 2 * D, SP_KV + 8, 96, 4, 4, SP_KV, 104)

    def drain(gen):
        for _ in gen:
            pass

    def take(gen, n):
        for _ in range(n):
            try:
                next(gen)
            except StopIteration:
                return
            yield

    def chain(*gens):
        for g_ in gens:
            yield from g_

    drain(gen_layer_mod(0))

    for n in range(16):
        bi = n % 2
        tb = n // 4
        P.op("sp", lambda e, n=n, bi=bi: e.dma_start(out=xld[bi], in_=x_d[n * 128:(n + 1) * 128, :]),
             writes=[f"xld{bi}"], dma=True)
        for half in range(2):
            bk = (0, 1, 3, 4)[(2 * n + half) % 4]
            for c4 in range(4):
                c = half * 4 + c4
                P.op("pe", lambda e, bi=bi, c=c, c4=c4, bk=bk: e.transpose(
                    ps[bk][:, c4 * 128:(c4 + 1) * 128], xld[bi][:, c * 128:(c + 1) * 128], cm(C_IDENT)),
                    reads=[f"xld{bi}", "cst"], writes=[f"ps{bk}"])
            dst = v3(xT[:, half * 4 * T:(half * 4 + 4) * T], 4)[:, :, n * 128:(n + 1) * 128]
            src = v3(ps[bk], 4)
            if half == 0:
                P.op("act", lambda e, dst=dst, src=src: e.copy(out=dst, in_=src), reads=[f"ps{bk}"], writes=[f"xT{tb}"])
            else:
                P.op("dve", lambda e, dst=dst, src=src: e.tensor_copy(out=dst, in_=src), reads=[f"ps{bk}"], writes=[f"xT{tb}"])

    def emit_norm_block(tb, slot, shift0):
        for c in range(8):
            sq_ = sqb[:, (c % 2) * 512:(c % 2 + 1) * 512]
            P.op("act", lambda e, c=c, sq_=sq_: e.activation(out=sq_, in_=xsl(c, tb), func=AF.Square),
                 reads=[f"xT{tb}"], writes=[f"sqb{c % 2}"])
            P.op("pe", lambda e, c=c, sq_=sq_: e.matmul(ps[4], cm(C_ONES, True), sq_,
                                               start=(c == 0), stop=(c == 7)), reads=[f"sqb{c % 2}", "cstb"], writes=["ps4"])
        P.op("act", lambda e: e.activation(out=rstd, in_=ps[4], func=AF.Ln, bias=EPS, scale=1.0 / D), reads=["ps4"], writes=["rstd"])
        P.op("act", lambda e: e.activation(out=rstd, in_=rstd, func=AF.Exp, scale=-0.5), reads=["rstd"], writes=["rstd"])
        for c in range(8):
            tm = tmpn[c % 2]
            P.op("dve", lambda e, c=c, tm=tm: e.scalar_tensor_tensor(
                out=tm, in0=xsl(c, tb), scalar=Acol[:, 8 * slot + c:8 * slot + c + 1], in1=rstd,
                op0=ALU.mult, op1=ALU.mult), reads=[f"xT{tb}", f"A{slot}", "rstd"], writes=[f"tmpn{c % 2}"])
            P.op("act", lambda e, c=c, tm=tm: e.activation(
                out=hT[:, c * 512:(c + 1) * 512], in_=tm, func=AF.Identity,
                bias=modT[:, shift0 + c:shift0 + c + 1], scale=1.0),
                reads=[f"tmpn{c % 2}", f"mod{slot}"], writes=["hT"])

    def emit_outproj(wd2, tb, slot, og=None, og_toks=("ogT",)):
        og = ogT if og is None else og
        for half in range(2):
            P.op("pool", lambda e, half=half: e.dma_start(out=v3(wq[half], 8), in_=wview(wd2, half * 512, 512)),
                 writes=[f"wq{half}"], dma=True)
        for m in range(8):
            half, mm_ = divmod(m, 4)
            bk = (0, 1, 3, 4)[m % 4]
            for h in range(8):
                P.op("pe", lambda e, half=half, mm_=mm_, h=h, bk=bk: e.matmul(
                    ps[bk], wq[half][:, h * 512 + mm_ * 128:h * 512 + (mm_ + 1) * 128], og[:, h * 512:(h + 1) * 512],
                    start=(h == 0), stop=(h == 7)), reads=[f"wq{half}"] + list(og_toks), writes=[f"ps{bk}"])
            P.op("dve", lambda e, m=m, bk=bk: e.scalar_tensor_tensor(
                out=xsl(m, tb), in0=ps[bk], scalar=modT[:, 24 * slot + 16 + m:24 * slot + 17 + m], in1=xsl(m, tb),
                op0=ALU.mult, op1=ALU.add), reads=[f"ps{bk}", f"mod{slot}", f"xT{tb}"], writes=[f"xT{tb}"])

    sb = sbp
    wba = sb("wba", [128, 8 * 16], BF16)
    carry = sb("carry", [128, 8 * 3 * 4])
    pre = [sb(f"pre{j}", [128, 516]) for j in range(3)]
    acc = [sb(f"acc{j}", [128, 512]) for j in range(3)]
    zs = [sb(f"zs{i}", [128, 512], BF16) for i in range(3)]
    rn = sb("rn", [128, 512])
    rn3 = sb("rn3", [128, 512])
    sqc = sb("sqc", [128, 512], BF16)
    qTb = [sb(f"qTb{i}", [128, 512], BF16) for i in range(3)]
    kTb = [sb(f"kTb{i}", [128, 512], BF16) for i in range(3)]
    qdec = [sb(f"qdec{i}", [128, 512], BF16) for i in range(2)]
    kdec = [sb(f"kdec{i}", [128, 512], BF16) for i in range(3)]
    vb = [sb(f"vb{i}", [128, 512]) for i in range(3)]
    egbc = sb("egbc", [128, 512])
    E1 = sb("E1", [128, 512])
    E2 = sb("E2", [128, 512])
    t1 = sb("t1", [128, 512])
    t2 = sb("t2", [128, 512])
    Lb = [LU[:, i * 512:(i + 1) * 512] for i in range(2)]
    Ub = [LU[:, (2 + i) * 512:(3 + i) * 512] for i in range(2)]
    Yb = [sb(f"Yb{i}", [128, 512]) for i in range(2)]
    Offb = sb("Offb", [128, 512])
    Ybf = [sb(f"Ybf{i}", [128, 512], BF16) for i in range(2)]
    attnT = [sb(f"attnT{i}", [128, 512], BF16) for i in range(2)]
    rv = [sb(f"rv{i}", [128, 128], BF16) for i in range(2)]
    vnw = [sb(f"vnw{i}", [128, 128], BF16) for i in range(2)]
    Sst = sb("Sst", [128, 8 * 128])
    Sbf = sb("Sbf", [128, 8 * 128], BF16)
    ob = sb("ob", [128, 512])
    tkU = sb("tkU", [128, 32])
    tkG = sb("tkG", [128, 32])
    tkBeta = sb("tkBeta", [128, 32])
    tkGc = sb("tkGc", [128, 32])
    tkNbeg = sb("tkNbeg", [128, 32])
    tkGl = sb("tkGl", [128, 32])
    tkDks = sb("tkDks", [128, 32])
    tkEgl = sb("tkEgl", [128, 32])
    negA = sb("negA", [128, 8])
    sb = sb_keep

    def emit_gdn_layer(l):
        slot = l
        g0 = SP_GDN + 113 * l
        P.op("pool", lambda e: e.dma_start(out=v3(wba, 8), in_=w_ba_d[l].rearrange("(c p) n -> p c n", p=128)),
             writes=["wba"], dma=True)
        P.op("pool", lambda e: e.memset(carry, 0.0), writes=["carry"])
        P.op("pool", lambda e: e.memset(Sst, 0.0), writes=[f"S{i}" for i in range(8)])
        P.op("pool", lambda e: e.memset(Sbf, 0.0), writes=[f"Sbf{i}" for i in range(8)])
        P.op("act", lambda e: e.activation(out=negA, in_=spm[:, g0 + 97:g0 + 105], func=AF.Exp), reads=["spm"], writes=["negA"])
        P.op("dve", lambda e: e.tensor_scalar(out=negA, in0=negA, scalar1=-1.0, scalar2=None, op0=ALU.mult), reads=["negA"], writes=["negA"])
        for tb in range(NB):
            emit_norm_block(tb, slot, 24 * l)
            for n in range(4):
                for c in range(8):
                    P.op("pe", lambda e, n=n, c=c: e.matmul(
                        ps[7][:, n * 16:(n + 1) * 16], hT[:, c * 512 + n * 128:c * 512 + (n + 1) * 128],
                        wba[:, c * 16:(c + 1) * 16], start=(c == 0), stop=(c == 7)),
                        reads=["hT", "wba"], writes=["ps7a"])
            ba3 = v3(ps[7][:, 0:64], 4)
            for n in range(4):
                P.op("dve", lambda e, n=n: e.tensor_tensor(out=tkU[:, n * 8:(n + 1) * 8], in0=ps[7][:, n * 16 + 8:n * 16 + 16],
                                                          in1=spm[:, g0 + 105:g0 + 113], op=ALU.add),
                     reads=["ps7a", "spm"], writes=["tkU"])
            P.op("act", lambda e: e.activation(out=v3(tkBeta, 4), in_=ba3[:, :, 0:8], func=AF.Exp, scale=-1.0),
                 reads=["ps7a"], writes=["tkBeta"])
            P.op("act", lambda e: e.activation(out=tkU, in_=tkU, func=AF.Exp), reads=["tkU"], writes=["tkU"])
            P.op("act", lambda e: e.activation(out=tkU, in_=tkU, func=AF.Ln, bias=1.0, scale=1.0), reads=["tkU"], writes=["tkU"])
            for n in range(4):
                P.op("dve", lambda e, n=n: e.tensor_tensor(out=tkG[:, n * 8:(n + 1) * 8], in0=tkU[:, n * 8:(n + 1) * 8],
                                                          in1=negA, op=ALU.mult), reads=["tkU", "negA"], writes=["tkG"])
            P.op("dve", lambda e: e.tensor_scalar(out=tkBeta, in0=tkBeta, scalar1=1.0, scalar2=None, op0=ALU.add),
                 reads=["tkBeta"], writes=["tkBeta"])
            P.op("dve", lambda e: e.reciprocal(out=tkBeta, in_=tkBeta), reads=["tkBeta"], writes=["tkBeta"])
            for n in range(4):
                P.op("pe", lambda e, n=n: e.matmul(ps[7][:, 64 + n * 8:64 + (n + 1) * 8], cm(C_TRIU_I), tkG[:, n * 8:(n + 1) * 8],
                                                   start=True, stop=True), reads=["cst", "tkG"], writes=["ps7b"])
                P.op("pe", lambda e, n=n: e.matmul(ps[7][:, 96 + n * 8:96 + (n + 1) * 8], cm(C_ONES), tkG[:, n * 8:(n + 1) * 8],
                                                   start=True, stop=True), reads=["cst", "tkG"], writes=["ps7b"])
            P.op("dve", lambda e: e.tensor_copy(out=tkGc, in_=ps[7][:, 64:96]), reads=["ps7b"], writes=["tkGc"])
            P.op("dve", lambda e: e.tensor_copy(out=tkGl, in_=ps[7][:, 96:128]), reads=["ps7b"], writes=["tkGl"])
            P.op("act", lambda e: e.activation(out=tkEgl, in_=tkGl, func=AF.Exp), reads=["tkGl"], writes=["tkEgl"])
            P.op("dve", lambda e: e.tensor_tensor(out=tkDks, in0=tkGl, in1=tkGc, op=ALU.subtract), reads=["tkGl", "tkGc"], writes=["tkDks"])
            P.op("act", lambda e: e.activation(out=tkDks, in_=tkDks, func=AF.Exp), reads=["tkDks"], writes=["tkDks"])
            P.op("act", lambda e: e.activation(out=tkNbeg, in_=tkGc, func=AF.Exp), reads=["tkGc"], writes=["tkNbeg"])
            P.op("dve", lambda e: e.scalar_tensor_tensor(out=tkNbeg, in0=tkNbeg, scalar=-1.0, in1=tkBeta, op0=ALU.mult, op1=ALU.mult),
                 reads=["tkNbeg", "tkBeta"], writes=["tkNbeg"])

            emit_gdn_block_heads(l, tb, g0)
            emit_outproj(w_outa_d[l], tb, slot)

    def run_interleaved(gens, weights=None):
        gens = list(gens)
        weights = list(weights) if weights is not None else [1] * len(gens)
        while gens:
            for g_, w_ in list(zip(gens, weights)):
                for _ in range(w_):
                    try:
                        next(g_)
                    except StopIteration:
                        k_ = gens.index(g_)
                        gens.pop(k_)
                        weights.pop(k_)
                        break

    def gdn_A(l, tb, h, g0):
        a = h % 3
        wb = wq[h % 2]
        wtk = f"wq{h % 2}"
        P.op("pool", lambda e: e.dma_start(out=v3(wb, 8), in_=wview(w_inp_d[l], h * 512, 512)), writes=[wtk], dma=True)
        yield
        def proj(j, bk):
            for c in range(8):
                P.op("pe", lambda e, c=c: e.matmul(ps[bk], wb[:, c * 512 + j * 128:c * 512 + (j + 1) * 128],
                                                   hT[:, c * 512:(c + 1) * 512], start=(c == 0), stop=(c == 7)),
                     reads=[wtk, "hT"], writes=[f"ps{bk}"])
                yield

        def evac(j, bk):
            cc = (h * 3 + j) * 4
            P.op("pool", lambda e: e.tensor_copy(out=pre[j][:, 0:3], in_=carry[:, cc:cc + 3]),
                 reads=["carry"], writes=[f"pre{j}"])
            P.op("act", lambda e: e.copy(out=pre[j][:, 3:515], in_=ps[bk]), reads=[f"ps{bk}"], writes=[f"pre{j}"])
            yield
            P.op("pool", lambda e: e.tensor_copy(out=carry[:, cc:cc + 3], in_=pre[j][:, 512:515]),
                 reads=[f"pre{j}"], writes=["carry"])
            yield

        yield from proj(0, 0)
        yield from proj(1, 1)
        yield from evac(0, 0)
        yield from evac(1, 1)
        yield from proj(2, 0)
        yield from proj(3, 1)
        yield from evac(2, 0)
        P.op("act", lambda e: e.activation(out=zs[a], in_=ps[1], func=AF.Silu), reads=["ps1"], writes=[f"zs{a}"])
        yield
        wc = lambda tap, j: spm[:, g0 + 24 * tap + 8 * j + h:g0 + 24 * tap + 8 * j + h + 1]
        for tap in range(4):
            for j in range(3):
                if tap == 0:
                    P.op("dve", lambda e, j=j: e.tensor_scalar(out=acc[j], in0=pre[j][:, 0:512], scalar1=wc(0, j), scalar2=0.0,
                                                               op0=ALU.mult, op1=ALU.add), reads=[f"pre{j}", "spm"], writes=[f"acc{j}"])
                else:
                    P.op("dve", lambda e, j=j, tap=tap: e.scalar_tensor_tensor(
                        out=acc[j], in0=pre[j][:, tap:tap + 512], scalar=wc(tap, j), in1=acc[j], op0=ALU.mult, op1=ALU.add),
                        reads=[f"pre{j}", "spm", f"acc{j}"], writes=[f"acc{j}"])
                yield
        for j in range(3):
            P.op("act", lambda e, j=j: e.activation(out=acc[j], in_=acc[j], func=AF.Silu), reads=[f"acc{j}"], writes=[f"acc{j}"])
            yield
        for j in range(2):
            P.op("act", lambda e, j=j: e.activation(out=sqb[:, j * 512:(j + 1) * 512], in_=acc[j], func=AF.Square),
                 reads=[f"acc{j}"], writes=[f"sqb{j}"])
            yield
            P.op("pe", lambda e, j=j: e.matmul(ps[j], cm(C_ONES, True), sqb[:, j * 512:(j + 1) * 512], start=True, stop=True),
                 reads=[f"sqb{j}", "cstb"], writes=[f"ps{j}"])
            yield
        for j in range(2):
            lb = lnb if j == 0 else rn
            tk_ = "lnb" if j == 0 else "rn"
            P.op("act", lambda e, j=j, lb=lb: e.activation(out=lb, in_=ps[j], func=AF.Ln, bias=EPS, scale=1.0),
                 reads=[f"ps{j}"], writes=[tk_])
            yield
            P.op("act", lambda e, j=j, lb=lb: e.activation(out=lb, in_=lb, func=AF.Exp, scale=-0.5,
                                                          bias=(float(np.log(128.0 ** -0.5)) if j == 0 else 0.0)),
                 reads=[tk_], writes=[tk_])
            yield
        P.op("dve", lambda e: e.tensor_tensor(out=qTb[a], in0=acc[0], in1=lnb, op=ALU.mult), reads=["acc0", "lnb"], writes=[f"qTb{a}"])
        yield
        P.op("dve", lambda e: e.tensor_tensor(out=acc[1], in0=acc[1], in1=rn, op=ALU.mult), reads=["acc1", "rn"], writes=["acc1"])
        yield
        P.op("pool", lambda e: e.tensor_copy(out=kTb[a], in_=acc[1]), reads=["acc1"], writes=[f"kTb{a}"])
        yield
        for n in range(4):
            sl = slice(n * 128, (n + 1) * 128)
            P.op("pe", lambda e, sl=sl: e.transpose(ps[0][:, sl], acc[1][:, sl], cm(C_IDENT)), reads=["acc1", "cst"], writes=["ps0"])
            yield
        for n in range(4):
            sl = slice(n * 128, (n + 1) * 128)
            P.op("pe", lambda e, sl=sl: e.transpose(ps[1][:, sl], acc[2][:, sl], cm(C_IDENT)), reads=["acc2", "cst"], writes=["ps1"])
            yield
        for n in range(4):
            sl = slice(n * 128, (n + 1) * 128)
            col = n * 8 + h
            P.op("act", lambda e, sl=sl, col=col: e.activation(out=kdec[a][:, sl], in_=ps[0][:, sl], func=AF.Identity,
                                                               scale=tkDks[:, col:col + 1]), reads=["ps0", "tkDks"], writes=[f"kdec{a}"])
            yield
        for n in range(4):
            sl = slice(n * 128, (n + 1) * 128)
            col = n * 8 + h
            P.op("act", lambda e, sl=sl, col=col: e.activation(out=vb[a][:, sl], in_=ps[1][:, sl], func=AF.Identity,
                                                               scale=tkBeta[:, col:col + 1]), reads=["ps1", "tkBeta"], writes=[f"vb{a}"])
            yield

    def gdn_B(l, tb, h, g0):
        a = h % 3
        bs = h % 2
        q_, k_, kd_, vb_, zs_ = qTb[a], kTb[a], kdec[a], vb[a], zs[a]
        qt, kt, kdt, vbt, zst = f"qTb{a}", f"kTb{a}", f"kdec{a}", f"vb{a}", f"zs{a}"
        tiles = [(n, slice(n * 128, (n + 1) * 128), n * 8 + h) for n in range(4)]
        for n, sl, col in tiles:
            P.op("pe", lambda e, sl=sl: e.matmul(ps[3][:, sl], k_[:, sl], k_[:, sl], start=True, stop=True), reads=[kt], writes=["ps3"])
            P.op("pe", lambda e, sl=sl: e.matmul(ps[4][:, sl], k_[:, sl], q_[:, sl], start=True, stop=True), reads=[kt, qt], writes=["ps4"])
            P.op("pool", lambda e, sl=sl, col=col: e.tensor_scalar(out=E1[:, sl], in0=cm(C_TRIU_I), scalar1=tkG[:, col:col + 1], scalar2=0.0,
                                                                   op0=ALU.mult, op1=ALU.add), reads=["cst", "tkG"], writes=["E1"])
            yield
            P.op("pe", lambda e, sl=sl: e.matmul(ps[5][:, sl], cm(C_ONES), E1[:, sl], start=True, stop=True), reads=["cst", "E1"], writes=["ps5"])
            yield
        P.op("act", lambda e: e.activation(out=egbc, in_=ps[5], func=AF.Exp), reads=["ps5"], writes=["egbc"])
        yield
        for n, sl, col in tiles:
            P.op("dve", lambda e, sl=sl, col=col: e.tensor_scalar(out=E1[:, sl], in0=ps[5][:, sl], scalar1=tkGc[:, col:col + 1], scalar2=0.0,
                                                                  op0=ALU.subtract, op1=ALU.min), reads=["ps5", "tkGc"], writes=["E1"])
            yield
            P.op("dve", lambda e, sl=sl, col=col: e.tensor_scalar(out=E2[:, sl], in0=ps[5][:, sl], scalar1=tkGc[:, col:col + 1], scalar2=0.0,
                                                                  op0=ALU.subtract, op1=ALU.max), reads=["ps5", "tkGc"], writes=["E2"])
            yield
        P.op("act", lambda e: e.activation(out=E1, in_=E1, func=AF.Exp), reads=["E1"], writes=["E1"])
        yield
        P.op("act", lambda e: e.activation(out=E2, in_=E2, func=AF.Exp, scale=-1.0), reads=["E2"], writes=["E2"])
        yield
        for n, sl, col in tiles:
            P.op("dve", lambda e, sl=sl, col=col: e.scalar_tensor_tensor(out=t1[:, sl], in0=ps[3][:, sl], scalar=tkBeta[:, col:col + 1],
                                                                      in1=E2[:, sl], op0=ALU.mult, op1=ALU.mult),
                 reads=["ps3", "E2", "tkBeta"], writes=["t1"])
            yield
        P.op("dve", lambda e: e.tensor_tensor(out=t2, in0=ps[4], in1=E1, op=ALU.mult), reads=["ps4", "E1"], writes=["t2"])
        yield
        P.op("pool", lambda e: e.tensor_tensor(out=qdec[bs], in0=q_, in1=egbc, op=ALU.mult), reads=[qt, "egbc"], writes=[f"qdec{bs}"])
        yield
        L0, U0, Y0 = Lb[0], Ub[0], Yb[0]
        for n, sl, col in tiles:
            P.op("pool", lambda e, sl=sl: e.tensor_tensor(out=L0[:, sl], in0=t1[:, sl], in1=cm(C_BDTRIL_S), op=ALU.mult),
                 reads=["t1", "cst"], writes=["Lb0"])
            yield
        for n, sl, col in tiles:
            P.op("pe", lambda e, sl=sl: e.transpose(ps[4][:, sl], L0[:, sl], cm(C_IDENT)), reads=["Lb0", "cst"], writes=["ps4"])
            yield
        P.op("act", lambda e: e.copy(out=U0, in_=ps[4]), reads=["ps4"], writes=["Ub0"])
        yield
        for n, sl, col in tiles:
            P.op("pool", lambda e, sl=sl: e.tensor_tensor(out=Y0[:, sl], in0=cm(C_IDENT), in1=U0[:, sl], op=ALU.subtract),
                 reads=["cst", "Ub0"], writes=["Yb0"])
            yield
            P.op("pool", lambda e, sl=sl: e.tensor_tensor(out=Offb[:, sl], in0=t1[:, sl], in1=cm(C_OFF), op=ALU.mult),
                 reads=["t1", "cst"], writes=["Offb"])
            yield
            P.op("pool", lambda e, sl=sl: e.tensor_tensor(out=attnT[bs][:, sl], in0=t2[:, sl], in1=cm(C_TRIU_I), op=ALU.mult),
                 reads=["t2", "cst"], writes=[f"attnT{bs}"])
            yield
        cur = 0
        for k in range(1, 6):
            nx = 1 - cur
            for n, sl, col in tiles:
                P.op("pe", lambda e, sl=sl, cur=cur: e.matmul(ps[3][:, sl], Ub[cur][:, sl], Lb[cur][:, sl], start=True, stop=True),
                     reads=[f"Ub{cur}", f"Lb{cur}"], writes=["ps3"])
                yield
            if k < 5:
                for n, sl, col in tiles:
                    P.op("pe", lambda e, sl=sl, cur=cur: e.matmul(ps[4][:, sl], Lb[cur][:, sl], Ub[cur][:, sl], start=True, stop=True),
                         reads=[f"Ub{cur}", f"Lb{cur}"], writes=["ps4"])
                    yield
            P.op("act", lambda e, nx=nx: e.copy(out=Lb[nx], in_=ps[3]), reads=["ps3"], writes=[f"Lb{nx}"])
            yield
            if k < 5:
                P.op("dve", lambda e, nx=nx: e.tensor_copy(out=Ub[nx], in_=ps[4]), reads=["ps4"], writes=[f"Ub{nx}"])
                yield
            for n, sl, col in tiles:
                P.op("pe", lambda e, sl=sl, nx=nx, cur=cur: e.matmul(ps[5][:, sl], Lb[nx][:, sl], Yb[cur][:, sl], start=True, stop=True),
                     reads=[f"Lb{nx}", f"Yb{cur}"], writes=["ps5"])
                yield
            P.op("dve", lambda e, nx=nx, cur=cur: e.tensor_tensor(out=Yb[nx], in0=ps[5], in1=Yb[cur], op=ALU.add),
                 reads=["ps5", f"Yb{cur}"], writes=[f"Yb{nx}"])
            yield
            cur = nx
        Yd = Yb[cur]
        ytk = f"Yb{cur}"
        for n, sl, col in tiles:
            P.op("pe", lambda e, sl=sl: e.transpose(ps[4][:, sl], Yd[:, sl], cm(C_IDENT)), reads=[ytk, "cst"], writes=["ps4"])
            P.op("pe", lambda e, sl=sl: e.matmul(ps[3][:, sl], Offb[:, sl], Yd[:, sl], start=True, stop=True), reads=["Offb", ytk], writes=["ps3"])
            yield
        P.op("act", lambda e: e.copy(out=t1, in_=ps[4]), reads=["ps4"], writes=["t1"])
        yield
        P.op("dve", lambda e: e.tensor_copy(out=t2, in_=ps[3]), reads=["ps3"], writes=["t2"])
        yield
        for n, sl, col in tiles:
            P.op("pe", lambda e, sl=sl: e.matmul(ps[5][:, sl], t1[:, sl], t2[:, sl], start=True, stop=True), reads=["t1", "t2"], writes=["ps5"])
            yield
        P.op("dve", lambda e: e.tensor_tensor(out=Ybf[bs], in0=Yd, in1=ps[5], op=ALU.subtract), reads=[ytk, "ps5"], writes=[f"Ybf{bs}"])
        yield
    def gdn_C(l, tb, h, g0):
        a = h % 3
        bs = h % 2
        q_, k_, kd_, vb_, zs_ = qTb[a], kTb[a], kdec[a], vb[a], zs[a]
        qt, kt, kdt, vbt, zst = f"qTb{a}", f"kTb{a}", f"kdec{a}", f"vb{a}", f"zs{a}"
        tiles = [(n, slice(n * 128, (n + 1) * 128), n * 8 + h) for n in range(4)]
        Sh = Sst[:, h * 128:(h + 1) * 128]
        Shb = Sbf[:, h * 128:(h + 1) * 128]
        for n, sl, col in tiles:
            r_ = rv[n % 2]
            v_ = vnw[n % 2]
            P.op("pe", lambda e, sl=sl: e.matmul(ps[7][:, 128:256], k_[:, sl], Shb, start=True, stop=True),
                 reads=[kt, f"Sbf{h}"], writes=["ps7"])
            yield
            P.op("dve", lambda e, sl=sl, col=col, r_=r_: e.scalar_tensor_tensor(out=r_, in0=ps[7][:, 128:256], scalar=tkNbeg[:, col:col + 1],
                                                                            in1=vb_[:, sl], op0=ALU.mult, op1=ALU.add),
                 reads=["ps7", "tkNbeg", vbt], writes=[f"rv{n % 2}"])
            yield
            P.op("pe", lambda e, sl=sl, r_=r_: e.matmul(ps[7][:, 384:512], Ybf[bs][:, sl], r_, start=True, stop=True),
                 reads=[f"Ybf{bs}", f"rv{n % 2}"], writes=["ps7"])
            yield
            P.op("act", lambda e, v_=v_: e.copy(out=v_, in_=ps[7][:, 384:512]), reads=["ps7"], writes=[f"vnw{n % 2}"])
            yield
            P.op("pe", lambda e, sl=sl: e.matmul(ps[6][:, sl], Shb, qdec[bs][:, sl], start=True, stop=False),
                 reads=[f"Sbf{h}", f"qdec{bs}"], writes=["ps6"])
            P.op("pe", lambda e, sl=sl, v_=v_: e.matmul(ps[6][:, sl], v_, attnT[bs][:, sl], start=False, stop=True),
                 reads=[f"vnw{n % 2}", f"attnT{bs}"], writes=["ps6"])
            yield
            P.op("pe", lambda e, sl=sl, v_=v_: e.matmul(ps[7][:, 256:384], kd_[:, sl], v_, start=True, stop=True),
                 reads=[kdt, f"vnw{n % 2}"], writes=["ps7"])
            yield
            P.op("dve", lambda e, col=col: e.scalar_tensor_tensor(out=Sh, in0=Sh, scalar=tkEgl[:, col:col + 1], in1=ps[7][:, 256:384],
                                                                  op0=ALU.mult, op1=ALU.add),
                 reads=[f"S{h}", "tkEgl", "ps7"], writes=[f"S{h}"])
            yield
            P.op("pool", lambda e: e.tensor_copy(out=Shb, in_=Sh), reads=[f"S{h}"], writes=[f"Sbf{h}"])
            yield
        P.op("act", lambda e: e.copy(out=ob, in_=ps[6]), reads=["ps6"], writes=["ob"])
        yield
        P.op("act", lambda e: e.activation(out=sqc, in_=ob, func=AF.Square), reads=["ob"], writes=["sqc"])
        yield
        P.op("pe", lambda e: e.matmul(ps[6], cm(C_ONES, True), sqc, start=True, stop=True), reads=["sqc", "cstb"], writes=["ps6"])
        yield
        P.op("act", lambda e: e.activation(out=rn3, in_=ps[6], func=AF.Ln, bias=EPS, scale=1.0 / 128), reads=["ps6"], writes=["rn3"])
        yield
        P.op("act", lambda e: e.activation(out=rn3, in_=rn3, func=AF.Exp, scale=-0.5), reads=["rn3"], writes=["rn3"])
        yield
        P.op("dve", lambda e: e.tensor_tensor(out=ob, in0=ob, in1=rn3, op=ALU.mult), reads=["ob", "rn3"], writes=["ob"])
        yield
        P.op("dve", lambda e: e.scalar_tensor_tensor(out=ogT[:, h * 512:(h + 1) * 512], in0=ob, scalar=spm[:, g0 + 96:g0 + 97], in1=zs_,
                                                     op0=ALU.mult, op1=ALU.mult), reads=["ob", "spm", zst], writes=["ogT"])
        yield

    aux = [None]

    def emit_gdn_block_heads(l, tb, g0):
        for step in range(-2, 8):
            gens = []
            wts = []
            if aux[0] is not None:
                gens.append(take(aux[0], AUX_N))
                wts.append(1)
            if 0 <= step < 8:
                gens.append(gdn_C(l, tb, step, g0))
                wts.append(GDN_W[0])
            if 0 <= step + 1 < 8:
                gens.append(gdn_B(l, tb, step + 1, g0))
                wts.append(GDN_W[1])
            if 0 <= step + 2 < 8:
                gens.append(gdn_A(l, tb, step + 2, g0))
                wts.append(GDN_W[2])
            run_interleaved(gens, wts)

    bar_n = [0]

    def emit_barrier():
        k = bar_n[0]
        bar_n[0] += 1
        P.mark_segment()
        P.op("act", lambda e: e.activation(out=bscr[:, 0:1], in_=cact[:, 0:1], func=AF.Identity), reads=["cact"], writes=[f"bar{k}_act", "bscr0"])
        P.op("dve", lambda e: e.tensor_copy(out=bscr[:, 1:2], in_=cact[:, 0:1]), reads=["cact"], writes=[f"bar{k}_dve", "bscr1"])
        P.op("pool", lambda e: e.tensor_copy(out=bscr[:, 2:3], in_=cact[:, 0:1]), reads=["cact"], writes=[f"bar{k}_pool", "bscr2"])
        P.op("pe", lambda e: e.matmul(ps[7][:, 0:8], cm(C_ONES), cact[:, 0:8], start=True, stop=True), reads=["cact", "cst"], writes=["ps7", f"bar{k}_pe"])
        allb = [f"bar{k}_{x}" for x in ("act", "dve", "pool", "pe")]
        P.op("act", lambda e: e.activation(out=bscr[:, 3:4], in_=cact[:, 0:1], func=AF.Identity), reads=allb + ["cact"], writes=["bscr3"])
        P.op("dve", lambda e: e.tensor_copy(out=bscr[:, 4:5], in_=ps[7][:, 0:1]), reads=allb, writes=["bscr4", "ps7"])
        P.op("pool", lambda e: e.tensor_copy(out=bscr[:, 5:6], in_=cact[:, 0:1]), reads=allb + ["cact"], writes=["bscr5"])
        P.op("pe", lambda e: e.matmul(ps[7][:, 0:8], cm(C_ONES), cact[:, 0:8], start=True, stop=True), reads=allb + ["cact", "cst"], writes=["ps7"])
        P.op("sp", lambda e: e.dma_start(out=bscr[:, 6:8], in_=spm_d[:, 0:2]), reads=allb, writes=["bscr6"], dma=True)
        P.mark_segment()

    emit_barrier()
    n_a = min(nlayers, 2)
    for l in range(n_a):
        if l == 0 and nlayers > 1:
            aux[0] = gen_layer_mod(1)
        if l == 1 and nlayers > 2:
            gl_ = [gen_kv_mod(), gen_layer_mod(2)]
            if nlayers > 3:
                gl_.append(gen_layer_mod(3))
            aux[0] = chain(*gl_)
        emit_gdn_layer(l)
        if aux[0] is not None:
            drain(aux[0])
            aux[0] = None
    emit_barrier()
    gstack.close()
    sb = sb_keep

    ost2 = [sb(f"Eo{i}", [128, 512]) for i in range(2)]
    if nlayers > 2:
        KT = sb("KT", [128, 8 * T], BF16)
        Vt = sb("Vt", [128, 16 * D], BF16)
        qn = sb("qn", [128, 8 * 512], BF16)
        Ebuf = ost2
        SPR = [[sb(f"SPR{i}{u}", [128, 512], F32R) for u in range(2)] for i in range(2)]
        dbf = [[sb(f"dbf{i}{u}", [128, 512]) for u in range(2)] for i in range(2)]
        Wb = [sb(f"Wb{i}", [128, 512], BF16) for i in range(2)]
        rn2 = tmpn[1]
        cstr = sb("cstr", [128, 256], F32R)
        P.op("dve", lambda e: e.tensor_copy(out=cstr[:, 0:128], in_=cm(C_TRIL_S)), reads=["cst"], writes=["cstr"])
        P.op("dve", lambda e: e.tensor_copy(out=cstr[:, 128:256], in_=cm(C_TRIU_I)), reads=["cst"], writes=["cstr"])

        def emit_headnorm(psb, gain_col, extra_bias, dst, dst_tok):
            tm = tmpn[0]
            P.op("act", lambda e: e.copy(out=tm, in_=psb), reads=[psb_tok[0]], writes=["tmpn0"])
            P.op("act", lambda e: e.activation(out=sqb[:, 0:512], in_=tm, func=AF.Square), reads=["tmpn0"], writes=["sqb0"])
            P.op("pe", lambda e: e.matmul(ps[4], cm(C_BDONES, True), sqb[:, 0:512], start=True, stop=True), reads=["sqb0", "cstb"], writes=["ps4"])
            P.op("act", lambda e: e.activation(out=rn2, in_=ps[4], func=AF.Ln, bias=EPS, scale=1.0 / 64), reads=["ps4"], writes=["tmpn1"])
            P.op("act", lambda e: e.activation(out=rn2, in_=rn2, func=AF.Exp, scale=-0.5, bias=extra_bias), reads=["tmpn1"], writes=["tmpn1"])
            P.op("dve", lambda e: e.scalar_tensor_tensor(out=dst, in0=tm, scalar=spm[:, gain_col:gain_col + 1], in1=rn2,
                                                         op0=ALU.mult, op1=ALU.mult), reads=["tmpn0", "spm", "tmpn1"], writes=[dst_tok])

        psb_tok = [None]

        def emit_kv():
            for tb in range(NB):
                emit_norm_block(tb, 4, 96)
                for piece in range(2):
                    P.op("pool", lambda e, piece=piece: e.dma_start(out=v3(wq[piece], 8), in_=wview(w_kv_d, piece * 512, 512)),
                         writes=[f"wq{piece}"], dma=True)
                    for fcl in range(4):
                        fc = piece * 4 + fcl
                        bk = fc % 4
                        for c in range(8):
                            P.op("pe", lambda e, piece=piece, fcl=fcl, c=c, bk=bk: e.matmul(
                                ps[bk], wq[piece][:, c * 512 + fcl * 128:c * 512 + (fcl + 1) * 128], hT[:, c * 512:(c + 1) * 512],
                                start=(c == 0), stop=(c == 7)), reads=[f"wq{piece}", "hT"], writes=[f"ps{bk}"])
                        psb_tok[0] = f"ps{bk}"
                        emit_headnorm(ps[bk], SP_KG, 0.0, KT[:, fc * T + tb * 512:fc * T + (tb + 1) * 512], "KT")
                for piece in range(2):
                    P.op("pool", lambda e, piece=piece: e.dma_start(out=v3(wq[piece], 8), in_=wview(w_kv_d, D + piece * 512, 512)),
                         writes=[f"wq{piece}"], dma=True)
                    for n in range(4):
                        bk = n % 4
                        tile = tb * 4 + n
                        for c in range(8):
                            P.op("pe", lambda e, piece=piece, n=n, c=c, bk=bk: e.matmul(
                                ps[bk], hT[:, c * 512 + n * 128:c * 512 + (n + 1) * 128], wq[piece][:, c * 512:(c + 1) * 512],
                                start=(c == 0), stop=(c == 7)), reads=[f"wq{piece}", "hT"], writes=[f"ps{bk}"])
                        dst = Vt[:, tile * D + piece * 512:tile * D + (piece + 1) * 512]
                        if n % 2 == 0:
                            P.op("act", lambda e, dst=dst, bk=bk: e.copy(out=dst, in_=ps[bk]), reads=[f"ps{bk}"], writes=["Vt"])
                        else:
                            P.op("dve", lambda e, dst=dst, bk=bk: e.tensor_copy(out=dst, in_=ps[bk]), reads=[f"ps{bk}"], writes=["Vt"])

        def sb_head(g, ch, hh, s_):
            h = 2 * ch + hh
            base = hh * 64
            bA, bB = (0, 1) if s_ == 0 else (2, 3)
            Eb, wbu = Ebuf[s_], Wb[s_]
            chunks = list(range(4 * g + 3, -1, -1))

            def geom(i):
                r0 = max(i - 4 * g, 0)
                return r0 * 128, (i >= 4 * g)

            def warm():
                for _ in range(SB_WARM):
                    P.op("pe", lambda e: e.matmul(ps[5][:, 0:128], cm(C_ONES, True), cm(C_IDENT, True), start=True, stop=True),
                         reads=["cstb"], writes=["ps5"])

            def front(k):
                i = chunks[k]
                c0, diag = geom(i)
                u = k % 2
                Sr = SPR[s_][u]
                P.op("pe", lambda e: e.matmul(
                    ps[bA][:, c0:512], KT[base:base + 64, ch * T + i * 128:ch * T + (i + 1) * 128],
                    qn[base:base + 64, ch * 512 + c0:ch * 512 + 512], start=True, stop=True),
                    reads=["KT", f"qn{ch}"], writes=[f"ps{bA}"])
                yield
                P.op("act", lambda e: e.activation(out=Eb[:, c0:512], in_=ps[bA][:, c0:512], func=AF.Exp),
                     reads=[f"ps{bA}"], writes=[f"E{s_}"])
                yield
                P.op("act", lambda e: e.activation(out=Sr[:, c0:512], in_=Eb[:, c0:512], func=AF.Ln, bias=1.0, scale=1.0),
                     reads=[f"E{s_}"], writes=[f"SPR{s_}{u}"])
                yield
                if diag:
                    P.op("pool", lambda e: e.tensor_tensor(out=Sr[:, c0:c0 + 128], in0=Sr[:, c0:c0 + 128].bitcast(F32),
                                                           in1=cm(C_TRIU_S), op=ALU.mult),
                         reads=[f"SPR{s_}{u}", "cst"], writes=[f"SPR{s_}{u}"])
                    yield

            def back1(k):
                i = chunks[k]
                c0, diag = geom(i)
                u = k % 2
                Sr, dbu = SPR[s_][u], dbf[s_][u]
                P.op("pe", lambda e: e.matmul(
                    ps[bB][:, c0:512], cstr[:, 0:128], Sr[:, c0:512], start=(k == 0), stop=False, skip_group_check=True),
                    reads=[f"SPR{s_}{u}", "cstr"], writes=[f"ps{bB}"])
                yield
                P.op("dve", lambda e: e.tensor_tensor(
                    out=dbu[:, c0:512], in0=ps[bA][:, c0:512], in1=Sr[:, c0:512].bitcast(F32), op=ALU.subtract),
                    reads=[f"ps{bA}", f"SPR{s_}{u}"], writes=[f"dbf{s_}{u}"])
                yield

            def back2(k):
                i = chunks[k]
                c0, diag = geom(i)
                u = k % 2
                Sr, dbu = SPR[s_][u], dbf[s_][u]
                P.op("dve", lambda e: e.tensor_tensor(
                    out=dbu[:, c0:512], in0=dbu[:, c0:512], in1=ps[bB][:, c0:512], op=ALU.subtract),
                    reads=[f"ps{bB}", f"dbf{s_}{u}"], writes=[f"dbf{s_}{u}"])
                yield
                if i > 0:
                    warm()
                    P.op("pe", lambda e: e.matmul(
                        ps[bB][:, c0:512], cstr[:, 128:256], Sr[:, c0:512], start=False, stop=False, skip_group_check=True),
                        reads=[f"SPR{s_}{u}", "cstr"], writes=[f"ps{bB}"])
                    yield
                P.op("act", lambda e: e.activation(out=wbu[:, c0:512], in_=dbu[:, c0:512], func=AF.Exp),
                     reads=[f"dbf{s_}{u}"], writes=[f"Wb{s_}"])
                yield
                if diag:
                    P.op("pool", lambda e: e.tensor_tensor(out=wbu[:, c0:c0 + 128], in0=wbu[:, c0:c0 + 128],
                                                           in1=cm(C_TRIU_S, True), op=ALU.mult),
                         reads=[f"Wb{s_}", "cstb"], writes=[f"Wb{s_}"])
                    yield
                P.op("pe", lambda e: e.matmul(
                    ps[6][base:base + 64, c0:512], Vt[:, i * D + h * 64:i * D + (h + 1) * 64], wbu[:, c0:512],
                    start=False, stop=False, skip_group_check=True, tile_position=(0, base)),
                    reads=["Vt", f"Wb{s_}"], writes=["ps6"])
                yield

            yield from front(0)
            for k in range(len(chunks)):
                yield from back1(k)
                if k + 1 < len(chunks):
                    yield from front(k + 1)
                yield from back2(k)

        def run_interleaved(gens):
            gens = list(gens)
            while gens:
                for g_ in list(gens):
                    try:
                        next(g_)
                    except StopIteration:
                        gens.remove(g_)

        def sb_projc(g, l2, fc):
            half, fcl = divmod(fc, 4)
            if fcl == 0:
                P.op("pool", lambda e: e.dma_start(out=v3(wq[0], 8), in_=wview(w_inb_d[l2], half * 512, 512)),
                     writes=["wq0"], dma=True)
                yield
                P.op("pool", lambda e: e.dma_start(out=v3(wq[1], 8), in_=wview(w_inb_d[l2], D + half * 512, 512)),
                     writes=["wq1"], dma=True)
                yield
            for c in range(8):
                P.op("pe", lambda e, c=c: e.matmul(
                    ps[4], wq[0][:, c * 512 + fcl * 128:c * 512 + (fcl + 1) * 128], hT[:, c * 512:(c + 1) * 512],
                    start=(c == 0), stop=(c == 7)), reads=["wq0", "hT"], writes=["ps4"])
                yield
            tm = tmpn[0]
            P.op("act", lambda e: e.copy(out=tm, in_=ps[4]), reads=["ps4"], writes=["tmpn0"])
            yield
            for c in range(8):
                P.op("pe", lambda e, c=c: e.matmul(
                    ps[5], wq[1][:, c * 512 + fcl * 128:c * 512 + (fcl + 1) * 128], hT[:, c * 512:(c + 1) * 512],
                    start=(c == 0), stop=(c == 7)), reads=["wq1", "hT"], writes=["ps5"])
                yield
            P.op("act", lambda e: e.activation(out=sqb[:, 0:512], in_=tm, func=AF.Square), reads=["tmpn0"], writes=["sqb0"])
            yield
            P.op("pe", lambda e: e.matmul(ps[7], cm(C_BDONES, True), sqb[:, 0:512], start=True, stop=True), reads=["sqb0", "cstb"], writes=["ps7"])
            yield
            P.op("act", lambda e: e.activation(out=rn2, in_=ps[7], func=AF.Ln, bias=EPS, scale=1.0 / 64), reads=["ps7"], writes=["tmpn1"])
            yield
            P.op("act", lambda e: e.activation(out=rn2, in_=rn2, func=AF.Exp, scale=-0.5, bias=float(np.log(0.125))), reads=["tmpn1"], writes=["tmpn1"])
            yield
            P.op("dve", lambda e: e.scalar_tensor_tensor(out=qn[:, fc * 512:(fc + 1) * 512], in0=tm, scalar=spm[:, SP_QG + l2:SP_QG + l2 + 1], in1=rn2,
                                                         op0=ALU.mult, op1=ALU.mult), reads=["tmpn0", "spm", "tmpn1"], writes=[f"qn{fc}"])
            yield
            P.op("act", lambda e: e.activation(out=ogT[:, fc * 512:(fc + 1) * 512], in_=ps[5], func=AF.Silu),
                 reads=["ps5"], writes=[f"og{fc}"])
            yield

        def emit_sb_layer(l2):
            L = 2 + l2
            for g in range(NB):
                emit_norm_block(g, L, 24 * L)
                drain(sb_projc(g, l2, 0))
                for ch in range(8):
                    P.op("dve", lambda e: e.memset(ps[6], 0.0), writes=["ps6"])
                    gens = [sb_head(g, ch, 0, 0), sb_head(g, ch, 1, 1)]
                    if ch + 1 < 8:
                        gens.append(sb_projc(g, l2, ch + 1))
                    run_interleaved(gens)
                    P.op("dve", lambda e, ch=ch: e.tensor_tensor(out=ogT[:, ch * 512:(ch + 1) * 512], in0=ps[6], in1=ogT[:, ch * 512:(ch + 1) * 512],
                                                                 op=ALU.mult), reads=["ps6", f"og{ch}"], writes=[f"og{ch}"])
                emit_outproj(w_outb_d[l2], g, L, og_toks=[f"og{i}" for i in range(8)])

        emit_kv()
        for l2 in range(nlayers - 2):
            emit_sb_layer(l2)

    outs = []
    for n in range(16):
        tb = n // 4
        for half in range(2):
            bk = (2 * n + half) % 4
            oi = (2 * n + half) % 2
            for c4 in range(4):
                c = half * 4 + c4
                P.op("pe", lambda e, n=n, c=c, c4=c4, bk=bk: e.transpose(
                    ps[bk][:, c4 * 128:(c4 + 1) * 128], xT[:, c * T + n * 128:c * T + (n + 1) * 128], cm(C_IDENT)),
                    reads=[f"xT{tb}", "cst"], writes=[f"ps{bk}"])
            if half == 0:
                P.op("act", lambda e, oi=oi, bk=bk: e.copy(out=ost2[oi], in_=ps[bk]), reads=[f"ps{bk}"], writes=[f"E{oi}"])
            else:
                P.op("dve", lambda e, oi=oi, bk=bk: e.tensor_copy(out=ost2[oi], in_=ps[bk]), reads=[f"ps{bk}"], writes=[f"E{oi}"])
            outs.append(P.op("sp", lambda e, n=n, half=half, oi=oi: e.dma_start(
                out=out_d[n * 128:(n + 1) * 128, half * 512:(half + 1) * 512], in_=ost2[oi]),
                reads=[f"E{oi}"], dma=True))
    P.finalize(final_wait_ops=outs)
    return nc


def _prep_inputs(inp):
    inp = {k: np.asarray(v) for k, v in inp.items()}
    w_in_a = inp["w_in_a"].astype(np.float32, copy=False)
    qkvz = w_in_a[:, :, :4096].reshape(2, D, 4, 8, 128).transpose(0, 1, 3, 2, 4).reshape(2, D, 4096)
    shared = {
        "cst": _consts(),
        "w_ada": np.ascontiguousarray(inp["w_ada"], np.float32),
        "w_inp": np.ascontiguousarray(qkvz),
        "w_ba": np.ascontiguousarray(w_in_a[:, :, 4096:4112]),
        "w_out_a": np.ascontiguousarray(inp["w_out_a"], np.float32),
        "w_ada_kv": np.ascontiguousarray(inp["w_ada_kv"], np.float32),
        "w_kv": np.ascontiguousarray(inp["w_kv"], np.float32),
        "w_in_b": np.ascontiguousarray(inp["w_in_b"], np.float32),
        "w_out_b": np.ascontiguousarray(inp["w_out_b"], np.float32),
    }
    in_maps = []
    for b in range(8):
        m = dict(shared)
        m["x"] = np.ascontiguousarray(inp["x"][b], np.float32)
        m["spm"] = _small_params(inp, b)
        in_maps.append(m)
    return in_maps


def kernel(**inputs):
    in_maps = _prep_inputs(inputs)
    nc = build(4)
    res = run_bass_kernel_spmd(nc, in_maps, core_ids=list(range(8)))
    return np.stack([np.asarray(r["out"], np.float32) for r in res.results], axis=0)
```

```python
import numpy as np
import concourse.bass as bass
import concourse.mybir as mybir
from concourse.bass_utils import run_bass_kernel_spmd
from contextlib import ExitStack

F32 = mybir.dt.float32
BF16 = mybir.dt.bfloat16
F32R = mybir.dt.float32r
AF = mybir.ActivationFunctionType
ALU = mybir.AluOpType

T = 2048
D = 1024
NB = 4
EPS = 1e-6
AUX_N = 13
SCHEDULE = True
PRIO_BLEVEL = False
PE_F32C = 2.0
XLAT = 0.3
DVE_C0 = 0.22
DVE_R = 1050.0
POOL_C0 = 0.2
POOL_R = 560.0
PE_F32RC = 2.0
DMA_R = 300e3
SB_WARM = 0
GDN_W = (1, 1, 1)


class Tok:
    __slots__ = ("name", "w", "r")

    def __init__(self, name=""):
        self.name = name
        self.w = None
        self.r = []


class Op:
    __slots__ = ("eng", "fn", "deps", "idx", "need_inc", "ev", "is_dma", "dsem", "dval", "seg", "gidx", "cost", "pair")

    def __init__(self, eng, fn):
        self.eng = eng
        self.fn = fn
        self.deps = []
        self.idx = None
        self.need_inc = False
        self.ev = None
        self.is_dma = False


class Prog:
    ENG = ("pe", "act", "dve", "pool", "sp")
    SEM_LIMIT = 30000
    NDMA = 16
    NEAR = 6

    def __init__(self, nc):
        self.nc = nc
        self.q = {e: [] for e in self.ENG}
        self.toks = {}
        self.seg = 0
        self.nops = 0

    def mark_segment(self):
        self.seg += 1

    COST = {"pe": 0.25, "act": 0.55, "dve": 0.6, "pool": 0.45, "sp": 0.15}

    class _Probe:
        def __init__(self):
            self.rec = None

        def __getattr__(self, name):
            def f(*a, **kw):
                self.rec = (name, a, kw)
                return self
            return f

    def estimate(self, o):
        try:
            pr = Prog._Probe()
            o.fn(pr)
            name, a, kw = pr.rec
            out = kw.get("out", a[0] if a else None)
            n = 1
            for d in out.shape[1:]:
                n *= int(d)
            if o.is_dma:
                return 2.0 + out.shape[0] * n * 4 / DMA_R
            if o.eng == "pe":
                if name == "transpose":
                    return 0.12
                lhs = kw.get("lhsT", a[1] if len(a) > 1 else None)
                cyc = {F32: PE_F32C, F32R: PE_F32RC}.get(lhs.dtype, 1.0) * max(n, 64)
                return 0.05 + cyc / 1700.0
            if o.eng == "act":
                return 0.2 + n / 1400.0
            if o.eng == "dve":
                return DVE_C0 + n / DVE_R
            if o.eng == "pool":
                return POOL_C0 + n / POOL_R
        except Exception:
            pass
        return self.COST[o.eng]

    def schedule(self):
        import heapq
        allops = []
        for e in self.ENG:
            allops.extend(self.q[e])
        allops.sort(key=lambda o: o.gidx)
        nseg = self.seg + 1
        bysegs = [[] for _ in range(nseg)]
        for o in allops:
            bysegs[o.seg].append(o)
        newq = {e: [] for e in self.ENG}
        for ops in bysegs:
            if not ops:
                continue
            inseg = set(id(o) for o in ops)
            succ = {}
            indeg = {}
            for o in ops:
                n = 0
                for d in o.deps:
                    if id(d) in inseg:
                        succ.setdefault(id(d), []).append(o)
                        n += 1
                indeg[id(o)] = n
            cost_ = {}
            for o in ops:
                cost_[id(o)] = o.cost if o.cost is not None else self.estimate(o)
            blev = {}
            for o in reversed(ops):
                b_ = 0.0
                for s_ in succ.get(id(o), ()):
                    v_ = blev[id(s_)] + (0.05 if s_.eng == o.eng else XLAT)
                    if v_ > b_:
                        b_ = v_
                blev[id(o)] = b_ + (cost_[id(o)] if not o.is_dma else cost_[id(o)])
            finish = {}
            last_pair = [None]
            free = {e: 0.0 for e in self.ENG}
            heaps = {e: [] for e in self.ENG}
            ready_t = {}
            for o in ops:
                if indeg[id(o)] == 0:
                    ready_t[id(o)] = 0.0
                    heapq.heappush(heaps[o.eng], (0.0, o.gidx, o))
            left = len(ops)
            while left:
                best = None
                for e in self.ENG:
                    h = heaps[e]
                    if not h:
                        continue
                    st = max(h[0][0], free[e])
                    if best is None or (st, h[0][1]) < (best[0], best[1]):
                        best = (st, h[0][1], e)
                st, _, e = best
                h = heaps[e]
                cand = []
                while h and h[0][0] <= st:
                    cand.append(heapq.heappop(h))
                if PRIO_BLEVEL:
                    cand.sort(key=lambda c: (-blev[id(c[2])], c[1]))
                else:
                    cand.sort(key=lambda c: c[1])
                pi = 0
                if e == "pe" and last_pair[0] is not None:
                    for j_, c in enumerate(cand):
                        if c[2].pair == last_pair[0]:
                            pi = j_
                            break
                pick = cand[pi]
                for j_, c in enumerate(cand):
                    if j_ != pi:
                        heapq.heappush(h, c)
                o = pick[2]
                if e == "pe":
                    last_pair[0] = o.pair if (o.pair is not None and o.pair != last_pair[0]) else None
                c_ = cost_[id(o)]
                if o.is_dma:
                    free[e] = st + (0.15 if e == "sp" else 0.4)
                    fin = st + c_
                else:
                    free[e] = st + c_
                    fin = free[e]
                finish[id(o)] = fin
                newq[e].append(o)
                left -= 1
                for s_ in succ.get(id(o), ()):
                    indeg[id(s_)] -= 1
                    r = max(ready_t.get(id(s_), 0.0), fin + (0.05 if s_.eng == o.eng else XLAT))
                    ready_t[id(s_)] = r
                    if indeg[id(s_)] == 0:
                        heapq.heappush(heaps[s_.eng], (r, s_.gidx, s_))
        for e in self.ENG:
            assert len(newq[e]) == len(self.q[e])
            self.q[e] = newq[e]
            for i, o in enumerate(newq[e]):
                o.idx = i

    def tk(self, name):
        t = self.toks.get(name)
        if t is None:
            t = Tok(name)
            self.toks[name] = t
        return t

    def op(self, eng, fn, reads=(), writes=(), dma=False, cost=None, pair=None):
        o = Op(eng, fn)
        o.pair = pair
        o.is_dma = dma
        o.seg = self.seg
        o.gidx = self.nops
        o.cost = cost
        self.nops += 1
        o.idx = len(self.q[eng])
        deps = set()
        rd, wr = [], []
        for t in reads:
            if isinstance(t, str) and t.startswith("ps") and t[2].isdigit():
                wr.append(t[:3])
            else:
                rd.append(t)
        for t in writes:
            if isinstance(t, str) and t.startswith("ps") and t[2].isdigit():
                wr.append(t[:3])
            else:
                wr.append(t)
        reads = [self.tk(t) if isinstance(t, str) else t for t in rd]
        writes = [self.tk(t) if isinstance(t, str) else t for t in dict.fromkeys(wr)]
        for t in reads:
            if t.w is not None:
                deps.add(t.w)
        for t in writes:
            if t.w is not None:
                deps.add(t.w)
            for r in t.r:
                deps.add(r)
        deps.discard(o)
        o.deps = list(deps)
        for t in reads:
            t.r.append(o)
        for t in writes:
            t.w = o
            t.r = []
        self.q[eng].append(o)
        return o

    def finalize(self, final_wait_ops=()):
        nc = self.nc
        if SCHEDULE:
            self.schedule()
        waits = {}
        for e in self.ENG:
            seen = {}
            for o in self.q[e]:
                best = {}
                res = []
                for d in o.deps:
                    if d.is_dma:
                        res.append(d)
                        continue
                    if d.eng == o.eng:
                        if e == "pe" or o.is_dma:
                            if not o.is_dma:
                                continue
                        if (not o.is_dma) and o.idx - d.idx > self.NEAR:
                            continue
                    if d.eng not in best or best[d.eng].idx < d.idx:
                        best[d.eng] = d
                for d in best.values():
                    if seen.get(d.eng, -1) >= d.idx:
                        continue
                    seen[d.eng] = d.idx
                    res.append(d)
                waits[o] = res
                for d in res:
                    if not d.is_dma:
                        d.need_inc = True
        stack = ExitStack()
        for e in self.ENG:
            n = sum(1 for o in self.q[e] if o.need_inc and not o.is_dma)
            k = max(1, (n + self.SEM_LIMIT - 1) // self.SEM_LIMIT)
            sems = [stack.enter_context(nc.semaphore(f"s_{e}_{i}")) for i in range(k)]
            c = 0
            for o in self.q[e]:
                if o.need_inc and not o.is_dma:
                    o.ev = (sems[c // self.SEM_LIMIT], c % self.SEM_LIMIT + 1)
                    c += 1
        dsems = {e: [stack.enter_context(nc.semaphore(f"s_dma_{e}_{i}")) for i in range(self.NDMA)]
                 for e in ("sp", "pool")}
        for e in ("sp", "pool"):
            di = 0
            for o in self.q[e]:
                if o.is_dma:
                    o.dsem = dsems[e][di % self.NDMA]
                    o.dval = 16 * (di // self.NDMA + 1)
                    o.ev = (o.dsem, o.dval)
                    di += 1
        final = list(final_wait_ops)
        with nc.Block() as block:
            def run(engname, eng):
                for o in self.q[engname]:
                    for d in waits[o]:
                        eng.wait_ge(d.ev[0], d.ev[1])
                    if o.is_dma and o.dval > 16:
                        eng.wait_ge(o.dsem, o.dval - 16)
                    ins = o.fn(eng)
                    if o.is_dma:
                        ins.then_inc(o.dsem, 16)
                    elif o.need_inc:
                        ins.then_inc(o.ev[0], 1)
                if engname == "sp":
                    for o in final:
                        eng.wait_ge(o.ev[0], o.ev[1])

            @block.tensor
            def _(t):
                run("pe", t)

            @block.scalar
            def _(t):
                run("act", t)

            @block.vector
            def _(t):
                run("dve", t)

            @block.gpsimd
            def _(t):
                run("pool", t)

            @block.sync
            def _(t):
                run("sp", t)
        stack.close()


C_IDENT, C_ONES, C_TRIU_I, C_TRIL_S, C_TRIU_S, C_BDTRIL_S, C_OFF, C_BDONES = range(8)
NCST = 8


def _consts():
    p = np.arange(128)[:, None]
    f = np.arange(128)[None, :]
    m = np.zeros((128, NCST, 128), np.float32)
    m[:, C_IDENT] = (p == f)
    m[:, C_ONES] = 1.0
    m[:, C_TRIU_I] = (p <= f)
    m[:, C_TRIL_S] = (f < p)
    m[:, C_TRIU_S] = (p < f)
    m[:, C_BDTRIL_S] = (f < p) & ((p // 64) == (f // 64))
    m[:, C_OFF] = (p >= 64) & (f < 64)
    m[:, C_BDONES] = ((p // 64) == (f // 64))
    return m.reshape(128, NCST * 128)


def _fm(v):
    v = np.asarray(v, np.float32).reshape(-1, 128)
    return np.ascontiguousarray(v.T)


SP_C = 0
SP_LAYER = 8
SP_KV = 136
SP_GDN = 160
SP_KG = 386
SP_QG = 387
NSP = 389


def _small_params(inp, b):
    sp = np.zeros((128, NSP), np.float32)
    sp[:, 0:8] = _fm(inp["c"][b])
    for l in range(4):
        o = SP_LAYER + 32 * l
        sp[:, o:o + 8] = _fm(inp["norm_g"][l])
        sp[:, o + 8:o + 32] = _fm(inp["b_ada"][l])
    sp[:, SP_KV:SP_KV + 8] = _fm(inp["kv_norm_g"])
    sp[:, SP_KV + 8:SP_KV + 24] = _fm(inp["b_ada_kv"])
    for l in range(2):
        o = SP_GDN + 113 * l
        cw = np.asarray(inp["conv_w_a"][l], np.float32)
        for j in range(4):
            sp[:, o + 24 * j:o + 24 * j + 24] = _fm(cw[j])
        sp[:, o + 96] = np.asarray(inp["o_gain_a"][l], np.float32)
        sp[:, o + 97:o + 105] = np.asarray(inp["a_log_a"][l], np.float32)[None, :]
        sp[:, o + 105:o + 113] = np.asarray(inp["dt_bias_a"][l], np.float32)[None, :]
    sp[:, SP_KG] = np.tile(np.asarray(inp["k_gain"], np.float32), 2)
    for l in range(2):
        sp[:, SP_QG + l] = np.tile(np.asarray(inp["q_gain_b"][l], np.float32), 2)
    return sp


def build(nlayers=4):
    nc = bass.Bass("TRN2", target_bir_lowering=False)
    dram = lambda n, s, k="ExternalInput": nc.dram_tensor(n, s, F32, kind=k).ap()
    x_d = dram("x", [T, D])
    spm_d = dram("spm", [128, NSP])
    cst_d = dram("cst", [128, NCST * 128])
    w_ada_d = dram("w_ada", [4, D, 3 * D])
    w_inp_d = dram("w_inp", [2, D, 8 * 512])
    w_ba_d = dram("w_ba", [2, D, 16])
    w_outa_d = dram("w_out_a", [2, D, D])
    w_adakv_d = dram("w_ada_kv", [D, 2 * D])
    w_kv_d = dram("w_kv", [D, 2 * D])
    w_inb_d = dram("w_in_b", [2, D, 2 * D])
    w_outb_d = dram("w_out_b", [2, D, D])
    out_d = dram("out", [T, D], "ExternalOutput")

    def wview(ap2d, c0, n):
        return ap2d[:, c0:c0 + n].rearrange("(c p) n -> p c n", p=128)

    sb = lambda n, s, d=F32: nc.alloc_sbuf_tensor("sb_" + n, s, d).ap()
    P = Prog(nc)

    def v3(ap, a):
        return ap.rearrange("p (a b) -> p a b", a=a)

    xT = sb("xT", [128, 8 * T])
    cst = sb("cst", [128, NCST * 128])
    cstb = sb("cstb", [128, NCST * 128], BF16)
    spm = sb("spm", [128, NSP])
    cact = sb("cact", [128, 8])
    modT = sb("modT", [128, 5 * 24])
    Acol = sb("Acol", [128, 5 * 8])
    hT = sb("hT", [128, 8 * 512], BF16)
    ogT = sb("ogT", [128, 8 * 512], BF16)
    sqb = sb("sqb", [128, 2 * 512], BF16)
    rstd = sb("rstd", [128, 512])
    bscr = sb("bscr", [128, 8])
    tmpn = [sb(f"tmpn{i}", [128, 512]) for i in range(2)]
    wq = [sb(f"wq{i}", [128, 8 * 512], BF16) for i in range(2)]
    ps = [nc.alloc_psum_tensor(f"ps{i}", [128, 512], F32).ap() for i in range(8)]
    gstack = ExitStack()
    sbp = lambda n, s, d=F32: gstack.enter_context(nc.sbuf_tensor("sb_" + n, s, d))[:]
    sb_keep = sb
    lnb = sbp("lnb", [128, 512])
    mrow = sbp("mrow", [1, 256])
    wa = [sbp("wa0", [128, 8 * 256])] * 2
    LU = sbp("LU", [128, 2048])
    xld = [LU[:, i * 1024:(i + 1) * 1024] for i in range(2)]

    def cm(i, bf=False):
        return (cstb if bf else cst)[:, i * 128:(i + 1) * 128]

    def xsl(c, tb):
        return xT[:, c * T + tb * 512:c * T + (tb + 1) * 512]

    P.op("sp", lambda e: e.dma_start(out=cst, in_=cst_d), writes=["cst"], dma=True)
    P.op("sp", lambda e: e.dma_start(out=spm, in_=spm_d), writes=["spm"], dma=True)
    P.op("pool", lambda e: e.tensor_copy(out=cstb, in_=cst), reads=["cst"], writes=["cstb"])
    P.op("act", lambda e: e.activation(out=cact, in_=spm[:, 0:8], func=AF.Silu), reads=["spm"], writes=["cact"])

    wa_i = [0]

    def gen_mod(wd2, ncols, bias0, dst0, slot, a_slot, g_col0, scale_col0):
        buf = wa[0]
        for piece in range(ncols // 256):
            P.op("sp", lambda e, piece=piece: e.dma_start(out=v3(buf, 8), in_=wview(wd2, piece * 256, 256)),
                 writes=["wa0"], dma=True)
            yield
            for c in range(8):
                P.op("pe", lambda e, c=c: e.matmul(ps[2][0:1, 0:256], cact[:, c:c + 1], buf[:, c * 256:(c + 1) * 256],
                                                   start=(c == 0), stop=(c == 7)), reads=["wa0", "cact"], writes=["ps2"])
                yield
            P.op("act", lambda e: e.copy(out=mrow, in_=ps[2][0:1, 0:256]), reads=["ps2"], writes=["mrow"])
            yield
            for fc in range(2):
                P.op("pe", lambda e, fc=fc: e.matmul(ps[2][:, 256 + fc:257 + fc], mrow[0:1, fc * 128:(fc + 1) * 128],
                                                     cst[0:1, C_ONES * 128:C_ONES * 128 + 1], start=True, stop=True),
                     reads=["mrow", "cst"], writes=["ps2"])
                yield
            d0 = dst0 + piece * 2
            b0 = bias0 + piece * 2
            P.op("dve", lambda e, d0=d0, b0=b0: e.tensor_tensor(out=modT[:, d0:d0 + 2], in0=ps[2][:, 256:258],
                                                                in1=spm[:, b0:b0 + 2], op=ALU.add),
                 reads=["ps2", "spm"], writes=[f"mod{slot}"])
            yield
        P.op("dve", lambda e: e.scalar_tensor_tensor(out=Acol[:, 8 * a_slot:8 * a_slot + 8], in0=modT[:, scale_col0:scale_col0 + 8],
                                                     scalar=1.0, in1=spm[:, g_col0:g_col0 + 8], op0=ALU.add, op1=ALU.mult),
             reads=[f"mod{slot}", "spm"], writes=[f"A{a_slot}"])
        yield

    def gen_layer_mod(l):
        return gen_mod(w_ada_d[l], 3 * D, SP_LAYER + 32 * l + 8, 24 * l, l, l, SP_LAYER + 32 * l, 24 * l + 8)

    def gen_kv_mod():
        return gen_mod(w_adakv_d, 2 * D, SP_KV + 8, 96, 4, 4, SP_KV, 104)

    def drain(gen):
        for _ in gen:
            pass

    def take(gen, n):
        for _ in range(n):
            try:
                next(gen)
            except StopIteration:
                return
            yield

    def chain(*gens):
        for g_ in gens:
            yield from g_

    drain(gen_layer_mod(0))

    for n in range(16):
        bi = n % 2
        tb = n // 4
        P.op("sp", lambda e, n=n, bi=bi: e.dma_start(out=xld[bi], in_=x_d[n * 128:(n + 1) * 128, :]),
             writes=[f"xld{bi}"], dma=True)
        for half in range(2):
            bk = (0, 1, 3, 4)[(2 * n + half) % 4]
            for c4 in range(4):
                c = half * 4 + c4
                P.op("pe", lambda e, bi=bi, c=c, c4=c4, bk=bk: e.transpose(
                    ps[bk][:, c4 * 128:(c4 + 1) * 128], xld[bi][:, c * 128:(c + 1) * 128], cm(C_IDENT)),
                    reads=[f"xld{bi}", "cst"], writes=[f"ps{bk}"])
            dst = v3(xT[:, half * 4 * T:(half * 4 + 4) * T], 4)[:, :, n * 128:(n + 1) * 128]
            src = v3(ps[bk], 4)
            if half == 0:
                P.op("act", lambda e, dst=dst, src=src: e.copy(out=dst, in_=src), reads=[f"ps{bk}"], writes=[f"xT{tb}"])
            else:
                P.op("dve", lambda e, dst=dst, src=src: e.tensor_copy(out=dst, in_=src), reads=[f"ps{bk}"], writes=[f"xT{tb}"])

    def emit_norm_block(tb, slot, shift0):
        for c in range(8):
            sq_ = sqb[:, (c % 2) * 512:(c % 2 + 1) * 512]
            P.op("act", lambda e, c=c, sq_=sq_: e.activation(out=sq_, in_=xsl(c, tb), func=AF.Square),
                 reads=[f"xT{tb}"], writes=[f"sqb{c % 2}"])
            P.op("pe", lambda e, c=c, sq_=sq_: e.matmul(ps[4], cm(C_ONES, True), sq_,
                                               start=(c == 0), stop=(c == 7)), reads=[f"sqb{c % 2}", "cstb"], writes=["ps4"])
        P.op("act", lambda e: e.activation(out=rstd, in_=ps[4], func=AF.Ln, bias=EPS, scale=1.0 / D), reads=["ps4"], writes=["rstd"])
        P.op("act", lambda e: e.activation(out=rstd, in_=rstd, func=AF.Exp, scale=-0.5), reads=["rstd"], writes=["rstd"])
        for c in range(8):
            tm = tmpn[c % 2]
            P.op("dve", lambda e, c=c, tm=tm: e.scalar_tensor_tensor(
                out=tm, in0=xsl(c, tb), scalar=Acol[:, 8 * slot + c:8 * slot + c + 1], in1=rstd,
                op0=ALU.mult, op1=ALU.mult), reads=[f"xT{tb}", f"A{slot}", "rstd"], writes=[f"tmpn{c % 2}"])
            P.op("act", lambda e, c=c, tm=tm: e.activation(
                out=hT[:, c * 512:(c + 1) * 512], in_=tm, func=AF.Identity,
                bias=modT[:, shift0 + c:shift0 + c + 1], scale=1.0),
                reads=[f"tmpn{c % 2}", f"mod{slot}"], writes=["hT"])

    def emit_outproj(wd2, tb, slot, og=None, og_toks=("ogT",)):
        og = ogT if og is None else og
        for half in range(2):
            P.op("pool", lambda e, half=half: e.dma_start(out=v3(wq[half], 8), in_=wview(wd2, half * 512, 512)),
                 writes=[f"wq{half}"], dma=True)
        for m in range(8):
            half, mm_ = divmod(m, 4)
            bk = (0, 1, 3, 4)[m % 4]
            for h in range(8):
                P.op("pe", lambda e, half=half, mm_=mm_, h=h, bk=bk: e.matmul(
                    ps[bk], wq[half][:, h * 512 + mm_ * 128:h * 512 + (mm_ + 1) * 128], og[:, h * 512:(h + 1) * 512],
                    start=(h == 0), stop=(h == 7)), reads=[f"wq{half}"] + list(og_toks), writes=[f"ps{bk}"])
            P.op("dve", lambda e, m=m, bk=bk: e.scalar_tensor_tensor(
                out=xsl(m, tb), in0=ps[bk], scalar=modT[:, 24 * slot + 16 + m:24 * slot + 17 + m], in1=xsl(m, tb),
                op0=ALU.mult, op1=ALU.add), reads=[f"ps{bk}", f"mod{slot}", f"xT{tb}"], writes=[f"xT{tb}"])

    sb = sbp
    wba = sb("wba", [128, 8 * 16], BF16)
    carry = sb("carry", [128, 8 * 3 * 4])
    pre = [sb(f"pre{j}", [128, 516]) for j in range(3)]
    acc = [sb(f"acc{j}", [128, 512]) for j in range(3)]
    zs = [sb(f"zs{i}", [128, 512], BF16) for i in range(3)]
    rn = sb("rn", [128, 512])
    rn3 = sb("rn3", [128, 512])
    sqc = sb("sqc", [128, 512], BF16)
    qTb = [sb(f"qTb{i}", [128, 512], BF16) for i in range(3)]
    kTb = [sb(f"kTb{i}", [128, 512], BF16) for i in range(3)]
    qdec = [sb(f"qdec{i}", [128, 512], BF16) for i in range(2)]
    kdec = [sb(f"kdec{i}", [128, 512], BF16) for i in range(3)]
    vb = [sb(f"vb{i}", [128, 512]) for i in range(3)]
    egbc = sb("egbc", [128, 512])
    E1 = sb("E1", [128, 512])
    E2 = sb("E2", [128, 512])
    t1 = sb("t1", [128, 512])
    t2 = sb("t2", [128, 512])
    Lb = [LU[:, i * 512:(i + 1) * 512] for i in range(2)]
    Ub = [LU[:, (2 + i) * 512:(3 + i) * 512] for i in range(2)]
    Yb = [sb(f"Yb{i}", [128, 512]) for i in range(2)]
    Offb = sb("Offb", [128, 512])
    Ybf = [sb(f"Ybf{i}", [128, 512], BF16) for i in range(2)]
    attnT = [sb(f"attnT{i}", [128, 512], BF16) for i in range(2)]
    rv = [sb(f"rv{i}", [128, 128], BF16) for i in range(2)]
    vnw = [sb(f"vnw{i}", [128, 128], BF16) for i in range(2)]
    Sst = sb("Sst", [128, 8 * 128])
    Sbf = sb("Sbf", [128, 8 * 128], BF16)
    ob = sb("ob", [128, 512])
    tkU = sb("tkU", [128, 32])
    tkG = sb("tkG", [128, 32])
    tkBeta = sb("tkBeta", [128, 32])
    tkGc = sb("tkGc", [128, 32])
    tkNbeg = sb("tkNbeg", [128, 32])
    tkGl = sb("tkGl", [128, 32])
    tkDks = sb("tkDks", [128, 32])
    tkEgl = sb("tkEgl", [128, 32])
    negA = sb("negA", [128, 8])
    sb = sb_keep

    def emit_gdn_layer(l):
        slot = l
        g0 = SP_GDN + 113 * l
        P.op("pool", lambda e: e.dma_start(out=v3(wba, 8), in_=w_ba_d[l].rearrange("(c p) n -> p c n", p=128)),
             writes=["wba"], dma=True)
        P.op("pool", lambda e: e.memset(carry, 0.0), writes=["carry"])
        P.op("pool", lambda e: e.memset(Sst, 0.0), writes=[f"S{i}" for i in range(8)])
        P.op("pool", lambda e: e.memset(Sbf, 0.0), writes=[f"Sbf{i}" for i in range(8)])
        P.op("act", lambda e: e.activation(out=negA, in_=spm[:, g0 + 97:g0 + 105], func=AF.Exp), reads=["spm"], writes=["negA"])
        P.op("dve", lambda e: e.tensor_scalar(out=negA, in0=negA, scalar1=-1.0, scalar2=None, op0=ALU.mult), reads=["negA"], writes=["negA"])
        for tb in range(NB):
            emit_norm_block(tb, slot, 24 * l)
            for n in range(4):
                for c in range(8):
                    P.op("pe", lambda e, n=n, c=c: e.matmul(
                        ps[7][:, n * 16:(n + 1) * 16], hT[:, c * 512 + n * 128:c * 512 + (n + 1) * 128],
                        wba[:, c * 16:(c + 1) * 16], start=(c == 0), stop=(c == 7)),
                        reads=["hT", "wba"], writes=["ps7a"])
            ba3 = v3(ps[7][:, 0:64], 4)
            for n in range(4):
                P.op("dve", lambda e, n=n: e.tensor_tensor(out=tkU[:, n * 8:(n + 1) * 8], in0=ps[7][:, n * 16 + 8:n * 16 + 16],
                                                          in1=spm[:, g0 + 105:g0 + 113], op=ALU.add),
                     reads=["ps7a", "spm"], writes=["tkU"])
            P.op("act", lambda e: e.activation(out=v3(tkBeta, 4), in_=ba3[:, :, 0:8], func=AF.Exp, scale=-1.0),
                 reads=["ps7a"], writes=["tkBeta"])
            P.op("act", lambda e: e.activation(out=tkU, in_=tkU, func=AF.Exp), reads=["tkU"], writes=["tkU"])
            P.op("act", lambda e: e.activation(out=tkU, in_=tkU, func=AF.Ln, bias=1.0, scale=1.0), reads=["tkU"], writes=["tkU"])
            for n in range(4):
                P.op("dve", lambda e, n=n: e.tensor_tensor(out=tkG[:, n * 8:(n + 1) * 8], in0=tkU[:, n * 8:(n + 1) * 8],
                                                          in1=negA, op=ALU.mult), reads=["tkU", "negA"], writes=["tkG"])
            P.op("dve", lambda e: e.tensor_scalar(out=tkBeta, in0=tkBeta, scalar1=1.0, scalar2=None, op0=ALU.add),
                 reads=["tkBeta"], writes=["tkBeta"])
            P.op("dve", lambda e: e.reciprocal(out=tkBeta, in_=tkBeta), reads=["tkBeta"], writes=["tkBeta"])
            for n in range(4):
                P.op("pe", lambda e, n=n: e.matmul(ps[7][:, 64 + n * 8:64 + (n + 1) * 8], cm(C_TRIU_I), tkG[:, n * 8:(n + 1) * 8],
                                                   start=True, stop=True), reads=["cst", "tkG"], writes=["ps7b"])
                P.op("pe", lambda e, n=n: e.matmul(ps[7][:, 96 + n * 8:96 + (n + 1) * 8], cm(C_ONES), tkG[:, n * 8:(n + 1) * 8],
                                                   start=True, stop=True), reads=["cst", "tkG"], writes=["ps7b"])
            P.op("dve", lambda e: e.tensor_copy(out=tkGc, in_=ps[7][:, 64:96]), reads=["ps7b"], writes=["tkGc"])
            P.op("dve", lambda e: e.tensor_copy(out=tkGl, in_=ps[7][:, 96:128]), reads=["ps7b"], writes=["tkGl"])
            P.op("act", lambda e: e.activation(out=tkEgl, in_=tkGl, func=AF.Exp), reads=["tkGl"], writes=["tkEgl"])
            P.op("dve", lambda e: e.tensor_tensor(out=tkDks, in0=tkGl, in1=tkGc, op=ALU.subtract), reads=["tkGl", "tkGc"], writes=["tkDks"])
            P.op("act", lambda e: e.activation(out=tkDks, in_=tkDks, func=AF.Exp), reads=["tkDks"], writes=["tkDks"])
            P.op("act", lambda e: e.activation(out=tkNbeg, in_=tkGc, func=AF.Exp), reads=["tkGc"], writes=["tkNbeg"])
            P.op("dve", lambda e: e.scalar_tensor_tensor(out=tkNbeg, in0=tkNbeg, scalar=-1.0, in1=tkBeta, op0=ALU.mult, op1=ALU.mult),
                 reads=["tkNbeg", "tkBeta"], writes=["tkNbeg"])

            emit_gdn_block_heads(l, tb, g0)
            emit_outproj(w_outa_d[l], tb, slot)

    def run_interleaved(gens, weights=None):
        gens = list(gens)
        weights = list(weights) if weights is not None else [1] * len(gens)
        while gens:
            for g_, w_ in list(zip(gens, weights)):
                for _ in range(w_):
                    try:
                        next(g_)
                    except StopIteration:
                        k_ = gens.index(g_)
                        gens.pop(k_)
                        weights.pop(k_)
                        break

    def gdn_A(l, tb, h, g0):
        a = h % 3
        wb = wq[h % 2]
        wtk = f"wq{h % 2}"
        P.op("pool", lambda e: e.dma_start(out=v3(wb, 8), in_=wview(w_inp_d[l], h * 512, 512)), writes=[wtk], dma=True)
        yield
        def proj(j, bk):
            for c in range(8):
                P.op("pe", lambda e, c=c: e.matmul(ps[bk], wb[:, c * 512 + j * 128:c * 512 + (j + 1) * 128],
                                                   hT[:, c * 512:(c + 1) * 512], start=(c == 0), stop=(c == 7)),
                     reads=[wtk, "hT"], writes=[f"ps{bk}"])
                yield

        def evac(j, bk):
            cc = (h * 3 + j) * 4
            P.op("pool", lambda e: e.tensor_copy(out=pre[j][:, 0:3], in_=carry[:, cc:cc + 3]),
                 reads=["carry"], writes=[f"pre{j}"])
            P.op("act", lambda e: e.copy(out=pre[j][:, 3:515], in_=ps[bk]), reads=[f"ps{bk}"], writes=[f"pre{j}"])
            yield
            P.op("pool", lambda e: e.tensor_copy(out=carry[:, cc:cc + 3], in_=pre[j][:, 512:515]),
                 reads=[f"pre{j}"], writes=["carry"])
            yield

        yield from proj(0, 0)
        yield from proj(1, 1)
        yield from evac(0, 0)
        yield from evac(1, 1)
        yield from proj(2, 0)
        yield from proj(3, 1)
        yield from evac(2, 0)
        P.op("act", lambda e: e.activation(out=zs[a], in_=ps[1], func=AF.Silu), reads=["ps1"], writes=[f"zs{a}"])
        yield
        wc = lambda tap, j: spm[:, g0 + 24 * tap + 8 * j + h:g0 + 24 * tap + 8 * j + h + 1]
        for tap in range(4):
            for j in range(3):
                if tap == 0:
                    P.op("dve", lambda e, j=j: e.tensor_scalar(out=acc[j], in0=pre[j][:, 0:512], scalar1=wc(0, j), scalar2=0.0,
                                                               op0=ALU.mult, op1=ALU.add), reads=[f"pre{j}", "spm"], writes=[f"acc{j}"])
                else:
                    P.op("dve", lambda e, j=j, tap=tap: e.scalar_tensor_tensor(
                        out=acc[j], in0=pre[j][:, tap:tap + 512], scalar=wc(tap, j), in1=acc[j], op0=ALU.mult, op1=ALU.add),
                        reads=[f"pre{j}", "spm", f"acc{j}"], writes=[f"acc{j}"])
                yield
        for j in range(3):
            P.op("act", lambda e, j=j: e.activation(out=acc[j], in_=acc[j], func=AF.Silu), reads=[f"acc{j}"], writes=[f"acc{j}"])
            yield
        for j in range(2):
            P.op("act", lambda e, j=j: e.activation(out=sqb[:, j * 512:(j + 1) * 512], in_=acc[j], func=AF.Square),
                 reads=[f"acc{j}"], writes=[f"sqb{j}"])
            yield
            P.op("pe", lambda e, j=j: e.matmul(ps[j], cm(C_ONES, True), sqb[:, j * 512:(j + 1) * 512], start=True, stop=True),
                 reads=[f"sqb{j}", "cstb"], writes=[f"ps{j}"])
            yield
        for j in range(2):
            lb = lnb if j == 0 else rn
            tk_ = "lnb" if j == 0 else "rn"
            P.op("act", lambda e, j=j, lb=lb: e.activation(out=lb, in_=ps[j], func=AF.Ln, bias=EPS, scale=1.0),
                 reads=[f"ps{j}"], writes=[tk_])
            yield
            P.op("act", lambda e, j=j, lb=lb: e.activation(out=lb, in_=lb, func=AF.Exp, scale=-0.5,
                                                          bias=(float(np.log(128.0 ** -0.5)) if j == 0 else 0.0)),
                 reads=[tk_], writes=[tk_])
            yield
        P.op("dve", lambda e: e.tensor_tensor(out=qTb[a], in0=acc[0], in1=lnb, op=ALU.mult), reads=["acc0", "lnb"], writes=[f"qTb{a}"])
        yield
        P.op("dve", lambda e: e.tensor_tensor(out=acc[1], in0=acc[1], in1=rn, op=ALU.mult), reads=["acc1", "rn"], writes=["acc1"])
        yield
        P.op("pool", lambda e: e.tensor_copy(out=kTb[a], in_=acc[1]), reads=["acc1"], writes=[f"kTb{a}"])
        yield
        for n in range(4):
            sl = slice(n * 128, (n + 1) * 128)
            P.op("pe", lambda e, sl=sl: e.transpose(ps[0][:, sl], acc[1][:, sl], cm(C_IDENT)), reads=["acc1", "cst"], writes=["ps0"])
            yield
        for n in range(4):
            sl = slice(n * 128, (n + 1) * 128)
            P.op("pe", lambda e, sl=sl: e.transpose(ps[1][:, sl], acc[2][:, sl], cm(C_IDENT)), reads=["acc2", "cst"], writes=["ps1"])
            yield
        for n in range(4):
            sl = slice(n * 128, (n + 1) * 128)
            col = n * 8 + h
            P.op("act", lambda e, sl=sl, col=col: e.activation(out=kdec[a][:, sl], in_=ps[0][:, sl], func=AF.Identity,
                                                               scale=tkDks[:, col:col + 1]), reads=["ps0", "tkDks"], writes=[f"kdec{a}"])
            yield
        for n in range(4):
            sl = slice(n * 128, (n + 1) * 128)
            col = n * 8 + h
            P.op("act", lambda e, sl=sl, col=col: e.activation(out=vb[a][:, sl], in_=ps[1][:, sl], func=AF.Identity,
                                                               scale=tkBeta[:, col:col + 1]), reads=["ps1", "tkBeta"], writes=[f"vb{a}"])
            yield

    def gdn_B(l, tb, h, g0):
        a = h % 3
        bs = h % 2
        q_, k_, kd_, vb_, zs_ = qTb[a], kTb[a], kdec[a], vb[a], zs[a]
        qt, kt, kdt, vbt, zst = f"qTb{a}", f"kTb{a}", f"kdec{a}", f"vb{a}", f"zs{a}"
        tiles = [(n, slice(n * 128, (n + 1) * 128), n * 8 + h) for n in range(4)]
        for n, sl, col in tiles:
            P.op("pe", lambda e, sl=sl: e.matmul(ps[3][:, sl], k_[:, sl], k_[:, sl], start=True, stop=True), reads=[kt], writes=["ps3"])
            P.op("pe", lambda e, sl=sl: e.matmul(ps[4][:, sl], k_[:, sl], q_[:, sl], start=True, stop=True), reads=[kt, qt], writes=["ps4"])
            P.op("pool", lambda e, sl=sl, col=col: e.tensor_scalar(out=E1[:, sl], in0=cm(C_TRIU_I), scalar1=tkG[:, col:col + 1], scalar2=0.0,
                                                                   op0=ALU.mult, op1=ALU.add), reads=["cst", "tkG"], writes=["E1"])
            yield
            P.op("pe", lambda e, sl=sl: e.matmul(ps[5][:, sl], cm(C_ONES), E1[:, sl], start=True, stop=True), reads=["cst", "E1"], writes=["ps5"])
            yield
        P.op("act", lambda e: e.activation(out=egbc, in_=ps[5], func=AF.Exp), reads=["ps5"], writes=["egbc"])
        yield
        for n, sl, col in tiles:
            P.op("dve", lambda e, sl=sl, col=col: e.tensor_scalar(out=E1[:, sl], in0=ps[5][:, sl], scalar1=tkGc[:, col:col + 1], scalar2=0.0,
                                                                  op0=ALU.subtract, op1=ALU.min), reads=["ps5", "tkGc"], writes=["E1"])
            yield
            P.op("dve", lambda e, sl=sl, col=col: e.tensor_scalar(out=E2[:, sl], in0=ps[5][:, sl], scalar1=tkGc[:, col:col + 1], scalar2=0.0,
                                                                  op0=ALU.subtract, op1=ALU.max), reads=["ps5", "tkGc"], writes=["E2"])
            yield
        P.op("act", lambda e: e.activation(out=E1, in_=E1, func=AF.Exp), reads=["E1"], writes=["E1"])
        yield
        P.op("act", lambda e: e.activation(out=E2, in_=E2, func=AF.Exp, scale=-1.0), reads=["E2"], writes=["E2"])
        yield
        for n, sl, col in tiles:
            P.op("dve", lambda e, sl=sl, col=col: e.scalar_tensor_tensor(out=t1[:, sl], in0=ps[3][:, sl], scalar=tkBeta[:, col:col + 1],
                                                                      in1=E2[:, sl], op0=ALU.mult, op1=ALU.mult),
                 reads=["ps3", "E2", "tkBeta"], writes=["t1"])
            yield
        P.op("dve", lambda e: e.tensor_tensor(out=t2, in0=ps[4], in1=E1, op=ALU.mult), reads=["ps4", "E1"], writes=["t2"])
        yield
        P.op("pool", lambda e: e.tensor_tensor(out=qdec[bs], in0=q_, in1=egbc, op=ALU.mult), reads=[qt, "egbc"], writes=[f"qdec{bs}"])
        yield
        L0, U0, Y0 = Lb[0], Ub[0], Yb[0]
        for n, sl, col in tiles:
            P.op("pool", lambda e, sl=sl: e.tensor_tensor(out=L0[:, sl], in0=t1[:, sl], in1=cm(C_BDTRIL_S), op=ALU.mult),
                 reads=["t1", "cst"], writes=["Lb0"])
            yield
        for n, sl, col in tiles:
            P.op("pe", lambda e, sl=sl: e.transpose(ps[4][:, sl], L0[:, sl], cm(C_IDENT)), reads=["Lb0", "cst"], writes=["ps4"])
            yield
        P.op("act", lambda e: e.copy(out=U0, in_=ps[4]), reads=["ps4"], writes=["Ub0"])
        yield
        for n, sl, col in tiles:
            P.op("pool", lambda e, sl=sl: e.tensor_tensor(out=Y0[:, sl], in0=cm(C_IDENT), in1=U0[:, sl], op=ALU.subtract),
                 reads=["cst", "Ub0"], writes=["Yb0"])
            yield
            P.op("pool", lambda e, sl=sl: e.tensor_tensor(out=Offb[:, sl], in0=t1[:, sl], in1=cm(C_OFF), op=ALU.mult),
                 reads=["t1", "cst"], writes=["Offb"])
            yield
            P.op("pool", lambda e, sl=sl: e.tensor_tensor(out=attnT[bs][:, sl], in0=t2[:, sl], in1=cm(C_TRIU_I), op=ALU.mult),
                 reads=["t2", "cst"], writes=[f"attnT{bs}"])
            yield
        cur = 0
        for k in range(1, 6):
            nx = 1 - cur
            for n, sl, col in tiles:
                P.op("pe", lambda e, sl=sl, cur=cur: e.matmul(ps[3][:, sl], Ub[cur][:, sl], Lb[cur][:, sl], start=True, stop=True),
                     reads=[f"Ub{cur}", f"Lb{cur}"], writes=["ps3"])
                yield
            if k < 5:
                for n, sl, col in tiles:
                    P.op("pe", lambda e, sl=sl, cur=cur: e.matmul(ps[4][:, sl], Lb[cur][:, sl], Ub[cur][:, sl], start=True, stop=True),
                         reads=[f"Ub{cur}", f"Lb{cur}"], writes=["ps4"])
                    yield
            P.op("act", lambda e, nx=nx: e.copy(out=Lb[nx], in_=ps[3]), reads=["ps3"], writes=[f"Lb{nx}"])
            yield
            if k < 5:
                P.op("dve", lambda e, nx=nx: e.tensor_copy(out=Ub[nx], in_=ps[4]), reads=["ps4"], writes=[f"Ub{nx}"])
                yield
            for n, sl, col in tiles:
                P.op("pe", lambda e, sl=sl, nx=nx, cur=cur: e.matmul(ps[5][:, sl], Lb[nx][:, sl], Yb[cur][:, sl], start=True, stop=True),
                     reads=[f"Lb{nx}", f"Yb{cur}"], writes=["ps5"])
                yield
            P.op("dve", lambda e, nx=nx, cur=cur: e.tensor_tensor(out=Yb[nx], in0=ps[5], in1=Yb[cur], op=ALU.add),
                 reads=["ps5", f"Yb{cur}"], writes=[f"Yb{nx}"])
            yield
            cur = nx
        Yd = Yb[cur]
        ytk = f"Yb{cur}"
        for n, sl, col in tiles:
            P.op("pe", lambda e, sl=sl: e.transpose(ps[4][:, sl], Yd[:, sl], cm(C_IDENT)), reads=[ytk, "cst"], writes=["ps4"])
            P.op("pe", lambda e, sl=sl: e.matmul(ps[3][:, sl], Offb[:, sl], Yd[:, sl], start=True, stop=True), reads=["Offb", ytk], writes=["ps3"])
            yield
        P.op("act", lambda e: e.copy(out=t1, in_=ps[4]), reads=["ps4"], writes=["t1"])
        yield
        P.op("dve", lambda e: e.tensor_copy(out=t2, in_=ps[3]), reads=["ps3"], writes=["t2"])
        yield
        for n, sl, col in tiles:
            P.op("pe", lambda e, sl=sl: e.matmul(ps[5][:, sl], t1[:, sl], t2[:, sl], start=True, stop=True), reads=["t1", "t2"], writes=["ps5"])
            yield
        P.op("dve", lambda e: e.tensor_tensor(out=Ybf[bs], in0=Yd, in1=ps[5], op=ALU.subtract), reads=[ytk, "ps5"], writes=[f"Ybf{bs}"])
        yield
    def gdn_C(l, tb, h, g0):
        a = h % 3
        bs = h % 2
        q_, k_, kd_, vb_, zs_ = qTb[a], kTb[a], kdec[a], vb[a], zs[a]
        qt, kt, kdt, vbt, zst = f"qTb{a}", f"kTb{a}", f"kdec{a}", f"vb{a}", f"zs{a}"
        tiles = [(n, slice(n * 128, (n + 1) * 128), n * 8 + h) for n in range(4)]
        Sh = Sst[:, h * 128:(h + 1) * 128]
        Shb = Sbf[:, h * 128:(h + 1) * 128]
        for n, sl, col in tiles:
            r_ = rv[n % 2]
            v_ = vnw[n % 2]
            P.op("pe", lambda e, sl=sl: e.matmul(ps[7][:, 128:256], k_[:, sl], Shb, start=True, stop=True),
                 reads=[kt, f"Sbf{h}"], writes=["ps7"])
            yield
            P.op("dve", lambda e, sl=sl, col=col, r_=r_: e.scalar_tensor_tensor(out=r_, in0=ps[7][:, 128:256], scalar=tkNbeg[:, col:col + 1],
                                                                            in1=vb_[:, sl], op0=ALU.mult, op1=ALU.add),
                 reads=["ps7", "tkNbeg", vbt], writes=[f"rv{n % 2}"])
            yield
            P.op("pe", lambda e, sl=sl, r_=r_: e.matmul(ps[7][:, 384:512], Ybf[bs][:, sl], r_, start=True, stop=True),
                 reads=[f"Ybf{bs}", f"rv{n % 2}"], writes=["ps7"])
            yield
            P.op("act", lambda e, v_=v_: e.copy(out=v_, in_=ps[7][:, 384:512]), reads=["ps7"], writes=[f"vnw{n % 2}"])
            yield
            P.op("pe", lambda e, sl=sl: e.matmul(ps[6][:, sl], Shb, qdec[bs][:, sl], start=True, stop=False),
                 reads=[f"Sbf{h}", f"qdec{bs}"], writes=["ps6"])
            P.op("pe", lambda e, sl=sl, v_=v_: e.matmul(ps[6][:, sl], v_, attnT[bs][:, sl], start=False, stop=True),
                 reads=[f"vnw{n % 2}", f"attnT{bs}"], writes=["ps6"])
            yield
            P.op("pe", lambda e, sl=sl, v_=v_: e.matmul(ps[7][:, 256:384], kd_[:, sl], v_, start=True, stop=True),
                 reads=[kdt, f"vnw{n % 2}"], writes=["ps7"])
            yield
            P.op("dve", lambda e, col=col: e.scalar_tensor_tensor(out=Sh, in0=Sh, scalar=tkEgl[:, col:col + 1], in1=ps[7][:, 256:384],
                                                                  op0=ALU.mult, op1=ALU.add),
                 reads=[f"S{h}", "tkEgl", "ps7"], writes=[f"S{h}"])
            yield
            P.op("pool", lambda e: e.tensor_copy(out=Shb, in_=Sh), reads=[f"S{h}"], writes=[f"Sbf{h}"])
            yield
        P.op("act", lambda e: e.copy(out=ob, in_=ps[6]), reads=["ps6"], writes=["ob"])
        yield
        P.op("act", lambda e: e.activation(out=sqc, in_=ob, func=AF.Square), reads=["ob"], writes=["sqc"])
        yield
        P.op("pe", lambda e: e.matmul(ps[6], cm(C_ONES, True), sqc, start=True, stop=True), reads=["sqc", "cstb"], writes=["ps6"])
        yield
        P.op("act", lambda e: e.activation(out=rn3, in_=ps[6], func=AF.Ln, bias=EPS, scale=1.0 / 128), reads=["ps6"], writes=["rn3"])
        yield
        P.op("act", lambda e: e.activation(out=rn3, in_=rn3, func=AF.Exp, scale=-0.5), reads=["rn3"], writes=["rn3"])
        yield
        P.op("dve", lambda e: e.tensor_tensor(out=ob, in0=ob, in1=rn3, op=ALU.mult), reads=["ob", "rn3"], writes=["ob"])
        yield
        P.op("dve", lambda e: e.scalar_tensor_tensor(out=ogT[:, h * 512:(h + 1) * 512], in0=ob, scalar=spm[:, g0 + 96:g0 + 97], in1=zs_,
                                                     op0=ALU.mult, op1=ALU.mult), reads=["ob", "spm", zst], writes=["ogT"])
        yield

    aux = [None]

    def emit_gdn_block_heads(l, tb, g0):
        for step in range(-2, 8):
            gens = []
            wts = []
            if aux[0] is not None:
                gens.append(take(aux[0], AUX_N))
                wts.append(1)
            if 0 <= step < 8:
                gens.append(gdn_C(l, tb, step, g0))
                wts.append(GDN_W[0])
            if 0 <= step + 1 < 8:
                gens.append(gdn_B(l, tb, step + 1, g0))
                wts.append(GDN_W[1])
            if 0 <= step + 2 < 8:
                gens.append(gdn_A(l, tb, step + 2, g0))
                wts.append(GDN_W[2])
            run_interleaved(gens, wts)

    bar_n = [0]

    def emit_barrier():
        k = bar_n[0]
        bar_n[0] += 1
        P.mark_segment()
        P.op("act", lambda e: e.activation(out=bscr[:, 0:1], in_=cact[:, 0:1], func=AF.Identity), reads=["cact"], writes=[f"bar{k}_act", "bscr0"])
        P.op("dve", lambda e: e.tensor_copy(out=bscr[:, 1:2], in_=cact[:, 0:1]), reads=["cact"], writes=[f"bar{k}_dve", "bscr1"])
        P.op("pool", lambda e: e.tensor_copy(out=bscr[:, 2:3], in_=cact[:, 0:1]), reads=["cact"], writes=[f"bar{k}_pool", "bscr2"])
        P.op("pe", lambda e: e.matmul(ps[7][:, 0:8], cm(C_ONES), cact[:, 0:8], start=True, stop=True), reads=["cact", "cst"], writes=["ps7", f"bar{k}_pe"])
        allb = [f"bar{k}_{x}" for x in ("act", "dve", "pool", "pe")]
        P.op("act", lambda e: e.activation(out=bscr[:, 3:4], in_=cact[:, 0:1], func=AF.Identity), reads=allb + ["cact"], writes=["bscr3"])
        P.op("dve", lambda e: e.tensor_copy(out=bscr[:, 4:5], in_=ps[7][:, 0:1]), reads=allb, writes=["bscr4", "ps7"])
        P.op("pool", lambda e: e.tensor_copy(out=bscr[:, 5:6], in_=cact[:, 0:1]), reads=allb + ["cact"], writes=["bscr5"])
        P.op("pe", lambda e: e.matmul(ps[7][:, 0:8], cm(C_ONES), cact[:, 0:8], start=True, stop=True), reads=allb + ["cact", "cst"], writes=["ps7"])
        P.op("sp", lambda e: e.dma_start(out=bscr[:, 6:8], in_=spm_d[:, 0:2]), reads=allb, writes=["bscr6"], dma=True)
        P.mark_segment()

    emit_barrier()
    n_a = min(nlayers, 2)
    for l in range(n_a):
        if l == 0 and nlayers > 1:
            aux[0] = gen_layer_mod(1)
        if l == 1 and nlayers > 2:
            gl_ = [gen_kv_mod(), gen_layer_mod(2)]
            if nlayers > 3:
                gl_.append(gen_layer_mod(3))
            aux[0] = chain(*gl_)
        emit_gdn_layer(l)
        if aux[0] is not None:
            drain(aux[0])
            aux[0] = None
    emit_barrier()
    gstack.close()
    sb = sb_keep

    ost2 = [sb(f"Eo{i}", [128, 512]) for i in range(2)]
    if nlayers > 2:
        KT = sb("KT", [128, 8 * T], BF16)
        Vt = sb("Vt", [128, 16 * D], BF16)
        qn = sb("qn", [128, 8 * 512], BF16)
        Ebuf = ost2
        SPR = [[sb(f"SPR{i}{u}", [128, 512], F32R) for u in range(2)] for i in range(2)]
        dbf = [[sb(f"dbf{i}{u}", [128, 512]) for u in range(2)] for i in range(2)]
        Wb = [sb(f"Wb{i}", [128, 512], BF16) for i in range(2)]
        rn2 = tmpn[1]
        cstr = sb("cstr", [128, 256], F32R)
        P.op("dve", lambda e: e.tensor_copy(out=cstr[:, 0:128], in_=cm(C_TRIL_S)), reads=["cst"], writes=["cstr"])
        P.op("dve", lambda e: e.tensor_copy(out=cstr[:, 128:256], in_=cm(C_TRIU_I)), reads=["cst"], writes=["cstr"])

        def emit_headnorm(psb, gain_col, extra_bias, dst, dst_tok):
            tm = tmpn[0]
            P.op("act", lambda e: e.copy(out=tm, in_=psb), reads=[psb_tok[0]], writes=["tmpn0"])
            P.op("act", lambda e: e.activation(out=sqb[:, 0:512], in_=tm, func=AF.Square), reads=["tmpn0"], writes=["sqb0"])
            P.op("pe", lambda e: e.matmul(ps[4], cm(C_BDONES, True), sqb[:, 0:512], start=True, stop=True), reads=["sqb0", "cstb"], writes=["ps4"])
            P.op("act", lambda e: e.activation(out=rn2, in_=ps[4], func=AF.Ln, bias=EPS, scale=1.0 / 64), reads=["ps4"], writes=["tmpn1"])
            P.op("act", lambda e: e.activation(out=rn2, in_=rn2, func=AF.Exp, scale=-0.5, bias=extra_bias), reads=["tmpn1"], writes=["tmpn1"])
            P.op("dve", lambda e: e.scalar_tensor_tensor(out=dst, in0=tm, scalar=spm[:, gain_col:gain_col + 1], in1=rn2,
                                                         op0=ALU.mult, op1=ALU.mult), reads=["tmpn0", "spm", "tmpn1"], writes=[dst_tok])

        psb_tok = [None]

        def emit_kv():
            for tb in range(NB):
                emit_norm_block(tb, 4, 96)
                for piece in range(2):
                    P.op("pool", lambda e, piece=piece: e.dma_start(out=v3(wq[piece], 8), in_=wview(w_kv_d, piece * 512, 512)),
                         writes=[f"wq{piece}"], dma=True)
                    for fcl in range(4):
                        fc = piece * 4 + fcl
                        bk = fc % 4
                        for c in range(8):
                            P.op("pe", lambda e, piece=piece, fcl=fcl, c=c, bk=bk: e.matmul(
                                ps[bk], wq[piece][:, c * 512 + fcl * 128:c * 512 + (fcl + 1) * 128], hT[:, c * 512:(c + 1) * 512],
                                start=(c == 0), stop=(c == 7)), reads=[f"wq{piece}", "hT"], writes=[f"ps{bk}"])
                        psb_tok[0] = f"ps{bk}"
                        emit_headnorm(ps[bk], SP_KG, 0.0, KT[:, fc * T + tb * 512:fc * T + (tb + 1) * 512], "KT")
                for piece in range(2):
                    P.op("pool", lambda e, piece=piece: e.dma_start(out=v3(wq[piece], 8), in_=wview(w_kv_d, D + piece * 512, 512)),
                         writes=[f"wq{piece}"], dma=True)
                    for n in range(4):
                        bk = n % 4
                        tile = tb * 4 + n
                        for c in range(8):
                            P.op("pe", lambda e, piece=piece, n=n, c=c, bk=bk: e.matmul(
                                ps[bk], hT[:, c * 512 + n * 128:c * 512 + (n + 1) * 128], wq[piece][:, c * 512:(c + 1) * 512],
                                start=(c == 0), stop=(c == 7)), reads=[f"wq{piece}", "hT"], writes=[f"ps{bk}"])
                        dst = Vt[:, tile * D + piece * 512:tile * D + (piece + 1) * 512]
                        if n % 2 == 0:
                            P.op("act", lambda e, dst=dst, bk=bk: e.copy(out=dst, in_=ps[bk]), reads=[f"ps{bk}"], writes=["Vt"])
                        else:
                            P.op("dve", lambda e, dst=dst, bk=bk: e.tensor_copy(out=dst, in_=ps[bk]), reads=[f"ps{bk}"], writes=["Vt"])

        def sb_head(g, ch, hh, s_):
            h = 2 * ch + hh
            base = hh * 64
            bA, bB = (0, 1) if s_ == 0 else (2, 3)
            Eb, wbu = Ebuf[s_], Wb[s_]
            chunks = list(range(4 * g + 3, -1, -1))

            def geom(i):
                r0 = max(i - 4 * g, 0)
                return r0 * 128, (i >= 4 * g)

            def warm():
                for _ in range(SB_WARM):
                    P.op("pe", lambda e: e.matmul(ps[5][:, 0:128], cm(C_ONES, True), cm(C_IDENT, True), start=True, stop=True),
                         reads=["cstb"], writes=["ps5"])

            def front(k):
                i = chunks[k]
                c0, diag = geom(i)
                u = k % 2
                Sr = SPR[s_][u]
                P.op("pe", lambda e: e.matmul(
                    ps[bA][:, c0:512], KT[base:base + 64, ch * T + i * 128:ch * T + (i + 1) * 128],
                    qn[base:base + 64, ch * 512 + c0:ch * 512 + 512], start=True, stop=True),
                    reads=["KT", f"qn{ch}"], writes=[f"ps{bA}"], pair=("st", g, ch, k))
                yield
                P.op("act", lambda e: e.activation(out=Eb[:, c0:512], in_=ps[bA][:, c0:512], func=AF.Exp),
                     reads=[f"ps{bA}"], writes=[f"E{s_}"])
                yield
                P.op("act", lambda e: e.activation(out=Sr[:, c0:512], in_=Eb[:, c0:512], func=AF.Ln, bias=1.0, scale=1.0),
                     reads=[f"E{s_}"], writes=[f"SPR{s_}{u}"])
                yield
                if diag:
                    P.op("pool", lambda e: e.tensor_tensor(out=Sr[:, c0:c0 + 128], in0=Sr[:, c0:c0 + 128].bitcast(F32),
                                                           in1=cm(C_TRIU_S), op=ALU.mult),
                         reads=[f"SPR{s_}{u}", "cst"], writes=[f"SPR{s_}{u}"])
                    yield

            def back1(k):
                i = chunks[k]
                c0, diag = geom(i)
                u = k % 2
                Sr, dbu = SPR[s_][u], dbf[s_][u]
                P.op("pe", lambda e: e.matmul(
                    ps[bB][:, c0:512], cstr[:, 0:128], Sr[:, c0:512], start=(k == 0), stop=False, skip_group_check=True),
                    reads=[f"SPR{s_}{u}", "cstr"], writes=[f"ps{bB}"])
                yield
                P.op("dve", lambda e: e.tensor_tensor(
                    out=dbu[:, c0:512], in0=ps[bA][:, c0:512], in1=Sr[:, c0:512].bitcast(F32), op=ALU.subtract),
                    reads=[f"ps{bA}", f"SPR{s_}{u}"], writes=[f"dbf{s_}{u}"])
                yield

            def back2(k):
                i = chunks[k]
                c0, diag = geom(i)
                u = k % 2
                Sr, dbu = SPR[s_][u], dbf[s_][u]
                P.op("dve", lambda e: e.tensor_tensor(
                    out=dbu[:, c0:512], in0=dbu[:, c0:512], in1=ps[bB][:, c0:512], op=ALU.subtract),
                    reads=[f"ps{bB}", f"dbf{s_}{u}"], writes=[f"dbf{s_}{u}"])
                yield
                if i > 0:
                    warm()
                    P.op("pe", lambda e: e.matmul(
                        ps[bB][:, c0:512], cstr[:, 128:256], Sr[:, c0:512], start=False, stop=False, skip_group_check=True),
                        reads=[f"SPR{s_}{u}", "cstr"], writes=[f"ps{bB}"])
                    yield
                P.op("act", lambda e: e.activation(out=wbu[:, c0:512], in_=dbu[:, c0:512], func=AF.Exp),
                     reads=[f"dbf{s_}{u}"], writes=[f"Wb{s_}"])
                yield
                if diag:
                    P.op("pool", lambda e: e.tensor_tensor(out=wbu[:, c0:c0 + 128], in0=wbu[:, c0:c0 + 128],
                                                           in1=cm(C_TRIU_S, True), op=ALU.mult),
                         reads=[f"Wb{s_}", "cstb"], writes=[f"Wb{s_}"])
                    yield
                P.op("pe", lambda e: e.matmul(
                    ps[6][base:base + 64, c0:512], Vt[:, i * D + h * 64:i * D + (h + 1) * 64], wbu[:, c0:512],
                    start=False, stop=False, skip_group_check=True, tile_position=(0, base)),
                    reads=["Vt", f"Wb{s_}"], writes=["ps6"])
                yield

            yield from front(0)
            for k in range(len(chunks)):
                yield from back1(k)
                if k + 1 < len(chunks):
                    yield from front(k + 1)
                yield from back2(k)

        def run_interleaved(gens):
            gens = list(gens)
            while gens:
                for g_ in list(gens):
                    try:
                        next(g_)
                    except StopIteration:
                        gens.remove(g_)

        def sb_projc(g, l2, fc):
            half, fcl = divmod(fc, 4)
            if fcl == 0:
                P.op("pool", lambda e: e.dma_start(out=v3(wq[0], 8), in_=wview(w_inb_d[l2], half * 512, 512)),
                     writes=["wq0"], dma=True)
                yield
                P.op("pool", lambda e: e.dma_start(out=v3(wq[1], 8), in_=wview(w_inb_d[l2], D + half * 512, 512)),
                     writes=["wq1"], dma=True)
                yield
            for c in range(8):
                P.op("pe", lambda e, c=c: e.matmul(
                    ps[4], wq[0][:, c * 512 + fcl * 128:c * 512 + (fcl + 1) * 128], hT[:, c * 512:(c + 1) * 512],
                    start=(c == 0), stop=(c == 7)), reads=["wq0", "hT"], writes=["ps4"])
                yield
            tm = tmpn[0]
            P.op("act", lambda e: e.copy(out=tm, in_=ps[4]), reads=["ps4"], writes=["tmpn0"])
            yield
            for c in range(8):
                P.op("pe", lambda e, c=c: e.matmul(
                    ps[5], wq[1][:, c * 512 + fcl * 128:c * 512 + (fcl + 1) * 128], hT[:, c * 512:(c + 1) * 512],
                    start=(c == 0), stop=(c == 7)), reads=["wq1", "hT"], writes=["ps5"])
                yield
            P.op("act", lambda e: e.activation(out=sqb[:, 0:512], in_=tm, func=AF.Square), reads=["tmpn0"], writes=["sqb0"])
            yield
            P.op("pe", lambda e: e.matmul(ps[7], cm(C_BDONES, True), sqb[:, 0:512], start=True, stop=True), reads=["sqb0", "cstb"], writes=["ps7"])
            yield
            P.op("act", lambda e: e.activation(out=rn2, in_=ps[7], func=AF.Ln, bias=EPS, scale=1.0 / 64), reads=["ps7"], writes=["tmpn1"])
            yield
            P.op("act", lambda e: e.activation(out=rn2, in_=rn2, func=AF.Exp, scale=-0.5, bias=float(np.log(0.125))), reads=["tmpn1"], writes=["tmpn1"])
            yield
            P.op("dve", lambda e: e.scalar_tensor_tensor(out=qn[:, fc * 512:(fc + 1) * 512], in0=tm, scalar=spm[:, SP_QG + l2:SP_QG + l2 + 1], in1=rn2,
                                                         op0=ALU.mult, op1=ALU.mult), reads=["tmpn0", "spm", "tmpn1"], writes=[f"qn{fc}"])
            yield
            P.op("act", lambda e: e.activation(out=ogT[:, fc * 512:(fc + 1) * 512], in_=ps[5], func=AF.Silu),
                 reads=["ps5"], writes=[f"og{fc}"])
            yield

        def emit_sb_layer(l2):
            L = 2 + l2
            for g in range(NB):
                emit_norm_block(g, L, 24 * L)
                drain(sb_projc(g, l2, 0))
                for ch in range(8):
                    P.op("dve", lambda e: e.memset(ps[6], 0.0), writes=["ps6"])
                    gens = [sb_head(g, ch, 0, 0), sb_head(g, ch, 1, 1)]
                    if ch + 1 < 8:
                        gens.append(sb_projc(g, l2, ch + 1))
                    run_interleaved(gens)
                    P.op("dve", lambda e, ch=ch: e.tensor_tensor(out=ogT[:, ch * 512:(ch + 1) * 512], in0=ps[6], in1=ogT[:, ch * 512:(ch + 1) * 512],
                                                                 op=ALU.mult), reads=["ps6", f"og{ch}"], writes=[f"og{ch}"])
                emit_outproj(w_outb_d[l2], g, L, og_toks=[f"og{i}" for i in range(8)])

        emit_kv()
        for l2 in range(nlayers - 2):
            emit_sb_layer(l2)

    outs = []
    for n in range(16):
        tb = n // 4
        for half in range(2):
            bk = (2 * n + half) % 4
            oi = (2 * n + half) % 2
            for c4 in range(4):
                c = half * 4 + c4
                P.op("pe", lambda e, n=n, c=c, c4=c4, bk=bk: e.transpose(
                    ps[bk][:, c4 * 128:(c4 + 1) * 128], xT[:, c * T + n * 128:c * T + (n + 1) * 128], cm(C_IDENT)),
                    reads=[f"xT{tb}", "cst"], writes=[f"ps{bk}"])
            if half == 0:
                P.op("act", lambda e, oi=oi, bk=bk: e.copy(out=ost2[oi], in_=ps[bk]), reads=[f"ps{bk}"], writes=[f"E{oi}"])
            else:
                P.op("dve", lambda e, oi=oi, bk=bk: e.tensor_copy(out=ost2[oi], in_=ps[bk]), reads=[f"ps{bk}"], writes=[f"E{oi}"])
            outs.append(P.op("sp", lambda e, n=n, half=half, oi=oi: e.dma_start(
                out=out_d[n * 128:(n + 1) * 128, half * 512:(half + 1) * 512], in_=ost2[oi]),
                reads=[f"E{oi}"], dma=True))
    P.finalize(final_wait_ops=outs)
    return nc


def _prep_inputs(inp):
    inp = {k: np.asarray(v) for k, v in inp.items()}
    w_in_a = inp["w_in_a"].astype(np.float32, copy=False)
    qkvz = w_in_a[:, :, :4096].reshape(2, D, 4, 8, 128).transpose(0, 1, 3, 2, 4).reshape(2, D, 4096)
    shared = {
        "cst": _consts(),
        "w_ada": np.ascontiguousarray(inp["w_ada"], np.float32),
        "w_inp": np.ascontiguousarray(qkvz),
        "w_ba": np.ascontiguousarray(w_in_a[:, :, 4096:4112]),
        "w_out_a": np.ascontiguousarray(inp["w_out_a"], np.float32),
        "w_ada_kv": np.ascontiguousarray(inp["w_ada_kv"], np.float32),
        "w_kv": np.ascontiguousarray(inp["w_kv"], np.float32),
        "w_in_b": np.ascontiguousarray(inp["w_in_b"], np.float32),
        "w_out_b": np.ascontiguousarray(inp["w_out_b"], np.float32),
    }
    in_maps = []
    for b in range(8):
        m = dict(shared)
        m["x"] = np.ascontiguousarray(inp["x"][b], np.float32)
        m["spm"] = _small_params(inp, b)
        in_maps.append(m)
    return in_maps


def kernel(**inputs):
    in_maps = _prep_inputs(inputs)
    nc = build(4)
    res = run_bass_kernel_spmd(nc, in_maps, core_ids=list(range(8)))
    return np.stack([np.asarray(r["out"], np.float32) for r in res.results], axis=0)
```
